# Optimizing a Trainium2 kernel written in Bass

```python
import math
import jax, jax.numpy as jnp
from jax import lax
import numpy as np


D_MODEL = 2048
BATCH = 4
SEQ = 2048
DEPTH = 1
DEC_BATCH = 128
DEC_SEQ = 1
PAST_LEN = 16384
PAGE_SIZE = 128

N_MEM = 256
D_FF = int(round(8 * D_MODEL / 3 / 128)) * 128
POOL_WINDOWS = (2, 4, 8, 16)
N_POOL_GROUPS = len(POOL_WINDOWS)
POOL_W = D_MODEL // 4
POOL_GROUP_DIM = POOL_W // N_POOL_GROUPS
POOL_BUF = max(POOL_WINDOWS) - 1
RWKV_W = D_MODEL // 2
RWKV_HEAD = 64
RWKV_HEADS = RWKV_W // RWKV_HEAD
DECAY_LORA = 64
AAA_LORA = 64
GATE_LORA = 160
RWKV_PROJ_W = 3 * RWKV_W + DECAY_LORA + AAA_LORA + GATE_LORA
XA_W = D_MODEL // 4
XA_HEADS = 4
XA_HEAD_DIM = XA_W // XA_HEADS
N_BRANCH = 3
IN_W = POOL_W + RWKV_PROJ_W + XA_W + N_BRANCH * D_MODEL
RMS_EPS = 1e-6
GN_EPS = 64e-5

kernel_name = "gated_pool_rwkv7_memxattn_macaron_step"


def rmsnorm(x, g):
    xf = x.astype(jnp.float32)
    y = xf * lax.rsqrt(jnp.mean(xf * xf, axis=-1, keepdims=True) + RMS_EPS)
    return (y * g.astype(jnp.float32)).astype(x.dtype)


def swiglu_ffn(x, w_gate, w_up, w_down):
    return (jax.nn.silu(x @ w_gate) * (x @ w_up)) @ w_down


def pool_branch(zp, buf, start, group_w, scale, out_w):
    b, t, _ = zp.shape
    full = jnp.concatenate([buf.astype(zp.dtype), zp], axis=1).astype(jnp.float32)
    csum = jnp.concatenate([jnp.zeros((b, 1, POOL_W), jnp.float32), jnp.cumsum(full, axis=1)], axis=1)
    end = csum[:, POOL_BUF + 1:POOL_BUF + 1 + t]
    pos = start + jnp.arange(t)
    means = []
    for gi, w in enumerate(POOL_WINDOWS):
        sl = slice(gi * POOL_GROUP_DIM, (gi + 1) * POOL_GROUP_DIM)
        begin = csum[:, POOL_BUF + 1 - w:POOL_BUF + 1 - w + t, sl]
        cnt = jnp.minimum(pos + 1, w).astype(jnp.float32)[None, :, None]
        means.append((end[..., sl] - begin) / cnt)
    pooled = jnp.concatenate(means, axis=-1) - full[:, POOL_BUF:]
    grp = pooled.reshape(b, t, N_POOL_GROUPS, POOL_GROUP_DIM).astype(zp.dtype)
    mixed = jnp.einsum('btgc,gcd->btgd', grp, group_w).reshape(b, t, POOL_W) * scale
    return mixed @ out_w, full[:, -POOL_BUF:].astype(zp.dtype)


def wkv_step(s, inp):
    r, w, k, v, a, bb = inp
    sa = jnp.einsum('bhij,bhj->bhi', s, a)
    s = s * w[:, :, None, :] + sa[..., None] * bb[:, :, None, :] + v[..., None] * k[:, :, None, :]
    y = jnp.einsum('bhij,bhj->bhi', s, r)
    return s, y


def rwkv7_branch(zr, shift_buf, s0, mu, w0, w_up, a0, a_up, g_up, k_k, k_a, r_k, ln_g, ln_b, out_w):
    f32 = jnp.float32
    b, t, _ = zr.shape
    prev = jnp.concatenate([shift_buf.astype(zr.dtype), zr[:, :-1]], axis=1)
    xm = zr + (prev - zr) * mu
    r, k, v, wl, al, gl = jnp.split(
        xm, [RWKV_W, 2 * RWKV_W, 3 * RWKV_W, 3 * RWKV_W + DECAY_LORA, 3 * RWKV_W + DECAY_LORA + AAA_LORA], axis=-1)
    w_log = -jax.nn.softplus(-(w0 + jnp.tanh(wl) @ w_up).astype(f32)) - 0.5
    decay = jnp.exp(-jnp.exp(w_log))
    a = jax.nn.sigmoid((a0 + al @ a_up).astype(f32))
    g = jax.nn.sigmoid(gl) @ g_up
    heads = lambda x: x.reshape(b, t, RWKV_HEADS, RWKV_HEAD)
    kf = k.astype(f32)
    kk = heads(kf * k_k.astype(f32))
    kk = kk * lax.rsqrt(jnp.maximum(jnp.sum(kk * kk, axis=-1, keepdims=True), 1e-24))
    kmod = heads(kf * (1.0 + (a - 1.0) * k_a.astype(f32)))
    rh, vh, wh, ah = heads(r.astype(f32)), heads(v.astype(f32)), heads(decay), heads(a)
    xs = tuple(jnp.swapaxes(x, 0, 1) for x in (rh, wh, kmod, vh, -kk, kk * ah))
    s_final, ys = lax.scan(wkv_step, s0.astype(f32), xs)
    y = jnp.swapaxes(ys, 0, 1)
    mean = jnp.mean(y, axis=-1, keepdims=True)
    var = jnp.mean(jnp.square(y - mean), axis=-1, keepdims=True)
    y = ((y - mean) * lax.rsqrt(var + GN_EPS)).reshape(b, t, RWKV_W) * ln_g.astype(f32) + ln_b.astype(f32)
    bonus = jnp.sum(rh * kmod * r_k.astype(f32), axis=-1, keepdims=True) * vh
    y = (y + bonus.reshape(b, t, RWKV_W)).astype(zr.dtype) * g
    return y @ out_w, s_final.astype(zr.dtype), zr[:, -1:]


def mem_kv(mem, g, wk, wv):
    b = mem.shape[0]
    m = rmsnorm(mem, g)
    return ((m @ wk).reshape(b, N_MEM, XA_HEADS, XA_HEAD_DIM),
            (m @ wv).reshape(b, N_MEM, XA_HEADS, XA_HEAD_DIM))


def cross_attn(zq, mk, mv, out_w):
    b, t, _ = zq.shape
    q = zq.reshape(b, t, XA_HEADS, XA_HEAD_DIM)
    s = jnp.einsum('bthd,bmhd->bhtm', q, mk.astype(zq.dtype)).astype(jnp.float32) * (XA_HEAD_DIM ** -0.5)
    p = jax.nn.softmax(s, axis=-1).astype(zq.dtype)
    o = jnp.einsum('bhtm,bmhd->bthd', p, mv.astype(zq.dtype)).reshape(b, t, XA_W)
    return o @ out_w


def setup_inputs(seed: int = 0) -> dict:
    key = jax.random.key(seed)
    ks = iter(jax.random.split(key, 64))
    L, D, F = DEPTH, D_MODEL, D_FF

    def nrm(shape, scale=1.0):
        return jax.random.normal(next(ks), shape, jnp.float32) * scale

    def unif(shape):
        return jax.random.uniform(next(ks), shape, jnp.float32)

    return {
        'x_prompt': nrm((BATCH, SEQ, D)),
        'x_sample': nrm((DEC_BATCH, DEC_SEQ, D)),
        'mem_prompt': nrm((BATCH, N_MEM, D)),
        'cache_mem_k': nrm((L, DEC_BATCH, N_MEM, XA_HEADS, XA_HEAD_DIM)),
        'cache_mem_v': nrm((L, DEC_BATCH, N_MEM, XA_HEADS, XA_HEAD_DIM)),
        'state_wkv': nrm((L, DEC_BATCH, RWKV_HEADS, RWKV_HEAD, RWKV_HEAD), 0.5),
        'state_shift': nrm((L, DEC_BATCH, 1, RWKV_PROJ_W)),
        'state_pool': nrm((L, DEC_BATCH, POOL_BUF, POOL_W)),
        'ffn1_norm_g': 1.0 + nrm((L, D), 0.02),
        'ffn1_w_gate': nrm((L, D, F), D ** -0.5),
        'ffn1_w_up': nrm((L, D, F), D ** -0.5),
        'ffn1_w_down': nrm((L, F, D), F ** -0.5),
        'mix_norm_g': 1.0 + nrm((L, D), 0.02),
        'w_in': nrm((L, D, IN_W), D ** -0.5),
        'pool_group_w': nrm((L, N_POOL_GROUPS, POOL_GROUP_DIM, POOL_GROUP_DIM), POOL_GROUP_DIM ** -0.5),
        'pool_scale': 1.0 + nrm((L, POOL_W), 0.02),
        'pool_out': nrm((L, POOL_W, D), POOL_W ** -0.5),
        'rwkv_mu': unif((L, RWKV_PROJ_W)),
        'rwkv_w0': nrm((L, RWKV_W), 0.5),
        'rwkv_w_up': nrm((L, DECAY_LORA, RWKV_W), 0.5 * DECAY_LORA ** -0.5),
        'rwkv_a0': nrm((L, RWKV_W), 0.1),
        'rwkv_a_up': nrm((L, AAA_LORA, RWKV_W), 0.5 * AAA_LORA ** -0.5),
        'rwkv_g_up': nrm((L, GATE_LORA, RWKV_W), GATE_LORA ** -0.5),
        'rwkv_k_k': 1.0 + nrm((L, RWKV_W), 0.1),
        'rwkv_k_a': 1.0 + nrm((L, RWKV_W), 0.1),
        'rwkv_r_k': nrm((L, RWKV_HEADS, RWKV_HEAD), 0.1),
        'rwkv_ln_g': 1.0 + nrm((L, RWKV_W), 0.02),
        'rwkv_ln_b': nrm((L, RWKV_W), 0.01),
        'rwkv_out': nrm((L, RWKV_W, D), RWKV_W ** -0.5),
        'mem_norm_g': 1.0 + nrm((L, D), 0.02),
        'w_mem_k': nrm((L, D, XA_W), D ** -0.5),
        'w_mem_v': nrm((L, D, XA_W), D ** -0.5),
        'xattn_out': nrm((L, XA_W, D), XA_W ** -0.5),
        'w_o': nrm((L, D, D), D ** -0.5),
        'ffn2_norm_g': 1.0 + nrm((L, D), 0.02),
        'ffn2_w_gate': nrm((L, D, F), D ** -0.5),
        'ffn2_w_up': nrm((L, D, F), D ** -0.5),
        'ffn2_w_down': nrm((L, F, D), F ** -0.5),
        'final_norm_g': 1.0 + nrm((D,), 0.02),
    }


def reference(x_prompt, x_sample, mem_prompt, cache_mem_k, cache_mem_v, state_wkv, state_shift, state_pool,
              ffn1_norm_g, ffn1_w_gate, ffn1_w_up, ffn1_w_down, mix_norm_g, w_in,
              pool_group_w, pool_scale, pool_out,
              rwkv_mu, rwkv_w0, rwkv_w_up, rwkv_a0, rwkv_a_up, rwkv_g_up, rwkv_k_k, rwkv_k_a, rwkv_r_k,
              rwkv_ln_g, rwkv_ln_b, rwkv_out,
              mem_norm_g, w_mem_k, w_mem_v, xattn_out, w_o,
              ffn2_norm_g, ffn2_w_gate, ffn2_w_up, ffn2_w_down, final_norm_g):
    splits = [POOL_W, POOL_W + RWKV_PROJ_W, POOL_W + RWKV_PROJ_W + XA_W]

    def block(h, mk, mv, s0, shift_buf, pool_buf, start, l):
        b, t, _ = h.shape
        h = h + 0.5 * swiglu_ffn(rmsnorm(h, ffn1_norm_g[l]), ffn1_w_gate[l], ffn1_w_up[l], ffn1_w_down[l])
        u = rmsnorm(h, mix_norm_g[l])
        z = u @ w_in[l]
        zp, zr, zq, zg = jnp.split(z, splits, axis=-1)
        out_a, new_pool = pool_branch(zp, pool_buf, start, pool_group_w[l], pool_scale[l], pool_out[l])
        out_b, new_wkv, new_shift = rwkv7_branch(
            zr, shift_buf, s0, rwkv_mu[l], rwkv_w0[l], rwkv_w_up[l], rwkv_a0[l], rwkv_a_up[l], rwkv_g_up[l],
            rwkv_k_k[l], rwkv_k_a[l], rwkv_r_k[l], rwkv_ln_g[l], rwkv_ln_b[l], rwkv_out[l])
        out_c = cross_attn(zq, mk, mv, xattn_out[l])
        gates = jax.nn.sigmoid(zg.astype(jnp.float32)).astype(h.dtype).reshape(b, t, N_BRANCH, D_MODEL)
        merged = gates[:, :, 0] * out_a + gates[:, :, 1] * out_b + gates[:, :, 2] * out_c
        h = h + merged @ w_o[l]
        h = h + 0.5 * swiglu_ffn(rmsnorm(h, ffn2_norm_g[l]), ffn2_w_gate[l], ffn2_w_up[l], ffn2_w_down[l])
        return h, new_wkv, new_shift, new_pool

    hp, hs = x_prompt, x_sample
    bp = x_prompt.shape[0]
    mk_p_l, mv_p_l, wkv_p_l, sh_p_l, pl_p_l = [], [], [], [], []
    wkv_s_l, sh_s_l, pl_s_l = [], [], []
    for l in range(DEPTH):
        mk_p, mv_p = mem_kv(mem_prompt, mem_norm_g[l], w_mem_k[l], w_mem_v[l])
        s0p = jnp.zeros((bp, RWKV_HEADS, RWKV_HEAD, RWKV_HEAD), jnp.float32)
        shift0 = jnp.zeros((bp, 1, RWKV_PROJ_W), hp.dtype)
        pool0 = jnp.zeros((bp, POOL_BUF, POOL_W), hp.dtype)
        hp, wkv_p, sh_p, pl_p = block(hp, mk_p, mv_p, s0p, shift0, pool0, 0, l)
        hs, wkv_s, sh_s, pl_s = block(hs, cache_mem_k[l], cache_mem_v[l], state_wkv[l], state_shift[l],
                                      state_pool[l], PAST_LEN, l)
        mk_p_l.append(mk_p); mv_p_l.append(mv_p); wkv_p_l.append(wkv_p); sh_p_l.append(sh_p); pl_p_l.append(pl_p)
        wkv_s_l.append(wkv_s); sh_s_l.append(sh_s); pl_s_l.append(pl_s)
    y_prompt = rmsnorm(hp, final_norm_g)
    y_sample = rmsnorm(hs, final_norm_g)
    return (y_prompt, y_sample, jnp.stack(mk_p_l), jnp.stack(mv_p_l), jnp.stack(wkv_p_l), jnp.stack(sh_p_l),
            jnp.stack(pl_p_l), jnp.stack(wkv_s_l), jnp.stack(sh_s_l), jnp.stack(pl_s_l))
```

```python
import numpy as np
from contextlib import ExitStack
import concourse.bass as bass
import concourse.mybir as mybir
from concourse.bass_utils import run_bass_kernel_spmd

F32 = mybir.dt.float32
BF16 = mybir.dt.bfloat16
AF = mybir.ActivationFunctionType
ALU = mybir.AluOpType
AX = mybir.AxisListType

D = 2048
F = 5504
KT = 16
FC = 43
NMEM = 256
RW = 1024
RPW = 3360
INW = 10528
C_POOL, C_R, C_XQ, C_G = 0, 512, 3872, 4384
NTB = 256
NS = 16
NTILE = 8
EDEC = float(np.exp(-0.5))
RMS_EPS = 1e-6
GN_EPS = 64e-5
EPOCH = 20000
NDSEM = 6


class Prog:
    COMPUTE = ("tensor", "vector", "scalar", "gpsimd")

    def __init__(self, nc, stack):
        self.nc = nc
        self.stack = stack
        self.eng = {"tensor": nc.tensor, "vector": nc.vector, "scalar": nc.scalar,
                    "gpsimd": nc.gpsimd, "sync": nc.sync}
        self.seq = {e: 0 for e in self.COMPUTE}
        self.esems = {e: [] for e in self.COMPUTE}
        self.dq = {}
        for q in ("sync", "scalar", "gpsimd"):
            self.dq[q] = {"n": 0, "sems": [self._sem(f"d_{q}_{i}") for i in range(NDSEM)]}
        self.seen = {e: {} for e in self.eng}
        self.state = {}
        self.subs = {}
        self.same_engine_sync = {"vector": True, "scalar": True, "gpsimd": True, "tensor": False}
        self.last_ev = {}

    def _sem(self, name):
        return self.stack.enter_context(self.nc.semaphore(name))

    def _esem(self, e, epoch):
        while len(self.esems[e]) <= epoch:
            self.esems[e].append(self._sem(f"e_{e}_{len(self.esems[e])}"))
        return self.esems[e][epoch]

    def _related(self, k):
        if isinstance(k, tuple):
            return [k, k[0]]
        out = [k]
        out.extend(self.subs.get(k, ()))
        return out

    def _deps(self, reads, writes):
        evs = []
        for k in reads:
            for kk in self._related(k):
                st = self.state.get(kk)
                if st and st[0] is not None:
                    evs.append(st[0])
        for k in writes:
            for kk in self._related(k):
                st = self.state.get(kk)
                if st:
                    if st[0] is not None:
                        evs.append(st[0])
                    evs.extend(st[1])
        return evs

    def _record(self, ev, reads, writes):
        for k in reads:
            if isinstance(k, tuple):
                self.subs.setdefault(k[0], set()).add(k)
            self.state.setdefault(k, [None, []])[1].append(ev)
        for k in writes:
            if isinstance(k, tuple):
                self.subs.setdefault(k[0], set()).add(k)
            self.state[k] = [ev, []]
            if not isinstance(k, tuple):
                for kk in self.subs.get(k, ()):
                    self.state[kk] = [ev, []]

    def _emit_waits(self, e, evs):
        need = {}
        for (semkey, sem, val, src) in evs:
            if src == e and not self.same_engine_sync.get(e, True):
                continue
            if self.seen[e].get(semkey, 0) >= val:
                continue
            if semkey not in need or need[semkey][1] < val:
                need[semkey] = (sem, val)
        for semkey, (sem, val) in need.items():
            self.eng[e].wait_ge(sem, val)
            self.seen[e][semkey] = val
            if semkey[0] == "E":
                for ep in range(semkey[2]):
                    self.seen[e][("E", semkey[1], ep)] = EPOCH

    def op(self, e, fn, reads=(), writes=()):
        evs = self._deps(reads, writes)
        self._emit_waits(e, evs)
        ins = fn(self.eng[e])
        s = self.seq[e]
        epoch, idx = divmod(s, EPOCH)
        sem = self._esem(e, epoch)
        ins.then_inc(sem, 1)
        self.seq[e] = s + 1
        ev = (("E", e, epoch), sem, idx + 1, e)
        self.last_ev[e] = ev
        self._record(ev, reads, writes)
        return ins

    def group(self, e, fns, reads=(), writes=()):
        evs = self._deps(reads, writes)
        self._emit_waits(e, evs)
        for fn in fns[:-1]:
            fn(self.eng[e])
        ins = fns[-1](self.eng[e])
        s = self.seq[e]
        epoch, idx = divmod(s, EPOCH)
        sem = self._esem(e, epoch)
        ins.then_inc(sem, 1)
        self.seq[e] = s + 1
        ev = (("E", e, epoch), sem, idx + 1, e)
        self.last_ev[e] = ev
        self._record(ev, reads, writes)
        return ins

    def dma(self, q, out, in_, reads=(), writes=(), **kw):
        d = self.dq[q]
        i = d["n"]
        slot, rnd = i % NDSEM, i // NDSEM
        sem = d["sems"][slot]
        evs = self._deps(reads, writes)
        if rnd > 0:
            evs.append((("D", q, slot), sem, 16 * rnd, None))
        self._emit_waits(q, evs)
        ins = self.eng[q].dma_start(out=out, in_=in_, **kw)
        ins.then_inc(sem, 16)
        d["n"] = i + 1
        ev = (("D", q, slot), sem, 16 * (rnd + 1), None)
        self._record(ev, reads, writes)
        return ins

    def _dma_events(self):
        evs = []
        for q, d in self.dq.items():
            n = d["n"]
            for slot in range(NDSEM):
                cnt = (n - slot + NDSEM - 1) // NDSEM if n > slot else 0
                if cnt > 0:
                    evs.append((("D", q, slot), d["sems"][slot], 16 * cnt, None))
        return evs

    def fence(self):
        evs = list(self.last_ev.values()) + self._dma_events()
        for e in self.eng:
            self._emit_waits(e, [ev for ev in evs if not (ev[3] == e and e == "tensor")])
        self.state = {}
        self.subs = {}

    def finish(self):
        for q in self.dq:
            self._emit_waits(q, [ev for ev in self._dma_events() if ev[0][1] == q])


IN_SHAPES = {}


def build_nc():
    nc = bass.Bass("TRN2", target_bir_lowering=False)
    din = {}
    dout = {}

    import os
    _small = os.environ.get("K_TEST", "")
    _need = set(os.environ.get("K_NEED", "memT,wmk,wmv,vecs,rows,cst,wup,aup,gup,pgw,spool,xT").split(","))

    def I(name, shape, dt=F32):
        if _small and name not in _need:
            shape = [1, 8]
        IN_SHAPES[name] = list(shape)
        din[name] = nc.dram_tensor(name, list(shape), dt, kind="ExternalInput").ap()
        return din[name]

    def O(name, shape):
        dout[name] = nc.dram_tensor(name, list(shape), F32, kind="ExternalOutput").ap()
        return dout[name]

    NTOK = 2048 + NS
    xT = I("xT", [D, NTOK])
    memT = I("memT", [D, NMEM])
    kc = I("kc", [NS, NMEM, 512])
    vc = I("vc", [NS, NMEM, 512])
    swkv = I("swkv", [64, NS, 16, 64])
    sshiftT = I("sshiftT", [27 * 128, NS])
    spoolT = I("spoolT", [512, NS, 15])
    spool = I("spool", [NS, 15, 512])
    f1g = I("f1g", [D, F]); f1u = I("f1u", [D, F]); f1d = I("f1d", [F, D])
    f2g = I("f2g", [D, F]); f2u = I("f2u", [D, F]); f2d = I("f2d", [F, D])
    win = I("win", [D, INW])
    pgw = I("pgw", [4, 128, 128]); pout = I("pout", [512, D])
    wup = I("wup", [64, RW]); aup = I("aup", [64, RW]); gup = I("gup", [160, RW])
    rout = I("rout", [RW, D])
    wmk = I("wmk", [D, 512]); wmv = I("wmv", [D, 512]); xout = I("xout", [512, D])
    wo = I("wo", [D, D])
    vecs = I("vecs", [128, 176])
    rows = I("rows", [1, 2048])
    cst = I("cst", [128, 3584])
    invc = I("invc", [4, 4 * (NTB + NS)])

    yT = O("yT", [128, KT, 1024 + NS])
    o_memkT = O("memkT", [128, 4, NMEM])
    o_memv = O("memv", [128, 2, 512])
    o_wkvp = O("wkvp", [128, 8 * 64])
    o_shiftT = O("shiftT", [128, 27, 1 + NS])
    o_poolpT = O("poolpT", [128, 4, 15])
    o_wkvs = O("wkvs", [64, NS, 16, 64])
    o_poolsn = O("poolsn", [128, 4, NS])
    o_poolso = O("poolso", [NS, 14, 512])

    scr_rows = nc.dram_tensor("scr_rows", [6, NS, 1024], F32, kind="Internal").ap()
    scr_y = nc.dram_tensor("scr_y", [NS, 1024], F32, kind="Internal").ap()
    scr_q = nc.dram_tensor("scr_q", [NS, 512], F32, kind="Internal").ap()

    with ExitStack() as st:
        P = Prog(nc, st)
        sb = lambda name, shape, dt=F32: st.enter_context(nc.sbuf_tensor(name, list(shape), dt))
        NTM = NTB + NS

        cst_sb = sb("cst_sb", [128, 3584])
        P.dma("sync", cst_sb[:], cst, writes=["cst"])
        ident = cst_sb[:, 0:128]
        triI = cst_sb[:, 128:256]
        triX = cst_sb[:, 256:384]
        triS = cst_sb[:, 384:512]
        bones = cst_sb[:, 512:640]
        bones64 = cst_sb[:, 640:768]
        mSU = cst_sb[:, 768:1280]
        mSL = cst_sb[:, 1280:1792]
        mIU = cst_sb[:, 1792:2304]
        I4 = cst_sb[:, 2304:2816]
        ones_row = cst_sb[0:1, 2816:3072]
        cbf = sb("cbf", [128, 256], BF16)
        P.dma("gpsimd", cbf[:], cst[:, 3072:3328], writes=["cbf"])
        ident_bf = cbf[:, 0:128]
        ones_bf = cbf[:, 128:256]
        vec = sb("vec", [128, 176])
        P.dma("sync", vec[:], vecs, writes=["vec"])
        G_F1, G_MIX, G_MEM, G_F2, G_FIN = 0, 16, 32, 48, 64
        V_MU, V_PS = 80, 107
        V_KK, V_KA, V_RK, V_LG, V_LB = 111, 119, 127, 135, 143
        rows_sb = sb("rows_sb", [1, 2048])
        P.dma("sync", rows_sb[:], rows, writes=["rows"])
        lora = sb("lora", [128, RW])
        P.dma("sync", lora[0:64, :], wup, writes=[("lora", 0)])
        P.dma("sync", lora[64:128, :], aup, writes=[("lora", 1)])
        gupb = sb("gupb", [128, 2, RW], BF16)
        P.dma("gpsimd", gupb[:, 0, :], gup[0:128, :], writes=[("gupb", 0)])
        P.dma("gpsimd", gupb[0:32, 1, :], gup[128:160, :], writes=[("gupb", 1)])
        pgwb = sb("pgwb", [128, 4, 128], BF16)
        P.dma("gpsimd", pgwb[:], pgw.rearrange("g c d -> c g d"), writes=["pgwb"])

        xh = sb("xh", [128, KT, NTM])
        xn = sb("xn", [128, KT, NTM], BF16)
        zr = sb("zr", [128, 27, 1 + NTM])
        zp = sb("zp", [128, 4, 15 + NTM])
        zq = sb("zq", [128, 4, NTM], BF16)
        mix = sb("mix", [128, 4, NTM], BF16)
        yg = sb("yg", [128, 8, NTM], BF16)
        oT = sb("oT", [128, 4, NTM], BF16)
        S32 = sb("S32", [128, 8, 64])
        Sbf = sb("Sbf", [128, 8, 64], BF16)
        KTb = sb("KTb", [128, 4, NMEM], BF16)
        Vb = sb("Vb", [128, 2, 512], BF16)
        rstd = sb("rstd", [128, NTM])
        tmpA = sb("tmpA", [128, NTM])
        tmpB = sb("tmpB", [128, NTM], BF16)
        icnt = sb("icnt", [128, 4, NTM])
        NWB = 3
        wbuf = [sb(f"wbuf{i}", [128, 4096], BF16) for i in range(NWB)]
        ARENA = 16384
        arena = sb("arena", [128, ARENA])
        psb = [st.enter_context(nc.psum_tensor(f"pb{i}", [128, 512], F32)) for i in range(8)]
        wctr = [0]
        pctr = [0]

        def ps():
            i = pctr[0] % 8
            pctr[0] += 1
            return psb[i], f"pb{i}"

        wcache = {}
        USE_CACHE = os.environ.get("K_CACHE", "1") == "1"

        def wload(W, r0, nrows, c0, ncols, kparts=128):
            i = wctr[0] % NWB
            wctr[0] += 1
            k = nrows // kparts
            view = wbuf[i][0:kparts, 0:k * ncols].rearrange("p (k n) -> p k n", n=ncols)
            name = W.tensor.name
            blk_id = (r0, nrows, c0, ncols)
            if USE_CACHE and name not in wcache:
                wcache[name] = (nc.dram_tensor("bfc_" + name, list(W.shape), BF16, kind="Internal").ap(), set())
            if USE_CACHE and blk_id in wcache[name][1]:
                csrc = wcache[name][0][r0:r0 + nrows, c0:c0 + ncols].rearrange("(k p) n -> p k n", p=kparts)
                if os.environ.get("K_NODMA") != "1":
                    P.dma("sync", view, csrc, writes=[f"wbuf{i}"])
            else:
                src = W[r0:r0 + nrows, c0:c0 + ncols].rearrange("(k p) n -> p k n", p=kparts)
                P.dma("gpsimd", view, src, writes=[f"wbuf{i}"])
                if USE_CACHE and name in CACHED:
                    cdst = wcache[name][0][r0:r0 + nrows, c0:c0 + ncols].rearrange("(k p) n -> p k n", p=kparts)
                    P.dma("sync", cdst, view, reads=[f"wbuf{i}"])
                    wcache[name][1].add(blk_id)
            return view, f"wbuf{i}"

        CACHED = {"f1g", "f1u", "f1d", "f2g", "f2u", "f2d", "win", "pout", "rout", "xout", "wo"}

        P.op("vector", lambda e: e.memset(S32[:], 0.0), writes=["S32"])
        P.op("vector", lambda e: e.memset(Sbf[:], 0.0), writes=["Sbf"])
        P.op("vector", lambda e: e.memset(zr[:, :, 0:1], 0.0), writes=["zr"])
        P.op("vector", lambda e: e.memset(zp[:, :, 0:15], 0.0), writes=["zp"])

        def rmsnorm(src, gcol, nt, dst, srckey, dstkey, kt=KT):
            sq = arena[:, 14208:14208 + (kt * nt + 1) // 2].bitcast(BF16)[:, 0:kt * nt].rearrange("p (k n) -> p k n", n=nt)
            pt, pk = ps()
            P.op("scalar", lambda e: e.activation(sq, src[:, 0:kt, 0:nt], AF.Square), reads=[srckey], writes=["sq"])
            P.group("tensor", [lambda e, k=k: e.matmul(pt[:, 0:nt], lhsT=ones_bf, rhs=sq[:, k, :],
                                                        start=(k == 0), stop=(k == kt - 1)) for k in range(kt)],
                    reads=["sq", "cbf"], writes=[pk])
            P.op("scalar", lambda e: e.activation(tmpA[:, 0:nt], pt[:, 0:nt], AF.Sqrt, bias=RMS_EPS, scale=1.0 / D),
                 reads=[pk], writes=["tmpA"])
            P.op("vector", lambda e: e.reciprocal(rstd[:, 0:nt], tmpA[:, 0:nt]), reads=["tmpA"], writes=["rstd"])
            for k in range(kt):
                P.op("vector", lambda e, k=k: e.scalar_tensor_tensor(
                    out=dst[:, k, 0:nt], in0=src[:, k, 0:nt], scalar=vec[:, gcol + k:gcol + k + 1],
                    in1=rstd[:, 0:nt], op0=ALU.mult, op1=ALU.mult),
                     reads=[srckey, "rstd", "vec"], writes=[(dstkey, k)])

        def ffn(Wg, Wu, Wd, gcol, nt, keep_res):
            rmsnorm(xh, gcol, nt, xn, "xh", "xn")
            act = arena[:, 0:FC * NTM // 2].bitcast(BF16).rearrange("p (c n) -> p c n", n=NTM)
            for half in range(2):
                c_lo, c_hi = (0, 22) if half == 0 else (22, FC)
                for cb in range(c_lo, c_hi, 2):
                    ncb = min(2, c_hi - cb)
                    wg, wgk = wload(Wg, 0, D, cb * 128, ncb * 128)
                    wu, wuk = wload(Wu, 0, D, cb * 128, ncb * 128)
                    for ci in range(ncb):
                        c = cb + ci
                        pg, pgk = ps()
                        pu, puk = ps()
                        P.group("tensor", [lambda e, k=k, ci=ci: e.matmul(
                            pg[:, 0:nt], lhsT=wg[:, k, ci * 128:(ci + 1) * 128], rhs=xn[:, k, 0:nt],
                            start=(k == 0), stop=(k == KT - 1)) for k in range(KT)], reads=[wgk, "xn"], writes=[pgk])
                        P.group("tensor", [lambda e, k=k, ci=ci: e.matmul(
                            pu[:, 0:nt], lhsT=wu[:, k, ci * 128:(ci + 1) * 128], rhs=xn[:, k, 0:nt],
                            start=(k == 0), stop=(k == KT - 1)) for k in range(KT)], reads=[wuk, "xn"], writes=[puk])
                        P.op("scalar", lambda e: e.activation(tmpB[:, 0:nt], pg[:, 0:nt], AF.Silu),
                             reads=[pgk], writes=["tmpB"])
                        P.op("vector", lambda e, c=c: e.tensor_tensor(act[:, c - c_lo, 0:nt], tmpB[:, 0:nt],
                                                                       pu[:, 0:nt], op=ALU.mult),
                             reads=["tmpB", puk], writes=[("act", c)])
                nch = c_hi - c_lo
                for dcol in range(KT):
                    wd, wdk = wload(Wd, c_lo * 128, nch * 128, dcol * 128, 128)
                    pd, pdk = ps()
                    P.group("tensor", [lambda e, c=c: e.matmul(
                        pd[:, 0:nt], lhsT=wd[:, c, :], rhs=act[:, c, 0:nt],
                        start=(c == 0), stop=(c == nch - 1)) for c in range(nch)],
                            reads=[wdk, "act"], writes=[pdk])
                    P.op("vector", lambda e, dcol=dcol: e.scalar_tensor_tensor(
                        out=xh[:, dcol, 0:nt], in0=pd[:, 0:nt], scalar=0.5, in1=xh[:, dcol, 0:nt],
                        op0=ALU.mult, op1=ALU.add), reads=[pdk, ("xh", dcol)], writes=[("xh", dcol)])
                if half == 0:
                    pass
            P.fence()

        def proj(W, c0, ncols_total, rhs, rhskey, nt, kt, sink, wrows=None):
            wrows = wrows or kt * 128
            nchunks = (ncols_total + 127) // 128
            j = 0
            while j < nchunks:
                nb = min(2, nchunks - j)
                ncols = min(ncols_total - j * 128, nb * 128)
                wv, wk = wload(W, 0, wrows, c0 + j * 128, ncols)
                for ci in range(nb):
                    m = min(128, ncols - ci * 128)
                    pt, pk = ps()
                    P.group("tensor", [lambda e, k=k, ci=ci, m=m, pt=pt: e.matmul(
                        pt[0:m, 0:nt], lhsT=wv[:, k, ci * 128:ci * 128 + m], rhs=rhs[:, k, 0:nt],
                        start=(k == 0), stop=(k == kt - 1)) for k in range(kt)], reads=[wk, rhskey], writes=[pk])
                    sink(j + ci, pt, pk, m)
                j += nb

        def mem_kv():
            mx = arena[:, 0:KT * NMEM].rearrange("p (k n) -> p k n", n=NMEM)
            P.dma("sync", mx, memT.rearrange("(k p) n -> p k n", p=128), writes=["mx"])
            mxn = arena[:, 4096:4096 + KT * NMEM // 2].bitcast(BF16).rearrange("p (k n) -> p k n", n=NMEM)
            if os.environ.get("K_H2", "0") == "1":
                mxn = xn[:, :, 0:NMEM]
            rmsnorm(mx, G_MEM, NMEM, mxn, "mx", "mxn")
            if _small == "norm":
                P.fence()
                return
            kf = arena[:, 8192:8192 + 4 * NMEM].rearrange("p (k n) -> p k n", n=NMEM)

            def sink_k(j, pt, pk, m):
                P.op("vector", lambda e: e.tensor_copy(kf[:, j, :], pt[:, 0:NMEM]), reads=[pk], writes=[("kf", j)])
                if True:
                    P.op("vector", lambda e: e.tensor_copy(KTb[:, j, :], pt[:, 0:NMEM]), reads=[pk], writes=[("KTb", j)])
                else:
                    P.op("scalar", lambda e: e.activation(KTb[:, j, :], pt[:, 0:NMEM], AF.Copy), reads=[pk],
                         writes=[("KTb", j)])
            proj(wmk, 0, 512, mxn, "mxn", NMEM, KT, sink_k)
            P.dma("sync", o_memkT, kf, reads=["kf"])
            if _small == "k":
                P.fence()
                return
            vf = arena[:, 12288:12288 + 1024].rearrange("p (k n) -> p k n", n=512)
            pts = [ps(), ps()]
            for cb in range(2):
                wv, wk = wload(wmv, 0, D, cb * 256, 256)
                for mt in range(2):
                    pt, pk = pts[mt]
                    for k in range(KT):
                        P.op("tensor", lambda e, k=k, mt=mt, pt=pt, cb=cb: e.matmul(
                            pt[:, cb * 256:(cb + 1) * 256], lhsT=mxn[:, k, mt * 128:(mt + 1) * 128], rhs=wv[:, k, :],
                            start=(k == 0), stop=(k == KT - 1)), reads=[wk, "mxn"], writes=[(pk, cb)])
            for mt in range(2):
                pt, pk = pts[mt]
                P.op("vector", lambda e, mt=mt, pt=pt: e.tensor_copy(vf[:, mt, :], pt[:, :]), reads=[pk],
                     writes=[("vf", mt)])
                P.op("vector", lambda e, mt=mt, pt=pt: e.tensor_copy(Vb[:, mt, :], pt[:, :]), reads=[pk],
                     writes=[("Vb", mt)])
            P.dma("sync", o_memv, vf, reads=["vf"])
            P.fence()


        A = lambda off, n: arena[:, off:off + n]
        AB = lambda off, n: arena[:, off:off + (n + 1) // 2].bitcast(BF16)[:, 0:n]
        v3 = lambda ap, n: ap.rearrange("p (k n) -> p k n", n=n)
        SCALE = float(128 ** -0.5)
        zc = sb("zc", [128, 27, 1])
        sshift = sb("sshift", [128, 27, NS])
        PCb = sb("PCb", [128, 8])
        P.dma("sync", sshift[:], sshiftT.rearrange("(k p) n -> p k n", p=128), writes=["sshift"])
        ypb = [(psb[6], "pb6"), (psb[7], "pb7")]

        def evac(i, out, in_, reads, writes):
            if i % 2 == 0:
                P.op("vector", lambda e: e.tensor_copy(out, in_), reads=reads, writes=writes)
            else:
                P.op("scalar", lambda e: e.activation(out, in_, AF.Identity), reads=reads, writes=writes)

        def token_shift(last):
            D3 = v3(A(0, 27 * NTB), NTB)
            P.op("vector", lambda e: e.tensor_copy(zc[:], zr[:, :, NTB:NTB + 1]), reads=["zr"], writes=["zc"])
            mub = vec[:, V_MU:V_MU + 27].unsqueeze(2).to_broadcast([128, 27, NTB])
            P.op("vector", lambda e: e.tensor_tensor(D3, zr[:, :, 0:NTB], zr[:, :, 1:1 + NTB], op=ALU.subtract),
                 reads=["zr"], writes=["D3"])
            P.op("vector", lambda e: e.tensor_tensor(D3, D3, mub, op=ALU.mult), reads=["D3", "vec"], writes=["D3"])
            P.op("vector", lambda e: e.tensor_tensor(zr[:, :, 1:1 + NTB], zr[:, :, 1:1 + NTB], D3, op=ALU.add),
                 reads=["D3", "zr", "zc"], writes=["zr"])
            if last:
                Ds = v3(A(27 * NTB, 27 * NS), NS)
                mus = vec[:, V_MU:V_MU + 27].unsqueeze(2).to_broadcast([128, 27, NS])
                zs = zr[:, :, 1 + NTB:1 + NTM]
                P.op("vector", lambda e: e.tensor_tensor(Ds, sshift[:], zs, op=ALU.subtract), reads=["zr", "sshift"], writes=["Ds"])
                P.op("vector", lambda e: e.tensor_tensor(Ds, Ds, mus, op=ALU.mult), reads=["Ds", "vec"], writes=["Ds"])
                P.op("vector", lambda e: e.tensor_tensor(zs, zs, Ds, op=ALU.add), reads=["Ds", "zr"], writes=["zr"])
            P.fence()

        def rwkv_elem(cs, n, sample):
            W8 = 8 * n
            xk = zr[:, 8:16, cs]
            sg = A(0, 1024)
            asig, kkn, kmod, bbv = v3(A(1024, W8), n), v3(A(2048, W8), n), v3(A(3072, W8), n), v3(A(4096, W8), n)
            et, t1 = v3(A(5120, W8), n), v3(A(6144, W8), n)
            th = A(8192, 128)
            vcol = lambda c: vec[:, c:c + 8].unsqueeze(2).to_broadcast([128, 8, n])
            P.op("scalar", lambda e: e.activation(th[0:64, 0:n], zr[0:64, 24, cs], AF.Tanh), reads=["zr"], writes=["th"])

            def fm_lora(prow, roff, rhs_ap, sink):
                for hb in range(2):
                    pt, pk = ps()
                    for c in range(4):
                        ch = hb * 4 + c
                        P.op("tensor", lambda e, pt=pt, c=c, ch=ch: e.matmul(
                            pt[:, c * n:(c + 1) * n], lhsT=lora[prow, ch * 128:(ch + 1) * 128], rhs=rhs_ap,
                            start=True, stop=False), reads=["lora", "zr", "th"], writes=[pk])
                        P.op("tensor", lambda e, pt=pt, c=c, ch=ch: e.matmul(
                            pt[:, c * n:(c + 1) * n], lhsT=rows_sb[:, roff + ch * 128:roff + (ch + 1) * 128],
                            rhs=ones_row[:, 0:n], start=False, stop=True), reads=["rows", "cst"], writes=[pk])
                    sink(hb, v3(pt[:, 0:4 * n], n), pk)

            if sample:
                sgf = v3(A(0, W8), n)

                def sink_w(hb, p3, pk):
                    P.op("scalar", lambda e: e.activation(sgf[:, hb * 4:(hb + 1) * 4, :], p3, AF.Sigmoid), reads=[pk], writes=["sg"])
                fm_lora(slice(0, 64), 0, th[0:64, 0:n], sink_w)
                P.op("scalar", lambda e: e.activation(sgf, sgf, AF.Exp, scale=-EDEC), reads=["sg"], writes=["sg"])
            else:
                for hb in range(2):
                    pt, pk = ps()
                    P.op("tensor", lambda e, pt=pt, hb=hb: e.matmul(pt[:, :], lhsT=th[0:64, 0:128], rhs=lora[0:64, hb * 512:(hb + 1) * 512],
                                                       start=True, stop=False), reads=["th", "lora"], writes=[pk])
                    P.op("tensor", lambda e, pt=pt, hb=hb: e.matmul(pt[:, :], lhsT=ones_row[:, 0:128], rhs=rows_sb[:, hb * 512:(hb + 1) * 512],
                                                       start=False, stop=True), reads=["cst", "rows"], writes=[pk])
                    P.op("scalar", lambda e, pt=pt, hb=hb: e.activation(sg[:, hb * 512:(hb + 1) * 512], pt[:, :], AF.Sigmoid),
                         reads=[pk], writes=["sg"])

            def sink_a(hb, p3, pk):
                P.op("scalar", lambda e: e.activation(asig[:, hb * 4:(hb + 1) * 4, :], p3, AF.Sigmoid), reads=[pk], writes=["asig"])
            fm_lora(slice(64, 128), 1024, zr[64:128, 24, cs], sink_a)
            P.op("vector", lambda e: e.tensor_tensor(t1, xk, vcol(V_KK), op=ALU.mult), reads=["zr", "vec"], writes=["t1"])
            P.op("scalar", lambda e: e.activation(et, t1, AF.Square), reads=["t1"], writes=["et"])
            for hb in range(2):
                pt, pk = ps()
                for c in range(4):
                    ch = hb * 4 + c
                    P.op("tensor", lambda e, pt=pt, c=c, ch=ch: e.matmul(pt[:, c * n:(c + 1) * n], lhsT=bones, rhs=et[:, ch, :],
                                                       start=True, stop=True), reads=["cst", "et"], writes=[pk])
                P.op("vector", lambda e, pt=pt, hb=hb: e.tensor_scalar_max(kkn[:, hb * 4:(hb + 1) * 4, :], v3(pt[:, 0:4 * n], n), 1e-24),
                     reads=[pk], writes=["kkn"])
            P.op("scalar", lambda e: e.activation(kkn, kkn, AF.Sqrt), reads=["kkn"], writes=["kkn"])
            P.op("vector", lambda e: e.reciprocal(kkn, kkn), reads=["kkn"], writes=["kkn"])
            P.op("vector", lambda e: e.tensor_tensor(kkn, kkn, t1, op=ALU.mult), reads=["kkn", "t1"], writes=["kkn"])
            P.op("vector", lambda e: e.scalar_tensor_tensor(out=et, in0=asig, scalar=-1.0, in1=vcol(V_KA), op0=ALU.add, op1=ALU.mult),
                 reads=["asig", "vec", "et"], writes=["et"])
            P.op("vector", lambda e: e.scalar_tensor_tensor(out=kmod, in0=et, scalar=1.0, in1=xk, op0=ALU.add, op1=ALU.mult),
                 reads=["et", "zr"], writes=["kmod"])
            P.op("vector", lambda e: e.tensor_tensor(bbv, kkn, asig, op=ALU.mult), reads=["kkn", "asig"], writes=["bbv"])

        def scan_prompt(cs, want_y):
            n = 128
            xr, xv = zr[:, 0:8, cs], zr[:, 16:24, cs]
            sg = A(0, 1024)
            kkn, kmod, bbv = v3(A(2048, 1024), n), v3(A(3072, 1024), n), v3(A(4096, 1024), n)
            et = v3(A(5120, 1024), n)
            b3 = lambda off: v3(AB(off, 1024), n)
            Afm, Bfm, Kfm, Rfm = b3(8320), b3(8832), b3(9344), b3(9856)
            AT, BhT, KhT, VT = AB(10368, 1024), AB(10880, 1024), AB(11392, 1024), AB(11904, 1024)
            tb = b3(12416)
            hs = lambda x, hb: x[:, hb * 4:(hb + 1) * 4, :]

            def cum(tri, sink):
                for hb in range(2):
                    pt, pk = ps()
                    for c in range(4):
                        ch = hb * 4 + c
                        P.op("tensor", lambda e, pt=pt, c=c, ch=ch: e.matmul(pt[:, c * 128:(c + 1) * 128], lhsT=sg[:, ch * 128:(ch + 1) * 128], rhs=tri,
                                                           start=True, stop=True), reads=["sg", "cst"], writes=[pk])
                    sink(hb, v3(pt[:, :], n), pk)

            def sink_incl(hb, p3, pk):
                if want_y:
                    P.op("scalar", lambda e: e.activation(hs(et, hb), p3, AF.Exp), reads=[pk], writes=["et"])
                    P.op("vector", lambda e: e.tensor_tensor(hs(Rfm, hb), hs(xr, hb), hs(et, hb), op=ALU.mult),
                         reads=["et", "zr"], writes=["Rfm"])
                P.op("scalar", lambda e: e.activation(PCb[:, hb * 4:(hb + 1) * 4], p3[:, :, 127], AF.Exp), reads=[pk], writes=["PCb"])
                P.op("scalar", lambda e: e.activation(hs(et, hb), p3, AF.Exp, scale=-1.0), reads=[pk, "Rfm"], writes=["et"])
                P.op("vector", lambda e: e.tensor_tensor(hs(Bfm, hb), hs(bbv, hb), hs(et, hb), op=ALU.mult),
                     reads=["et", "bbv"], writes=["Bfm"])
                P.op("vector", lambda e: e.tensor_tensor(hs(Kfm, hb), hs(kmod, hb), hs(et, hb), op=ALU.mult),
                     reads=["et", "kmod"], writes=["Kfm"])
            cum(triI, sink_incl)
            if int(os.environ.get("K_SCAN", "99")) < 2:
                return

            def sink_excl(hb, p3, pk):
                P.op("scalar", lambda e: e.activation(hs(et, hb), p3, AF.Exp), reads=[pk, "Bfm", "Kfm"], writes=["et"])
                P.op("vector", lambda e: e.scalar_tensor_tensor(out=hs(Afm, hb), in0=hs(kkn, hb), scalar=-1.0, in1=hs(et, hb),
                                                                 op0=ALU.mult, op1=ALU.mult), reads=["et", "kkn"], writes=["Afm"])
            cum(triX, sink_excl)

            def tr_to(src3, srckey, dst, dstkey):
                for hb in range(2):
                    pt, pk = ps()
                    for c in range(4):
                        ch = hb * 4 + c
                        P.op("tensor", lambda e, pt=pt, c=c, ch=ch: e.matmul(pt[:, c * 128:(c + 1) * 128], lhsT=src3[:, ch, :], rhs=ident_bf,
                                                           start=True, stop=True), reads=[srckey, "cbf"], writes=[pk])
                    evac(hb, dst[:, hb * 512:(hb + 1) * 512], pt[:, :], [pk], [dstkey])
            tr_to(Afm, "Afm", AT, "AT")
            if int(os.environ.get("K_SCAN", "99")) < 3:
                return

            def sink_suf(hb, p3, pk):
                P.op("scalar", lambda e: e.activation(hs(et, hb), p3, AF.Exp), reads=[pk, "Afm"], writes=["et"])
                P.op("vector", lambda e: e.tensor_tensor(hs(tb, hb), hs(bbv, hb), hs(et, hb), op=ALU.mult),
                     reads=["et", "bbv"], writes=["tb"])
            cum(triS, sink_suf)
            tr_to(tb, "tb", BhT, "BhT")
            P.op("vector", lambda e: e.tensor_tensor(tb, kmod, et, op=ALU.mult), reads=["et", "kmod", "BhT", "tb"], writes=["tb"])
            tr_to(tb, "tb", KhT, "KhT")
            P.op("vector", lambda e: e.tensor_copy(tb, xv), reads=["zr", "KhT", "tb"], writes=["tb"])
            tr_to(tb, "tb", VT, "VT")
            if int(os.environ.get("K_SCAN", "99")) < 4:
                return

            GOFF = [13056, 13568, 14080, 14592, 15104, 15616, 0, 512, 1024, 1536, 5120]
            gm = lambda i: AB(GOFF[i], 1024)
            UT = AB(16128, 512)
            pos = lambda i: (i % 2) * 4 + i // 2
            blk = lambda i: slice(pos(i) * 128, (pos(i) + 1) * 128)
            yfm = v3(A(7168, 1024), n)
            P.fence()

            def mm8(dst, dk, lt, ltk, rt, rtk, add=None, addk=None):
                bank = [ps(), ps()]
                for i in range(8):
                    pt, pk = bank[i // 4]
                    P.op("tensor", lambda e, pt=pt, i=i: e.matmul(pt[:, (i % 4) * 128:(i % 4 + 1) * 128], lhsT=lt[:, i * 128:(i + 1) * 128],
                                                      rhs=rt[:, i * 128:(i + 1) * 128], start=True, stop=True),
                         reads=[ltk, rtk], writes=[pk])
                for hb in range(2):
                    pt, pk = bank[hb]
                    dk_ = (dk, hb)
                    if add is None:
                        evac(hb, dst[:, hb * 512:(hb + 1) * 512], pt[:, :], [pk], [dk_])
                    else:
                        P.op("vector", lambda e, pt=pt, hb=hb: e.tensor_tensor(dst[:, hb * 512:(hb + 1) * 512], pt[:, :],
                                                                  add[:, hb * 512:(hb + 1) * 512], op=ALU.add),
                             reads=[pk, addk], writes=[dk_])

            for G in range(2):
                heads = [8 * G + i for i in range(8)]
                hrow = lambda x3, h: x3[(h % 2) * 64:(h % 2) * 64 + 64, h // 2, :]
                NakT, Mrb, Mrk, Apf, Nakp = gm(6), gm(7), gm(8), gm(9), gm(10)
                specs = [(gm(0), "g0", Bfm, "Bfm", Afm, "Afm", mSU), (gm(1), "g1", Afm, "Afm", Bfm, "Bfm", mSL),
                         (NakT, "g6", Afm, "Afm", Kfm, "Kfm", mSL)]
                if want_y:
                    specs += [(Mrb, "g7", Bfm, "Bfm", Rfm, "Rfm", mIU), (Mrk, "g8", Kfm, "Kfm", Rfm, "Rfm", mIU)]
                for (dst, dk, la, lak, ra, rak, msk) in specs:
                    bank = [ps(), ps()]
                    for i, h in enumerate(heads):
                        pt, pk = bank[h % 2]
                        P.op("tensor", lambda e, pt=pt, i=i, h=h, la=la, ra=ra: e.matmul(
                            pt[:, (i // 2) * 128:(i // 2 + 1) * 128], lhsT=hrow(la, h), rhs=hrow(ra, h), start=True, stop=True),
                             reads=[lak, rak], writes=[pk])
                    for par in range(2):
                        pt, pk = bank[par]
                        P.op("vector", lambda e, pt=pt, dst=dst, msk=msk, par=par: e.tensor_tensor(
                            dst[:, par * 512:(par + 1) * 512], pt[:, :], msk, op=ALU.mult),
                             reads=[pk, "cst"], writes=[(dk, par)])
                Nc, Nck, Lc, Lck = gm(0), "g0", gm(1), "g1"
                Nn, Nnk, Ln, Lnk = gm(2), "g2", gm(3), "g3"
                Tc, Tck, Tn, Tnk = gm(4), "g4", gm(5), "g5"
                for hb in range(2):
                    P.op("vector", lambda e, hb=hb: e.tensor_tensor(Tc[:, hb * 512:(hb + 1) * 512], Nc[:, hb * 512:(hb + 1) * 512], I4, op=ALU.add),
                         reads=[Nck, "cst"], writes=[(Tck, hb)])
                for k in range(1, 7):
                    if k < 6:
                        mm8(Nn, Nnk, Lc, Lck, Nc, Nck)
                    mm8(Ln, Lnk, Nc, Nck, Lc, Lck)
                    mm8(Tn, Tnk, Ln, Lnk, Tc, Tck, add=Tc, addk=Tck)
                    Nc, Nck, Nn, Nnk = Nn, Nnk, Nc, Nck
                    Lc, Lck, Ln, Lnk = Ln, Lnk, Lc, Lck
                    Tc, Tck, Tn, Tnk = Tn, Tnk, Tc, Tck
                T_, Tk = Tc, Tck
                bank = [ps(), ps()]
                for i, h in enumerate(heads):
                    pt, pk = bank[h % 2]
                    P.op("tensor", lambda e, pt=pt, i=i, h=h: e.matmul(
                        pt[(h % 2) * 64:(h % 2) * 64 + 64, (i // 2) * 128:(i // 2 + 1) * 128],
                        lhsT=AT[:, h * 64:(h + 1) * 64], rhs=T_[:, blk(i)], start=True, stop=True),
                         reads=["AT", Tk], writes=[pk])
                for par in range(2):
                    pt, pk = bank[par]
                    rs = slice(par * 64, par * 64 + 64)
                    evac(par, Apf[rs, 0:512], pt[rs, 0:512], [pk], [("g9", par)])
                mm8(Nakp, "g10", NakT, "g6", T_, Tk)
                bank = [ps(), ps()]
                for i, h in enumerate(heads):
                    hp, h2 = h // 2, h % 2
                    rs = slice(h2 * 64, h2 * 64 + 64)
                    pt, pk = bank[h2]
                    P.op("tensor", lambda e, pt=pt, i=i, hp=hp, rs=rs: e.matmul(
                        pt[:, (i // 2) * 64:(i // 2 + 1) * 64], lhsT=Apf[rs, (i // 2) * 128:(i // 2 + 1) * 128],
                        rhs=Sbf[rs, hp, :], start=True, stop=False), reads=["g9", "Sbf"], writes=[pk])
                    P.op("tensor", lambda e, pt=pt, i=i, h=h: e.matmul(
                        pt[:, (i // 2) * 64:(i // 2 + 1) * 64], lhsT=Nakp[:, blk(i)],
                        rhs=VT[:, h * 64:(h + 1) * 64], start=False, stop=True), reads=["g10", "VT"], writes=[pk])
                for par in range(2):
                    pt, pk = bank[par]
                    evac(par, UT[:, par * 256:(par + 1) * 256], pt[:, 0:256], [pk], [("UT", par)])
                UTb = lambda i: UT[:, pos(i) * 64:(pos(i) + 1) * 64]
                if want_y:
                    bank = [ps(), ps()]
                    for i, h in enumerate(heads):
                        hp, h2 = h // 2, h % 2
                        rs = slice(h2 * 64, h2 * 64 + 64)
                        pt, pk = bank[h2]
                        yo = pt[rs, (i // 2) * 128:(i // 2 + 1) * 128]
                        P.op("tensor", lambda e, yo=yo, rs=rs, hp=hp: e.matmul(yo, lhsT=Sbf[rs, hp, :], rhs=Rfm[rs, hp, :], start=True, stop=False),
                             reads=["Sbf", "Rfm"], writes=[pk])
                        P.op("tensor", lambda e, yo=yo, i=i: e.matmul(yo, lhsT=UTb(i), rhs=Mrb[:, blk(i)], start=False, stop=False),
                             reads=["UT", "g7"], writes=[pk])
                        P.op("tensor", lambda e, yo=yo, i=i, h=h: e.matmul(yo, lhsT=VT[:, h * 64:(h + 1) * 64], rhs=Mrk[:, blk(i)], start=False, stop=True),
                             reads=["VT", "g8"], writes=[pk])
                    for par in range(2):
                        pt, pk = bank[par]
                        rs = slice(par * 64, par * 64 + 64)
                        evac(par, yfm[rs, 4 * G:4 * G + 4, :], v3(pt[rs, 0:512], 128), [pk], [("yfm", G * 2 + par)])
                bank = [ps(), ps()]
                for i, h in enumerate(heads):
                    h2 = h % 2
                    rs = slice(h2 * 64, h2 * 64 + 64)
                    pt, pk = bank[h2]
                    so = pt[rs, (i // 2) * 64:(i // 2 + 1) * 64]
                    P.op("tensor", lambda e, so=so, i=i, h=h: e.matmul(so, lhsT=BhT[:, h * 64:(h + 1) * 64], rhs=UTb(i), start=True, stop=False),
                         reads=["BhT", "UT"], writes=[pk])
                    P.op("tensor", lambda e, so=so, h=h: e.matmul(so, lhsT=KhT[:, h * 64:(h + 1) * 64], rhs=VT[:, h * 64:(h + 1) * 64], start=False, stop=True),
                         reads=["KhT", "VT"], writes=[pk])
                hps = slice(4 * G, 4 * G + 4)
                P.op("vector", lambda e, hps=hps: e.tensor_tensor(S32[:, hps, :], S32[:, hps, :],
                                                                   PCb[:, hps].unsqueeze(2).to_broadcast([128, 4, 64]), op=ALU.mult),
                     reads=["PCb", "S32", "Sbf"], writes=["S32"])
                for par in range(2):
                    pt, pk = bank[par]
                    rs = slice(par * 64, par * 64 + 64)
                    P.op("vector", lambda e, hps=hps, pt=pt, rs=rs: e.tensor_tensor(S32[rs, hps, :], S32[rs, hps, :], v3(pt[rs, 0:256], 64), op=ALU.add),
                         reads=[pk, "S32"], writes=["S32"])
                P.op("vector", lambda e, hps=hps: e.tensor_copy(Sbf[:, hps, :], S32[:, hps, :]), reads=["S32", "Sbf"], writes=["Sbf"])
            P.fence()

        def rwkv_post(cs, n, ycol0, from_psum):
            W8 = 8 * n
            xr, xv = zr[:, 0:8, cs], zr[:, 16:24, cs]
            rstdv, kmod, bbv = v3(A(1024, W8), n), v3(A(3072, W8), n), v3(A(4096, W8), n)
            et, t1, yfm = v3(A(5120, W8), n), v3(A(6144, W8), n), v3(A(7168, W8), n)
            sgl = v3(AB(12928, 256), 128)
            vcol = lambda c: vec[:, c:c + 8].unsqueeze(2).to_broadcast([128, 8, n])
            hs = lambda x, hb: x[:, hb * 4:(hb + 1) * 4, :]
            if from_psum:
                for hb in range(2):
                    evac(hb, hs(yfm, hb), v3(ypb[hb][0][:, :], n), [ypb[hb][1]], ["yfm"])

            def bmm(lhs, src, srck, sink):
                for hb in range(2):
                    pt, pk = ps()
                    for c in range(4):
                        ch = hb * 4 + c
                        P.op("tensor", lambda e, pt=pt, c=c, ch=ch: e.matmul(pt[:, c * n:(c + 1) * n], lhsT=lhs, rhs=src[:, ch, :],
                                                           start=True, stop=True), reads=["cst", srck], writes=[pk])
                    sink(hb, v3(pt[:, 0:4 * n], n), pk)
            bmm(bones64, yfm, "yfm", lambda hb, p3, pk: P.op(
                "vector", lambda e: e.tensor_tensor(hs(t1, hb), hs(yfm, hb), p3, op=ALU.subtract), reads=["yfm", pk], writes=["t1"]))
            P.op("scalar", lambda e: e.activation(et, t1, AF.Square), reads=["t1"], writes=["et"])
            bmm(bones64, et, "et", lambda hb, p3, pk: P.op(
                "scalar", lambda e: e.activation(hs(rstdv, hb), p3, AF.Sqrt, bias=GN_EPS, scale=1.0), reads=[pk], writes=["rstdv"]))
            P.op("vector", lambda e: e.reciprocal(rstdv, rstdv), reads=["rstdv"], writes=["rstdv"])
            P.op("vector", lambda e: e.tensor_tensor(t1, t1, rstdv, op=ALU.mult), reads=["t1", "rstdv"], writes=["t1"])
            P.op("vector", lambda e: e.tensor_tensor(t1, t1, vcol(V_LG), op=ALU.mult), reads=["t1", "vec"], writes=["t1"])
            P.op("vector", lambda e: e.tensor_tensor(t1, t1, vcol(V_LB), op=ALU.add), reads=["t1", "vec"], writes=["t1"])
            P.op("vector", lambda e: e.tensor_tensor(et, xr, kmod, op=ALU.mult), reads=["zr", "kmod", "et"], writes=["et"])
            P.op("vector", lambda e: e.tensor_tensor(et, et, vcol(V_RK), op=ALU.mult), reads=["et", "vec"], writes=["et"])
            bmm(bones, et, "et", lambda hb, p3, pk: P.op(
                "vector", lambda e: e.tensor_tensor(hs(bbv, hb), p3, hs(xv, hb), op=ALU.mult), reads=[pk, "zr", "bbv"], writes=["bbv"]))
            P.op("vector", lambda e: e.tensor_tensor(t1, t1, bbv, op=ALU.add), reads=["t1", "bbv"], writes=["t1"])
            P.op("scalar", lambda e: e.activation(sgl[:, 0, 0:n], zr[:, 25, cs], AF.Sigmoid), reads=["zr"], writes=["sgl"])
            P.op("scalar", lambda e: e.activation(sgl[0:32, 1, 0:n], zr[0:32, 26, cs], AF.Sigmoid), reads=["zr", "sgl"], writes=["sgl"])
            for hb in range(2):
                pt, pk = ps()
                for c in range(4):
                    ch = hb * 4 + c
                    P.op("tensor", lambda e, pt=pt, c=c, ch=ch: e.matmul(pt[:, c * n:(c + 1) * n], lhsT=gupb[:, 0, ch * 128:(ch + 1) * 128],
                                                       rhs=sgl[:, 0, 0:n], start=True, stop=False), reads=["gupb", "sgl"], writes=[pk])
                    P.op("tensor", lambda e, pt=pt, c=c, ch=ch: e.matmul(pt[:, c * n:(c + 1) * n], lhsT=gupb[0:32, 1, ch * 128:(ch + 1) * 128],
                                                       rhs=sgl[0:32, 1, 0:n], start=False, stop=True), reads=["gupb", "sgl"], writes=[pk])
                P.op("vector", lambda e, pt=pt, hb=hb: e.tensor_tensor(yg[:, hb * 4:(hb + 1) * 4, ycol0:ycol0 + n], hs(t1, hb),
                                                          v3(pt[:, 0:4 * n], n), op=ALU.mult), reads=[pk, "t1"], writes=["yg"])

        def sample_scan():
            n = NS
            cs = slice(1 + NTB, 1 + NTM)
            xr, xv = zr[:, 0:8, cs], zr[:, 16:24, cs]
            sgf, kkn, kmod, bbv = v3(A(0, 128), n), v3(A(2048, 128), n), v3(A(3072, 128), n), v3(A(4096, 128), n)
            Vs = A(5120, 256)[0:64, :].rearrange("p (h n) -> p h n", n=n)
            Ys = A(5376, 256)[0:64, :].rearrange("p (h n) -> p h n", n=n)
            rowb = [arena[0:16, 7168:8192], arena[0:16, 9216:10240]]
            ops = [(kkn, "kkn", -1.0), (sgf, "sg", 1.0), (bbv, "bbv", 1.0), (kmod, "kmod", 1.0), (xr, "zr", 1.0)]
            for oi, (X, xk_, sc) in enumerate(ops):
                rb = rowb[oi % 2]
                rbk = f"rowb{oi % 2}"
                for hb in range(2):
                    pt, pk = ps()
                    for c in range(4):
                        ch = hb * 4 + c
                        P.op("tensor", lambda e, pt=pt, c=c, ch=ch, X=X: e.matmul(pt[0:16, c * 128:(c + 1) * 128], lhsT=X[:, ch, :], rhs=ident,
                                                                start=True, stop=True), reads=[xk_, "cst"], writes=[pk])
                    P.op("scalar", lambda e, pt=pt, hb=hb, rb=rb, sc=sc: e.activation(rb[:, hb * 512:(hb + 1) * 512], pt[0:16, :], AF.Identity, scale=sc),
                         reads=[pk], writes=[rbk])
                P.dma("sync", scr_rows[oi], rb, reads=[rbk], writes=["scr_rows"])
            Vs4 = Vs.rearrange("p (hp h2) n -> p hp h2 n", h2=2)
            P.op("vector", lambda e: e.tensor_copy(Vs4[:, :, 0, :], xv[0:64, :, :]), reads=["zr"], writes=["Vs"])
            P.dma("sync", Vs4[:, :, 1, :], xv[64:128, :, :], reads=["zr"], writes=["Vs"])
            P.fence()
            S = v3(A(8192, 1024)[0:64, :], 64)
            bc = [v3(A(9216 + i * 1024, 1024)[0:64, :], 64) for i in range(5)]
            tt = v3(A(14336, 1024)[0:64, :], 64)
            S1 = v3(A(15360, 1024)[0:64, :], 64)
            sa = A(7168, 16)[0:64, :]
            for s_ in range(NS):
                P.dma("sync", S, swkv[:, s_], writes=["S"])
                for i in range(5):
                    P.dma("sync", A(9216 + i * 1024, 1024)[0:64, :], scr_rows[i, s_:s_ + 1, :].to_broadcast([64, 1024]),
                          writes=[f"bc{i}"])
                a_, w_, b_, k_, r_ = bc
                P.op("vector", lambda e: e.tensor_tensor(tt, S, a_, op=ALU.mult), reads=["S", "bc0", "tt"], writes=["tt"])
                P.op("vector", lambda e: e.tensor_reduce(out=sa, in_=tt, axis=AX.X, op=ALU.add), reads=["tt"], writes=["sa"])
                P.op("vector", lambda e: e.tensor_tensor(S1, S, w_, op=ALU.mult), reads=["S", "bc1", "S1"], writes=["S1"])
                P.op("vector", lambda e: e.tensor_tensor(tt, b_, sa.unsqueeze(2).to_broadcast([64, 16, 64]), op=ALU.mult),
                     reads=["bc2", "sa", "tt"], writes=["tt"])
                P.op("vector", lambda e: e.tensor_tensor(S1, S1, tt, op=ALU.add), reads=["S1", "tt"], writes=["S1"])
                P.op("vector", lambda e, s_=s_: e.tensor_tensor(tt, k_, Vs[:, :, s_].unsqueeze(2).to_broadcast([64, 16, 64]), op=ALU.mult),
                     reads=["bc3", "Vs", "tt"], writes=["tt"])
                P.op("vector", lambda e: e.tensor_tensor(S1, S1, tt, op=ALU.add), reads=["S1", "tt"], writes=["S1"])
                P.op("vector", lambda e: e.tensor_tensor(tt, S1, r_, op=ALU.mult), reads=["S1", "bc4", "tt"], writes=["tt"])
                P.op("vector", lambda e, s_=s_: e.tensor_reduce(out=Ys[:, :, s_], in_=tt, axis=AX.X, op=ALU.add), reads=["tt"], writes=["Ys"])
                P.dma("sync", o_wkvs[:, s_], S1, reads=["S1"])
            yfm = v3(A(7168, 128), n)
            Ys4 = Ys.rearrange("p (hp h2) n -> p hp h2 n", h2=2)
            P.op("vector", lambda e: e.tensor_copy(yfm[0:64, :, :], Ys4[:, :, 0, :]), reads=["Ys", "sa"], writes=["yfm"])
            P.dma("sync", yfm[64:128, :, :], Ys4[:, :, 1, :], reads=["Ys", "sa"], writes=["yfm"])
            P.fence()

        def pool_branch(t_own, nt, last):
            E = 15 + NTB
            P.dma("sync", icnt[:], invc[t_own:t_own + 1, :].to_broadcast([128, 4 * NTM]).rearrange("p (g n) -> p g n", n=NTM),
                  writes=["icnt"])
            lv = [A(i * 288, E) for i in range(4)]
            pbf = AB(1152, NTM)
            if last:
                spl = A(1408, 4 * NS * 15).rearrange("p (g n t) -> p g n t", g=4, n=NS)
                P.dma("sync", spl, spoolT.rearrange("(g p) n t -> p g n t", p=128), writes=["spl"])
                ssum = A(2368, NS)
            for g in range(4):
                w = 2 << g
                x = zp[:, g, 0:E]
                src = x
                for l in range(g + 1):
                    sh = 1 << l
                    lo = 2 * sh - 1
                    dst = lv[l]
                    P.op("vector", lambda e, dst=dst, src=src, lo=lo, sh=sh: e.tensor_tensor(dst[:, lo:E], src[:, lo:E], src[:, lo - sh:E - sh], op=ALU.add),
                         reads=["zp", "lv"], writes=["lv"])
                    src = dst
                P.op("vector", lambda e, src=src, g=g: e.tensor_tensor(lv[3][:, 15:E] if g < 3 else lv[2][:, 15:E], src[:, 15:E], icnt[:, g, 0:NTB], op=ALU.mult),
                     reads=["lv", "icnt"], writes=["lv"])
                pm = lv[3] if g < 3 else lv[2]
                P.op("vector", lambda e, pm=pm, x=x: e.tensor_tensor(pbf[:, 0:NTB], pm[:, 15:E], x[:, 15:E], op=ALU.subtract),
                     reads=["lv", "zp", "pbf"], writes=["pbf"])
                if last:
                    xs = zp[:, g, 15 + NTB:15 + NTM]
                    P.op("vector", lambda e, g=g, w=w: e.tensor_reduce(out=ssum, in_=spl[:, g, :, 15 - (w - 1):15], axis=AX.X, op=ALU.add),
                         reads=["spl", "ssum"], writes=["ssum"])
                    P.op("vector", lambda e, xs=xs: e.tensor_tensor(ssum, ssum, xs, op=ALU.add), reads=["ssum", "zp"], writes=["ssum"])
                    P.op("vector", lambda e, xs=xs, w=w: e.scalar_tensor_tensor(out=pbf[:, NTB:NTM], in0=ssum, scalar=1.0 / w, in1=xs,
                                                                          op0=ALU.mult, op1=ALU.subtract), reads=["ssum", "zp", "pbf"], writes=["pbf"])
                pt, pk = ps()
                P.op("tensor", lambda e, pt=pt, g=g: e.matmul(pt[:, 0:nt], lhsT=pgwb[:, g, :], rhs=pbf[:, 0:nt], start=True, stop=True),
                     reads=["pgwb", "pbf"], writes=[pk])
                P.op("vector", lambda e, pt=pt, g=g: e.tensor_scalar_mul(mix[:, g, 0:nt], pt[:, 0:nt], vec[:, V_PS + g:V_PS + g + 1]),
                     reads=[pk, "vec"], writes=["mix"])

        def attn_prompt(c0):
            n = 128
            sc = v3(A(4096, 1024), 256)
            pnb = v3(AB(5120, 1024), 256)
            pTb = v3(AB(5632, 1024), 128)
            mx, nb, sm, rs = A(6144, 4), A(6148, 4), A(6152, 4), A(6156, 4)
            for hp_ in range(2):
                pt, pk = ps()
                for hh in range(2):
                    h = hp_ * 2 + hh
                    P.op("tensor", lambda e, pt=pt, hh=hh, h=h: e.matmul(pt[:, hh * 256:(hh + 1) * 256], lhsT=zq[:, h, c0:c0 + n], rhs=KTb[:, h, :],
                                                          start=True, stop=True), reads=["zq", "KTb"], writes=[pk])
                P.op("vector", lambda e, pt=pt, hp_=hp_: e.tensor_reduce(out=mx[:, hp_ * 2:hp_ * 2 + 2], in_=v3(pt[:, :], 256), axis=AX.X, op=ALU.max),
                     reads=[pk, "mx"], writes=["mx"])
                P.op("vector", lambda e, hp_=hp_: e.tensor_scalar_mul(nb[:, hp_ * 2:hp_ * 2 + 2], mx[:, hp_ * 2:hp_ * 2 + 2], -SCALE),
                     reads=["mx", "nb"], writes=["nb"])
                for hh in range(2):
                    h = hp_ * 2 + hh
                    P.op("scalar", lambda e, pt=pt, hh=hh, h=h: e.activation(sc[:, h, :], pt[:, hh * 256:(hh + 1) * 256], AF.Exp, bias=nb[:, h:h + 1],
                                                              scale=SCALE, accum_out=sm[:, h:h + 1]), reads=[pk, "nb", "sc", "sm"], writes=["sc", "sm"])
            P.op("vector", lambda e: e.reciprocal(rs, sm), reads=["sm", "rs"], writes=["rs"])
            P.op("vector", lambda e: e.tensor_tensor(pnb, sc, rs.unsqueeze(2).to_broadcast([128, 4, 256]), op=ALU.mult),
                 reads=["sc", "rs", "pnb"], writes=["pnb"])
            for hp_ in range(2):
                pt, pk = ps()
                for hh in range(2):
                    h = hp_ * 2 + hh
                    for mt in range(2):
                        j = hh * 2 + mt
                        P.op("tensor", lambda e, pt=pt, j=j, h=h, mt=mt: e.matmul(pt[:, j * 128:(j + 1) * 128], lhsT=pnb[:, h, mt * 128:(mt + 1) * 128], rhs=ident_bf,
                                                                start=True, stop=True), reads=["pnb", "cbf"], writes=[pk])
                evac(hp_, pTb[:, hp_ * 4:(hp_ + 1) * 4, :], v3(pt[:, :], 128), [pk, "pTb"], ["pTb"])
            pt, pk = ps()
            for h in range(4):
                for mt in range(2):
                    P.op("tensor", lambda e, pt=pt, h=h, mt=mt: e.matmul(pt[:, h * 128:(h + 1) * 128], lhsT=Vb[:, mt, h * 128:(h + 1) * 128], rhs=pTb[:, h * 2 + mt, :],
                                                          start=(mt == 0), stop=(mt == 1)), reads=["Vb", "pTb"], writes=[pk])
            P.op("vector", lambda e, pt=pt: e.tensor_copy(oT[:, :, c0:c0 + n], v3(pt[:, :], 128)), reads=[pk], writes=["oT"])

        def attn_sample():
            qrow = arena[0:16, 0:512]
            pt, pk = ps()
            for h in range(4):
                P.op("tensor", lambda e, pt=pt, h=h: e.matmul(pt[0:16, h * 128:(h + 1) * 128], lhsT=zq[:, h, NTB:NTM], rhs=ident_bf, start=True, stop=True),
                     reads=["zq", "cbf"], writes=[pk])
            P.op("vector", lambda e, pt=pt: e.tensor_copy(qrow, pt[0:16, :]), reads=[pk], writes=["qrow"])
            P.dma("sync", scr_q, qrow, reads=["qrow"], writes=["scr_q"])
            qbc = v3(A(8192, NS * 512), 512)
            P.dma("sync", qbc, scr_q.rearrange("n c -> (n c)").unsqueeze(0).to_broadcast([128, NS * 512]).rearrange("p (n c) -> p n c", c=512),
                  reads=["scr_q"], writes=["qbc"])
            KV = [v3(A(512 + i * 1024, 1024), 512) for i in range(2)]
            prod = A(2560, 512)
            s_all = v3(A(3072, 128), 64)
            p64 = A(3200, 256)
            pTs = v3(A(3456, 128), 64)
            mx, nb, sm, rs = A(3584, 1), A(3585, 1), A(3586, 1), A(3587, 1)
            for s_ in range(NS):
                kb = KV[s_ % 2]
                kbk = f"kv{s_ % 2}"
                P.dma("sync", kb, kc[s_].rearrange("(mt p) c -> p mt c", p=128), writes=[kbk])
                for mt in range(2):
                    P.op("vector", lambda e, kb=kb, mt=mt, s_=s_: e.tensor_tensor(prod, kb[:, mt, :], qbc[:, s_, :], op=ALU.mult),
                         reads=[kbk, "qbc", "prod"], writes=["prod"])
                    P.op("vector", lambda e, mt=mt, s_=s_: e.tensor_reduce(out=s_all[:, mt, s_ * 4:(s_ + 1) * 4], in_=v3(prod, 128), axis=AX.X, op=ALU.add),
                         reads=["prod", "s_all"], writes=["s_all"])
            pt, pk = ps()
            for mt in range(2):
                P.op("tensor", lambda e, pt=pt, mt=mt: e.matmul(pt[0:64, mt * 128:(mt + 1) * 128], lhsT=s_all[:, mt, :], rhs=ident, start=True, stop=True),
                     reads=["s_all", "cst"], writes=[pk])
            P.op("vector", lambda e, pt=pt: e.tensor_reduce(out=mx[0:64, :], in_=pt[0:64, 0:256], axis=AX.X, op=ALU.max), reads=[pk], writes=["mx"])
            P.op("vector", lambda e: e.tensor_scalar_mul(nb[0:64, :], mx[0:64, :], -SCALE), reads=["mx"], writes=["nb"])
            P.op("scalar", lambda e, pt=pt: e.activation(p64[0:64, :], pt[0:64, 0:256], AF.Exp, bias=nb[0:64, :], scale=SCALE, accum_out=sm[0:64, :]),
                 reads=[pk, "nb"], writes=["p64", "sm"])
            P.op("vector", lambda e: e.reciprocal(rs[0:64, :], sm[0:64, :]), reads=["sm"], writes=["rs"])
            P.op("vector", lambda e: e.tensor_scalar_mul(p64[0:64, :], p64[0:64, :], rs[0:64, :]), reads=["p64", "rs"], writes=["p64"])
            pt2, pk2 = ps()
            for mt in range(2):
                P.op("tensor", lambda e, mt=mt: e.matmul(pt2[:, mt * 64:(mt + 1) * 64], lhsT=p64[0:64, mt * 128:(mt + 1) * 128], rhs=ident[0:64, 0:64],
                                                  start=True, stop=True), reads=["p64", "cst"], writes=[pk2])
            P.op("vector", lambda e: e.tensor_copy(pTs, v3(pt2[:, 0:128], 64)), reads=[pk2], writes=["pTs"])
            pt3, pk3 = ps()
            for s_ in range(NS):
                vb_ = KV[s_ % 2]
                vbk = f"kv{s_ % 2}"
                P.dma("sync", vb_, vc[s_].rearrange("(mt p) c -> p mt c", p=128), writes=[vbk])
                for h in range(4):
                    col = s_ * 4 + h
                    for mt in range(2):
                        P.op("tensor", lambda e, vb_=vb_, h=h, mt=mt, col=col: e.matmul(pt3[:, col:col + 1], lhsT=vb_[:, mt, h * 128:(h + 1) * 128],
                                                                  rhs=pTs[:, mt, col:col + 1], start=(mt == 0), stop=(mt == 1)),
                             reads=[vbk, "pTs"], writes=[pk3])
            P.op("vector", lambda e: e.tensor_copy(oT[:, :, NTB:NTM], pt3[:, 0:64].rearrange("p (n h) -> p h n", h=4)), reads=[pk3], writes=["oT"])

        def merge_wo(nt):
            acc = [A(0, NTM), A(288, NTM)]
            sgt = A(576, NTM)
            merged = v3(AB(1024, KT * NTM), NTM)
            branches = [(C_G, pout, mix, "mix", 4), (C_G + D, rout, yg, "yg", 8), (C_G + 2 * D, xout, oT, "oT", 4)]
            for ip in range(8):
                for bi, (gc0, Wb, src, srck, ktb) in enumerate(branches):
                    wgt, wgk = wload(win, 0, D, gc0 + ip * 256, 256)
                    wbr, wbk = wload(Wb, 0, ktb * 128, ip * 256, 256)
                    for ci in range(2):
                        pg, pgk = ps()
                        po, pok = ps()
                        P.group("tensor", [lambda e, pg=pg, k=k, ci=ci, wgt=wgt: e.matmul(
                            pg[:, 0:nt], lhsT=wgt[:, k, ci * 128:(ci + 1) * 128], rhs=xn[:, k, 0:nt],
                            start=(k == 0), stop=(k == KT - 1)) for k in range(KT)], reads=[wgk, "xn"], writes=[pgk])
                        P.group("tensor", [lambda e, po=po, k=k, ci=ci, wbr=wbr, src=src, ktb=ktb: e.matmul(
                            po[:, 0:nt], lhsT=wbr[:, k, ci * 128:(ci + 1) * 128], rhs=src[:, k, 0:nt],
                            start=(k == 0), stop=(k == ktb - 1)) for k in range(ktb)], reads=[wbk, srck], writes=[pok])
                        P.op("scalar", lambda e, pg=pg: e.activation(sgt[:, 0:nt], pg[:, 0:nt], AF.Sigmoid), reads=[pgk, "sgt"], writes=["sgt"])
                        if bi == 0:
                            P.op("vector", lambda e, po=po, ci=ci: e.tensor_tensor(acc[ci][:, 0:nt], sgt[:, 0:nt], po[:, 0:nt], op=ALU.mult),
                                 reads=["sgt", pok, f"acc{ci}"], writes=[f"acc{ci}"])
                        else:
                            P.op("vector", lambda e, po=po: e.tensor_tensor(sgt[:, 0:nt], sgt[:, 0:nt], po[:, 0:nt], op=ALU.mult),
                                 reads=["sgt", pok], writes=["sgt"])
                            P.op("vector", lambda e, ci=ci: e.tensor_tensor(acc[ci][:, 0:nt], acc[ci][:, 0:nt], sgt[:, 0:nt], op=ALU.add),
                                 reads=["sgt", f"acc{ci}"], writes=[f"acc{ci}"])
                for ci in range(2):
                    i = ip * 2 + ci
                    P.op("vector", lambda e, ci=ci, i=i: e.tensor_copy(merged[:, i, 0:nt], acc[ci][:, 0:nt]), reads=[f"acc{ci}"], writes=[("merged", i)])

            def sink_o(j, pt, pk, m):
                P.op("vector", lambda e: e.tensor_tensor(xh[:, j, 0:nt], xh[:, j, 0:nt], pt[:, 0:nt], op=ALU.add),
                     reads=[pk, ("xh", j)], writes=[("xh", j)])
            proj(wo, 0, D, merged, "merged", nt, KT, sink_o)
            P.fence()

        P.op("vector", lambda e: e.memset(zr[:], 0.0), writes=["zr"])
        mem_kv()
        P.dma("sync", o_poolso, spool[:, 1:15, :])
        xT3 = xT.rearrange("(k p) n -> p k n", p=128)
        TSEL = [int(v) for v in os.environ.get('K_TILES', '0,1,2,3,4,5,6,7').split(',') if v != '']
        STAGE = int(os.environ.get("K_STAGE", "9"))
        for t in TSEL:
            own = t >= 4
            last = t == 7
            nt = NTB + (NS if last else 0)
            P.dma("sync", xh[:, :, 0:NTB], xT3[:, :, t * NTB:(t + 1) * NTB], writes=["xh"])
            if last:
                P.dma("sync", xh[:, :, NTB:NTM], xT3[:, :, 2048:2048 + NS], writes=["xh"])
            ffn(f1g, f1u, f1d, G_F1, nt, True)
            rmsnorm(xh, G_MIX, nt, xn, "xh", "xn")
            if t >= 3:
                def sink_zp(j, pt, pk, m, nt=nt):
                    P.op("vector", lambda e: e.tensor_copy(zp[:, j, 15:15 + nt], pt[:, 0:nt]), reads=[pk], writes=["zp"])
                proj(win, C_POOL, 512, xn, "xn", nt, KT, sink_zp)

            def sink_zr(j, pt, pk, m, nt=nt):
                evac(j, zr[0:m, j, 1:1 + nt], pt[0:m, 0:nt], [pk], ["zr"])
            proj(win, C_R, RPW, xn, "xn", nt, KT, sink_zr)
            if own:
                def sink_zq(j, pt, pk, m, nt=nt):
                    evac(j, zq[:, j, 0:nt], pt[:, 0:nt], [pk], ["zq"])
                proj(win, C_XQ, 512, xn, "xn", nt, KT, sink_zq)
            if last:
                P.dma("sync", o_shiftT, zr[:, :, NTB:NTB + 1 + NS], reads=["zr"])
                P.dma("sync", o_poolpT, zp[:, :, NTB:NTB + 15], reads=["zp"])
                P.dma("sync", o_poolsn, zp[:, :, 15 + NTB:15 + NTM], reads=["zp"])
            P.fence()
            SUB = int(os.environ.get("K_SUB", "9"))
            if STAGE >= 2:
                token_shift(last)
                for sub in range(2):
                    cs = slice(1 + sub * 128, 1 + (sub + 1) * 128)
                    if SUB >= 2:
                        rwkv_elem(cs, 128, False)
                    if SUB >= 3:
                        scan_prompt(cs, own)
                    if own and SUB >= 4:
                        rwkv_post(cs, 128, sub * 128, False)
                    P.fence()
                if last:
                    cs = slice(1 + NTB, 1 + NTM)
                    rwkv_elem(cs, NS, True)
                    P.fence()
                    sample_scan()
                    rwkv_post(cs, NS, NTB, False)
                    P.fence()
                P.op("vector", lambda e: e.tensor_copy(zr[:, :, 0:1], zc[:]), reads=["zc", "zr"], writes=["zr"])
            if own and STAGE >= 3:
                pool_branch(t - 4, nt, last)
                for sub in range(2):
                    attn_prompt(sub * 128)
                P.fence()
                if last:
                    attn_sample()
                    P.fence()
            if t >= 3:
                P.op("vector", lambda e: e.tensor_copy(zp[:, :, 0:15], zp[:, :, NTB:NTB + 15]), reads=["zp"], writes=["zp"])
            if own:
                P.fence()
                if STAGE >= 4:
                    merge_wo(nt)
                ffn(f2g, f2u, f2d, G_F2, nt, True)
                yo = v3(A(0, KT * NTM), NTM)
                rmsnorm(xh, G_FIN, nt, yo, "xh", "yo")
                P.dma("sync", yT[:, :, (t - 4) * NTB:(t - 3) * NTB], yo[:, :, 0:NTB], reads=["yo"])
                if last:
                    P.dma("sync", yT[:, :, 1024:1024 + NS], yo[:, :, NTB:NTM], reads=["yo"])
            P.fence()
        P.dma("sync", o_wkvp, S32[:].rearrange("p k n -> p (k n)"), reads=["S32"])
        P.finish()
    return nc


def _consts():
    c = np.zeros((128, 3584), np.float32)
    s = np.arange(128)[:, None]
    t = np.arange(128)[None, :]
    eye = np.eye(128, dtype=np.float32)
    c[:, 0:128] = eye
    c[:, 128:256] = -EDEC * (s <= t)
    c[:, 256:384] = -EDEC * (s < t)
    c[:, 384:512] = -EDEC * (s > t)
    bo = ((s // 64) == (t // 64)).astype(np.float32)
    c[:, 512:640] = bo
    c[:, 640:768] = bo / 64.0
    c[:, 768:1280] = np.tile((s < t).astype(np.float32), (1, 4))
    c[:, 1280:1792] = np.tile((s > t).astype(np.float32), (1, 4))
    c[:, 1792:2304] = np.tile((s <= t).astype(np.float32), (1, 4))
    c[:, 2304:2816] = np.tile(eye, (1, 4))
    c[:, 2816:3072] = 1.0
    c[:, 3072:3200] = eye
    c[:, 3200:3328] = 1.0
    return c


_NC_CACHE = {}
_PACK_ONLY = False


def kernel(x_prompt, x_sample, mem_prompt, cache_mem_k, cache_mem_v, state_wkv, state_shift, state_pool,
           ffn1_norm_g, ffn1_w_gate, ffn1_w_up, ffn1_w_down, mix_norm_g, w_in,
           pool_group_w, pool_scale, pool_out,
           rwkv_mu, rwkv_w0, rwkv_w_up, rwkv_a0, rwkv_a_up, rwkv_g_up, rwkv_k_k, rwkv_k_a, rwkv_r_k,
           rwkv_ln_g, rwkv_ln_b, rwkv_out,
           mem_norm_g, w_mem_k, w_mem_v, xattn_out, w_o,
           ffn2_norm_g, ffn2_w_gate, ffn2_w_up, ffn2_w_down, final_norm_g):
    f = lambda a: np.ascontiguousarray(np.asarray(a, dtype=np.float32))
    x_prompt, x_sample, mem_prompt = f(x_prompt), f(x_sample), f(mem_prompt)
    B = x_prompt.shape[0]

    def kcols(v, n):
        v = f(v).reshape(-1)
        pad = np.zeros(n * 128, np.float32)
        pad[:v.size] = v
        return pad.reshape(n, 128).T

    vecs = np.zeros((128, 176), np.float32)
    vecs[:, 0:16] = kcols(ffn1_norm_g[0], 16)
    vecs[:, 16:32] = kcols(mix_norm_g[0], 16)
    vecs[:, 32:48] = kcols(mem_norm_g[0], 16)
    vecs[:, 48:64] = kcols(ffn2_norm_g[0], 16)
    vecs[:, 64:80] = kcols(final_norm_g, 16)
    vecs[:, 80:107] = kcols(rwkv_mu[0], 27)
    vecs[:, 107:111] = kcols(pool_scale[0], 4)
    vecs[:, 111:119] = kcols(rwkv_k_k[0], 8)
    vecs[:, 119:127] = kcols(rwkv_k_a[0], 8)
    vecs[:, 127:135] = kcols(rwkv_r_k[0], 8)
    vecs[:, 135:143] = kcols(rwkv_ln_g[0], 8)
    vecs[:, 143:151] = kcols(rwkv_ln_b[0], 8)
    rows = np.concatenate([f(rwkv_w0[0]), f(rwkv_a0[0])])[None, :]
    cst = _consts()
    shared = {
        "f1g": f(ffn1_w_gate[0]), "f1u": f(ffn1_w_up[0]), "f1d": f(ffn1_w_down[0]),
        "f2g": f(ffn2_w_gate[0]), "f2u": f(ffn2_w_up[0]), "f2d": f(ffn2_w_down[0]),
        "win": f(w_in[0]), "pgw": f(pool_group_w[0]), "pout": f(pool_out[0]),
        "wup": f(rwkv_w_up[0]), "aup": f(rwkv_a_up[0]), "gup": f(rwkv_g_up[0]), "rout": f(rwkv_out[0]),
        "wmk": f(w_mem_k[0]), "wmv": f(w_mem_v[0]), "xout": f(xattn_out[0]), "wo": f(w_o[0]),
        "vecs": vecs, "rows": rows, "cst": cst,
    }
    in_maps = []
    for c in range(8):
        b, half = c // 2, c % 2
        own = x_prompt[b, half * 1024:(half + 1) * 1024]
        prev = x_prompt[b, 0:1024] if half == 1 else np.zeros_like(own)
        xs = x_sample[c * NS:(c + 1) * NS, 0]
        xTc = np.ascontiguousarray(np.concatenate([prev, own, xs], axis=0).T)
        sl = slice(c * NS, (c + 1) * NS)
        sshT = np.zeros((27 * 128, NS), np.float32)
        sshT[:RPW] = f(state_shift[0, sl, 0]).T
        invc = np.zeros((4, 4, NTB + NS), np.float32)
        for t in range(4):
            pos = half * 1024 + t * NTB + np.arange(NTB)
            for g, w in enumerate((2, 4, 8, 16)):
                invc[t, g, :NTB] = 1.0 / np.minimum(pos + 1, w)
                invc[t, g, NTB:] = 1.0 / w
        m = dict(shared)
        m.update({
            "xT": xTc, "memT": np.ascontiguousarray(mem_prompt[b].T),
            "kc": f(cache_mem_k[0, sl]).reshape(NS, NMEM, 512), "vc": f(cache_mem_v[0, sl]).reshape(NS, NMEM, 512),
            "swkv": np.ascontiguousarray(f(state_wkv[0, sl]).transpose(2, 0, 1, 3)),
            "sshiftT": sshT,
            "spoolT": np.ascontiguousarray(f(state_pool[0, sl]).transpose(2, 0, 1)),
            "spool": f(state_pool[0, sl]),
            "invc": invc.reshape(4, -1),
        })
        in_maps.append(m)
    if _PACK_ONLY:
        return in_maps
    if "nc" not in _NC_CACHE:
        _NC_CACHE["nc"] = build_nc()
    res = run_bass_kernel_spmd(_NC_CACHE["nc"], in_maps, core_ids=list(range(8)))
    R = res.results
    return unpack(R, B)


def unpack(R, B=4):
    ND = NS * 8
    y_prompt = np.zeros((B, 2048, D), np.float32)
    y_sample = np.zeros((ND, 1, D), np.float32)
    mem_k = np.zeros((1, B, NMEM, 4, 128), np.float32)
    mem_v = np.zeros((1, B, NMEM, 4, 128), np.float32)
    wkv_p = np.zeros((1, B, 16, 64, 64), np.float32)
    sh_p = np.zeros((1, B, 1, RPW), np.float32)
    pl_p = np.zeros((1, B, 15, 512), np.float32)
    wkv_s = np.zeros((1, ND, 16, 64, 64), np.float32)
    sh_s = np.zeros((1, ND, 1, RPW), np.float32)
    pl_s = np.zeros((1, ND, 15, 512), np.float32)
    for c in range(8):
        b, half = c // 2, c % 2
        r = R[c]
        yT = np.asarray(r["yT"]).reshape(128, KT, 1024 + NS)
        yfull = yT.transpose(2, 1, 0).reshape(1024 + NS, D)
        y_prompt[b, half * 1024:(half + 1) * 1024] = yfull[:1024]
        y_sample[c * NS:(c + 1) * NS, 0] = yfull[1024:]
        sh = np.asarray(r["shiftT"]).reshape(128, 27, 1 + NS).transpose(2, 1, 0).reshape(1 + NS, 27 * 128)[:, :RPW]
        sh_s[0, c * NS:(c + 1) * NS, 0] = sh[1:]
        pl_s[0, c * NS:(c + 1) * NS, 0:14] = np.asarray(r["poolso"]).reshape(NS, 14, 512)
        pl_s[0, c * NS:(c + 1) * NS, 14] = np.asarray(r["poolsn"]).reshape(128, 4, NS).transpose(2, 1, 0).reshape(NS, 512)
        wkv_s[0, c * NS:(c + 1) * NS] = np.asarray(r["wkvs"]).reshape(64, NS, 16, 64).transpose(1, 2, 0, 3)
        if half == 0:
            mem_k[0, b] = np.asarray(r["memkT"]).reshape(128, 4, NMEM).transpose(2, 1, 0)
            mem_v[0, b] = np.asarray(r["memv"]).reshape(128, 2, 512).transpose(1, 0, 2).reshape(NMEM, 4, 128)
        else:
            sh_p[0, b, 0] = sh[0]
            pl_p[0, b] = np.asarray(r["poolpT"]).reshape(128, 4, 15).transpose(2, 1, 0).reshape(15, 512)
            wkv_p[0, b] = np.asarray(r["wkvp"]).reshape(2, 64, 8, 64).transpose(2, 0, 3, 1).reshape(16, 64, 64)
    return (y_prompt, y_sample, mem_k, mem_v, wkv_p, sh_p, pl_p, wkv_s, sh_s, pl_s)
```

```python
import numpy as np
from contextlib import ExitStack
import concourse.bass as bass
import concourse.mybir as mybir
from concourse.bass_utils import run_bass_kernel_spmd

F32 = mybir.dt.float32
BF16 = mybir.dt.bfloat16
AF = mybir.ActivationFunctionType
ALU = mybir.AluOpType
AX = mybir.AxisListType

D = 2048
F = 5504
KT = 16
FC = 43
NMEM = 256
RW = 1024
RPW = 3360
INW = 10528
C_POOL, C_R, C_XQ, C_G = 0, 512, 3872, 4384
NTB = 256
NS = 16
NTILE = 8
EDEC = float(np.exp(-0.5))
RMS_EPS = 1e-6
GN_EPS = 64e-5
EPOCH = 20000
NDSEM = 6


class Prog:
    COMPUTE = ("tensor", "vector", "scalar", "gpsimd")

    def __init__(self, nc, stack):
        self.nc = nc
        self.stack = stack
        self.eng = {"tensor": nc.tensor, "vector": nc.vector, "scalar": nc.scalar,
                    "gpsimd": nc.gpsimd, "sync": nc.sync}
        self.seq = {e: 0 for e in self.COMPUTE}
        self.esems = {e: [] for e in self.COMPUTE}
        self.dq = {}
        for q in ("sync", "scalar", "gpsimd"):
            self.dq[q] = {"n": 0, "sems": [self._sem(f"d_{q}_{i}") for i in range(NDSEM)]}
        self.seen = {e: {} for e in self.eng}
        self.state = {}
        self.subs = {}
        self.same_engine_sync = {"vector": True, "scalar": True, "gpsimd": True, "tensor": False}
        self.last_ev = {}

    def _sem(self, name):
        return self.stack.enter_context(self.nc.semaphore(name))

    def _esem(self, e, epoch):
        while len(self.esems[e]) <= epoch:
            self.esems[e].append(self._sem(f"e_{e}_{len(self.esems[e])}"))
        return self.esems[e][epoch]

    def _related(self, k):
        if isinstance(k, tuple):
            return [k, k[0]]
        out = [k]
        out.extend(self.subs.get(k, ()))
        return out

    def _deps(self, reads, writes):
        evs = []
        for k in reads:
            for kk in self._related(k):
                st = self.state.get(kk)
                if st and st[0] is not None:
                    evs.append(st[0])
        for k in writes:
            for kk in self._related(k):
                st = self.state.get(kk)
                if st:
                    if st[0] is not None:
                        evs.append(st[0])
                    evs.extend(st[1])
        return evs

    def _record(self, ev, reads, writes):
        for k in reads:
            if isinstance(k, tuple):
                self.subs.setdefault(k[0], set()).add(k)
            self.state.setdefault(k, [None, []])[1].append(ev)
        for k in writes:
            if isinstance(k, tuple):
                self.subs.setdefault(k[0], set()).add(k)
            self.state[k] = [ev, []]
            if not isinstance(k, tuple):
                for kk in self.subs.get(k, ()):
                    self.state[kk] = [ev, []]

    def _emit_waits(self, e, evs):
        need = {}
        for (semkey, sem, val, src) in evs:
            if src == e and not self.same_engine_sync.get(e, True):
                continue
            if self.seen[e].get(semkey, 0) >= val:
                continue
            if semkey not in need or need[semkey][1] < val:
                need[semkey] = (sem, val)
        for semkey, (sem, val) in need.items():
            self.eng[e].wait_ge(sem, val)
            self.seen[e][semkey] = val
            if semkey[0] == "E":
                for ep in range(semkey[2]):
                    self.seen[e][("E", semkey[1], ep)] = EPOCH

    def op(self, e, fn, reads=(), writes=()):
        evs = self._deps(reads, writes)
        self._emit_waits(e, evs)
        ins = fn(self.eng[e])
        s = self.seq[e]
        epoch, idx = divmod(s, EPOCH)
        sem = self._esem(e, epoch)
        ins.then_inc(sem, 1)
        self.seq[e] = s + 1
        ev = (("E", e, epoch), sem, idx + 1, e)
        self.last_ev[e] = ev
        self._record(ev, reads, writes)
        return ins

    def group(self, e, fns, reads=(), writes=()):
        evs = self._deps(reads, writes)
        self._emit_waits(e, evs)
        for fn in fns[:-1]:
            fn(self.eng[e])
        ins = fns[-1](self.eng[e])
        s = self.seq[e]
        epoch, idx = divmod(s, EPOCH)
        sem = self._esem(e, epoch)
        ins.then_inc(sem, 1)
        self.seq[e] = s + 1
        ev = (("E", e, epoch), sem, idx + 1, e)
        self.last_ev[e] = ev
        self._record(ev, reads, writes)
        return ins

    def dma(self, q, out, in_, reads=(), writes=(), **kw):
        d = self.dq[q]
        i = d["n"]
        slot, rnd = i % NDSEM, i // NDSEM
        sem = d["sems"][slot]
        evs = self._deps(reads, writes)
        if rnd > 0:
            evs.append((("D", q, slot), sem, 16 * rnd, None))
        self._emit_waits(q, evs)
        ins = self.eng[q].dma_start(out=out, in_=in_, **kw)
        ins.then_inc(sem, 16)
        d["n"] = i + 1
        ev = (("D", q, slot), sem, 16 * (rnd + 1), None)
        self._record(ev, reads, writes)
        return ins

    def _dma_events(self):
        evs = []
        for q, d in self.dq.items():
            n = d["n"]
            for slot in range(NDSEM):
                cnt = (n - slot + NDSEM - 1) // NDSEM if n > slot else 0
                if cnt > 0:
                    evs.append((("D", q, slot), d["sems"][slot], 16 * cnt, None))
        return evs

    def fence(self):
        evs = list(self.last_ev.values()) + self._dma_events()
        for e in self.eng:
            self._emit_waits(e, [ev for ev in evs if not (ev[3] == e and e == "tensor")])
        self.state = {}
        self.subs = {}

    def finish(self):
        for q in self.dq:
            self._emit_waits(q, [ev for ev in self._dma_events() if ev[0][1] == q])


IN_SHAPES = {}


def build_nc():
    nc = bass.Bass("TRN2", target_bir_lowering=False)
    din = {}
    dout = {}

    import os
    _small = os.environ.get("K_TEST", "")
    _need = set(os.environ.get("K_NEED", "memT,wmk,wmv,vecs,rows,cst,wup,aup,gup,pgw,spool,xT").split(","))

    def I(name, shape, dt=F32):
        if _small and name not in _need:
            shape = [1, 8]
        IN_SHAPES[name] = list(shape)
        din[name] = nc.dram_tensor(name, list(shape), dt, kind="ExternalInput").ap()
        return din[name]

    def O(name, shape):
        dout[name] = nc.dram_tensor(name, list(shape), F32, kind="ExternalOutput").ap()
        return dout[name]

    NTOK = 2048 + NS
    xT = I("xT", [D, NTOK])
    memT = I("memT", [D, NMEM])
    kc = I("kc", [NS, NMEM, 512])
    vc = I("vc", [NS, NMEM, 512])
    swkv = I("swkv", [64, NS, 16, 64])
    sshiftT = I("sshiftT", [27 * 128, NS])
    spoolT = I("spoolT", [512, NS, 15])
    spool = I("spool", [NS, 15, 512])
    f1g = I("f1g", [D, F]); f1u = I("f1u", [D, F]); f1d = I("f1d", [F, D])
    f2g = I("f2g", [D, F]); f2u = I("f2u", [D, F]); f2d = I("f2d", [F, D])
    win = I("win", [D, INW])
    pgw = I("pgw", [4, 128, 128]); pout = I("pout", [512, D])
    wup = I("wup", [64, RW]); aup = I("aup", [64, RW]); gup = I("gup", [160, RW])
    rout = I("rout", [RW, D])
    wmk = I("wmk", [D, 512]); wmv = I("wmv", [D, 512]); xout = I("xout", [512, D])
    wo = I("wo", [D, D])
    vecs = I("vecs", [128, 176])
    rows = I("rows", [1, 2048])
    cst = I("cst", [128, 3584])
    invc = I("invc", [4, 4 * (NTB + NS)])

    yT = O("yT", [128, KT, 1024 + NS])
    o_memkT = O("memkT", [128, 4, NMEM])
    o_memv = O("memv", [128, 2, 512])
    o_wkvp = O("wkvp", [128, 8 * 64])
    o_shiftT = O("shiftT", [128, 27, 1 + NS])
    o_poolpT = O("poolpT", [128, 4, 15])
    o_wkvs = O("wkvs", [64, NS, 16, 64])
    o_poolsn = O("poolsn", [128, 4, NS])
    o_poolso = O("poolso", [NS, 14, 512])

    scr_rows = nc.dram_tensor("scr_rows", [6, NS, 1024], F32, kind="Internal").ap()
    scr_y = nc.dram_tensor("scr_y", [NS, 1024], F32, kind="Internal").ap()
    scr_q = nc.dram_tensor("scr_q", [NS, 512], F32, kind="Internal").ap()

    with ExitStack() as st:
        P = Prog(nc, st)
        sb = lambda name, shape, dt=F32: st.enter_context(nc.sbuf_tensor(name, list(shape), dt))
        NTM = NTB + NS

        cst_sb = sb("cst_sb", [128, 3584])
        P.dma("sync", cst_sb[:], cst, writes=["cst"])
        ident = cst_sb[:, 0:128]
        triI = cst_sb[:, 128:256]
        triX = cst_sb[:, 256:384]
        triS = cst_sb[:, 384:512]
        bones = cst_sb[:, 512:640]
        bones64 = cst_sb[:, 640:768]
        mSU = cst_sb[:, 768:1280]
        mSL = cst_sb[:, 1280:1792]
        mIU = cst_sb[:, 1792:2304]
        I4 = cst_sb[:, 2304:2816]
        ones_row = cst_sb[0:1, 2816:3072]
        cbf = sb("cbf", [128, 256], BF16)
        P.dma("gpsimd", cbf[:], cst[:, 3072:3328], writes=["cbf"])
        ident_bf = cbf[:, 0:128]
        ones_bf = cbf[:, 128:256]
        vec = sb("vec", [128, 176])
        P.dma("sync", vec[:], vecs, writes=["vec"])
        G_F1, G_MIX, G_MEM, G_F2, G_FIN = 0, 16, 32, 48, 64
        V_MU, V_PS = 80, 107
        V_KK, V_KA, V_RK, V_LG, V_LB = 111, 119, 127, 135, 143
        rows_sb = sb("rows_sb", [1, 2048])
        P.dma("sync", rows_sb[:], rows, writes=["rows"])
        lora = sb("lora", [128, RW])
        P.dma("sync", lora[0:64, :], wup, writes=[("lora", 0)])
        P.dma("sync", lora[64:128, :], aup, writes=[("lora", 1)])
        gupb = sb("gupb", [128, 2, RW], BF16)
        P.dma("gpsimd", gupb[:, 0, :], gup[0:128, :], writes=[("gupb", 0)])
        P.dma("gpsimd", gupb[0:32, 1, :], gup[128:160, :], writes=[("gupb", 1)])
        pgwb = sb("pgwb", [128, 4, 128], BF16)
        P.dma("gpsimd", pgwb[:], pgw.rearrange("g c d -> c g d"), writes=["pgwb"])

        xh = sb("xh", [128, KT, NTM])
        xn = sb("xn", [128, KT, NTM], BF16)
        zr = sb("zr", [128, 27, 1 + NTM])
        zp = sb("zp", [128, 4, 15 + NTM])
        zq = sb("zq", [128, 4, NTM], BF16)
        mix = sb("mix", [128, 4, NTM], BF16)
        yg = sb("yg", [128, 8, NTM], BF16)
        oT = sb("oT", [128, 4, NTM], BF16)
        S32 = sb("S32", [128, 8, 64])
        Sbf = sb("Sbf", [128, 8, 64], BF16)
        KTb = sb("KTb", [128, 4, NMEM], BF16)
        Vb = sb("Vb", [128, 2, 512], BF16)
        rstd = sb("rstd", [128, NTM])
        tmpA = sb("tmpA", [128, NTM])
        tmpB = sb("tmpB", [128, NTM], BF16)
        NWB = 4
        wbuf = [sb(f"wbuf{i}", [128, 4096], BF16) for i in range(NWB)]
        ARENA = 16384
        arena = sb("arena", [128, ARENA])
        psb = [st.enter_context(nc.psum_tensor(f"pb{i}", [128, 512], F32)) for i in range(8)]
        wctr = [0]
        pctr = [0]

        def ps():
            i = pctr[0] % 8
            pctr[0] += 1
            return psb[i], f"pb{i}"

        wcache = {}
        USE_CACHE = os.environ.get("K_CACHE", "1") == "1"
        NBLK = {"f1g": 22, "f1u": 22, "f1d": 32, "f2g": 22, "f2u": 22, "f2d": 32, "win": 44,
                "pout": 8, "rout": 8, "xout": 8, "wo": 8}

        def wload(W, r0, nrows, c0, ncols, kparts=128):
            i = wctr[0] % NWB
            wctr[0] += 1
            k = nrows // kparts
            view = wbuf[i][0:kparts, 0:k * ncols].rearrange("p (k n) -> p k n", n=ncols)
            flat = wbuf[i][0:kparts, 0:k * ncols]
            name = W.tensor.name
            blk_id = (r0, nrows, c0, ncols)
            cached = USE_CACHE and name in NBLK
            if cached and name not in wcache:
                wcache[name] = (nc.dram_tensor("bfc_" + name, [NBLK[name], 128, 4096], BF16, kind="Internal").ap(), {})
            if cached and blk_id in wcache[name][1]:
                bi = wcache[name][1][blk_id]
                P.dma("sync", flat, wcache[name][0][bi, 0:kparts, 0:k * ncols], writes=[f"wbuf{i}"])
            else:
                src = W[r0:r0 + nrows, c0:c0 + ncols].rearrange("(k p) n -> p k n", p=kparts)
                P.dma("gpsimd", view, src, writes=[f"wbuf{i}"])
                if cached:
                    bi = len(wcache[name][1])
                    assert bi < NBLK[name], name
                    P.dma("sync", wcache[name][0][bi, 0:kparts, 0:k * ncols], flat, reads=[f"wbuf{i}"])
                    wcache[name][1][blk_id] = bi
            return view, f"wbuf{i}"

        CACHED = {"f1g", "f1u", "f1d", "f2g", "f2u", "f2d", "win", "pout", "rout", "xout", "wo"}

        P.op("vector", lambda e: e.memset(S32[:], 0.0), writes=["S32"])
        P.op("vector", lambda e: e.memset(Sbf[:], 0.0), writes=["Sbf"])
        P.op("vector", lambda e: e.memset(zr[:, :, 0:1], 0.0), writes=["zr"])
        P.op("vector", lambda e: e.memset(zp[:, :, 0:15], 0.0), writes=["zp"])

        def rmsnorm(src, gcol, nt, dst, srckey, dstkey, kt=KT):
            sq = arena[:, 14208:14208 + (kt * nt + 1) // 2].bitcast(BF16)[:, 0:kt * nt].rearrange("p (k n) -> p k n", n=nt)
            pt, pk = ps()
            P.op("scalar", lambda e: e.activation(sq, src[:, 0:kt, 0:nt], AF.Square), reads=[srckey], writes=["sq"])
            P.group("tensor", [lambda e, k=k: e.matmul(pt[:, 0:nt], lhsT=ones_bf, rhs=sq[:, k, :],
                                                        start=(k == 0), stop=(k == kt - 1)) for k in range(kt)],
                    reads=["sq", "cbf"], writes=[pk])
            P.op("scalar", lambda e: e.activation(tmpA[:, 0:nt], pt[:, 0:nt], AF.Sqrt, bias=RMS_EPS, scale=1.0 / D),
                 reads=[pk], writes=["tmpA"])
            P.op("vector", lambda e: e.reciprocal(rstd[:, 0:nt], tmpA[:, 0:nt]), reads=["tmpA"], writes=["rstd"])
            for k in range(kt):
                P.op("vector", lambda e, k=k: e.scalar_tensor_tensor(
                    out=dst[:, k, 0:nt], in0=src[:, k, 0:nt], scalar=vec[:, gcol + k:gcol + k + 1],
                    in1=rstd[:, 0:nt], op0=ALU.mult, op1=ALU.mult),
                     reads=[srckey, "rstd", "vec"], writes=[(dstkey, k)])

        def ffn(Wg, Wu, Wd, gcol, nt, keep_res):
            rmsnorm(xh, gcol, nt, xn, "xh", "xn")
            act = arena[:, 0:FC * NTM // 2].bitcast(BF16).rearrange("p (c n) -> p c n", n=NTM)
            for half in range(2):
                c_lo, c_hi = (0, 22) if half == 0 else (22, FC)
                for cb in range(c_lo, c_hi, 2):
                    ncb = min(2, c_hi - cb)
                    wg, wgk = wload(Wg, 0, D, cb * 128, ncb * 128)
                    wu, wuk = wload(Wu, 0, D, cb * 128, ncb * 128)
                    for ci in range(ncb):
                        c = cb + ci
                        pg, pgk = ps()
                        pu, puk = ps()
                        P.group("tensor", [lambda e, k=k, ci=ci: e.matmul(
                            pg[:, 0:nt], lhsT=wg[:, k, ci * 128:(ci + 1) * 128], rhs=xn[:, k, 0:nt],
                            start=(k == 0), stop=(k == KT - 1)) for k in range(KT)], reads=[wgk, "xn"], writes=[pgk])
                        P.group("tensor", [lambda e, k=k, ci=ci: e.matmul(
                            pu[:, 0:nt], lhsT=wu[:, k, ci * 128:(ci + 1) * 128], rhs=xn[:, k, 0:nt],
                            start=(k == 0), stop=(k == KT - 1)) for k in range(KT)], reads=[wuk, "xn"], writes=[puk])
                        P.op("scalar", lambda e: e.activation(tmpB[:, 0:nt], pg[:, 0:nt], AF.Silu),
                             reads=[pgk], writes=["tmpB"])
                        P.op("vector", lambda e, c=c: e.tensor_tensor(act[:, c - c_lo, 0:nt], tmpB[:, 0:nt],
                                                                       pu[:, 0:nt], op=ALU.mult),
                             reads=["tmpB", puk], writes=[("act", c)])
                nch = c_hi - c_lo
                for dcol in range(KT):
                    wd, wdk = wload(Wd, c_lo * 128, nch * 128, dcol * 128, 128)
                    pd, pdk = ps()
                    P.group("tensor", [lambda e, c=c: e.matmul(
                        pd[:, 0:nt], lhsT=wd[:, c, :], rhs=act[:, c, 0:nt],
                        start=(c == 0), stop=(c == nch - 1)) for c in range(nch)],
                            reads=[wdk, "act"], writes=[pdk])
                    P.op("vector", lambda e, dcol=dcol: e.scalar_tensor_tensor(
                        out=xh[:, dcol, 0:nt], in0=pd[:, 0:nt], scalar=0.5, in1=xh[:, dcol, 0:nt],
                        op0=ALU.mult, op1=ALU.add), reads=[pdk, ("xh", dcol)], writes=[("xh", dcol)])
                if half == 0:
                    pass
            P.fence()

        def proj(W, c0, ncols_total, rhs, rhskey, nt, kt, sink, wrows=None):
            wrows = wrows or kt * 128
            nchunks = (ncols_total + 127) // 128
            j = 0
            while j < nchunks:
                nb = min(2, nchunks - j)
                ncols = min(ncols_total - j * 128, nb * 128)
                wv, wk = wload(W, 0, wrows, c0 + j * 128, ncols)
                for ci in range(nb):
                    m = min(128, ncols - ci * 128)
                    pt, pk = ps()
                    P.group("tensor", [lambda e, k=k, ci=ci, m=m, pt=pt: e.matmul(
                        pt[0:m, 0:nt], lhsT=wv[:, k, ci * 128:ci * 128 + m], rhs=rhs[:, k, 0:nt],
                        start=(k == 0), stop=(k == kt - 1)) for k in range(kt)], reads=[wk, rhskey], writes=[pk])
                    sink(j + ci, pt, pk, m)
                j += nb

        def mem_kv():
            mx = arena[:, 0:KT * NMEM].rearrange("p (k n) -> p k n", n=NMEM)
            P.dma("sync", mx, memT.rearrange("(k p) n -> p k n", p=128), writes=["mx"])
            mxn = arena[:, 4096:4096 + KT * NMEM // 2].bitcast(BF16).rearrange("p (k n) -> p k n", n=NMEM)
            if os.environ.get("K_H2", "0") == "1":
                mxn = xn[:, :, 0:NMEM]
            rmsnorm(mx, G_MEM, NMEM, mxn, "mx", "mxn")
            if _small == "norm":
                P.fence()
                return
            kf = arena[:, 8192:8192 + 4 * NMEM].rearrange("p (k n) -> p k n", n=NMEM)

            def sink_k(j, pt, pk, m):
                P.op("vector", lambda e: e.tensor_copy(kf[:, j, :], pt[:, 0:NMEM]), reads=[pk], writes=[("kf", j)])
                if True:
                    P.op("vector", lambda e: e.tensor_copy(KTb[:, j, :], pt[:, 0:NMEM]), reads=[pk], writes=[("KTb", j)])
                else:
                    P.op("scalar", lambda e: e.activation(KTb[:, j, :], pt[:, 0:NMEM], AF.Copy), reads=[pk],
                         writes=[("KTb", j)])
            proj(wmk, 0, 512, mxn, "mxn", NMEM, KT, sink_k)
            P.dma("sync", o_memkT, kf, reads=["kf"])
            if _small == "k":
                P.fence()
                return
            vf = arena[:, 12288:12288 + 1024].rearrange("p (k n) -> p k n", n=512)
            pts = [ps(), ps()]
            for cb in range(2):
                wv, wk = wload(wmv, 0, D, cb * 256, 256)
                for mt in range(2):
                    pt, pk = pts[mt]
                    for k in range(KT):
                        P.op("tensor", lambda e, k=k, mt=mt, pt=pt, cb=cb: e.matmul(
                            pt[:, cb * 256:(cb + 1) * 256], lhsT=mxn[:, k, mt * 128:(mt + 1) * 128], rhs=wv[:, k, :],
                            start=(k == 0), stop=(k == KT - 1)), reads=[wk, "mxn"], writes=[(pk, cb)])
            for mt in range(2):
                pt, pk = pts[mt]
                P.op("vector", lambda e, mt=mt, pt=pt: e.tensor_copy(vf[:, mt, :], pt[:, :]), reads=[pk],
                     writes=[("vf", mt)])
                P.op("vector", lambda e, mt=mt, pt=pt: e.tensor_copy(Vb[:, mt, :], pt[:, :]), reads=[pk],
                     writes=[("Vb", mt)])
            P.dma("sync", o_memv, vf, reads=["vf"])
            P.fence()


        A = lambda off, n: arena[:, off:off + n]
        AB = lambda off, n: arena[:, off:off + (n + 1) // 2].bitcast(BF16)[:, 0:n]
        v3 = lambda ap, n: ap.rearrange("p (k n) -> p k n", n=n)
        SCALE = float(128 ** -0.5)
        zc = sb("zc", [128, 27, 1])
        PCb = sb("PCb", [128, 8])
        ypb = [(psb[6], "pb6"), (psb[7], "pb7")]

        def evac(i, out, in_, reads, writes):
            if i % 2 == 0:
                P.op("vector", lambda e: e.tensor_copy(out, in_), reads=reads, writes=writes)
            else:
                P.op("scalar", lambda e: e.activation(out, in_, AF.Identity), reads=reads, writes=writes)

        def token_shift(last):
            D3 = v3(A(0, 27 * NTB), NTB)
            P.op("vector", lambda e: e.tensor_copy(zc[:], zr[:, :, NTB:NTB + 1]), reads=["zr"], writes=["zc"])
            mub = vec[:, V_MU:V_MU + 27].unsqueeze(2).to_broadcast([128, 27, NTB])
            P.op("vector", lambda e: e.tensor_tensor(D3, zr[:, :, 0:NTB], zr[:, :, 1:1 + NTB], op=ALU.subtract),
                 reads=["zr"], writes=["D3"])
            P.op("vector", lambda e: e.tensor_tensor(D3, D3, mub, op=ALU.mult), reads=["D3", "vec"], writes=["D3"])
            P.op("vector", lambda e: e.tensor_tensor(zr[:, :, 1:1 + NTB], zr[:, :, 1:1 + NTB], D3, op=ALU.add),
                 reads=["D3", "zr", "zc"], writes=["zr"])
            if last:
                sshift = v3(A(7424, 27 * NS), NS)
                P.dma("sync", sshift, sshiftT.rearrange("(k p) n -> p k n", p=128), writes=["sshift"])
                Ds = v3(A(27 * NTB, 27 * NS), NS)
                mus = vec[:, V_MU:V_MU + 27].unsqueeze(2).to_broadcast([128, 27, NS])
                zs = zr[:, :, 1 + NTB:1 + NTM]
                P.op("vector", lambda e: e.tensor_tensor(Ds, sshift, zs, op=ALU.subtract), reads=["zr", "sshift"], writes=["Ds"])
                P.op("vector", lambda e: e.tensor_tensor(Ds, Ds, mus, op=ALU.mult), reads=["Ds", "vec"], writes=["Ds"])
                P.op("vector", lambda e: e.tensor_tensor(zs, zs, Ds, op=ALU.add), reads=["Ds", "zr"], writes=["zr"])
            P.fence()

        def rwkv_elem(cs, n, sample):
            W8 = 8 * n
            xk = zr[:, 8:16, cs]
            sg = A(0, 1024)
            asig, kkn, kmod, bbv = v3(A(1024, W8), n), v3(A(2048, W8), n), v3(A(3072, W8), n), v3(A(4096, W8), n)
            et, t1 = v3(A(5120, W8), n), v3(A(6144, W8), n)
            th = A(8192, 128)
            vcol = lambda c: vec[:, c:c + 8].unsqueeze(2).to_broadcast([128, 8, n])
            P.op("scalar", lambda e: e.activation(th[0:64, 0:n], zr[0:64, 24, cs], AF.Tanh), reads=["zr"], writes=["th"])

            def fm_lora(prow, roff, rhs_ap, sink):
                for hb in range(2):
                    pt, pk = ps()
                    for c in range(4):
                        ch = hb * 4 + c
                        P.op("tensor", lambda e, pt=pt, c=c, ch=ch: e.matmul(
                            pt[:, c * n:(c + 1) * n], lhsT=lora[prow, ch * 128:(ch + 1) * 128], rhs=rhs_ap,
                            start=True, stop=False), reads=["lora", "zr", "th"], writes=[pk])
                        P.op("tensor", lambda e, pt=pt, c=c, ch=ch: e.matmul(
                            pt[:, c * n:(c + 1) * n], lhsT=rows_sb[:, roff + ch * 128:roff + (ch + 1) * 128],
                            rhs=ones_row[:, 0:n], start=False, stop=True), reads=["rows", "cst"], writes=[pk])
                    sink(hb, v3(pt[:, 0:4 * n], n), pk)

            if sample:
                sgf = v3(A(0, W8), n)

                def sink_w(hb, p3, pk):
                    P.op("scalar", lambda e: e.activation(sgf[:, hb * 4:(hb + 1) * 4, :], p3, AF.Sigmoid), reads=[pk], writes=["sg"])
                fm_lora(slice(0, 64), 0, th[0:64, 0:n], sink_w)
                P.op("scalar", lambda e: e.activation(sgf, sgf, AF.Exp, scale=-EDEC), reads=["sg"], writes=["sg"])
            else:
                for hb in range(2):
                    pt, pk = ps()
                    P.op("tensor", lambda e, pt=pt, hb=hb: e.matmul(pt[:, :], lhsT=th[0:64, 0:128], rhs=lora[0:64, hb * 512:(hb + 1) * 512],
                                                       start=True, stop=False), reads=["th", "lora"], writes=[pk])
                    P.op("tensor", lambda e, pt=pt, hb=hb: e.matmul(pt[:, :], lhsT=ones_row[:, 0:128], rhs=rows_sb[:, hb * 512:(hb + 1) * 512],
                                                       start=False, stop=True), reads=["cst", "rows"], writes=[pk])
                    P.op("scalar", lambda e, pt=pt, hb=hb: e.activation(sg[:, hb * 512:(hb + 1) * 512], pt[:, :], AF.Sigmoid),
                         reads=[pk], writes=["sg"])

            def sink_a(hb, p3, pk):
                P.op("scalar", lambda e: e.activation(asig[:, hb * 4:(hb + 1) * 4, :], p3, AF.Sigmoid), reads=[pk], writes=["asig"])
            fm_lora(slice(64, 128), 1024, zr[64:128, 24, cs], sink_a)
            P.op("vector", lambda e: e.tensor_tensor(t1, xk, vcol(V_KK), op=ALU.mult), reads=["zr", "vec"], writes=["t1"])
            P.op("scalar", lambda e: e.activation(et, t1, AF.Square), reads=["t1"], writes=["et"])
            for hb in range(2):
                pt, pk = ps()
                for c in range(4):
                    ch = hb * 4 + c
                    P.op("tensor", lambda e, pt=pt, c=c, ch=ch: e.matmul(pt[:, c * n:(c + 1) * n], lhsT=bones, rhs=et[:, ch, :],
                                                       start=True, stop=True), reads=["cst", "et"], writes=[pk])
                P.op("vector", lambda e, pt=pt, hb=hb: e.tensor_scalar_max(kkn[:, hb * 4:(hb + 1) * 4, :], v3(pt[:, 0:4 * n], n), 1e-24),
                     reads=[pk], writes=["kkn"])
            P.op("scalar", lambda e: e.activation(kkn, kkn, AF.Sqrt), reads=["kkn"], writes=["kkn"])
            P.op("vector", lambda e: e.reciprocal(kkn, kkn), reads=["kkn"], writes=["kkn"])
            P.op("vector", lambda e: e.tensor_tensor(kkn, kkn, t1, op=ALU.mult), reads=["kkn", "t1"], writes=["kkn"])
            P.op("vector", lambda e: e.scalar_tensor_tensor(out=et, in0=asig, scalar=-1.0, in1=vcol(V_KA), op0=ALU.add, op1=ALU.mult),
                 reads=["asig", "vec", "et"], writes=["et"])
            P.op("vector", lambda e: e.scalar_tensor_tensor(out=kmod, in0=et, scalar=1.0, in1=xk, op0=ALU.add, op1=ALU.mult),
                 reads=["et", "zr"], writes=["kmod"])
            P.op("vector", lambda e: e.tensor_tensor(bbv, kkn, asig, op=ALU.mult), reads=["kkn", "asig"], writes=["bbv"])

        def scan_prompt(cs, want_y):
            n = 128
            xr, xv = zr[:, 0:8, cs], zr[:, 16:24, cs]
            sg = A(0, 1024)
            kkn, kmod, bbv = v3(A(2048, 1024), n), v3(A(3072, 1024), n), v3(A(4096, 1024), n)
            et = v3(A(5120, 1024), n)
            b3 = lambda off: v3(AB(off, 1024), n)
            Afm, Bfm, Kfm, Rfm = b3(8320), b3(8832), b3(9344), b3(9856)
            AT, BhT, KhT, VT = AB(10368, 1024), AB(10880, 1024), AB(11392, 1024), AB(11904, 1024)
            tb = b3(12416)
            hs = lambda x, hb: x[:, hb * 4:(hb + 1) * 4, :]

            def cum(tri, sink):
                for hb in range(2):
                    pt, pk = ps()
                    for c in range(4):
                        ch = hb * 4 + c
                        P.op("tensor", lambda e, pt=pt, c=c, ch=ch: e.matmul(pt[:, c * 128:(c + 1) * 128], lhsT=sg[:, ch * 128:(ch + 1) * 128], rhs=tri,
                                                           start=True, stop=True), reads=["sg", "cst"], writes=[pk])
                    sink(hb, v3(pt[:, :], n), pk)

            def sink_incl(hb, p3, pk):
                if want_y:
                    P.op("scalar", lambda e: e.activation(hs(et, hb), p3, AF.Exp), reads=[pk], writes=["et"])
                    P.op("vector", lambda e: e.tensor_tensor(hs(Rfm, hb), hs(xr, hb), hs(et, hb), op=ALU.mult),
                         reads=["et", "zr"], writes=["Rfm"])
                P.op("scalar", lambda e: e.activation(PCb[:, hb * 4:(hb + 1) * 4], p3[:, :, 127], AF.Exp), reads=[pk], writes=["PCb"])
                P.op("scalar", lambda e: e.activation(hs(et, hb), p3, AF.Exp, scale=-1.0), reads=[pk, "Rfm"], writes=["et"])
                P.op("vector", lambda e: e.tensor_tensor(hs(Bfm, hb), hs(bbv, hb), hs(et, hb), op=ALU.mult),
                     reads=["et", "bbv"], writes=["Bfm"])
                P.op("vector", lambda e: e.tensor_tensor(hs(Kfm, hb), hs(kmod, hb), hs(et, hb), op=ALU.mult),
                     reads=["et", "kmod"], writes=["Kfm"])
            cum(triI, sink_incl)
            if int(os.environ.get("K_SCAN", "99")) < 2:
                return

            def sink_excl(hb, p3, pk):
                P.op("scalar", lambda e: e.activation(hs(et, hb), p3, AF.Exp), reads=[pk, "Bfm", "Kfm"], writes=["et"])
                P.op("vector", lambda e: e.scalar_tensor_tensor(out=hs(Afm, hb), in0=hs(kkn, hb), scalar=-1.0, in1=hs(et, hb),
                                                                 op0=ALU.mult, op1=ALU.mult), reads=["et", "kkn"], writes=["Afm"])
            cum(triX, sink_excl)

            def tr_to(src3, srckey, dst, dstkey):
                for hb in range(2):
                    pt, pk = ps()
                    for c in range(4):
                        ch = hb * 4 + c
                        P.op("tensor", lambda e, pt=pt, c=c, ch=ch: e.matmul(pt[:, c * 128:(c + 1) * 128], lhsT=src3[:, ch, :], rhs=ident_bf,
                                                           start=True, stop=True), reads=[srckey, "cbf"], writes=[pk])
                    evac(hb, dst[:, hb * 512:(hb + 1) * 512], pt[:, :], [pk], [dstkey])
            tr_to(Afm, "Afm", AT, "AT")
            if int(os.environ.get("K_SCAN", "99")) < 3:
                return

            def sink_suf(hb, p3, pk):
                P.op("scalar", lambda e: e.activation(hs(et, hb), p3, AF.Exp), reads=[pk, "Afm"], writes=["et"])
                P.op("vector", lambda e: e.tensor_tensor(hs(tb, hb), hs(bbv, hb), hs(et, hb), op=ALU.mult),
                     reads=["et", "bbv"], writes=["tb"])
            cum(triS, sink_suf)
            tr_to(tb, "tb", BhT, "BhT")
            P.op("vector", lambda e: e.tensor_tensor(tb, kmod, et, op=ALU.mult), reads=["et", "kmod", "BhT", "tb"], writes=["tb"])
            tr_to(tb, "tb", KhT, "KhT")
            P.op("vector", lambda e: e.tensor_copy(tb, xv), reads=["zr", "KhT", "tb"], writes=["tb"])
            tr_to(tb, "tb", VT, "VT")
            if int(os.environ.get("K_SCAN", "99")) < 4:
                return

            GOFF = [13056, 13568, 14080, 14592, 15104, 15616, 0, 512, 1024, 1536, 5120]
            gm = lambda i: AB(GOFF[i], 1024)
            UT = AB(16128, 512)
            pos = lambda i: (i % 2) * 4 + i // 2
            blk = lambda i: slice(pos(i) * 128, (pos(i) + 1) * 128)
            yfm = v3(A(7168, 1024), n)
            P.fence()

            def mm8(dst, dk, lt, ltk, rt, rtk, add=None, addk=None):
                bank = [ps(), ps()]
                for i in range(8):
                    pt, pk = bank[i // 4]
                    P.op("tensor", lambda e, pt=pt, i=i: e.matmul(pt[:, (i % 4) * 128:(i % 4 + 1) * 128], lhsT=lt[:, i * 128:(i + 1) * 128],
                                                      rhs=rt[:, i * 128:(i + 1) * 128], start=True, stop=True),
                         reads=[ltk, rtk], writes=[pk])
                for hb in range(2):
                    pt, pk = bank[hb]
                    dk_ = (dk, hb)
                    if add is None:
                        evac(hb, dst[:, hb * 512:(hb + 1) * 512], pt[:, :], [pk], [dk_])
                    else:
                        P.op("vector", lambda e, pt=pt, hb=hb: e.tensor_tensor(dst[:, hb * 512:(hb + 1) * 512], pt[:, :],
                                                                  add[:, hb * 512:(hb + 1) * 512], op=ALU.add),
                             reads=[pk, addk], writes=[dk_])

            for G in range(2):
                heads = [8 * G + i for i in range(8)]
                hrow = lambda x3, h: x3[(h % 2) * 64:(h % 2) * 64 + 64, h // 2, :]
                NakT, Mrb, Mrk, Apf, Nakp = gm(6), gm(7), gm(8), gm(9), gm(10)
                specs = [(gm(0), "g0", Bfm, "Bfm", Afm, "Afm", mSU), (gm(1), "g1", Afm, "Afm", Bfm, "Bfm", mSL),
                         (NakT, "g6", Afm, "Afm", Kfm, "Kfm", mSL)]
                if want_y:
                    specs += [(Mrb, "g7", Bfm, "Bfm", Rfm, "Rfm", mIU), (Mrk, "g8", Kfm, "Kfm", Rfm, "Rfm", mIU)]
                for (dst, dk, la, lak, ra, rak, msk) in specs:
                    bank = [ps(), ps()]
                    for i, h in enumerate(heads):
                        pt, pk = bank[h % 2]
                        P.op("tensor", lambda e, pt=pt, i=i, h=h, la=la, ra=ra: e.matmul(
                            pt[:, (i // 2) * 128:(i // 2 + 1) * 128], lhsT=hrow(la, h), rhs=hrow(ra, h), start=True, stop=True),
                             reads=[lak, rak], writes=[pk])
                    for par in range(2):
                        pt, pk = bank[par]
                        P.op("vector", lambda e, pt=pt, dst=dst, msk=msk, par=par: e.tensor_tensor(
                            dst[:, par * 512:(par + 1) * 512], pt[:, :], msk, op=ALU.mult),
                             reads=[pk, "cst"], writes=[(dk, par)])
                Nc, Nck, Lc, Lck = gm(0), "g0", gm(1), "g1"
                Nn, Nnk, Ln, Lnk = gm(2), "g2", gm(3), "g3"
                Tc, Tck, Tn, Tnk = gm(4), "g4", gm(5), "g5"
                for hb in range(2):
                    P.op("vector", lambda e, hb=hb: e.tensor_tensor(Tc[:, hb * 512:(hb + 1) * 512], Nc[:, hb * 512:(hb + 1) * 512], I4, op=ALU.add),
                         reads=[Nck, "cst"], writes=[(Tck, hb)])
                for k in range(1, 7):
                    if k < 6:
                        mm8(Nn, Nnk, Lc, Lck, Nc, Nck)
                    mm8(Ln, Lnk, Nc, Nck, Lc, Lck)
                    mm8(Tn, Tnk, Ln, Lnk, Tc, Tck, add=Tc, addk=Tck)
                    Nc, Nck, Nn, Nnk = Nn, Nnk, Nc, Nck
                    Lc, Lck, Ln, Lnk = Ln, Lnk, Lc, Lck
                    Tc, Tck, Tn, Tnk = Tn, Tnk, Tc, Tck
                T_, Tk = Tc, Tck
                bank = [ps(), ps()]
                for i, h in enumerate(heads):
                    pt, pk = bank[h % 2]
                    P.op("tensor", lambda e, pt=pt, i=i, h=h: e.matmul(
                        pt[(h % 2) * 64:(h % 2) * 64 + 64, (i // 2) * 128:(i // 2 + 1) * 128],
                        lhsT=AT[:, h * 64:(h + 1) * 64], rhs=T_[:, blk(i)], start=True, stop=True),
                         reads=["AT", Tk], writes=[pk])
                for par in range(2):
                    pt, pk = bank[par]
                    rs = slice(par * 64, par * 64 + 64)
                    evac(par, Apf[rs, 0:512], pt[rs, 0:512], [pk], [("g9", par)])
                mm8(Nakp, "g10", NakT, "g6", T_, Tk)
                bank = [ps(), ps()]
                for i, h in enumerate(heads):
                    hp, h2 = h // 2, h % 2
                    rs = slice(h2 * 64, h2 * 64 + 64)
                    pt, pk = bank[h2]
                    P.op("tensor", lambda e, pt=pt, i=i, hp=hp, rs=rs: e.matmul(
                        pt[:, (i // 2) * 64:(i // 2 + 1) * 64], lhsT=Apf[rs, (i // 2) * 128:(i // 2 + 1) * 128],
                        rhs=Sbf[rs, hp, :], start=True, stop=False), reads=["g9", "Sbf"], writes=[pk])
                    P.op("tensor", lambda e, pt=pt, i=i, h=h: e.matmul(
                        pt[:, (i // 2) * 64:(i // 2 + 1) * 64], lhsT=Nakp[:, blk(i)],
                        rhs=VT[:, h * 64:(h + 1) * 64], start=False, stop=True), reads=["g10", "VT"], writes=[pk])
                for par in range(2):
                    pt, pk = bank[par]
                    evac(par, UT[:, par * 256:(par + 1) * 256], pt[:, 0:256], [pk], [("UT", par)])
                UTb = lambda i: UT[:, pos(i) * 64:(pos(i) + 1) * 64]
                if want_y:
                    bank = [ps(), ps()]
                    for i, h in enumerate(heads):
                        hp, h2 = h // 2, h % 2
                        rs = slice(h2 * 64, h2 * 64 + 64)
                        pt, pk = bank[h2]
                        yo = pt[rs, (i // 2) * 128:(i // 2 + 1) * 128]
                        P.op("tensor", lambda e, yo=yo, rs=rs, hp=hp: e.matmul(yo, lhsT=Sbf[rs, hp, :], rhs=Rfm[rs, hp, :], start=True, stop=False),
                             reads=["Sbf", "Rfm"], writes=[pk])
                        P.op("tensor", lambda e, yo=yo, i=i: e.matmul(yo, lhsT=UTb(i), rhs=Mrb[:, blk(i)], start=False, stop=False),
                             reads=["UT", "g7"], writes=[pk])
                        P.op("tensor", lambda e, yo=yo, i=i, h=h: e.matmul(yo, lhsT=VT[:, h * 64:(h + 1) * 64], rhs=Mrk[:, blk(i)], start=False, stop=True),
                             reads=["VT", "g8"], writes=[pk])
                    for par in range(2):
                        pt, pk = bank[par]
                        rs = slice(par * 64, par * 64 + 64)
                        evac(par, yfm[rs, 4 * G:4 * G + 4, :], v3(pt[rs, 0:512], 128), [pk], [("yfm", G * 2 + par)])
                bank = [ps(), ps()]
                for i, h in enumerate(heads):
                    h2 = h % 2
                    rs = slice(h2 * 64, h2 * 64 + 64)
                    pt, pk = bank[h2]
                    so = pt[rs, (i // 2) * 64:(i // 2 + 1) * 64]
                    P.op("tensor", lambda e, so=so, i=i, h=h: e.matmul(so, lhsT=BhT[:, h * 64:(h + 1) * 64], rhs=UTb(i), start=True, stop=False),
                         reads=["BhT", "UT"], writes=[pk])
                    P.op("tensor", lambda e, so=so, h=h: e.matmul(so, lhsT=KhT[:, h * 64:(h + 1) * 64], rhs=VT[:, h * 64:(h + 1) * 64], start=False, stop=True),
                         reads=["KhT", "VT"], writes=[pk])
                hps = slice(4 * G, 4 * G + 4)
                P.op("vector", lambda e, hps=hps: e.tensor_tensor(S32[:, hps, :], S32[:, hps, :],
                                                                   PCb[:, hps].unsqueeze(2).to_broadcast([128, 4, 64]), op=ALU.mult),
                     reads=["PCb", "S32", "Sbf"], writes=["S32"])
                for par in range(2):
                    pt, pk = bank[par]
                    rs = slice(par * 64, par * 64 + 64)
                    P.op("vector", lambda e, hps=hps, pt=pt, rs=rs: e.tensor_tensor(S32[rs, hps, :], S32[rs, hps, :], v3(pt[rs, 0:256], 64), op=ALU.add),
                         reads=[pk, "S32"], writes=["S32"])
                P.op("vector", lambda e, hps=hps: e.tensor_copy(Sbf[:, hps, :], S32[:, hps, :]), reads=["S32", "Sbf"], writes=["Sbf"])
            P.fence()

        def rwkv_post(cs, n, ycol0, from_psum):
            W8 = 8 * n
            xr, xv = zr[:, 0:8, cs], zr[:, 16:24, cs]
            rstdv, kmod, bbv = v3(A(1024, W8), n), v3(A(3072, W8), n), v3(A(4096, W8), n)
            et, t1, yfm = v3(A(5120, W8), n), v3(A(6144, W8), n), v3(A(7168, W8), n)
            sgl = v3(AB(12928, 256), 128)
            vcol = lambda c: vec[:, c:c + 8].unsqueeze(2).to_broadcast([128, 8, n])
            hs = lambda x, hb: x[:, hb * 4:(hb + 1) * 4, :]
            if from_psum:
                for hb in range(2):
                    evac(hb, hs(yfm, hb), v3(ypb[hb][0][:, :], n), [ypb[hb][1]], ["yfm"])

            def bmm(lhs, src, srck, sink):
                for hb in range(2):
                    pt, pk = ps()
                    for c in range(4):
                        ch = hb * 4 + c
                        P.op("tensor", lambda e, pt=pt, c=c, ch=ch: e.matmul(pt[:, c * n:(c + 1) * n], lhsT=lhs, rhs=src[:, ch, :],
                                                           start=True, stop=True), reads=["cst", srck], writes=[pk])
                    sink(hb, v3(pt[:, 0:4 * n], n), pk)
            bmm(bones64, yfm, "yfm", lambda hb, p3, pk: P.op(
                "vector", lambda e: e.tensor_tensor(hs(t1, hb), hs(yfm, hb), p3, op=ALU.subtract), reads=["yfm", pk], writes=["t1"]))
            P.op("scalar", lambda e: e.activation(et, t1, AF.Square), reads=["t1"], writes=["et"])
            bmm(bones64, et, "et", lambda hb, p3, pk: P.op(
                "scalar", lambda e: e.activation(hs(rstdv, hb), p3, AF.Sqrt, bias=GN_EPS, scale=1.0), reads=[pk], writes=["rstdv"]))
            P.op("vector", lambda e: e.reciprocal(rstdv, rstdv), reads=["rstdv"], writes=["rstdv"])
            P.op("vector", lambda e: e.tensor_tensor(t1, t1, rstdv, op=ALU.mult), reads=["t1", "rstdv"], writes=["t1"])
            P.op("vector", lambda e: e.tensor_tensor(t1, t1, vcol(V_LG), op=ALU.mult), reads=["t1", "vec"], writes=["t1"])
            P.op("vector", lambda e: e.tensor_tensor(t1, t1, vcol(V_LB), op=ALU.add), reads=["t1", "vec"], writes=["t1"])
            P.op("vector", lambda e: e.tensor_tensor(et, xr, kmod, op=ALU.mult), reads=["zr", "kmod", "et"], writes=["et"])
            P.op("vector", lambda e: e.tensor_tensor(et, et, vcol(V_RK), op=ALU.mult), reads=["et", "vec"], writes=["et"])
            bmm(bones, et, "et", lambda hb, p3, pk: P.op(
                "vector", lambda e: e.tensor_tensor(hs(bbv, hb), p3, hs(xv, hb), op=ALU.mult), reads=[pk, "zr", "bbv"], writes=["bbv"]))
            P.op("vector", lambda e: e.tensor_tensor(t1, t1, bbv, op=ALU.add), reads=["t1", "bbv"], writes=["t1"])
            P.op("scalar", lambda e: e.activation(sgl[:, 0, 0:n], zr[:, 25, cs], AF.Sigmoid), reads=["zr"], writes=["sgl"])
            P.op("scalar", lambda e: e.activation(sgl[0:32, 1, 0:n], zr[0:32, 26, cs], AF.Sigmoid), reads=["zr", "sgl"], writes=["sgl"])
            for hb in range(2):
                pt, pk = ps()
                for c in range(4):
                    ch = hb * 4 + c
                    P.op("tensor", lambda e, pt=pt, c=c, ch=ch: e.matmul(pt[:, c * n:(c + 1) * n], lhsT=gupb[:, 0, ch * 128:(ch + 1) * 128],
                                                       rhs=sgl[:, 0, 0:n], start=True, stop=False), reads=["gupb", "sgl"], writes=[pk])
                    P.op("tensor", lambda e, pt=pt, c=c, ch=ch: e.matmul(pt[:, c * n:(c + 1) * n], lhsT=gupb[0:32, 1, ch * 128:(ch + 1) * 128],
                                                       rhs=sgl[0:32, 1, 0:n], start=False, stop=True), reads=["gupb", "sgl"], writes=[pk])
                P.op("vector", lambda e, pt=pt, hb=hb: e.tensor_tensor(yg[:, hb * 4:(hb + 1) * 4, ycol0:ycol0 + n], hs(t1, hb),
                                                          v3(pt[:, 0:4 * n], n), op=ALU.mult), reads=[pk, "t1"], writes=["yg"])

        def sample_scan():
            n = NS
            cs = slice(1 + NTB, 1 + NTM)
            xr, xv = zr[:, 0:8, cs], zr[:, 16:24, cs]
            sgf, kkn, kmod, bbv = v3(A(0, 128), n), v3(A(2048, 128), n), v3(A(3072, 128), n), v3(A(4096, 128), n)
            Vs = A(5120, 256)[0:64, :].rearrange("p (h n) -> p h n", n=n)
            Ys = A(5376, 256)[0:64, :].rearrange("p (h n) -> p h n", n=n)
            rowb = [arena[0:16, 7168:8192], arena[0:16, 9216:10240]]
            ops = [(kkn, "kkn", -1.0), (sgf, "sg", 1.0), (bbv, "bbv", 1.0), (kmod, "kmod", 1.0), (xr, "zr", 1.0)]
            for oi, (X, xk_, sc) in enumerate(ops):
                rb = rowb[oi % 2]
                rbk = f"rowb{oi % 2}"
                for hb in range(2):
                    pt, pk = ps()
                    for c in range(4):
                        ch = hb * 4 + c
                        P.op("tensor", lambda e, pt=pt, c=c, ch=ch, X=X: e.matmul(pt[0:16, c * 128:(c + 1) * 128], lhsT=X[:, ch, :], rhs=ident,
                                                                start=True, stop=True), reads=[xk_, "cst"], writes=[pk])
                    P.op("scalar", lambda e, pt=pt, hb=hb, rb=rb, sc=sc: e.activation(rb[:, hb * 512:(hb + 1) * 512], pt[0:16, :], AF.Identity, scale=sc),
                         reads=[pk], writes=[rbk])
                P.dma("sync", scr_rows[oi], rb, reads=[rbk], writes=["scr_rows"])
            Vs4 = Vs.rearrange("p (hp h2) n -> p hp h2 n", h2=2)
            P.op("vector", lambda e: e.tensor_copy(Vs4[:, :, 0, :], xv[0:64, :, :]), reads=["zr"], writes=["Vs"])
            P.dma("sync", Vs4[:, :, 1, :], xv[64:128, :, :], reads=["zr"], writes=["Vs"])
            P.fence()
            S = v3(A(8192, 1024)[0:64, :], 64)
            bc = [v3(A(9216 + i * 1024, 1024)[0:64, :], 64) for i in range(5)]
            tt = v3(A(14336, 1024)[0:64, :], 64)
            S1 = v3(A(15360, 1024)[0:64, :], 64)
            sa = A(7168, 16)[0:64, :]
            for s_ in range(NS):
                P.dma("sync", S, swkv[:, s_], writes=["S"])
                for i in range(5):
                    P.dma("sync", A(9216 + i * 1024, 1024)[0:64, :], scr_rows[i, s_:s_ + 1, :].to_broadcast([64, 1024]),
                          writes=[f"bc{i}"])
                a_, w_, b_, k_, r_ = bc
                P.op("vector", lambda e: e.tensor_tensor(tt, S, a_, op=ALU.mult), reads=["S", "bc0", "tt"], writes=["tt"])
                P.op("vector", lambda e: e.tensor_reduce(out=sa, in_=tt, axis=AX.X, op=ALU.add), reads=["tt"], writes=["sa"])
                P.op("vector", lambda e: e.tensor_tensor(S1, S, w_, op=ALU.mult), reads=["S", "bc1", "S1"], writes=["S1"])
                P.op("vector", lambda e: e.tensor_tensor(tt, b_, sa.unsqueeze(2).to_broadcast([64, 16, 64]), op=ALU.mult),
                     reads=["bc2", "sa", "tt"], writes=["tt"])
                P.op("vector", lambda e: e.tensor_tensor(S1, S1, tt, op=ALU.add), reads=["S1", "tt"], writes=["S1"])
                P.op("vector", lambda e, s_=s_: e.tensor_tensor(tt, k_, Vs[:, :, s_].unsqueeze(2).to_broadcast([64, 16, 64]), op=ALU.mult),
                     reads=["bc3", "Vs", "tt"], writes=["tt"])
                P.op("vector", lambda e: e.tensor_tensor(S1, S1, tt, op=ALU.add), reads=["S1", "tt"], writes=["S1"])
                P.op("vector", lambda e: e.tensor_tensor(tt, S1, r_, op=ALU.mult), reads=["S1", "bc4", "tt"], writes=["tt"])
                P.op("vector", lambda e, s_=s_: e.tensor_reduce(out=Ys[:, :, s_], in_=tt, axis=AX.X, op=ALU.add), reads=["tt"], writes=["Ys"])
                P.dma("sync", o_wkvs[:, s_], S1, reads=["S1"])
            yfm = v3(A(7168, 128), n)
            Ys4 = Ys.rearrange("p (hp h2) n -> p hp h2 n", h2=2)
            P.op("vector", lambda e: e.tensor_copy(yfm[0:64, :, :], Ys4[:, :, 0, :]), reads=["Ys", "sa"], writes=["yfm"])
            P.dma("sync", yfm[64:128, :, :], Ys4[:, :, 1, :], reads=["Ys", "sa"], writes=["yfm"])
            P.fence()

        def pool_branch(t_own, nt, last):
            E = 15 + NTB
            icnt = v3(A(8192, 4 * NTM), NTM)
            P.dma("sync", icnt, invc[t_own:t_own + 1, :].to_broadcast([128, 4 * NTM]).rearrange("p (g n) -> p g n", n=NTM),
                  writes=["icnt"])
            lv = [A(i * 288, E) for i in range(4)]
            pbf = AB(1152, NTM)
            if last:
                spl = A(1408, 4 * NS * 15).rearrange("p (g n t) -> p g n t", g=4, n=NS)
                P.dma("sync", spl, spoolT.rearrange("(g p) n t -> p g n t", p=128), writes=["spl"])
                ssum = A(2368, NS)
            for g in range(4):
                w = 2 << g
                x = zp[:, g, 0:E]
                src = x
                for l in range(g + 1):
                    sh = 1 << l
                    lo = 2 * sh - 1
                    dst = lv[l]
                    P.op("vector", lambda e, dst=dst, src=src, lo=lo, sh=sh: e.tensor_tensor(dst[:, lo:E], src[:, lo:E], src[:, lo - sh:E - sh], op=ALU.add),
                         reads=["zp", "lv"], writes=["lv"])
                    src = dst
                P.op("vector", lambda e, src=src, g=g: e.tensor_tensor(lv[3][:, 15:E] if g < 3 else lv[2][:, 15:E], src[:, 15:E], icnt[:, g, 0:NTB], op=ALU.mult),
                     reads=["lv", "icnt"], writes=["lv"])
                pm = lv[3] if g < 3 else lv[2]
                P.op("vector", lambda e, pm=pm, x=x: e.tensor_tensor(pbf[:, 0:NTB], pm[:, 15:E], x[:, 15:E], op=ALU.subtract),
                     reads=["lv", "zp", "pbf"], writes=["pbf"])
                if last:
                    xs = zp[:, g, 15 + NTB:15 + NTM]
                    P.op("vector", lambda e, g=g, w=w: e.tensor_reduce(out=ssum, in_=spl[:, g, :, 15 - (w - 1):15], axis=AX.X, op=ALU.add),
                         reads=["spl", "ssum"], writes=["ssum"])
                    P.op("vector", lambda e, xs=xs: e.tensor_tensor(ssum, ssum, xs, op=ALU.add), reads=["ssum", "zp"], writes=["ssum"])
                    P.op("vector", lambda e, xs=xs, w=w: e.scalar_tensor_tensor(out=pbf[:, NTB:NTM], in0=ssum, scalar=1.0 / w, in1=xs,
                                                                          op0=ALU.mult, op1=ALU.subtract), reads=["ssum", "zp", "pbf"], writes=["pbf"])
                pt, pk = ps()
                P.op("tensor", lambda e, pt=pt, g=g: e.matmul(pt[:, 0:nt], lhsT=pgwb[:, g, :], rhs=pbf[:, 0:nt], start=True, stop=True),
                     reads=["pgwb", "pbf"], writes=[pk])
                P.op("vector", lambda e, pt=pt, g=g: e.tensor_scalar_mul(mix[:, g, 0:nt], pt[:, 0:nt], vec[:, V_PS + g:V_PS + g + 1]),
                     reads=[pk, "vec"], writes=["mix"])

        def attn_prompt(c0):
            n = 128
            sc = v3(A(4096, 1024), 256)
            pnb = v3(AB(5120, 1024), 256)
            pTb = v3(AB(5632, 1024), 128)
            mx, nb, sm, rs = A(6144, 4), A(6148, 4), A(6152, 4), A(6156, 4)
            for hp_ in range(2):
                pt, pk = ps()
                for hh in range(2):
                    h = hp_ * 2 + hh
                    P.op("tensor", lambda e, pt=pt, hh=hh, h=h: e.matmul(pt[:, hh * 256:(hh + 1) * 256], lhsT=zq[:, h, c0:c0 + n], rhs=KTb[:, h, :],
                                                          start=True, stop=True), reads=["zq", "KTb"], writes=[pk])
                P.op("vector", lambda e, pt=pt, hp_=hp_: e.tensor_reduce(out=mx[:, hp_ * 2:hp_ * 2 + 2], in_=v3(pt[:, :], 256), axis=AX.X, op=ALU.max),
                     reads=[pk, "mx"], writes=["mx"])
                P.op("vector", lambda e, hp_=hp_: e.tensor_scalar_mul(nb[:, hp_ * 2:hp_ * 2 + 2], mx[:, hp_ * 2:hp_ * 2 + 2], -SCALE),
                     reads=["mx", "nb"], writes=["nb"])
                for hh in range(2):
                    h = hp_ * 2 + hh
                    P.op("scalar", lambda e, pt=pt, hh=hh, h=h: e.activation(sc[:, h, :], pt[:, hh * 256:(hh + 1) * 256], AF.Exp, bias=nb[:, h:h + 1],
                                                              scale=SCALE, accum_out=sm[:, h:h + 1]), reads=[pk, "nb", "sc", "sm"], writes=["sc", "sm"])
            P.op("vector", lambda e: e.reciprocal(rs, sm), reads=["sm", "rs"], writes=["rs"])
            P.op("vector", lambda e: e.tensor_tensor(pnb, sc, rs.unsqueeze(2).to_broadcast([128, 4, 256]), op=ALU.mult),
                 reads=["sc", "rs", "pnb"], writes=["pnb"])
            for hp_ in range(2):
                pt, pk = ps()
                for hh in range(2):
                    h = hp_ * 2 + hh
                    for mt in range(2):
                        j = hh * 2 + mt
                        P.op("tensor", lambda e, pt=pt, j=j, h=h, mt=mt: e.matmul(pt[:, j * 128:(j + 1) * 128], lhsT=pnb[:, h, mt * 128:(mt + 1) * 128], rhs=ident_bf,
                                                                start=True, stop=True), reads=["pnb", "cbf"], writes=[pk])
                evac(hp_, pTb[:, hp_ * 4:(hp_ + 1) * 4, :], v3(pt[:, :], 128), [pk, "pTb"], ["pTb"])
            pt, pk = ps()
            for h in range(4):
                for mt in range(2):
                    P.op("tensor", lambda e, pt=pt, h=h, mt=mt: e.matmul(pt[:, h * 128:(h + 1) * 128], lhsT=Vb[:, mt, h * 128:(h + 1) * 128], rhs=pTb[:, h * 2 + mt, :],
                                                          start=(mt == 0), stop=(mt == 1)), reads=["Vb", "pTb"], writes=[pk])
            P.op("vector", lambda e, pt=pt: e.tensor_copy(oT[:, :, c0:c0 + n], v3(pt[:, :], 128)), reads=[pk], writes=["oT"])

        def attn_sample():
            qrow = arena[0:16, 0:512]
            pt, pk = ps()
            for h in range(4):
                P.op("tensor", lambda e, pt=pt, h=h: e.matmul(pt[0:16, h * 128:(h + 1) * 128], lhsT=zq[:, h, NTB:NTM], rhs=ident_bf, start=True, stop=True),
                     reads=["zq", "cbf"], writes=[pk])
            P.op("vector", lambda e, pt=pt: e.tensor_copy(qrow, pt[0:16, :]), reads=[pk], writes=["qrow"])
            P.dma("sync", scr_q, qrow, reads=["qrow"], writes=["scr_q"])
            qbc = v3(A(8192, NS * 512), 512)
            P.dma("sync", qbc, scr_q.rearrange("n c -> (n c)").unsqueeze(0).to_broadcast([128, NS * 512]).rearrange("p (n c) -> p n c", c=512),
                  reads=["scr_q"], writes=["qbc"])
            KV = [v3(A(512 + i * 1024, 1024), 512) for i in range(2)]
            prod = A(2560, 512)
            s_all = v3(A(3072, 128), 64)
            p64 = A(3200, 256)
            pTs = v3(A(3456, 128), 64)
            mx, nb, sm, rs = A(3584, 1), A(3585, 1), A(3586, 1), A(3587, 1)
            for s_ in range(NS):
                kb = KV[s_ % 2]
                kbk = f"kv{s_ % 2}"
                P.dma("sync", kb, kc[s_].rearrange("(mt p) c -> p mt c", p=128), writes=[kbk])
                for mt in range(2):
                    P.op("vector", lambda e, kb=kb, mt=mt, s_=s_: e.tensor_tensor(prod, kb[:, mt, :], qbc[:, s_, :], op=ALU.mult),
                         reads=[kbk, "qbc", "prod"], writes=["prod"])
                    P.op("vector", lambda e, mt=mt, s_=s_: e.tensor_reduce(out=s_all[:, mt, s_ * 4:(s_ + 1) * 4], in_=v3(prod, 128), axis=AX.X, op=ALU.add),
                         reads=["prod", "s_all"], writes=["s_all"])
            pt, pk = ps()
            for mt in range(2):
                P.op("tensor", lambda e, pt=pt, mt=mt: e.matmul(pt[0:64, mt * 128:(mt + 1) * 128], lhsT=s_all[:, mt, :], rhs=ident, start=True, stop=True),
                     reads=["s_all", "cst"], writes=[pk])
            P.op("vector", lambda e, pt=pt: e.tensor_reduce(out=mx[0:64, :], in_=pt[0:64, 0:256], axis=AX.X, op=ALU.max), reads=[pk], writes=["mx"])
            P.op("vector", lambda e: e.tensor_scalar_mul(nb[0:64, :], mx[0:64, :], -SCALE), reads=["mx"], writes=["nb"])
            P.op("scalar", lambda e, pt=pt: e.activation(p64[0:64, :], pt[0:64, 0:256], AF.Exp, bias=nb[0:64, :], scale=SCALE, accum_out=sm[0:64, :]),
                 reads=[pk, "nb"], writes=["p64", "sm"])
            P.op("vector", lambda e: e.reciprocal(rs[0:64, :], sm[0:64, :]), reads=["sm"], writes=["rs"])
            P.op("vector", lambda e: e.tensor_scalar_mul(p64[0:64, :], p64[0:64, :], rs[0:64, :]), reads=["p64", "rs"], writes=["p64"])
            pt2, pk2 = ps()
            for mt in range(2):
                P.op("tensor", lambda e, mt=mt: e.matmul(pt2[:, mt * 64:(mt + 1) * 64], lhsT=p64[0:64, mt * 128:(mt + 1) * 128], rhs=ident[0:64, 0:64],
                                                  start=True, stop=True), reads=["p64", "cst"], writes=[pk2])
            P.op("vector", lambda e: e.tensor_copy(pTs, v3(pt2[:, 0:128], 64)), reads=[pk2], writes=["pTs"])
            pt3, pk3 = ps()
            for s_ in range(NS):
                vb_ = KV[s_ % 2]
                vbk = f"kv{s_ % 2}"
                P.dma("sync", vb_, vc[s_].rearrange("(mt p) c -> p mt c", p=128), writes=[vbk])
                for h in range(4):
                    col = s_ * 4 + h
                    for mt in range(2):
                        P.op("tensor", lambda e, vb_=vb_, h=h, mt=mt, col=col: e.matmul(pt3[:, col:col + 1], lhsT=vb_[:, mt, h * 128:(h + 1) * 128],
                                                                  rhs=pTs[:, mt, col:col + 1], start=(mt == 0), stop=(mt == 1)),
                             reads=[vbk, "pTs"], writes=[pk3])
            P.op("vector", lambda e: e.tensor_copy(oT[:, :, NTB:NTM], pt3[:, 0:64].rearrange("p (n h) -> p h n", h=4)), reads=[pk3], writes=["oT"])

        def merge_wo(nt):
            acc = [A(0, NTM), A(288, NTM)]
            sgt = A(576, NTM)
            merged = v3(AB(1024, KT * NTM), NTM)
            branches = [(C_G, pout, mix, "mix", 4), (C_G + D, rout, yg, "yg", 8), (C_G + 2 * D, xout, oT, "oT", 4)]
            for ip in range(8):
                for bi, (gc0, Wb, src, srck, ktb) in enumerate(branches):
                    wgt, wgk = wload(win, 0, D, gc0 + ip * 256, 256)
                    wbr, wbk = wload(Wb, 0, ktb * 128, ip * 256, 256)
                    for ci in range(2):
                        pg, pgk = ps()
                        po, pok = ps()
                        P.group("tensor", [lambda e, pg=pg, k=k, ci=ci, wgt=wgt: e.matmul(
                            pg[:, 0:nt], lhsT=wgt[:, k, ci * 128:(ci + 1) * 128], rhs=xn[:, k, 0:nt],
                            start=(k == 0), stop=(k == KT - 1)) for k in range(KT)], reads=[wgk, "xn"], writes=[pgk])
                        P.group("tensor", [lambda e, po=po, k=k, ci=ci, wbr=wbr, src=src, ktb=ktb: e.matmul(
                            po[:, 0:nt], lhsT=wbr[:, k, ci * 128:(ci + 1) * 128], rhs=src[:, k, 0:nt],
                            start=(k == 0), stop=(k == ktb - 1)) for k in range(ktb)], reads=[wbk, srck], writes=[pok])
                        P.op("scalar", lambda e, pg=pg: e.activation(sgt[:, 0:nt], pg[:, 0:nt], AF.Sigmoid), reads=[pgk, "sgt"], writes=["sgt"])
                        if bi == 0:
                            P.op("vector", lambda e, po=po, ci=ci: e.tensor_tensor(acc[ci][:, 0:nt], sgt[:, 0:nt], po[:, 0:nt], op=ALU.mult),
                                 reads=["sgt", pok, f"acc{ci}"], writes=[f"acc{ci}"])
                        else:
                            P.op("vector", lambda e, po=po: e.tensor_tensor(sgt[:, 0:nt], sgt[:, 0:nt], po[:, 0:nt], op=ALU.mult),
                                 reads=["sgt", pok], writes=["sgt"])
                            P.op("vector", lambda e, ci=ci: e.tensor_tensor(acc[ci][:, 0:nt], acc[ci][:, 0:nt], sgt[:, 0:nt], op=ALU.add),
                                 reads=["sgt", f"acc{ci}"], writes=[f"acc{ci}"])
                for ci in range(2):
                    i = ip * 2 + ci
                    P.op("vector", lambda e, ci=ci, i=i: e.tensor_copy(merged[:, i, 0:nt], acc[ci][:, 0:nt]), reads=[f"acc{ci}"], writes=[("merged", i)])

            def sink_o(j, pt, pk, m):
                P.op("vector", lambda e: e.tensor_tensor(xh[:, j, 0:nt], xh[:, j, 0:nt], pt[:, 0:nt], op=ALU.add),
                     reads=[pk, ("xh", j)], writes=[("xh", j)])
            proj(wo, 0, D, merged, "merged", nt, KT, sink_o)
            P.fence()

        P.op("vector", lambda e: e.memset(zr[:], 0.0), writes=["zr"])
        mem_kv()
        P.dma("sync", o_poolso, spool[:, 1:15, :])
        xT3 = xT.rearrange("(k p) n -> p k n", p=128)
        TSEL = [int(v) for v in os.environ.get('K_TILES', '0,1,2,3,4,5,6,7').split(',') if v != '']
        STAGE = int(os.environ.get("K_STAGE", "9"))
        for t in TSEL:
            own = t >= 4
            last = t == 7
            nt = NTB + (NS if last else 0)
            P.dma("sync", xh[:, :, 0:NTB], xT3[:, :, t * NTB:(t + 1) * NTB], writes=["xh"])
            if last:
                P.dma("sync", xh[:, :, NTB:NTM], xT3[:, :, 2048:2048 + NS], writes=["xh"])
            ffn(f1g, f1u, f1d, G_F1, nt, True)
            rmsnorm(xh, G_MIX, nt, xn, "xh", "xn")
            if t >= 3:
                def sink_zp(j, pt, pk, m, nt=nt):
                    P.op("vector", lambda e: e.tensor_copy(zp[:, j, 15:15 + nt], pt[:, 0:nt]), reads=[pk], writes=["zp"])
                proj(win, C_POOL, 512, xn, "xn", nt, KT, sink_zp)

            def sink_zr(j, pt, pk, m, nt=nt):
                evac(j, zr[0:m, j, 1:1 + nt], pt[0:m, 0:nt], [pk], ["zr"])
            proj(win, C_R, RPW, xn, "xn", nt, KT, sink_zr)
            if own:
                def sink_zq(j, pt, pk, m, nt=nt):
                    evac(j, zq[:, j, 0:nt], pt[:, 0:nt], [pk], ["zq"])
                proj(win, C_XQ, 512, xn, "xn", nt, KT, sink_zq)
            if last:
                P.dma("sync", o_shiftT, zr[:, :, NTB:NTB + 1 + NS], reads=["zr"])
                P.dma("sync", o_poolpT, zp[:, :, NTB:NTB + 15], reads=["zp"])
                P.dma("sync", o_poolsn, zp[:, :, 15 + NTB:15 + NTM], reads=["zp"])
            P.fence()
            SUB = int(os.environ.get("K_SUB", "9"))
            if STAGE >= 2:
                token_shift(last)
                for sub in range(2):
                    cs = slice(1 + sub * 128, 1 + (sub + 1) * 128)
                    if SUB >= 2:
                        rwkv_elem(cs, 128, False)
                    if SUB >= 3:
                        scan_prompt(cs, own)
                    if own and SUB >= 4:
                        rwkv_post(cs, 128, sub * 128, False)
                    P.fence()
                if last:
                    cs = slice(1 + NTB, 1 + NTM)
                    rwkv_elem(cs, NS, True)
                    P.fence()
                    sample_scan()
                    rwkv_post(cs, NS, NTB, False)
                    P.fence()
                P.op("vector", lambda e: e.tensor_copy(zr[:, :, 0:1], zc[:]), reads=["zc", "zr"], writes=["zr"])
            if own and STAGE >= 3:
                pool_branch(t - 4, nt, last)
                for sub in range(2):
                    attn_prompt(sub * 128)
                P.fence()
                if last:
                    attn_sample()
                    P.fence()
            if t >= 3:
                P.op("vector", lambda e: e.tensor_copy(zp[:, :, 0:15], zp[:, :, NTB:NTB + 15]), reads=["zp"], writes=["zp"])
            if own:
                P.fence()
                if STAGE >= 4:
                    merge_wo(nt)
                ffn(f2g, f2u, f2d, G_F2, nt, True)
                yo = v3(A(0, KT * NTM), NTM)
                rmsnorm(xh, G_FIN, nt, yo, "xh", "yo")
                P.dma("sync", yT[:, :, (t - 4) * NTB:(t - 3) * NTB], yo[:, :, 0:NTB], reads=["yo"])
                if last:
                    P.dma("sync", yT[:, :, 1024:1024 + NS], yo[:, :, NTB:NTM], reads=["yo"])
            P.fence()
        P.dma("sync", o_wkvp, S32[:].rearrange("p k n -> p (k n)"), reads=["S32"])
        P.finish()
    return nc


def _consts():
    c = np.zeros((128, 3584), np.float32)
    s = np.arange(128)[:, None]
    t = np.arange(128)[None, :]
    eye = np.eye(128, dtype=np.float32)
    c[:, 0:128] = eye
    c[:, 128:256] = -EDEC * (s <= t)
    c[:, 256:384] = -EDEC * (s < t)
    c[:, 384:512] = -EDEC * (s > t)
    bo = ((s // 64) == (t // 64)).astype(np.float32)
    c[:, 512:640] = bo
    c[:, 640:768] = bo / 64.0
    c[:, 768:1280] = np.tile((s < t).astype(np.float32), (1, 4))
    c[:, 1280:1792] = np.tile((s > t).astype(np.float32), (1, 4))
    c[:, 1792:2304] = np.tile((s <= t).astype(np.float32), (1, 4))
    c[:, 2304:2816] = np.tile(eye, (1, 4))
    c[:, 2816:3072] = 1.0
    c[:, 3072:3200] = eye
    c[:, 3200:3328] = 1.0
    return c


_NC_CACHE = {}
_PACK_ONLY = False


def kernel(x_prompt, x_sample, mem_prompt, cache_mem_k, cache_mem_v, state_wkv, state_shift, state_pool,
           ffn1_norm_g, ffn1_w_gate, ffn1_w_up, ffn1_w_down, mix_norm_g, w_in,
           pool_group_w, pool_scale, pool_out,
           rwkv_mu, rwkv_w0, rwkv_w_up, rwkv_a0, rwkv_a_up, rwkv_g_up, rwkv_k_k, rwkv_k_a, rwkv_r_k,
           rwkv_ln_g, rwkv_ln_b, rwkv_out,
           mem_norm_g, w_mem_k, w_mem_v, xattn_out, w_o,
           ffn2_norm_g, ffn2_w_gate, ffn2_w_up, ffn2_w_down, final_norm_g):
    f = lambda a: np.ascontiguousarray(np.asarray(a, dtype=np.float32))
    x_prompt, x_sample, mem_prompt = f(x_prompt), f(x_sample), f(mem_prompt)
    B = x_prompt.shape[0]

    def kcols(v, n):
        v = f(v).reshape(-1)
        pad = np.zeros(n * 128, np.float32)
        pad[:v.size] = v
        return pad.reshape(n, 128).T

    vecs = np.zeros((128, 176), np.float32)
    vecs[:, 0:16] = kcols(ffn1_norm_g[0], 16)
    vecs[:, 16:32] = kcols(mix_norm_g[0], 16)
    vecs[:, 32:48] = kcols(mem_norm_g[0], 16)
    vecs[:, 48:64] = kcols(ffn2_norm_g[0], 16)
    vecs[:, 64:80] = kcols(final_norm_g, 16)
    vecs[:, 80:107] = kcols(rwkv_mu[0], 27)
    vecs[:, 107:111] = kcols(pool_scale[0], 4)
    vecs[:, 111:119] = kcols(rwkv_k_k[0], 8)
    vecs[:, 119:127] = kcols(rwkv_k_a[0], 8)
    vecs[:, 127:135] = kcols(rwkv_r_k[0], 8)
    vecs[:, 135:143] = kcols(rwkv_ln_g[0], 8)
    vecs[:, 143:151] = kcols(rwkv_ln_b[0], 8)
    rows = np.concatenate([f(rwkv_w0[0]), f(rwkv_a0[0])])[None, :]
    cst = _consts()
    shared = {
        "f1g": f(ffn1_w_gate[0]), "f1u": f(ffn1_w_up[0]), "f1d": f(ffn1_w_down[0]),
        "f2g": f(ffn2_w_gate[0]), "f2u": f(ffn2_w_up[0]), "f2d": f(ffn2_w_down[0]),
        "win": f(w_in[0]), "pgw": f(pool_group_w[0]), "pout": f(pool_out[0]),
        "wup": f(rwkv_w_up[0]), "aup": f(rwkv_a_up[0]), "gup": f(rwkv_g_up[0]), "rout": f(rwkv_out[0]),
        "wmk": f(w_mem_k[0]), "wmv": f(w_mem_v[0]), "xout": f(xattn_out[0]), "wo": f(w_o[0]),
        "vecs": vecs, "rows": rows, "cst": cst,
    }
    in_maps = []
    for c in range(8):
        b, half = c // 2, c % 2
        own = x_prompt[b, half * 1024:(half + 1) * 1024]
        prev = x_prompt[b, 0:1024] if half == 1 else np.zeros_like(own)
        xs = x_sample[c * NS:(c + 1) * NS, 0]
        xTc = np.ascontiguousarray(np.concatenate([prev, own, xs], axis=0).T)
        sl = slice(c * NS, (c + 1) * NS)
        sshT = np.zeros((27 * 128, NS), np.float32)
        sshT[:RPW] = f(state_shift[0, sl, 0]).T
        invc = np.zeros((4, 4, NTB + NS), np.float32)
        for t in range(4):
            pos = half * 1024 + t * NTB + np.arange(NTB)
            for g, w in enumerate((2, 4, 8, 16)):
                invc[t, g, :NTB] = 1.0 / np.minimum(pos + 1, w)
                invc[t, g, NTB:] = 1.0 / w
        m = dict(shared)
        m.update({
            "xT": xTc, "memT": np.ascontiguousarray(mem_prompt[b].T),
            "kc": f(cache_mem_k[0, sl]).reshape(NS, NMEM, 512), "vc": f(cache_mem_v[0, sl]).reshape(NS, NMEM, 512),
            "swkv": np.ascontiguousarray(f(state_wkv[0, sl]).transpose(2, 0, 1, 3)),
            "sshiftT": sshT,
            "spoolT": np.ascontiguousarray(f(state_pool[0, sl]).transpose(2, 0, 1)),
            "spool": f(state_pool[0, sl]),
            "invc": invc.reshape(4, -1),
        })
        in_maps.append(m)
    if _PACK_ONLY:
        return in_maps
    if "nc" not in _NC_CACHE:
        _NC_CACHE["nc"] = build_nc()
    res = run_bass_kernel_spmd(_NC_CACHE["nc"], in_maps, core_ids=list(range(8)))
    R = res.results
    return unpack(R, B)


def unpack(R, B=4):
    ND = NS * 8
    y_prompt = np.zeros((B, 2048, D), np.float32)
    y_sample = np.zeros((ND, 1, D), np.float32)
    mem_k = np.zeros((1, B, NMEM, 4, 128), np.float32)
    mem_v = np.zeros((1, B, NMEM, 4, 128), np.float32)
    wkv_p = np.zeros((1, B, 16, 64, 64), np.float32)
    sh_p = np.zeros((1, B, 1, RPW), np.float32)
    pl_p = np.zeros((1, B, 15, 512), np.float32)
    wkv_s = np.zeros((1, ND, 16, 64, 64), np.float32)
    sh_s = np.zeros((1, ND, 1, RPW), np.float32)
    pl_s = np.zeros((1, ND, 15, 512), np.float32)
    for c in range(8):
        b, half = c // 2, c % 2
        r = R[c]
        yT = np.asarray(r["yT"]).reshape(128, KT, 1024 + NS)
        yfull = yT.transpose(2, 1, 0).reshape(1024 + NS, D)
        y_prompt[b, half * 1024:(half + 1) * 1024] = yfull[:1024]
        y_sample[c * NS:(c + 1) * NS, 0] = yfull[1024:]
        sh = np.asarray(r["shiftT"]).reshape(128, 27, 1 + NS).transpose(2, 1, 0).reshape(1 + NS, 27 * 128)[:, :RPW]
        sh_s[0, c * NS:(c + 1) * NS, 0] = sh[1:]
        pl_s[0, c * NS:(c + 1) * NS, 0:14] = np.asarray(r["poolso"]).reshape(NS, 14, 512)
        pl_s[0, c * NS:(c + 1) * NS, 14] = np.asarray(r["poolsn"]).reshape(128, 4, NS).transpose(2, 1, 0).reshape(NS, 512)
        wkv_s[0, c * NS:(c + 1) * NS] = np.asarray(r["wkvs"]).reshape(64, NS, 16, 64).transpose(1, 2, 0, 3)
        if half == 0:
            mem_k[0, b] = np.asarray(r["memkT"]).reshape(128, 4, NMEM).transpose(2, 1, 0)
            mem_v[0, b] = np.asarray(r["memv"]).reshape(128, 2, 512).transpose(1, 0, 2).reshape(NMEM, 4, 128)
        else:
            sh_p[0, b, 0] = sh[0]
            pl_p[0, b] = np.asarray(r["poolpT"]).reshape(128, 4, 15).transpose(2, 1, 0).reshape(15, 512)
            wkv_p[0, b] = np.asarray(r["wkvp"]).reshape(2, 64, 8, 64).transpose(2, 0, 3, 1).reshape(16, 64, 64)
    return (y_prompt, y_sample, mem_k, mem_v, wkv_p, sh_p, pl_p, wkv_s, sh_s, pl_s)
```

```python
import numpy as np
from contextlib import ExitStack
import concourse.bass as bass
import concourse.mybir as mybir
from concourse.bass_utils import run_bass_kernel_spmd

F32 = mybir.dt.float32
BF16 = mybir.dt.bfloat16
AF = mybir.ActivationFunctionType
ALU = mybir.AluOpType
AX = mybir.AxisListType

D = 2048
F = 5504
KT = 16
FC = 43
NMEM = 256
RW = 1024
RPW = 3360
INW = 10528
C_POOL, C_R, C_XQ, C_G = 0, 512, 3872, 4384
NTB = 256
NS = 16
NTILE = 8
EDEC = float(np.exp(-0.5))
RMS_EPS = 1e-6
GN_EPS = 64e-5
EPOCH = 20000
NDSEM = 6


class Prog:
    COMPUTE = ("tensor", "vector", "scalar", "gpsimd")

    def __init__(self, nc, stack):
        self.nc = nc
        self.stack = stack
        self.eng = {"tensor": nc.tensor, "vector": nc.vector, "scalar": nc.scalar,
                    "gpsimd": nc.gpsimd, "sync": nc.sync}
        self.seq = {e: 0 for e in self.COMPUTE}
        self.esems = {e: [] for e in self.COMPUTE}
        self.dq = {}
        for q in ("sync", "scalar", "gpsimd"):
            self.dq[q] = {"n": 0, "sems": [self._sem(f"d_{q}_{i}") for i in range(NDSEM)]}
        self.seen = {e: {} for e in self.eng}
        self.state = {}
        self.subs = {}
        self.same_engine_sync = {"vector": True, "scalar": True, "gpsimd": True, "tensor": False}
        self.last_ev = {}

    def _sem(self, name):
        return self.stack.enter_context(self.nc.semaphore(name))

    def _esem(self, e, epoch):
        while len(self.esems[e]) <= epoch:
            self.esems[e].append(self._sem(f"e_{e}_{len(self.esems[e])}"))
        return self.esems[e][epoch]

    def _related(self, k):
        if isinstance(k, tuple):
            return [k, k[0]]
        out = [k]
        out.extend(self.subs.get(k, ()))
        return out

    def _deps(self, reads, writes):
        evs = []
        for k in reads:
            for kk in self._related(k):
                st = self.state.get(kk)
                if st and st[0] is not None:
                    evs.append(st[0])
        for k in writes:
            for kk in self._related(k):
                st = self.state.get(kk)
                if st:
                    if st[0] is not None:
                        evs.append(st[0])
                    evs.extend(st[1])
        return evs

    def _record(self, ev, reads, writes):
        for k in reads:
            if isinstance(k, tuple):
                self.subs.setdefault(k[0], set()).add(k)
            self.state.setdefault(k, [None, []])[1].append(ev)
        for k in writes:
            if isinstance(k, tuple):
                self.subs.setdefault(k[0], set()).add(k)
            self.state[k] = [ev, []]
            if not isinstance(k, tuple):
                for kk in self.subs.get(k, ()):
                    self.state[kk] = [ev, []]

    def _emit_waits(self, e, evs):
        need = {}
        for (semkey, sem, val, src) in evs:
            if src == e and not self.same_engine_sync.get(e, True):
                continue
            if self.seen[e].get(semkey, 0) >= val:
                continue
            if semkey not in need or need[semkey][1] < val:
                need[semkey] = (sem, val)
        for semkey, (sem, val) in need.items():
            self.eng[e].wait_ge(sem, val)
            self.seen[e][semkey] = val
            if semkey[0] == "E":
                for ep in range(semkey[2]):
                    self.seen[e][("E", semkey[1], ep)] = EPOCH

    def op(self, e, fn, reads=(), writes=()):
        evs = self._deps(reads, writes)
        self._emit_waits(e, evs)
        ins = fn(self.eng[e])
        s = self.seq[e]
        epoch, idx = divmod(s, EPOCH)
        sem = self._esem(e, epoch)
        ins.then_inc(sem, 1)
        self.seq[e] = s + 1
        ev = (("E", e, epoch), sem, idx + 1, e)
        self.last_ev[e] = ev
        self._record(ev, reads, writes)
        return ins

    def group(self, e, fns, reads=(), writes=()):
        evs = self._deps(reads, writes)
        self._emit_waits(e, evs)
        for fn in fns[:-1]:
            fn(self.eng[e])
        ins = fns[-1](self.eng[e])
        s = self.seq[e]
        epoch, idx = divmod(s, EPOCH)
        sem = self._esem(e, epoch)
        ins.then_inc(sem, 1)
        self.seq[e] = s + 1
        ev = (("E", e, epoch), sem, idx + 1, e)
        self.last_ev[e] = ev
        self._record(ev, reads, writes)
        return ins

    def dma(self, q, out, in_, reads=(), writes=(), **kw):
        d = self.dq[q]
        i = d["n"]
        slot, rnd = i % NDSEM, i // NDSEM
        sem = d["sems"][slot]
        evs = self._deps(reads, writes)
        if rnd > 0:
            evs.append((("D", q, slot), sem, 16 * rnd, None))
        self._emit_waits(q, evs)
        ins = self.eng[q].dma_start(out=out, in_=in_, **kw)
        ins.then_inc(sem, 16)
        d["n"] = i + 1
        ev = (("D", q, slot), sem, 16 * (rnd + 1), None)
        self._record(ev, reads, writes)
        return ins

    def _dma_events(self):
        evs = []
        for q, d in self.dq.items():
            n = d["n"]
            for slot in range(NDSEM):
                cnt = (n - slot + NDSEM - 1) // NDSEM if n > slot else 0
                if cnt > 0:
                    evs.append((("D", q, slot), d["sems"][slot], 16 * cnt, None))
        return evs

    def fence(self):
        evs = list(self.last_ev.values()) + self._dma_events()
        for e in self.eng:
            self._emit_waits(e, [ev for ev in evs if not (ev[3] == e and e == "tensor")])
        self.state = {}
        self.subs = {}

    def finish(self):
        for q in self.dq:
            self._emit_waits(q, [ev for ev in self._dma_events() if ev[0][1] == q])


IN_SHAPES = {}
BLOCKS = {}


def build_nc():
    nc = bass.Bass("TRN2", target_bir_lowering=False)
    din = {}
    dout = {}

    import os
    _small = os.environ.get("K_TEST", "")
    _need = set(os.environ.get("K_NEED", "memT,wmk,wmv,vecs,rows,cst,wup,aup,gup,pgw,spool,xT").split(","))

    def I(name, shape, dt=F32):
        if _small and name not in _need:
            shape = [1, 8]
        IN_SHAPES[name] = list(shape)
        din[name] = nc.dram_tensor(name, list(shape), dt, kind="ExternalInput").ap()
        return din[name]

    def O(name, shape):
        dout[name] = nc.dram_tensor(name, list(shape), F32, kind="ExternalOutput").ap()
        return dout[name]

    NTOK = 2048 + NS
    xT = I("xT", [D, NTOK])
    memT = I("memT", [D, NMEM])
    kc = I("kc", [NS, NMEM, 512])
    vc = I("vc", [NS, NMEM, 512])
    swkv = I("swkv", [64, NS, 16, 64])
    sshiftT = I("sshiftT", [27 * 128, NS])
    spoolT = I("spoolT", [512, NS, 15])
    spool = I("spool", [NS, 15, 512])
    NBLK = {"f1g": 22, "f1u": 22, "f1d": 32, "f2g": 22, "f2u": 22, "f2d": 32, "win": 44,
            "pout": 8, "rout": 8, "xout": 8, "wo": 8}
    WSHAPE = {"f1g": [D, F], "f1u": [D, F], "f1d": [F, D], "f2g": [D, F], "f2u": [D, F], "f2d": [F, D],
              "win": [D, INW], "pout": [512, D], "rout": [RW, D], "xout": [512, D], "wo": [D, D]}
    f1g, f1u, f1d, f2g, f2u, f2d, win = "f1g", "f1u", "f1d", "f2g", "f2u", "f2d", "win"
    WB = {nm: I(nm, [NBLK[nm], 128, 4096]) for nm in NBLK}
    pgw = I("pgw", [4, 128, 128]); pout = "pout"
    wup = I("wup", [64, RW]); aup = I("aup", [64, RW]); gup = I("gup", [160, RW])
    rout = "rout"
    wmk = I("wmk", [D, 512]); wmv = I("wmv", [D, 512]); xout = "xout"
    wo = "wo"
    vecs = I("vecs", [128, 176])
    rows = I("rows", [1, 2048])
    cst = I("cst", [128, 3584])
    invc = I("invc", [4, 4 * (NTB + NS)])

    yT = O("yT", [128, KT, 1024 + NS])
    o_memkT = O("memkT", [128, 4, NMEM])
    o_memv = O("memv", [128, 2, 512])
    o_wkvp = O("wkvp", [128, 8 * 64])
    o_shiftT = O("shiftT", [128, 27, 1 + NS])
    o_poolpT = O("poolpT", [128, 4, 15])
    o_wkvs = O("wkvs", [64, NS, 16, 64])
    o_poolsn = O("poolsn", [128, 4, NS])
    o_poolso = O("poolso", [NS, 14, 512])

    scr_rows = nc.dram_tensor("scr_rows", [6, NS, 1024], F32, kind="Internal").ap()
    scr_y = nc.dram_tensor("scr_y", [NS, 1024], F32, kind="Internal").ap()
    scr_q = nc.dram_tensor("scr_q", [NS, 512], F32, kind="Internal").ap()

    with ExitStack() as st:
        P = Prog(nc, st)
        sb = lambda name, shape, dt=F32: st.enter_context(nc.sbuf_tensor(name, list(shape), dt))
        NTM = NTB + NS

        cst_sb = sb("cst_sb", [128, 3584])
        P.dma("sync", cst_sb[:], cst, writes=["cst"])
        ident = cst_sb[:, 0:128]
        triI = cst_sb[:, 128:256]
        triX = cst_sb[:, 256:384]
        triS = cst_sb[:, 384:512]
        bones = cst_sb[:, 512:640]
        bones64 = cst_sb[:, 640:768]
        mSU = cst_sb[:, 768:1280]
        mSL = cst_sb[:, 1280:1792]
        mIU = cst_sb[:, 1792:2304]
        I4 = cst_sb[:, 2304:2816]
        ones_row = cst_sb[0:1, 2816:3072]
        cbf = sb("cbf", [128, 256], BF16)
        P.dma("gpsimd", cbf[:], cst[:, 3072:3328], writes=["cbf"])
        ident_bf = cbf[:, 0:128]
        ones_bf = cbf[:, 128:256]
        vec = sb("vec", [128, 176])
        P.dma("sync", vec[:], vecs, writes=["vec"])
        G_F1, G_MIX, G_MEM, G_F2, G_FIN = 0, 16, 32, 48, 64
        V_MU, V_PS = 80, 107
        V_KK, V_KA, V_RK, V_LG, V_LB = 111, 119, 127, 135, 143
        rows_sb = sb("rows_sb", [1, 2048])
        P.dma("sync", rows_sb[:], rows, writes=["rows"])
        lora = sb("lora", [128, RW])
        P.dma("sync", lora[0:64, :], wup, writes=[("lora", 0)])
        P.dma("sync", lora[64:128, :], aup, writes=[("lora", 1)])
        gupb = sb("gupb", [128, 2, RW], BF16)
        P.dma("gpsimd", gupb[:, 0, :], gup[0:128, :], writes=[("gupb", 0)])
        P.dma("gpsimd", gupb[0:32, 1, :], gup[128:160, :], writes=[("gupb", 1)])
        pgwb = sb("pgwb", [128, 4, 128], BF16)
        P.dma("gpsimd", pgwb[:], pgw.rearrange("g c d -> c g d"), writes=["pgwb"])

        xh = sb("xh", [128, KT, NTM])
        xn = sb("xn", [128, KT, NTM], BF16)
        zr = sb("zr", [128, 27, 1 + NTM])
        zp = sb("zp", [128, 4, 15 + NTM])
        zq = sb("zq", [128, 4, NTM], BF16)
        mix = sb("mix", [128, 4, NTM], BF16)
        yg = sb("yg", [128, 8, NTM], BF16)
        oT = sb("oT", [128, 4, NTM], BF16)
        S32 = sb("S32", [128, 8, 64])
        Sbf = sb("Sbf", [128, 8, 64], BF16)
        KTb = sb("KTb", [128, 4, NMEM], BF16)
        Vb = sb("Vb", [128, 2, 512], BF16)
        rstd = sb("rstd", [128, NTM])
        tmpA = sb("tmpA", [128, NTM])
        tmpB = sb("tmpB", [128, NTM], BF16)
        NWB = 4
        wbuf = [sb(f"wbuf{i}", [128, 4096], BF16) for i in range(NWB)]
        ARENA = 16384
        arena = sb("arena", [128, ARENA])
        psb = [st.enter_context(nc.psum_tensor(f"pb{i}", [128, 512], F32)) for i in range(8)]
        wctr = [0]
        pctr = [0]

        def ps():
            i = pctr[0] % 8
            pctr[0] += 1
            return psb[i], f"pb{i}"

        wcache = {}

        def wload(W, r0, nrows, c0, ncols, kparts=128):
            i = wctr[0] % NWB
            wctr[0] += 1
            k = nrows // kparts
            view = wbuf[i][0:kparts, 0:k * ncols].rearrange("p (k n) -> p k n", n=ncols)
            flat = wbuf[i][0:kparts, 0:k * ncols]
            if not isinstance(W, str):
                src = W[r0:r0 + nrows, c0:c0 + ncols].rearrange("(k p) n -> p k n", p=kparts)
                P.dma("gpsimd", view, src, writes=[f"wbuf{i}"])
                return view, f"wbuf{i}"
            name = W
            blk_id = (r0, nrows, c0, ncols)
            if name not in wcache:
                wcache[name] = (nc.dram_tensor("bfc_" + name, [NBLK[name], 128, 4096], BF16, kind="Internal").ap(), {})
                BLOCKS[name] = []
            if blk_id in wcache[name][1]:
                bi = wcache[name][1][blk_id]
                P.dma("sync", flat, wcache[name][0][bi, 0:kparts, 0:k * ncols], writes=[f"wbuf{i}"])
            else:
                bi = len(wcache[name][1])
                assert bi < NBLK[name], name
                BLOCKS[name].append(blk_id)
                P.dma("gpsimd", flat, WB[name][bi, 0:kparts, 0:k * ncols], writes=[f"wbuf{i}"])
                P.dma("sync", wcache[name][0][bi, 0:kparts, 0:k * ncols], flat, reads=[f"wbuf{i}"])
                wcache[name][1][blk_id] = bi
            return view, f"wbuf{i}"

        CACHED = {"f1g", "f1u", "f1d", "f2g", "f2u", "f2d", "win", "pout", "rout", "xout", "wo"}

        P.op("vector", lambda e: e.memset(S32[:], 0.0), writes=["S32"])
        P.op("vector", lambda e: e.memset(Sbf[:], 0.0), writes=["Sbf"])
        P.op("vector", lambda e: e.memset(zr[:, :, 0:1], 0.0), writes=["zr"])
        P.op("vector", lambda e: e.memset(zp[:, :, 0:15], 0.0), writes=["zp"])

        def rmsnorm(src, gcol, nt, dst, srckey, dstkey, kt=KT):
            sq = arena[:, 14208:14208 + (kt * nt + 1) // 2].bitcast(BF16)[:, 0:kt * nt].rearrange("p (k n) -> p k n", n=nt)
            pt, pk = ps()
            P.op("scalar", lambda e: e.activation(sq, src[:, 0:kt, 0:nt], AF.Square), reads=[srckey], writes=["sq"])
            P.group("tensor", [lambda e, k=k: e.matmul(pt[:, 0:nt], lhsT=ones_bf, rhs=sq[:, k, :],
                                                        start=(k == 0), stop=(k == kt - 1)) for k in range(kt)],
                    reads=["sq", "cbf"], writes=[pk])
            P.op("scalar", lambda e: e.activation(tmpA[:, 0:nt], pt[:, 0:nt], AF.Sqrt, bias=RMS_EPS, scale=1.0 / D),
                 reads=[pk], writes=["tmpA"])
            P.op("vector", lambda e: e.reciprocal(rstd[:, 0:nt], tmpA[:, 0:nt]), reads=["tmpA"], writes=["rstd"])
            for k in range(kt):
                P.op("vector", lambda e, k=k: e.scalar_tensor_tensor(
                    out=dst[:, k, 0:nt], in0=src[:, k, 0:nt], scalar=vec[:, gcol + k:gcol + k + 1],
                    in1=rstd[:, 0:nt], op0=ALU.mult, op1=ALU.mult),
                     reads=[srckey, "rstd", "vec"], writes=[(dstkey, k)])

        def ffn(Wg, Wu, Wd, gcol, nt, keep_res):
            rmsnorm(xh, gcol, nt, xn, "xh", "xn")
            act = arena[:, 0:FC * NTM // 2].bitcast(BF16).rearrange("p (c n) -> p c n", n=NTM)
            for half in range(2):
                c_lo, c_hi = (0, 22) if half == 0 else (22, FC)
                for cb in range(c_lo, c_hi, 2):
                    ncb = min(2, c_hi - cb)
                    wg, wgk = wload(Wg, 0, D, cb * 128, ncb * 128)
                    wu, wuk = wload(Wu, 0, D, cb * 128, ncb * 128)
                    for ci in range(ncb):
                        c = cb + ci
                        pg, pgk = ps()
                        pu, puk = ps()
                        P.group("tensor", [lambda e, k=k, ci=ci: e.matmul(
                            pg[:, 0:nt], lhsT=wg[:, k, ci * 128:(ci + 1) * 128], rhs=xn[:, k, 0:nt],
                            start=(k == 0), stop=(k == KT - 1)) for k in range(KT)], reads=[wgk, "xn"], writes=[pgk])
                        P.group("tensor", [lambda e, k=k, ci=ci: e.matmul(
                            pu[:, 0:nt], lhsT=wu[:, k, ci * 128:(ci + 1) * 128], rhs=xn[:, k, 0:nt],
                            start=(k == 0), stop=(k == KT - 1)) for k in range(KT)], reads=[wuk, "xn"], writes=[puk])
                        P.op("scalar", lambda e: e.activation(tmpB[:, 0:nt], pg[:, 0:nt], AF.Silu),
                             reads=[pgk], writes=["tmpB"])
                        P.op("vector", lambda e, c=c: e.tensor_tensor(act[:, c - c_lo, 0:nt], tmpB[:, 0:nt],
                                                                       pu[:, 0:nt], op=ALU.mult),
                             reads=["tmpB", puk], writes=[("act", c)])
                nch = c_hi - c_lo
                for dcol in range(KT):
                    wd, wdk = wload(Wd, c_lo * 128, nch * 128, dcol * 128, 128)
                    pd, pdk = ps()
                    P.group("tensor", [lambda e, c=c: e.matmul(
                        pd[:, 0:nt], lhsT=wd[:, c, :], rhs=act[:, c, 0:nt],
                        start=(c == 0), stop=(c == nch - 1)) for c in range(nch)],
                            reads=[wdk, "act"], writes=[pdk])
                    P.op("vector", lambda e, dcol=dcol: e.scalar_tensor_tensor(
                        out=xh[:, dcol, 0:nt], in0=pd[:, 0:nt], scalar=0.5, in1=xh[:, dcol, 0:nt],
                        op0=ALU.mult, op1=ALU.add), reads=[pdk, ("xh", dcol)], writes=[("xh", dcol)])
                if half == 0:
                    pass
            P.fence()

        def proj(W, c0, ncols_total, rhs, rhskey, nt, kt, sink, wrows=None):
            wrows = wrows or kt * 128
            nchunks = (ncols_total + 127) // 128
            j = 0
            while j < nchunks:
                nb = min(2, nchunks - j)
                ncols = min(ncols_total - j * 128, nb * 128)
                wv, wk = wload(W, 0, wrows, c0 + j * 128, ncols)
                for ci in range(nb):
                    m = min(128, ncols - ci * 128)
                    pt, pk = ps()
                    P.group("tensor", [lambda e, k=k, ci=ci, m=m, pt=pt: e.matmul(
                        pt[0:m, 0:nt], lhsT=wv[:, k, ci * 128:ci * 128 + m], rhs=rhs[:, k, 0:nt],
                        start=(k == 0), stop=(k == kt - 1)) for k in range(kt)], reads=[wk, rhskey], writes=[pk])
                    sink(j + ci, pt, pk, m)
                j += nb

        def mem_kv():
            mx = arena[:, 0:KT * NMEM].rearrange("p (k n) -> p k n", n=NMEM)
            P.dma("sync", mx, memT.rearrange("(k p) n -> p k n", p=128), writes=["mx"])
            mxn = arena[:, 4096:4096 + KT * NMEM // 2].bitcast(BF16).rearrange("p (k n) -> p k n", n=NMEM)
            if os.environ.get("K_H2", "0") == "1":
                mxn = xn[:, :, 0:NMEM]
            rmsnorm(mx, G_MEM, NMEM, mxn, "mx", "mxn")
            if _small == "norm":
                P.fence()
                return
            kf = arena[:, 8192:8192 + 4 * NMEM].rearrange("p (k n) -> p k n", n=NMEM)

            def sink_k(j, pt, pk, m):
                P.op("vector", lambda e: e.tensor_copy(kf[:, j, :], pt[:, 0:NMEM]), reads=[pk], writes=[("kf", j)])
                if True:
                    P.op("vector", lambda e: e.tensor_copy(KTb[:, j, :], pt[:, 0:NMEM]), reads=[pk], writes=[("KTb", j)])
                else:
                    P.op("scalar", lambda e: e.activation(KTb[:, j, :], pt[:, 0:NMEM], AF.Copy), reads=[pk],
                         writes=[("KTb", j)])
            proj(wmk, 0, 512, mxn, "mxn", NMEM, KT, sink_k)
            P.dma("sync", o_memkT, kf, reads=["kf"])
            if _small == "k":
                P.fence()
                return
            vf = arena[:, 12288:12288 + 1024].rearrange("p (k n) -> p k n", n=512)
            pts = [ps(), ps()]
            for cb in range(2):
                wv, wk = wload(wmv, 0, D, cb * 256, 256)
                for mt in range(2):
                    pt, pk = pts[mt]
                    for k in range(KT):
                        P.op("tensor", lambda e, k=k, mt=mt, pt=pt, cb=cb: e.matmul(
                            pt[:, cb * 256:(cb + 1) * 256], lhsT=mxn[:, k, mt * 128:(mt + 1) * 128], rhs=wv[:, k, :],
                            start=(k == 0), stop=(k == KT - 1)), reads=[wk, "mxn"], writes=[(pk, cb)])
            for mt in range(2):
                pt, pk = pts[mt]
                P.op("vector", lambda e, mt=mt, pt=pt: e.tensor_copy(vf[:, mt, :], pt[:, :]), reads=[pk],
                     writes=[("vf", mt)])
                P.op("vector", lambda e, mt=mt, pt=pt: e.tensor_copy(Vb[:, mt, :], pt[:, :]), reads=[pk],
                     writes=[("Vb", mt)])
            P.dma("sync", o_memv, vf, reads=["vf"])
            P.fence()


        A = lambda off, n: arena[:, off:off + n]
        AB = lambda off, n: arena[:, off:off + (n + 1) // 2].bitcast(BF16)[:, 0:n]
        v3 = lambda ap, n: ap.rearrange("p (k n) -> p k n", n=n)
        SCALE = float(128 ** -0.5)
        zc = sb("zc", [128, 27, 1])
        PCb = sb("PCb", [128, 8])
        ypb = [(psb[6], "pb6"), (psb[7], "pb7")]

        def evac(i, out, in_, reads, writes):
            if i % 2 == 0:
                P.op("vector", lambda e: e.tensor_copy(out, in_), reads=reads, writes=writes)
            else:
                P.op("scalar", lambda e: e.activation(out, in_, AF.Identity), reads=reads, writes=writes)

        def token_shift(last):
            D3 = v3(A(0, 27 * NTB), NTB)
            P.op("vector", lambda e: e.tensor_copy(zc[:], zr[:, :, NTB:NTB + 1]), reads=["zr"], writes=["zc"])
            mub = vec[:, V_MU:V_MU + 27].unsqueeze(2).to_broadcast([128, 27, NTB])
            P.op("vector", lambda e: e.tensor_tensor(D3, zr[:, :, 0:NTB], zr[:, :, 1:1 + NTB], op=ALU.subtract),
                 reads=["zr"], writes=["D3"])
            P.op("vector", lambda e: e.tensor_tensor(D3, D3, mub, op=ALU.mult), reads=["D3", "vec"], writes=["D3"])
            P.op("vector", lambda e: e.tensor_tensor(zr[:, :, 1:1 + NTB], zr[:, :, 1:1 + NTB], D3, op=ALU.add),
                 reads=["D3", "zr", "zc"], writes=["zr"])
            if last:
                sshift = v3(A(7424, 27 * NS), NS)
                P.dma("sync", sshift, sshiftT.rearrange("(k p) n -> p k n", p=128), writes=["sshift"])
                Ds = v3(A(27 * NTB, 27 * NS), NS)
                mus = vec[:, V_MU:V_MU + 27].unsqueeze(2).to_broadcast([128, 27, NS])
                zs = zr[:, :, 1 + NTB:1 + NTM]
                P.op("vector", lambda e: e.tensor_tensor(Ds, sshift, zs, op=ALU.subtract), reads=["zr", "sshift"], writes=["Ds"])
                P.op("vector", lambda e: e.tensor_tensor(Ds, Ds, mus, op=ALU.mult), reads=["Ds", "vec"], writes=["Ds"])
                P.op("vector", lambda e: e.tensor_tensor(zs, zs, Ds, op=ALU.add), reads=["Ds", "zr"], writes=["zr"])
            P.fence()

        def rwkv_elem(cs, n, sample):
            W8 = 8 * n
            xk = zr[:, 8:16, cs]
            sg = A(0, 1024)
            asig, kkn, kmod, bbv = v3(A(1024, W8), n), v3(A(2048, W8), n), v3(A(3072, W8), n), v3(A(4096, W8), n)
            et, t1 = v3(A(5120, W8), n), v3(A(6144, W8), n)
            th = A(8192, 128)
            vcol = lambda c: vec[:, c:c + 8].unsqueeze(2).to_broadcast([128, 8, n])
            P.op("scalar", lambda e: e.activation(th[0:64, 0:n], zr[0:64, 24, cs], AF.Tanh), reads=["zr"], writes=["th"])

            def fm_lora(prow, roff, rhs_ap, sink):
                for hb in range(2):
                    pt, pk = ps()
                    for c in range(4):
                        ch = hb * 4 + c
                        P.op("tensor", lambda e, pt=pt, c=c, ch=ch: e.matmul(
                            pt[:, c * n:(c + 1) * n], lhsT=lora[prow, ch * 128:(ch + 1) * 128], rhs=rhs_ap,
                            start=True, stop=False), reads=["lora", "zr", "th"], writes=[pk])
                        P.op("tensor", lambda e, pt=pt, c=c, ch=ch: e.matmul(
                            pt[:, c * n:(c + 1) * n], lhsT=rows_sb[:, roff + ch * 128:roff + (ch + 1) * 128],
                            rhs=ones_row[:, 0:n], start=False, stop=True), reads=["rows", "cst"], writes=[pk])
                    sink(hb, v3(pt[:, 0:4 * n], n), pk)

            if sample:
                sgf = v3(A(0, W8), n)

                def sink_w(hb, p3, pk):
                    P.op("scalar", lambda e: e.activation(sgf[:, hb * 4:(hb + 1) * 4, :], p3, AF.Sigmoid), reads=[pk], writes=["sg"])
                fm_lora(slice(0, 64), 0, th[0:64, 0:n], sink_w)
                P.op("scalar", lambda e: e.activation(sgf, sgf, AF.Exp, scale=-EDEC), reads=["sg"], writes=["sg"])
            else:
                for hb in range(2):
                    pt, pk = ps()
                    P.op("tensor", lambda e, pt=pt, hb=hb: e.matmul(pt[:, :], lhsT=th[0:64, 0:128], rhs=lora[0:64, hb * 512:(hb + 1) * 512],
                                                       start=True, stop=False), reads=["th", "lora"], writes=[pk])
                    P.op("tensor", lambda e, pt=pt, hb=hb: e.matmul(pt[:, :], lhsT=ones_row[:, 0:128], rhs=rows_sb[:, hb * 512:(hb + 1) * 512],
                                                       start=False, stop=True), reads=["cst", "rows"], writes=[pk])
                    P.op("scalar", lambda e, pt=pt, hb=hb: e.activation(sg[:, hb * 512:(hb + 1) * 512], pt[:, :], AF.Sigmoid),
                         reads=[pk], writes=["sg"])

            def sink_a(hb, p3, pk):
                P.op("scalar", lambda e: e.activation(asig[:, hb * 4:(hb + 1) * 4, :], p3, AF.Sigmoid), reads=[pk], writes=["asig"])
            fm_lora(slice(64, 128), 1024, zr[64:128, 24, cs], sink_a)
            P.op("vector", lambda e: e.tensor_tensor(t1, xk, vcol(V_KK), op=ALU.mult), reads=["zr", "vec"], writes=["t1"])
            P.op("scalar", lambda e: e.activation(et, t1, AF.Square), reads=["t1"], writes=["et"])
            for hb in range(2):
                pt, pk = ps()
                for c in range(4):
                    ch = hb * 4 + c
                    P.op("tensor", lambda e, pt=pt, c=c, ch=ch: e.matmul(pt[:, c * n:(c + 1) * n], lhsT=bones, rhs=et[:, ch, :],
                                                       start=True, stop=True), reads=["cst", "et"], writes=[pk])
                P.op("vector", lambda e, pt=pt, hb=hb: e.tensor_scalar_max(kkn[:, hb * 4:(hb + 1) * 4, :], v3(pt[:, 0:4 * n], n), 1e-24),
                     reads=[pk], writes=["kkn"])
            P.op("scalar", lambda e: e.activation(kkn, kkn, AF.Sqrt), reads=["kkn"], writes=["kkn"])
            P.op("vector", lambda e: e.reciprocal(kkn, kkn), reads=["kkn"], writes=["kkn"])
            P.op("vector", lambda e: e.tensor_tensor(kkn, kkn, t1, op=ALU.mult), reads=["kkn", "t1"], writes=["kkn"])
            P.op("vector", lambda e: e.scalar_tensor_tensor(out=et, in0=asig, scalar=-1.0, in1=vcol(V_KA), op0=ALU.add, op1=ALU.mult),
                 reads=["asig", "vec", "et"], writes=["et"])
            P.op("vector", lambda e: e.scalar_tensor_tensor(out=kmod, in0=et, scalar=1.0, in1=xk, op0=ALU.add, op1=ALU.mult),
                 reads=["et", "zr"], writes=["kmod"])
            P.op("vector", lambda e: e.tensor_tensor(bbv, kkn, asig, op=ALU.mult), reads=["kkn", "asig"], writes=["bbv"])

        def scan_prompt(cs, want_y):
            n = 128
            xr, xv = zr[:, 0:8, cs], zr[:, 16:24, cs]
            sg = A(0, 1024)
            kkn, kmod, bbv = v3(A(2048, 1024), n), v3(A(3072, 1024), n), v3(A(4096, 1024), n)
            et = v3(A(5120, 1024), n)
            b3 = lambda off: v3(AB(off, 1024), n)
            Afm, Bfm, Kfm, Rfm = b3(8320), b3(8832), b3(9344), b3(9856)
            AT, BhT, KhT, VT = AB(10368, 1024), AB(10880, 1024), AB(11392, 1024), AB(11904, 1024)
            tb = b3(12416)
            hs = lambda x, hb: x[:, hb * 4:(hb + 1) * 4, :]

            def cum(tri, sink):
                for hb in range(2):
                    pt, pk = ps()
                    for c in range(4):
                        ch = hb * 4 + c
                        P.op("tensor", lambda e, pt=pt, c=c, ch=ch: e.matmul(pt[:, c * 128:(c + 1) * 128], lhsT=sg[:, ch * 128:(ch + 1) * 128], rhs=tri,
                                                           start=True, stop=True), reads=["sg", "cst"], writes=[pk])
                    sink(hb, v3(pt[:, :], n), pk)

            def sink_incl(hb, p3, pk):
                if want_y:
                    P.op("scalar", lambda e: e.activation(hs(et, hb), p3, AF.Exp), reads=[pk], writes=["et"])
                    P.op("vector", lambda e: e.tensor_tensor(hs(Rfm, hb), hs(xr, hb), hs(et, hb), op=ALU.mult),
                         reads=["et", "zr"], writes=["Rfm"])
                P.op("scalar", lambda e: e.activation(PCb[:, hb * 4:(hb + 1) * 4], p3[:, :, 127], AF.Exp), reads=[pk], writes=["PCb"])
                P.op("scalar", lambda e: e.activation(hs(et, hb), p3, AF.Exp, scale=-1.0), reads=[pk, "Rfm"], writes=["et"])
                P.op("vector", lambda e: e.tensor_tensor(hs(Bfm, hb), hs(bbv, hb), hs(et, hb), op=ALU.mult),
                     reads=["et", "bbv"], writes=["Bfm"])
                P.op("vector", lambda e: e.tensor_tensor(hs(Kfm, hb), hs(kmod, hb), hs(et, hb), op=ALU.mult),
                     reads=["et", "kmod"], writes=["Kfm"])
            cum(triI, sink_incl)
            if int(os.environ.get("K_SCAN", "99")) < 2:
                return

            def sink_excl(hb, p3, pk):
                P.op("scalar", lambda e: e.activation(hs(et, hb), p3, AF.Exp), reads=[pk, "Bfm", "Kfm"], writes=["et"])
                P.op("vector", lambda e: e.scalar_tensor_tensor(out=hs(Afm, hb), in0=hs(kkn, hb), scalar=-1.0, in1=hs(et, hb),
                                                                 op0=ALU.mult, op1=ALU.mult), reads=["et", "kkn"], writes=["Afm"])
            cum(triX, sink_excl)

            def tr_to(src3, srckey, dst, dstkey):
                for hb in range(2):
                    pt, pk = ps()
                    for c in range(4):
                        ch = hb * 4 + c
                        P.op("tensor", lambda e, pt=pt, c=c, ch=ch: e.matmul(pt[:, c * 128:(c + 1) * 128], lhsT=src3[:, ch, :], rhs=ident_bf,
                                                           start=True, stop=True), reads=[srckey, "cbf"], writes=[pk])
                    evac(hb, dst[:, hb * 512:(hb + 1) * 512], pt[:, :], [pk], [dstkey])
            tr_to(Afm, "Afm", AT, "AT")
            if int(os.environ.get("K_SCAN", "99")) < 3:
                return

            def sink_suf(hb, p3, pk):
                P.op("scalar", lambda e: e.activation(hs(et, hb), p3, AF.Exp), reads=[pk, "Afm"], writes=["et"])
                P.op("vector", lambda e: e.tensor_tensor(hs(tb, hb), hs(bbv, hb), hs(et, hb), op=ALU.mult),
                     reads=["et", "bbv"], writes=["tb"])
            cum(triS, sink_suf)
            tr_to(tb, "tb", BhT, "BhT")
            P.op("vector", lambda e: e.tensor_tensor(tb, kmod, et, op=ALU.mult), reads=["et", "kmod", "BhT", "tb"], writes=["tb"])
            tr_to(tb, "tb", KhT, "KhT")
            P.op("vector", lambda e: e.tensor_copy(tb, xv), reads=["zr", "KhT", "tb"], writes=["tb"])
            tr_to(tb, "tb", VT, "VT")
            if int(os.environ.get("K_SCAN", "99")) < 4:
                return

            GOFF = [13056, 13568, 14080, 14592, 15104, 15616, 0, 512, 1024, 1536, 5120]
            gm = lambda i: AB(GOFF[i], 1024)
            UT = AB(16128, 512)
            pos = lambda i: (i % 2) * 4 + i // 2
            blk = lambda i: slice(pos(i) * 128, (pos(i) + 1) * 128)
            yfm = v3(A(7168, 1024), n)
            P.fence()

            def mm8(dst, dk, lt, ltk, rt, rtk, add=None, addk=None):
                bank = [ps(), ps()]
                for i in range(8):
                    pt, pk = bank[i // 4]
                    P.op("tensor", lambda e, pt=pt, i=i: e.matmul(pt[:, (i % 4) * 128:(i % 4 + 1) * 128], lhsT=lt[:, i * 128:(i + 1) * 128],
                                                      rhs=rt[:, i * 128:(i + 1) * 128], start=True, stop=True),
                         reads=[ltk, rtk], writes=[pk])
                for hb in range(2):
                    pt, pk = bank[hb]
                    dk_ = (dk, hb)
                    if add is None:
                        evac(hb, dst[:, hb * 512:(hb + 1) * 512], pt[:, :], [pk], [dk_])
                    else:
                        P.op("vector", lambda e, pt=pt, hb=hb: e.tensor_tensor(dst[:, hb * 512:(hb + 1) * 512], pt[:, :],
                                                                  add[:, hb * 512:(hb + 1) * 512], op=ALU.add),
                             reads=[pk, addk], writes=[dk_])

            for G in range(2):
                heads = [8 * G + i for i in range(8)]
                hrow = lambda x3, h: x3[(h % 2) * 64:(h % 2) * 64 + 64, h // 2, :]
                NakT, Mrb, Mrk, Apf, Nakp = gm(6), gm(7), gm(8), gm(9), gm(10)
                specs = [(gm(0), "g0", Bfm, "Bfm", Afm, "Afm", mSU), (gm(1), "g1", Afm, "Afm", Bfm, "Bfm", mSL),
                         (NakT, "g6", Afm, "Afm", Kfm, "Kfm", mSL)]
                if want_y:
                    specs += [(Mrb, "g7", Bfm, "Bfm", Rfm, "Rfm", mIU), (Mrk, "g8", Kfm, "Kfm", Rfm, "Rfm", mIU)]
                for (dst, dk, la, lak, ra, rak, msk) in specs:
                    bank = [ps(), ps()]
                    for i, h in enumerate(heads):
                        pt, pk = bank[h % 2]
                        P.op("tensor", lambda e, pt=pt, i=i, h=h, la=la, ra=ra: e.matmul(
                            pt[:, (i // 2) * 128:(i // 2 + 1) * 128], lhsT=hrow(la, h), rhs=hrow(ra, h), start=True, stop=True),
                             reads=[lak, rak], writes=[pk])
                    for par in range(2):
                        pt, pk = bank[par]
                        P.op("vector", lambda e, pt=pt, dst=dst, msk=msk, par=par: e.tensor_tensor(
                            dst[:, par * 512:(par + 1) * 512], pt[:, :], msk, op=ALU.mult),
                             reads=[pk, "cst"], writes=[(dk, par)])
                Nc, Nck, Lc, Lck = gm(0), "g0", gm(1), "g1"
                Nn, Nnk, Ln, Lnk = gm(2), "g2", gm(3), "g3"
                Tc, Tck, Tn, Tnk = gm(4), "g4", gm(5), "g5"
                for hb in range(2):
                    P.op("vector", lambda e, hb=hb: e.tensor_tensor(Tc[:, hb * 512:(hb + 1) * 512], Nc[:, hb * 512:(hb + 1) * 512], I4, op=ALU.add),
                         reads=[Nck, "cst"], writes=[(Tck, hb)])
                for k in range(1, 7):
                    if k < 6:
                        mm8(Nn, Nnk, Lc, Lck, Nc, Nck)
                    mm8(Ln, Lnk, Nc, Nck, Lc, Lck)
                    mm8(Tn, Tnk, Ln, Lnk, Tc, Tck, add=Tc, addk=Tck)
                    Nc, Nck, Nn, Nnk = Nn, Nnk, Nc, Nck
                    Lc, Lck, Ln, Lnk = Ln, Lnk, Lc, Lck
                    Tc, Tck, Tn, Tnk = Tn, Tnk, Tc, Tck
                T_, Tk = Tc, Tck
                bank = [ps(), ps()]
                for i, h in enumerate(heads):
                    pt, pk = bank[h % 2]
                    P.op("tensor", lambda e, pt=pt, i=i, h=h: e.matmul(
                        pt[(h % 2) * 64:(h % 2) * 64 + 64, (i // 2) * 128:(i // 2 + 1) * 128],
                        lhsT=AT[:, h * 64:(h + 1) * 64], rhs=T_[:, blk(i)], start=True, stop=True),
                         reads=["AT", Tk], writes=[pk])
                for par in range(2):
                    pt, pk = bank[par]
                    rs = slice(par * 64, par * 64 + 64)
                    evac(par, Apf[rs, 0:512], pt[rs, 0:512], [pk], [("g9", par)])
                mm8(Nakp, "g10", NakT, "g6", T_, Tk)
                bank = [ps(), ps()]
                for i, h in enumerate(heads):
                    hp, h2 = h // 2, h % 2
                    rs = slice(h2 * 64, h2 * 64 + 64)
                    pt, pk = bank[h2]
                    P.op("tensor", lambda e, pt=pt, i=i, hp=hp, rs=rs: e.matmul(
                        pt[:, (i // 2) * 64:(i // 2 + 1) * 64], lhsT=Apf[rs, (i // 2) * 128:(i // 2 + 1) * 128],
                        rhs=Sbf[rs, hp, :], start=True, stop=False), reads=["g9", "Sbf"], writes=[pk])
                    P.op("tensor", lambda e, pt=pt, i=i, h=h: e.matmul(
                        pt[:, (i // 2) * 64:(i // 2 + 1) * 64], lhsT=Nakp[:, blk(i)],
                        rhs=VT[:, h * 64:(h + 1) * 64], start=False, stop=True), reads=["g10", "VT"], writes=[pk])
                for par in range(2):
                    pt, pk = bank[par]
                    evac(par, UT[:, par * 256:(par + 1) * 256], pt[:, 0:256], [pk], [("UT", par)])
                UTb = lambda i: UT[:, pos(i) * 64:(pos(i) + 1) * 64]
                if want_y:
                    bank = [ps(), ps()]
                    for i, h in enumerate(heads):
                        hp, h2 = h // 2, h % 2
                        rs = slice(h2 * 64, h2 * 64 + 64)
                        pt, pk = bank[h2]
                        yo = pt[rs, (i // 2) * 128:(i // 2 + 1) * 128]
                        P.op("tensor", lambda e, yo=yo, rs=rs, hp=hp: e.matmul(yo, lhsT=Sbf[rs, hp, :], rhs=Rfm[rs, hp, :], start=True, stop=False),
                             reads=["Sbf", "Rfm"], writes=[pk])
                        P.op("tensor", lambda e, yo=yo, i=i: e.matmul(yo, lhsT=UTb(i), rhs=Mrb[:, blk(i)], start=False, stop=False),
                             reads=["UT", "g7"], writes=[pk])
                        P.op("tensor", lambda e, yo=yo, i=i, h=h: e.matmul(yo, lhsT=VT[:, h * 64:(h + 1) * 64], rhs=Mrk[:, blk(i)], start=False, stop=True),
                             reads=["VT", "g8"], writes=[pk])
                    for par in range(2):
                        pt, pk = bank[par]
                        rs = slice(par * 64, par * 64 + 64)
                        evac(par, yfm[rs, 4 * G:4 * G + 4, :], v3(pt[rs, 0:512], 128), [pk], [("yfm", G * 2 + par)])
                bank = [ps(), ps()]
                for i, h in enumerate(heads):
                    h2 = h % 2
                    rs = slice(h2 * 64, h2 * 64 + 64)
                    pt, pk = bank[h2]
                    so = pt[rs, (i // 2) * 64:(i // 2 + 1) * 64]
                    P.op("tensor", lambda e, so=so, i=i, h=h: e.matmul(so, lhsT=BhT[:, h * 64:(h + 1) * 64], rhs=UTb(i), start=True, stop=False),
                         reads=["BhT", "UT"], writes=[pk])
                    P.op("tensor", lambda e, so=so, h=h: e.matmul(so, lhsT=KhT[:, h * 64:(h + 1) * 64], rhs=VT[:, h * 64:(h + 1) * 64], start=False, stop=True),
                         reads=["KhT", "VT"], writes=[pk])
                hps = slice(4 * G, 4 * G + 4)
                P.op("vector", lambda e, hps=hps: e.tensor_tensor(S32[:, hps, :], S32[:, hps, :],
                                                                   PCb[:, hps].unsqueeze(2).to_broadcast([128, 4, 64]), op=ALU.mult),
                     reads=["PCb", "S32", "Sbf"], writes=["S32"])
                for par in range(2):
                    pt, pk = bank[par]
                    rs = slice(par * 64, par * 64 + 64)
                    P.op("vector", lambda e, hps=hps, pt=pt, rs=rs: e.tensor_tensor(S32[rs, hps, :], S32[rs, hps, :], v3(pt[rs, 0:256], 64), op=ALU.add),
                         reads=[pk, "S32"], writes=["S32"])
                P.op("vector", lambda e, hps=hps: e.tensor_copy(Sbf[:, hps, :], S32[:, hps, :]), reads=["S32", "Sbf"], writes=["Sbf"])
            P.fence()

        def rwkv_post(cs, n, ycol0, from_psum):
            W8 = 8 * n
            xr, xv = zr[:, 0:8, cs], zr[:, 16:24, cs]
            rstdv, kmod, bbv = v3(A(1024, W8), n), v3(A(3072, W8), n), v3(A(4096, W8), n)
            et, t1, yfm = v3(A(5120, W8), n), v3(A(6144, W8), n), v3(A(7168, W8), n)
            sgl = v3(AB(12928, 256), 128)
            vcol = lambda c: vec[:, c:c + 8].unsqueeze(2).to_broadcast([128, 8, n])
            hs = lambda x, hb: x[:, hb * 4:(hb + 1) * 4, :]
            if from_psum:
                for hb in range(2):
                    evac(hb, hs(yfm, hb), v3(ypb[hb][0][:, :], n), [ypb[hb][1]], ["yfm"])

            def bmm(lhs, src, srck, sink):
                for hb in range(2):
                    pt, pk = ps()
                    for c in range(4):
                        ch = hb * 4 + c
                        P.op("tensor", lambda e, pt=pt, c=c, ch=ch: e.matmul(pt[:, c * n:(c + 1) * n], lhsT=lhs, rhs=src[:, ch, :],
                                                           start=True, stop=True), reads=["cst", srck], writes=[pk])
                    sink(hb, v3(pt[:, 0:4 * n], n), pk)
            bmm(bones64, yfm, "yfm", lambda hb, p3, pk: P.op(
                "vector", lambda e: e.tensor_tensor(hs(t1, hb), hs(yfm, hb), p3, op=ALU.subtract), reads=["yfm", pk], writes=["t1"]))
            P.op("scalar", lambda e: e.activation(et, t1, AF.Square), reads=["t1"], writes=["et"])
            bmm(bones64, et, "et", lambda hb, p3, pk: P.op(
                "scalar", lambda e: e.activation(hs(rstdv, hb), p3, AF.Sqrt, bias=GN_EPS, scale=1.0), reads=[pk], writes=["rstdv"]))
            P.op("vector", lambda e: e.reciprocal(rstdv, rstdv), reads=["rstdv"], writes=["rstdv"])
            P.op("vector", lambda e: e.tensor_tensor(t1, t1, rstdv, op=ALU.mult), reads=["t1", "rstdv"], writes=["t1"])
            P.op("vector", lambda e: e.tensor_tensor(t1, t1, vcol(V_LG), op=ALU.mult), reads=["t1", "vec"], writes=["t1"])
            P.op("vector", lambda e: e.tensor_tensor(t1, t1, vcol(V_LB), op=ALU.add), reads=["t1", "vec"], writes=["t1"])
            P.op("vector", lambda e: e.tensor_tensor(et, xr, kmod, op=ALU.mult), reads=["zr", "kmod", "et"], writes=["et"])
            P.op("vector", lambda e: e.tensor_tensor(et, et, vcol(V_RK), op=ALU.mult), reads=["et", "vec"], writes=["et"])
            bmm(bones, et, "et", lambda hb, p3, pk: P.op(
                "vector", lambda e: e.tensor_tensor(hs(bbv, hb), p3, hs(xv, hb), op=ALU.mult), reads=[pk, "zr", "bbv"], writes=["bbv"]))
            P.op("vector", lambda e: e.tensor_tensor(t1, t1, bbv, op=ALU.add), reads=["t1", "bbv"], writes=["t1"])
            P.op("scalar", lambda e: e.activation(sgl[:, 0, 0:n], zr[:, 25, cs], AF.Sigmoid), reads=["zr"], writes=["sgl"])
            P.op("scalar", lambda e: e.activation(sgl[0:32, 1, 0:n], zr[0:32, 26, cs], AF.Sigmoid), reads=["zr", "sgl"], writes=["sgl"])
            for hb in range(2):
                pt, pk = ps()
                for c in range(4):
                    ch = hb * 4 + c
                    P.op("tensor", lambda e, pt=pt, c=c, ch=ch: e.matmul(pt[:, c * n:(c + 1) * n], lhsT=gupb[:, 0, ch * 128:(ch + 1) * 128],
                                                       rhs=sgl[:, 0, 0:n], start=True, stop=False), reads=["gupb", "sgl"], writes=[pk])
                    P.op("tensor", lambda e, pt=pt, c=c, ch=ch: e.matmul(pt[:, c * n:(c + 1) * n], lhsT=gupb[0:32, 1, ch * 128:(ch + 1) * 128],
                                                       rhs=sgl[0:32, 1, 0:n], start=False, stop=True), reads=["gupb", "sgl"], writes=[pk])
                P.op("vector", lambda e, pt=pt, hb=hb: e.tensor_tensor(yg[:, hb * 4:(hb + 1) * 4, ycol0:ycol0 + n], hs(t1, hb),
                                                          v3(pt[:, 0:4 * n], n), op=ALU.mult), reads=[pk, "t1"], writes=["yg"])

        def sample_scan():
            n = NS
            cs = slice(1 + NTB, 1 + NTM)
            xr, xv = zr[:, 0:8, cs], zr[:, 16:24, cs]
            sgf, kkn, kmod, bbv = v3(A(0, 128), n), v3(A(2048, 128), n), v3(A(3072, 128), n), v3(A(4096, 128), n)
            Vs = A(5120, 256)[0:64, :].rearrange("p (h n) -> p h n", n=n)
            Ys = A(5376, 256)[0:64, :].rearrange("p (h n) -> p h n", n=n)
            rowb = [arena[0:16, 7168:8192], arena[0:16, 9216:10240]]
            ops = [(kkn, "kkn", -1.0), (sgf, "sg", 1.0), (bbv, "bbv", 1.0), (kmod, "kmod", 1.0), (xr, "zr", 1.0)]
            for oi, (X, xk_, sc) in enumerate(ops):
                rb = rowb[oi % 2]
                rbk = f"rowb{oi % 2}"
                for hb in range(2):
                    pt, pk = ps()
                    for c in range(4):
                        ch = hb * 4 + c
                        P.op("tensor", lambda e, pt=pt, c=c, ch=ch, X=X: e.matmul(pt[0:16, c * 128:(c + 1) * 128], lhsT=X[:, ch, :], rhs=ident,
                                                                start=True, stop=True), reads=[xk_, "cst"], writes=[pk])
                    P.op("scalar", lambda e, pt=pt, hb=hb, rb=rb, sc=sc: e.activation(rb[:, hb * 512:(hb + 1) * 512], pt[0:16, :], AF.Identity, scale=sc),
                         reads=[pk], writes=[rbk])
                P.dma("sync", scr_rows[oi], rb, reads=[rbk], writes=["scr_rows"])
            Vs4 = Vs.rearrange("p (hp h2) n -> p hp h2 n", h2=2)
            P.op("vector", lambda e: e.tensor_copy(Vs4[:, :, 0, :], xv[0:64, :, :]), reads=["zr"], writes=["Vs"])
            P.dma("sync", Vs4[:, :, 1, :], xv[64:128, :, :], reads=["zr"], writes=["Vs"])
            P.fence()
            S = v3(A(8192, 1024)[0:64, :], 64)
            bc = [v3(A(9216 + i * 1024, 1024)[0:64, :], 64) for i in range(5)]
            tt = v3(A(14336, 1024)[0:64, :], 64)
            S1 = v3(A(15360, 1024)[0:64, :], 64)
            sa = A(7168, 16)[0:64, :]
            for s_ in range(NS):
                P.dma("sync", S, swkv[:, s_], writes=["S"])
                for i in range(5):
                    P.dma("sync", A(9216 + i * 1024, 1024)[0:64, :], scr_rows[i, s_:s_ + 1, :].to_broadcast([64, 1024]),
                          writes=[f"bc{i}"])
                a_, w_, b_, k_, r_ = bc
                P.op("vector", lambda e: e.tensor_tensor(tt, S, a_, op=ALU.mult), reads=["S", "bc0", "tt"], writes=["tt"])
                P.op("vector", lambda e: e.tensor_reduce(out=sa, in_=tt, axis=AX.X, op=ALU.add), reads=["tt"], writes=["sa"])
                P.op("vector", lambda e: e.tensor_tensor(S1, S, w_, op=ALU.mult), reads=["S", "bc1", "S1"], writes=["S1"])
                P.op("vector", lambda e: e.tensor_tensor(tt, b_, sa.unsqueeze(2).to_broadcast([64, 16, 64]), op=ALU.mult),
                     reads=["bc2", "sa", "tt"], writes=["tt"])
                P.op("vector", lambda e: e.tensor_tensor(S1, S1, tt, op=ALU.add), reads=["S1", "tt"], writes=["S1"])
                P.op("vector", lambda e, s_=s_: e.tensor_tensor(tt, k_, Vs[:, :, s_].unsqueeze(2).to_broadcast([64, 16, 64]), op=ALU.mult),
                     reads=["bc3", "Vs", "tt"], writes=["tt"])
                P.op("vector", lambda e: e.tensor_tensor(S1, S1, tt, op=ALU.add), reads=["S1", "tt"], writes=["S1"])
                P.op("vector", lambda e: e.tensor_tensor(tt, S1, r_, op=ALU.mult), reads=["S1", "bc4", "tt"], writes=["tt"])
                P.op("vector", lambda e, s_=s_: e.tensor_reduce(out=Ys[:, :, s_], in_=tt, axis=AX.X, op=ALU.add), reads=["tt"], writes=["Ys"])
                P.dma("sync", o_wkvs[:, s_], S1, reads=["S1"])
            yfm = v3(A(7168, 128), n)
            Ys4 = Ys.rearrange("p (hp h2) n -> p hp h2 n", h2=2)
            P.op("vector", lambda e: e.tensor_copy(yfm[0:64, :, :], Ys4[:, :, 0, :]), reads=["Ys", "sa"], writes=["yfm"])
            P.dma("sync", yfm[64:128, :, :], Ys4[:, :, 1, :], reads=["Ys", "sa"], writes=["yfm"])
            P.fence()

        def pool_branch(t_own, nt, last):
            E = 15 + NTB
            icnt = v3(A(8192, 4 * NTM), NTM)
            P.dma("sync", icnt, invc[t_own:t_own + 1, :].to_broadcast([128, 4 * NTM]).rearrange("p (g n) -> p g n", n=NTM),
                  writes=["icnt"])
            lv = [A(i * 288, E) for i in range(4)]
            pbf = AB(1152, NTM)
            if last:
                spl = A(1408, 4 * NS * 15).rearrange("p (g n t) -> p g n t", g=4, n=NS)
                P.dma("sync", spl, spoolT.rearrange("(g p) n t -> p g n t", p=128), writes=["spl"])
                ssum = A(2368, NS)
            for g in range(4):
                w = 2 << g
                x = zp[:, g, 0:E]
                src = x
                for l in range(g + 1):
                    sh = 1 << l
                    lo = 2 * sh - 1
                    dst = lv[l]
                    P.op("vector", lambda e, dst=dst, src=src, lo=lo, sh=sh: e.tensor_tensor(dst[:, lo:E], src[:, lo:E], src[:, lo - sh:E - sh], op=ALU.add),
                         reads=["zp", "lv"], writes=["lv"])
                    src = dst
                P.op("vector", lambda e, src=src, g=g: e.tensor_tensor(lv[3][:, 15:E] if g < 3 else lv[2][:, 15:E], src[:, 15:E], icnt[:, g, 0:NTB], op=ALU.mult),
                     reads=["lv", "icnt"], writes=["lv"])
                pm = lv[3] if g < 3 else lv[2]
                P.op("vector", lambda e, pm=pm, x=x: e.tensor_tensor(pbf[:, 0:NTB], pm[:, 15:E], x[:, 15:E], op=ALU.subtract),
                     reads=["lv", "zp", "pbf"], writes=["pbf"])
                if last:
                    xs = zp[:, g, 15 + NTB:15 + NTM]
                    P.op("vector", lambda e, g=g, w=w: e.tensor_reduce(out=ssum, in_=spl[:, g, :, 15 - (w - 1):15], axis=AX.X, op=ALU.add),
                         reads=["spl", "ssum"], writes=["ssum"])
                    P.op("vector", lambda e, xs=xs: e.tensor_tensor(ssum, ssum, xs, op=ALU.add), reads=["ssum", "zp"], writes=["ssum"])
                    P.op("vector", lambda e, xs=xs, w=w: e.scalar_tensor_tensor(out=pbf[:, NTB:NTM], in0=ssum, scalar=1.0 / w, in1=xs,
                                                                          op0=ALU.mult, op1=ALU.subtract), reads=["ssum", "zp", "pbf"], writes=["pbf"])
                pt, pk = ps()
                P.op("tensor", lambda e, pt=pt, g=g: e.matmul(pt[:, 0:nt], lhsT=pgwb[:, g, :], rhs=pbf[:, 0:nt], start=True, stop=True),
                     reads=["pgwb", "pbf"], writes=[pk])
                P.op("vector", lambda e, pt=pt, g=g: e.tensor_scalar_mul(mix[:, g, 0:nt], pt[:, 0:nt], vec[:, V_PS + g:V_PS + g + 1]),
                     reads=[pk, "vec"], writes=["mix"])

        def attn_prompt(c0):
            n = 128
            sc = v3(A(4096, 1024), 256)
            pnb = v3(AB(5120, 1024), 256)
            pTb = v3(AB(5632, 1024), 128)
            mx, nb, sm, rs = A(6144, 4), A(6148, 4), A(6152, 4), A(6156, 4)
            for hp_ in range(2):
                pt, pk = ps()
                for hh in range(2):
                    h = hp_ * 2 + hh
                    P.op("tensor", lambda e, pt=pt, hh=hh, h=h: e.matmul(pt[:, hh * 256:(hh + 1) * 256], lhsT=zq[:, h, c0:c0 + n], rhs=KTb[:, h, :],
                                                          start=True, stop=True), reads=["zq", "KTb"], writes=[pk])
                P.op("vector", lambda e, pt=pt, hp_=hp_: e.tensor_reduce(out=mx[:, hp_ * 2:hp_ * 2 + 2], in_=v3(pt[:, :], 256), axis=AX.X, op=ALU.max),
                     reads=[pk, "mx"], writes=["mx"])
                P.op("vector", lambda e, hp_=hp_: e.tensor_scalar_mul(nb[:, hp_ * 2:hp_ * 2 + 2], mx[:, hp_ * 2:hp_ * 2 + 2], -SCALE),
                     reads=["mx", "nb"], writes=["nb"])
                for hh in range(2):
                    h = hp_ * 2 + hh
                    P.op("scalar", lambda e, pt=pt, hh=hh, h=h: e.activation(sc[:, h, :], pt[:, hh * 256:(hh + 1) * 256], AF.Exp, bias=nb[:, h:h + 1],
                                                              scale=SCALE, accum_out=sm[:, h:h + 1]), reads=[pk, "nb", "sc", "sm"], writes=["sc", "sm"])
            P.op("vector", lambda e: e.reciprocal(rs, sm), reads=["sm", "rs"], writes=["rs"])
            P.op("vector", lambda e: e.tensor_tensor(pnb, sc, rs.unsqueeze(2).to_broadcast([128, 4, 256]), op=ALU.mult),
                 reads=["sc", "rs", "pnb"], writes=["pnb"])
            for hp_ in range(2):
                pt, pk = ps()
                for hh in range(2):
                    h = hp_ * 2 + hh
                    for mt in range(2):
                        j = hh * 2 + mt
                        P.op("tensor", lambda e, pt=pt, j=j, h=h, mt=mt: e.matmul(pt[:, j * 128:(j + 1) * 128], lhsT=pnb[:, h, mt * 128:(mt + 1) * 128], rhs=ident_bf,
                                                                start=True, stop=True), reads=["pnb", "cbf"], writes=[pk])
                evac(hp_, pTb[:, hp_ * 4:(hp_ + 1) * 4, :], v3(pt[:, :], 128), [pk, "pTb"], ["pTb"])
            pt, pk = ps()
            for h in range(4):
                for mt in range(2):
                    P.op("tensor", lambda e, pt=pt, h=h, mt=mt: e.matmul(pt[:, h * 128:(h + 1) * 128], lhsT=Vb[:, mt, h * 128:(h + 1) * 128], rhs=pTb[:, h * 2 + mt, :],
                                                          start=(mt == 0), stop=(mt == 1)), reads=["Vb", "pTb"], writes=[pk])
            P.op("vector", lambda e, pt=pt: e.tensor_copy(oT[:, :, c0:c0 + n], v3(pt[:, :], 128)), reads=[pk], writes=["oT"])

        def attn_sample():
            qrow = arena[0:16, 0:512]
            pt, pk = ps()
            for h in range(4):
                P.op("tensor", lambda e, pt=pt, h=h: e.matmul(pt[0:16, h * 128:(h + 1) * 128], lhsT=zq[:, h, NTB:NTM], rhs=ident_bf, start=True, stop=True),
                     reads=["zq", "cbf"], writes=[pk])
            P.op("vector", lambda e, pt=pt: e.tensor_copy(qrow, pt[0:16, :]), reads=[pk], writes=["qrow"])
            P.dma("sync", scr_q, qrow, reads=["qrow"], writes=["scr_q"])
            qbc = v3(A(8192, NS * 512), 512)
            P.dma("sync", qbc, scr_q.rearrange("n c -> (n c)").unsqueeze(0).to_broadcast([128, NS * 512]).rearrange("p (n c) -> p n c", c=512),
                  reads=["scr_q"], writes=["qbc"])
            KV = [v3(A(512 + i * 1024, 1024), 512) for i in range(2)]
            prod = A(2560, 512)
            s_all = v3(A(3072, 128), 64)
            p64 = A(3200, 256)
            pTs = v3(A(3456, 128), 64)
            mx, nb, sm, rs = A(3584, 1), A(3585, 1), A(3586, 1), A(3587, 1)
            for s_ in range(NS):
                kb = KV[s_ % 2]
                kbk = f"kv{s_ % 2}"
                P.dma("sync", kb, kc[s_].rearrange("(mt p) c -> p mt c", p=128), writes=[kbk])
                for mt in range(2):
                    P.op("vector", lambda e, kb=kb, mt=mt, s_=s_: e.tensor_tensor(prod, kb[:, mt, :], qbc[:, s_, :], op=ALU.mult),
                         reads=[kbk, "qbc", "prod"], writes=["prod"])
                    P.op("vector", lambda e, mt=mt, s_=s_: e.tensor_reduce(out=s_all[:, mt, s_ * 4:(s_ + 1) * 4], in_=v3(prod, 128), axis=AX.X, op=ALU.add),
                         reads=["prod", "s_all"], writes=["s_all"])
            pt, pk = ps()
            for mt in range(2):
                P.op("tensor", lambda e, pt=pt, mt=mt: e.matmul(pt[0:64, mt * 128:(mt + 1) * 128], lhsT=s_all[:, mt, :], rhs=ident, start=True, stop=True),
                     reads=["s_all", "cst"], writes=[pk])
            P.op("vector", lambda e, pt=pt: e.tensor_reduce(out=mx[0:64, :], in_=pt[0:64, 0:256], axis=AX.X, op=ALU.max), reads=[pk], writes=["mx"])
            P.op("vector", lambda e: e.tensor_scalar_mul(nb[0:64, :], mx[0:64, :], -SCALE), reads=["mx"], writes=["nb"])
            P.op("scalar", lambda e, pt=pt: e.activation(p64[0:64, :], pt[0:64, 0:256], AF.Exp, bias=nb[0:64, :], scale=SCALE, accum_out=sm[0:64, :]),
                 reads=[pk, "nb"], writes=["p64", "sm"])
            P.op("vector", lambda e: e.reciprocal(rs[0:64, :], sm[0:64, :]), reads=["sm"], writes=["rs"])
            P.op("vector", lambda e: e.tensor_scalar_mul(p64[0:64, :], p64[0:64, :], rs[0:64, :]), reads=["p64", "rs"], writes=["p64"])
            pt2, pk2 = ps()
            for mt in range(2):
                P.op("tensor", lambda e, mt=mt: e.matmul(pt2[:, mt * 64:(mt + 1) * 64], lhsT=p64[0:64, mt * 128:(mt + 1) * 128], rhs=ident[0:64, 0:64],
                                                  start=True, stop=True), reads=["p64", "cst"], writes=[pk2])
            P.op("vector", lambda e: e.tensor_copy(pTs, v3(pt2[:, 0:128], 64)), reads=[pk2], writes=["pTs"])
            pt3, pk3 = ps()
            for s_ in range(NS):
                vb_ = KV[s_ % 2]
                vbk = f"kv{s_ % 2}"
                P.dma("sync", vb_, vc[s_].rearrange("(mt p) c -> p mt c", p=128), writes=[vbk])
                for h in range(4):
                    col = s_ * 4 + h
                    for mt in range(2):
                        P.op("tensor", lambda e, vb_=vb_, h=h, mt=mt, col=col: e.matmul(pt3[:, col:col + 1], lhsT=vb_[:, mt, h * 128:(h + 1) * 128],
                                                                  rhs=pTs[:, mt, col:col + 1], start=(mt == 0), stop=(mt == 1)),
                             reads=[vbk, "pTs"], writes=[pk3])
            P.op("vector", lambda e: e.tensor_copy(oT[:, :, NTB:NTM], pt3[:, 0:64].rearrange("p (n h) -> p h n", h=4)), reads=[pk3], writes=["oT"])

        def merge_wo(nt):
            acc = [A(0, NTM), A(288, NTM)]
            sgt = A(576, NTM)
            merged = v3(AB(1024, KT * NTM), NTM)
            branches = [(C_G, pout, mix, "mix", 4), (C_G + D, rout, yg, "yg", 8), (C_G + 2 * D, xout, oT, "oT", 4)]
            for ip in range(8):
                for bi, (gc0, Wb, src, srck, ktb) in enumerate(branches):
                    wgt, wgk = wload(win, 0, D, gc0 + ip * 256, 256)
                    wbr, wbk = wload(Wb, 0, ktb * 128, ip * 256, 256)
                    for ci in range(2):
                        pg, pgk = ps()
                        po, pok = ps()
                        P.group("tensor", [lambda e, pg=pg, k=k, ci=ci, wgt=wgt: e.matmul(
                            pg[:, 0:nt], lhsT=wgt[:, k, ci * 128:(ci + 1) * 128], rhs=xn[:, k, 0:nt],
                            start=(k == 0), stop=(k == KT - 1)) for k in range(KT)], reads=[wgk, "xn"], writes=[pgk])
                        P.group("tensor", [lambda e, po=po, k=k, ci=ci, wbr=wbr, src=src, ktb=ktb: e.matmul(
                            po[:, 0:nt], lhsT=wbr[:, k, ci * 128:(ci + 1) * 128], rhs=src[:, k, 0:nt],
                            start=(k == 0), stop=(k == ktb - 1)) for k in range(ktb)], reads=[wbk, srck], writes=[pok])
                        P.op("scalar", lambda e, pg=pg: e.activation(sgt[:, 0:nt], pg[:, 0:nt], AF.Sigmoid), reads=[pgk, "sgt"], writes=["sgt"])
                        if bi == 0:
                            P.op("vector", lambda e, po=po, ci=ci: e.tensor_tensor(acc[ci][:, 0:nt], sgt[:, 0:nt], po[:, 0:nt], op=ALU.mult),
                                 reads=["sgt", pok, f"acc{ci}"], writes=[f"acc{ci}"])
                        else:
                            P.op("vector", lambda e, po=po: e.tensor_tensor(sgt[:, 0:nt], sgt[:, 0:nt], po[:, 0:nt], op=ALU.mult),
                                 reads=["sgt", pok], writes=["sgt"])
                            P.op("vector", lambda e, ci=ci: e.tensor_tensor(acc[ci][:, 0:nt], acc[ci][:, 0:nt], sgt[:, 0:nt], op=ALU.add),
                                 reads=["sgt", f"acc{ci}"], writes=[f"acc{ci}"])
                for ci in range(2):
                    i = ip * 2 + ci
                    P.op("vector", lambda e, ci=ci, i=i: e.tensor_copy(merged[:, i, 0:nt], acc[ci][:, 0:nt]), reads=[f"acc{ci}"], writes=[("merged", i)])

            def sink_o(j, pt, pk, m):
                P.op("vector", lambda e: e.tensor_tensor(xh[:, j, 0:nt], xh[:, j, 0:nt], pt[:, 0:nt], op=ALU.add),
                     reads=[pk, ("xh", j)], writes=[("xh", j)])
            proj(wo, 0, D, merged, "merged", nt, KT, sink_o)
            P.fence()

        P.op("vector", lambda e: e.memset(zr[:], 0.0), writes=["zr"])
        mem_kv()
        P.dma("sync", o_poolso, spool[:, 1:15, :])
        xT3 = xT.rearrange("(k p) n -> p k n", p=128)
        TSEL = [int(v) for v in os.environ.get('K_TILES', '0,1,2,3,4,5,6,7').split(',') if v != '']
        STAGE = int(os.environ.get("K_STAGE", "9"))
        for t in TSEL:
            own = t >= 4
            last = t == 7
            nt = NTB + (NS if last else 0)
            P.dma("sync", xh[:, :, 0:NTB], xT3[:, :, t * NTB:(t + 1) * NTB], writes=["xh"])
            if last:
                P.dma("sync", xh[:, :, NTB:NTM], xT3[:, :, 2048:2048 + NS], writes=["xh"])
            ffn(f1g, f1u, f1d, G_F1, nt, True)
            rmsnorm(xh, G_MIX, nt, xn, "xh", "xn")
            if t >= 3:
                def sink_zp(j, pt, pk, m, nt=nt):
                    P.op("vector", lambda e: e.tensor_copy(zp[:, j, 15:15 + nt], pt[:, 0:nt]), reads=[pk], writes=["zp"])
                proj(win, C_POOL, 512, xn, "xn", nt, KT, sink_zp)

            def sink_zr(j, pt, pk, m, nt=nt):
                evac(j, zr[0:m, j, 1:1 + nt], pt[0:m, 0:nt], [pk], ["zr"])
            proj(win, C_R, RPW, xn, "xn", nt, KT, sink_zr)
            if own:
                def sink_zq(j, pt, pk, m, nt=nt):
                    evac(j, zq[:, j, 0:nt], pt[:, 0:nt], [pk], ["zq"])
                proj(win, C_XQ, 512, xn, "xn", nt, KT, sink_zq)
            if last:
                P.dma("sync", o_shiftT, zr[:, :, NTB:NTB + 1 + NS], reads=["zr"])
                P.dma("sync", o_poolpT, zp[:, :, NTB:NTB + 15], reads=["zp"])
                P.dma("sync", o_poolsn, zp[:, :, 15 + NTB:15 + NTM], reads=["zp"])
            P.fence()
            SUB = int(os.environ.get("K_SUB", "9"))
            if STAGE >= 2:
                token_shift(last)
                for sub in range(2):
                    cs = slice(1 + sub * 128, 1 + (sub + 1) * 128)
                    if SUB >= 2:
                        rwkv_elem(cs, 128, False)
                    if SUB >= 3:
                        scan_prompt(cs, own)
                    if own and SUB >= 4:
                        rwkv_post(cs, 128, sub * 128, False)
                    P.fence()
                if last:
                    cs = slice(1 + NTB, 1 + NTM)
                    rwkv_elem(cs, NS, True)
                    P.fence()
                    sample_scan()
                    rwkv_post(cs, NS, NTB, False)
                    P.fence()
                P.op("vector", lambda e: e.tensor_copy(zr[:, :, 0:1], zc[:]), reads=["zc", "zr"], writes=["zr"])
            if own and STAGE >= 3:
                pool_branch(t - 4, nt, last)
                for sub in range(2):
                    attn_prompt(sub * 128)
                P.fence()
                if last:
                    attn_sample()
                    P.fence()
            if t >= 3:
                P.op("vector", lambda e: e.tensor_copy(zp[:, :, 0:15], zp[:, :, NTB:NTB + 15]), reads=["zp"], writes=["zp"])
            if own:
                P.fence()
                if STAGE >= 4:
                    merge_wo(nt)
                ffn(f2g, f2u, f2d, G_F2, nt, True)
                yo = v3(A(0, KT * NTM), NTM)
                rmsnorm(xh, G_FIN, nt, yo, "xh", "yo")
                P.dma("sync", yT[:, :, (t - 4) * NTB:(t - 3) * NTB], yo[:, :, 0:NTB], reads=["yo"])
                if last:
                    P.dma("sync", yT[:, :, 1024:1024 + NS], yo[:, :, NTB:NTM], reads=["yo"])
            P.fence()
        P.dma("sync", o_wkvp, S32[:].rearrange("p k n -> p (k n)"), reads=["S32"])
        P.finish()
    return nc


def _consts():
    c = np.zeros((128, 3584), np.float32)
    s = np.arange(128)[:, None]
    t = np.arange(128)[None, :]
    eye = np.eye(128, dtype=np.float32)
    c[:, 0:128] = eye
    c[:, 128:256] = -EDEC * (s <= t)
    c[:, 256:384] = -EDEC * (s < t)
    c[:, 384:512] = -EDEC * (s > t)
    bo = ((s // 64) == (t // 64)).astype(np.float32)
    c[:, 512:640] = bo
    c[:, 640:768] = bo / 64.0
    c[:, 768:1280] = np.tile((s < t).astype(np.float32), (1, 4))
    c[:, 1280:1792] = np.tile((s > t).astype(np.float32), (1, 4))
    c[:, 1792:2304] = np.tile((s <= t).astype(np.float32), (1, 4))
    c[:, 2304:2816] = np.tile(eye, (1, 4))
    c[:, 2816:3072] = 1.0
    c[:, 3072:3200] = eye
    c[:, 3200:3328] = 1.0
    return c


_NC_CACHE = {}
_PACK_ONLY = False


def kernel(x_prompt, x_sample, mem_prompt, cache_mem_k, cache_mem_v, state_wkv, state_shift, state_pool,
           ffn1_norm_g, ffn1_w_gate, ffn1_w_up, ffn1_w_down, mix_norm_g, w_in,
           pool_group_w, pool_scale, pool_out,
           rwkv_mu, rwkv_w0, rwkv_w_up, rwkv_a0, rwkv_a_up, rwkv_g_up, rwkv_k_k, rwkv_k_a, rwkv_r_k,
           rwkv_ln_g, rwkv_ln_b, rwkv_out,
           mem_norm_g, w_mem_k, w_mem_v, xattn_out, w_o,
           ffn2_norm_g, ffn2_w_gate, ffn2_w_up, ffn2_w_down, final_norm_g):
    f = lambda a: np.ascontiguousarray(np.asarray(a, dtype=np.float32))
    x_prompt, x_sample, mem_prompt = f(x_prompt), f(x_sample), f(mem_prompt)
    B = x_prompt.shape[0]

    def kcols(v, n):
        v = f(v).reshape(-1)
        pad = np.zeros(n * 128, np.float32)
        pad[:v.size] = v
        return pad.reshape(n, 128).T

    vecs = np.zeros((128, 176), np.float32)
    vecs[:, 0:16] = kcols(ffn1_norm_g[0], 16)
    vecs[:, 16:32] = kcols(mix_norm_g[0], 16)
    vecs[:, 32:48] = kcols(mem_norm_g[0], 16)
    vecs[:, 48:64] = kcols(ffn2_norm_g[0], 16)
    vecs[:, 64:80] = kcols(final_norm_g, 16)
    vecs[:, 80:107] = kcols(rwkv_mu[0], 27)
    vecs[:, 107:111] = kcols(pool_scale[0], 4)
    vecs[:, 111:119] = kcols(rwkv_k_k[0], 8)
    vecs[:, 119:127] = kcols(rwkv_k_a[0], 8)
    vecs[:, 127:135] = kcols(rwkv_r_k[0], 8)
    vecs[:, 135:143] = kcols(rwkv_ln_g[0], 8)
    vecs[:, 143:151] = kcols(rwkv_ln_b[0], 8)
    rows = np.concatenate([f(rwkv_w0[0]), f(rwkv_a0[0])])[None, :]
    cst = _consts()
    if "nc" not in _NC_CACHE:
        _NC_CACHE["nc"] = build_nc()
    raw = {"f1g": ffn1_w_gate[0], "f1u": ffn1_w_up[0], "f1d": ffn1_w_down[0],
           "f2g": ffn2_w_gate[0], "f2u": ffn2_w_up[0], "f2d": ffn2_w_down[0],
           "win": w_in[0], "pout": pool_out[0], "rout": rwkv_out[0], "xout": xattn_out[0], "wo": w_o[0]}
    shared = {
        "pgw": f(pool_group_w[0]),
        "wup": f(rwkv_w_up[0]), "aup": f(rwkv_a_up[0]), "gup": f(rwkv_g_up[0]),
        "wmk": f(w_mem_k[0]), "wmv": f(w_mem_v[0]),
        "vecs": vecs, "rows": rows, "cst": cst,
    }
    for nm, W in raw.items():
        W = f(W)
        blks = BLOCKS.get(nm, [])
        arr = np.zeros((IN_SHAPES[nm][0], 128, 4096), np.float32)
        for bi, (r0, nrows, c0, ncols) in enumerate(blks):
            k = nrows // 128
            arr[bi, :, :k * ncols] = W[r0:r0 + nrows, c0:c0 + ncols].reshape(k, 128, ncols).transpose(1, 0, 2).reshape(128, k * ncols)
        shared[nm] = arr
    in_maps = []
    for c in range(8):
        b, half = c // 2, c % 2
        own = x_prompt[b, half * 1024:(half + 1) * 1024]
        prev = x_prompt[b, 0:1024] if half == 1 else np.zeros_like(own)
        xs = x_sample[c * NS:(c + 1) * NS, 0]
        xTc = np.ascontiguousarray(np.concatenate([prev, own, xs], axis=0).T)
        sl = slice(c * NS, (c + 1) * NS)
        sshT = np.zeros((27 * 128, NS), np.float32)
        sshT[:RPW] = f(state_shift[0, sl, 0]).T
        invc = np.zeros((4, 4, NTB + NS), np.float32)
        for t in range(4):
            pos = half * 1024 + t * NTB + np.arange(NTB)
            for g, w in enumerate((2, 4, 8, 16)):
                invc[t, g, :NTB] = 1.0 / np.minimum(pos + 1, w)
                invc[t, g, NTB:] = 1.0 / w
        m = dict(shared)
        m.update({
            "xT": xTc, "memT": np.ascontiguousarray(mem_prompt[b].T),
            "kc": f(cache_mem_k[0, sl]).reshape(NS, NMEM, 512), "vc": f(cache_mem_v[0, sl]).reshape(NS, NMEM, 512),
            "swkv": np.ascontiguousarray(f(state_wkv[0, sl]).transpose(2, 0, 1, 3)),
            "sshiftT": sshT,
            "spoolT": np.ascontiguousarray(f(state_pool[0, sl]).transpose(2, 0, 1)),
            "spool": f(state_pool[0, sl]),
            "invc": invc.reshape(4, -1),
        })
        in_maps.append(m)
    if _PACK_ONLY:
        return in_maps
    if "nc" not in _NC_CACHE:
        _NC_CACHE["nc"] = build_nc()
    res = run_bass_kernel_spmd(_NC_CACHE["nc"], in_maps, core_ids=list(range(8)))
    R = res.results
    return unpack(R, B)


def unpack(R, B=4):
    ND = NS * 8
    y_prompt = np.zeros((B, 2048, D), np.float32)
    y_sample = np.zeros((ND, 1, D), np.float32)
    mem_k = np.zeros((1, B, NMEM, 4, 128), np.float32)
    mem_v = np.zeros((1, B, NMEM, 4, 128), np.float32)
    wkv_p = np.zeros((1, B, 16, 64, 64), np.float32)
    sh_p = np.zeros((1, B, 1, RPW), np.float32)
    pl_p = np.zeros((1, B, 15, 512), np.float32)
    wkv_s = np.zeros((1, ND, 16, 64, 64), np.float32)
    sh_s = np.zeros((1, ND, 1, RPW), np.float32)
    pl_s = np.zeros((1, ND, 15, 512), np.float32)
    for c in range(8):
        b, half = c // 2, c % 2
        r = R[c]
        yT = np.asarray(r["yT"]).reshape(128, KT, 1024 + NS)
        yfull = yT.transpose(2, 1, 0).reshape(1024 + NS, D)
        y_prompt[b, half * 1024:(half + 1) * 1024] = yfull[:1024]
        y_sample[c * NS:(c + 1) * NS, 0] = yfull[1024:]
        sh = np.asarray(r["shiftT"]).reshape(128, 27, 1 + NS).transpose(2, 1, 0).reshape(1 + NS, 27 * 128)[:, :RPW]
        sh_s[0, c * NS:(c + 1) * NS, 0] = sh[1:]
        pl_s[0, c * NS:(c + 1) * NS, 0:14] = np.asarray(r["poolso"]).reshape(NS, 14, 512)
        pl_s[0, c * NS:(c + 1) * NS, 14] = np.asarray(r["poolsn"]).reshape(128, 4, NS).transpose(2, 1, 0).reshape(NS, 512)
        wkv_s[0, c * NS:(c + 1) * NS] = np.asarray(r["wkvs"]).reshape(64, NS, 16, 64).transpose(1, 2, 0, 3)
        if half == 0:
            mem_k[0, b] = np.asarray(r["memkT"]).reshape(128, 4, NMEM).transpose(2, 1, 0)
            mem_v[0, b] = np.asarray(r["memv"]).reshape(128, 2, 512).transpose(1, 0, 2).reshape(NMEM, 4, 128)
        else:
            sh_p[0, b, 0] = sh[0]
            pl_p[0, b] = np.asarray(r["poolpT"]).reshape(128, 4, 15).transpose(2, 1, 0).reshape(15, 512)
            wkv_p[0, b] = np.asarray(r["wkvp"]).reshape(2, 64, 8, 64).transpose(2, 0, 3, 1).reshape(16, 64, 64)
    return (y_prompt, y_sample, mem_k, mem_v, wkv_p, sh_p, pl_p, wkv_s, sh_s, pl_s)
```

```python
import numpy as np
from contextlib import ExitStack
import concourse.bass as bass
import concourse.mybir as mybir
from concourse.bass_utils import run_bass_kernel_spmd

F32 = mybir.dt.float32
BF16 = mybir.dt.bfloat16
AF = mybir.ActivationFunctionType
ALU = mybir.AluOpType
AX = mybir.AxisListType

D = 2048
F = 5504
KT = 16
FC = 43
NMEM = 256
RW = 1024
RPW = 3360
INW = 10528
C_POOL, C_R, C_XQ, C_G = 0, 512, 3872, 4384
NTB = 256
NS = 16
NTILE = 8
EDEC = float(np.exp(-0.5))
RMS_EPS = 1e-6
GN_EPS = 64e-5
EPOCH = 20000
NDSEM = 6


class Prog:
    COMPUTE = ("tensor", "vector", "scalar", "gpsimd")

    def __init__(self, nc, stack):
        self.nc = nc
        self.stack = stack
        self.eng = {"tensor": nc.tensor, "vector": nc.vector, "scalar": nc.scalar,
                    "gpsimd": nc.gpsimd, "sync": nc.sync}
        self.seq = {e: 0 for e in self.COMPUTE}
        self.esems = {e: [] for e in self.COMPUTE}
        self.dq = {}
        for q in ("sync", "scalar", "gpsimd"):
            self.dq[q] = {"n": 0, "sems": [self._sem(f"d_{q}_{i}") for i in range(NDSEM)]}
        self.seen = {e: {} for e in self.eng}
        self.state = {}
        self.subs = {}
        self.same_engine_sync = {"vector": True, "scalar": True, "gpsimd": True, "tensor": False}
        self.last_ev = {}

    def _sem(self, name):
        return self.stack.enter_context(self.nc.semaphore(name))

    def _esem(self, e, epoch):
        while len(self.esems[e]) <= epoch:
            self.esems[e].append(self._sem(f"e_{e}_{len(self.esems[e])}"))
        return self.esems[e][epoch]

    def _related(self, k):
        if isinstance(k, tuple):
            return [k, k[0]]
        out = [k]
        out.extend(self.subs.get(k, ()))
        return out

    def _deps(self, reads, writes):
        evs = []
        for k in reads:
            for kk in self._related(k):
                st = self.state.get(kk)
                if st and st[0] is not None:
                    evs.append(st[0])
        for k in writes:
            for kk in self._related(k):
                st = self.state.get(kk)
                if st:
                    if st[0] is not None:
                        evs.append(st[0])
                    evs.extend(st[1])
        return evs

    def _record(self, ev, reads, writes):
        for k in reads:
            if isinstance(k, tuple):
                self.subs.setdefault(k[0], set()).add(k)
            self.state.setdefault(k, [None, []])[1].append(ev)
        for k in writes:
            if isinstance(k, tuple):
                self.subs.setdefault(k[0], set()).add(k)
            self.state[k] = [ev, []]
            if not isinstance(k, tuple):
                for kk in self.subs.get(k, ()):
                    self.state[kk] = [ev, []]

    def _emit_waits(self, e, evs):
        need = {}
        for (semkey, sem, val, src) in evs:
            if src == e and not self.same_engine_sync.get(e, True):
                continue
            if self.seen[e].get(semkey, 0) >= val:
                continue
            if semkey not in need or need[semkey][1] < val:
                need[semkey] = (sem, val)
        for semkey, (sem, val) in need.items():
            self.eng[e].wait_ge(sem, val)
            self.seen[e][semkey] = val
            if semkey[0] == "E":
                for ep in range(semkey[2]):
                    self.seen[e][("E", semkey[1], ep)] = EPOCH

    def op(self, e, fn, reads=(), writes=()):
        evs = self._deps(reads, writes)
        self._emit_waits(e, evs)
        ins = fn(self.eng[e])
        s = self.seq[e]
        epoch, idx = divmod(s, EPOCH)
        sem = self._esem(e, epoch)
        ins.then_inc(sem, 1)
        self.seq[e] = s + 1
        ev = (("E", e, epoch), sem, idx + 1, e)
        self.last_ev[e] = ev
        self._record(ev, reads, writes)
        return ins

    def group(self, e, fns, reads=(), writes=()):
        evs = self._deps(reads, writes)
        self._emit_waits(e, evs)
        for fn in fns[:-1]:
            fn(self.eng[e])
        ins = fns[-1](self.eng[e])
        s = self.seq[e]
        epoch, idx = divmod(s, EPOCH)
        sem = self._esem(e, epoch)
        ins.then_inc(sem, 1)
        self.seq[e] = s + 1
        ev = (("E", e, epoch), sem, idx + 1, e)
        self.last_ev[e] = ev
        self._record(ev, reads, writes)
        return ins

    def dma(self, q, out, in_, reads=(), writes=(), **kw):
        d = self.dq[q]
        i = d["n"]
        slot, rnd = i % NDSEM, i // NDSEM
        sem = d["sems"][slot]
        evs = self._deps(reads, writes)
        if rnd > 0:
            evs.append((("D", q, slot), sem, 16 * rnd, None))
        self._emit_waits(q, evs)
        ins = self.eng[q].dma_start(out=out, in_=in_, **kw)
        ins.then_inc(sem, 16)
        d["n"] = i + 1
        ev = (("D", q, slot), sem, 16 * (rnd + 1), None)
        self._record(ev, reads, writes)
        return ins

    def _dma_events(self):
        evs = []
        for q, d in self.dq.items():
            n = d["n"]
            for slot in range(NDSEM):
                cnt = (n - slot + NDSEM - 1) // NDSEM if n > slot else 0
                if cnt > 0:
                    evs.append((("D", q, slot), d["sems"][slot], 16 * cnt, None))
        return evs

    def fence(self):
        evs = list(self.last_ev.values()) + self._dma_events()
        for e in self.eng:
            self._emit_waits(e, [ev for ev in evs if not (ev[3] == e and e == "tensor")])
        self.state = {}
        self.subs = {}

    def finish(self):
        for q in self.dq:
            self._emit_waits(q, [ev for ev in self._dma_events() if ev[0][1] == q])


IN_SHAPES = {}
BLOCKS = {}


def build_nc():
    nc = bass.Bass("TRN2", target_bir_lowering=False)
    din = {}
    dout = {}

    import os
    _small = ""
    _need = set()

    def I(name, shape, dt=F32):
        if _small and name not in _need:
            shape = [1, 8]
        IN_SHAPES[name] = list(shape)
        din[name] = nc.dram_tensor(name, list(shape), dt, kind="ExternalInput").ap()
        return din[name]

    def O(name, shape):
        dout[name] = nc.dram_tensor(name, list(shape), F32, kind="ExternalOutput").ap()
        return dout[name]

    NTOK = 2048 + NS
    xT = I("xT", [D, NTOK])
    memT = I("memT", [D, NMEM])
    kc = I("kc", [NS, NMEM, 512])
    vc = I("vc", [NS, NMEM, 512])
    swkv = I("swkv", [64, NS, 16, 64])
    sshiftT = I("sshiftT", [27 * 128, NS])
    spoolT = I("spoolT", [512, NS, 15])
    spool = I("spool", [NS, 15, 512])
    NBLK = {"f1g": 22, "f1u": 22, "f1d": 32, "f2g": 22, "f2u": 22, "f2d": 32, "win": 44,
            "pout": 8, "rout": 8, "xout": 8, "wo": 8}
    WSHAPE = {"f1g": [D, F], "f1u": [D, F], "f1d": [F, D], "f2g": [D, F], "f2u": [D, F], "f2d": [F, D],
              "win": [D, INW], "pout": [512, D], "rout": [RW, D], "xout": [512, D], "wo": [D, D]}
    f1g, f1u, f1d, f2g, f2u, f2d, win = "f1g", "f1u", "f1d", "f2g", "f2u", "f2d", "win"
    WB = {nm: I(nm, [NBLK[nm], 128, 4096]) for nm in NBLK}
    pgw = I("pgw", [4, 128, 128]); pout = "pout"
    wup = I("wup", [64, RW]); aup = I("aup", [64, RW]); gup = I("gup", [160, RW])
    rout = "rout"
    wmk = I("wmk", [D, 512]); wmv = I("wmv", [D, 512]); xout = "xout"
    wo = "wo"
    vecs = I("vecs", [128, 176])
    rows = I("rows", [1, 2048])
    cst = I("cst", [128, 3584])
    invc = I("invc", [4, 4 * (NTB + NS)])

    yT = O("yT", [128, KT, 1024 + NS])
    o_memkT = O("memkT", [128, 4, NMEM])
    o_memv = O("memv", [128, 2, 512])
    o_wkvp = O("wkvp", [128, 8 * 64])
    o_shiftT = O("shiftT", [128, 27, 1 + NS])
    o_poolpT = O("poolpT", [128, 4, 15])
    o_wkvs = O("wkvs", [64, NS, 16, 64])
    o_poolsn = O("poolsn", [128, 4, NS])
    o_poolso = O("poolso", [NS, 14, 512])

    scr_rows = nc.dram_tensor("scr_rows", [6, NS, 1024], F32, kind="Internal").ap()
    scr_y = nc.dram_tensor("scr_y", [NS, 1024], F32, kind="Internal").ap()
    scr_q = nc.dram_tensor("scr_q", [NS, 512], F32, kind="Internal").ap()

    with ExitStack() as st:
        P = Prog(nc, st)
        sb = lambda name, shape, dt=F32: st.enter_context(nc.sbuf_tensor(name, list(shape), dt))
        NTM = NTB + NS

        cst_sb = sb("cst_sb", [128, 3584])
        P.dma("sync", cst_sb[:], cst, writes=["cst"])
        ident = cst_sb[:, 0:128]
        triI = cst_sb[:, 128:256]
        triX = cst_sb[:, 256:384]
        triS = cst_sb[:, 384:512]
        bones = cst_sb[:, 512:640]
        bones64 = cst_sb[:, 640:768]
        mSU = cst_sb[:, 768:1280]
        mSL = cst_sb[:, 1280:1792]
        mIU = cst_sb[:, 1792:2304]
        I4 = cst_sb[:, 2304:2816]
        ones_row = cst_sb[0:1, 2816:3072]
        cbf = sb("cbf", [128, 256], BF16)
        P.dma("gpsimd", cbf[:], cst[:, 3072:3328], writes=["cbf"])
        ident_bf = cbf[:, 0:128]
        ones_bf = cbf[:, 128:256]
        vec = sb("vec", [128, 176])
        P.dma("sync", vec[:], vecs, writes=["vec"])
        G_F1, G_MIX, G_MEM, G_F2, G_FIN = 0, 16, 32, 48, 64
        V_MU, V_PS = 80, 107
        V_KK, V_KA, V_RK, V_LG, V_LB = 111, 119, 127, 135, 143
        rows_sb = sb("rows_sb", [1, 2048])
        P.dma("sync", rows_sb[:], rows, writes=["rows"])
        lora = sb("lora", [128, RW])
        P.dma("sync", lora[0:64, :], wup, writes=[("lora", 0)])
        P.dma("sync", lora[64:128, :], aup, writes=[("lora", 1)])
        gupb = sb("gupb", [128, 2, RW], BF16)
        P.dma("gpsimd", gupb[:, 0, :], gup[0:128, :], writes=[("gupb", 0)])
        P.dma("gpsimd", gupb[0:32, 1, :], gup[128:160, :], writes=[("gupb", 1)])
        pgwb = sb("pgwb", [128, 4, 128], BF16)
        P.dma("gpsimd", pgwb[:], pgw.rearrange("g c d -> c g d"), writes=["pgwb"])

        xh = sb("xh", [128, KT, NTM])
        xn = sb("xn", [128, KT, NTM], BF16)
        zr = sb("zr", [128, 27, 1 + NTM])
        zp = sb("zp", [128, 4, 15 + NTM])
        zq = sb("zq", [128, 4, NTM], BF16)
        mix = sb("mix", [128, 4, NTM], BF16)
        yg = sb("yg", [128, 8, NTM], BF16)
        oT = sb("oT", [128, 4, NTM], BF16)
        S32 = sb("S32", [128, 8, 64])
        Sbf = sb("Sbf", [128, 8, 64], BF16)
        KTb = sb("KTb", [128, 4, NMEM], BF16)
        Vb = sb("Vb", [128, 2, 512], BF16)
        rstd = sb("rstd", [128, NTM])
        tmpA = sb("tmpA", [128, NTM])
        tmpB = sb("tmpB", [128, NTM], BF16)
        NWB = 4
        wbuf = [sb(f"wbuf{i}", [128, 4096], BF16) for i in range(NWB)]
        ARENA = 16384
        arena = sb("arena", [128, ARENA])
        psb = [st.enter_context(nc.psum_tensor(f"pb{i}", [128, 512], F32)) for i in range(8)]
        wctr = [0]
        pctr = [0]

        def ps():
            i = pctr[0] % 8
            pctr[0] += 1
            return psb[i], f"pb{i}"

        wcache = {}

        def wload(W, r0, nrows, c0, ncols, kparts=128):
            i = wctr[0] % NWB
            wctr[0] += 1
            k = nrows // kparts
            view = wbuf[i][0:kparts, 0:k * ncols].rearrange("p (k n) -> p k n", n=ncols)
            flat = wbuf[i][0:kparts, 0:k * ncols]
            if not isinstance(W, str):
                src = W[r0:r0 + nrows, c0:c0 + ncols].rearrange("(k p) n -> p k n", p=kparts)
                P.dma("gpsimd", view, src, writes=[f"wbuf{i}"])
                return view, f"wbuf{i}"
            name = W
            blk_id = (r0, nrows, c0, ncols)
            if name not in wcache:
                wcache[name] = (nc.dram_tensor("bfc_" + name, [NBLK[name], 128, 4096], BF16, kind="Internal").ap(), {})
                BLOCKS[name] = []
            if blk_id in wcache[name][1]:
                bi = wcache[name][1][blk_id]
                P.dma("sync", flat, wcache[name][0][bi, 0:kparts, 0:k * ncols], writes=[f"wbuf{i}"])
            else:
                bi = len(wcache[name][1])
                assert bi < NBLK[name], name
                BLOCKS[name].append(blk_id)
                P.dma("gpsimd", flat, WB[name][bi, 0:kparts, 0:k * ncols], writes=[f"wbuf{i}"])
                P.dma("sync", wcache[name][0][bi, 0:kparts, 0:k * ncols], flat, reads=[f"wbuf{i}"])
                wcache[name][1][blk_id] = bi
            return view, f"wbuf{i}"

        CACHED = {"f1g", "f1u", "f1d", "f2g", "f2u", "f2d", "win", "pout", "rout", "xout", "wo"}

        P.op("vector", lambda e: e.memset(S32[:], 0.0), writes=["S32"])
        P.op("vector", lambda e: e.memset(Sbf[:], 0.0), writes=["Sbf"])
        P.op("vector", lambda e: e.memset(zr[:, :, 0:1], 0.0), writes=["zr"])
        P.op("vector", lambda e: e.memset(zp[:, :, 0:15], 0.0), writes=["zp"])

        def rmsnorm(src, gcol, nt, dst, srckey, dstkey, kt=KT):
            sq = arena[:, 14208:14208 + (kt * nt + 1) // 2].bitcast(BF16)[:, 0:kt * nt].rearrange("p (k n) -> p k n", n=nt)
            pt, pk = ps()
            P.op("scalar", lambda e: e.activation(sq, src[:, 0:kt, 0:nt], AF.Square), reads=[srckey], writes=["sq"])
            P.group("tensor", [lambda e, k=k: e.matmul(pt[:, 0:nt], lhsT=ones_bf, rhs=sq[:, k, :],
                                                        start=(k == 0), stop=(k == kt - 1)) for k in range(kt)],
                    reads=["sq", "cbf"], writes=[pk])
            P.op("scalar", lambda e: e.activation(tmpA[:, 0:nt], pt[:, 0:nt], AF.Sqrt, bias=RMS_EPS, scale=1.0 / D),
                 reads=[pk], writes=["tmpA"])
            P.op("vector", lambda e: e.reciprocal(rstd[:, 0:nt], tmpA[:, 0:nt]), reads=["tmpA"], writes=["rstd"])
            for k in range(kt):
                P.op("vector", lambda e, k=k: e.scalar_tensor_tensor(
                    out=dst[:, k, 0:nt], in0=src[:, k, 0:nt], scalar=vec[:, gcol + k:gcol + k + 1],
                    in1=rstd[:, 0:nt], op0=ALU.mult, op1=ALU.mult),
                     reads=[srckey, "rstd", "vec"], writes=[(dstkey, k)])

        def ffn(Wg, Wu, Wd, gcol, nt, keep_res):
            rmsnorm(xh, gcol, nt, xn, "xh", "xn")
            act = arena[:, 0:FC * NTM // 2].bitcast(BF16).rearrange("p (c n) -> p c n", n=NTM)
            for half in range(2):
                c_lo, c_hi = (0, 22) if half == 0 else (22, FC)
                for cb in range(c_lo, c_hi, 2):
                    ncb = min(2, c_hi - cb)
                    wg, wgk = wload(Wg, 0, D, cb * 128, ncb * 128)
                    wu, wuk = wload(Wu, 0, D, cb * 128, ncb * 128)
                    for ci in range(ncb):
                        c = cb + ci
                        pg, pgk = ps()
                        pu, puk = ps()
                        P.group("tensor", [lambda e, k=k, ci=ci: e.matmul(
                            pg[:, 0:nt], lhsT=wg[:, k, ci * 128:(ci + 1) * 128], rhs=xn[:, k, 0:nt],
                            start=(k == 0), stop=(k == KT - 1)) for k in range(KT)], reads=[wgk, "xn"], writes=[pgk])
                        P.group("tensor", [lambda e, k=k, ci=ci: e.matmul(
                            pu[:, 0:nt], lhsT=wu[:, k, ci * 128:(ci + 1) * 128], rhs=xn[:, k, 0:nt],
                            start=(k == 0), stop=(k == KT - 1)) for k in range(KT)], reads=[wuk, "xn"], writes=[puk])
                        P.op("scalar", lambda e: e.activation(tmpB[:, 0:nt], pg[:, 0:nt], AF.Silu),
                             reads=[pgk], writes=["tmpB"])
                        P.op("vector", lambda e, c=c: e.tensor_tensor(act[:, c - c_lo, 0:nt], tmpB[:, 0:nt],
                                                                       pu[:, 0:nt], op=ALU.mult),
                             reads=["tmpB", puk], writes=[("act", c)])
                nch = c_hi - c_lo
                for dcol in range(KT):
                    wd, wdk = wload(Wd, c_lo * 128, nch * 128, dcol * 128, 128)
                    pd, pdk = ps()
                    P.group("tensor", [lambda e, c=c: e.matmul(
                        pd[:, 0:nt], lhsT=wd[:, c, :], rhs=act[:, c, 0:nt],
                        start=(c == 0), stop=(c == nch - 1)) for c in range(nch)],
                            reads=[wdk, "act"], writes=[pdk])
                    P.op("vector", lambda e, dcol=dcol: e.scalar_tensor_tensor(
                        out=xh[:, dcol, 0:nt], in0=pd[:, 0:nt], scalar=0.5, in1=xh[:, dcol, 0:nt],
                        op0=ALU.mult, op1=ALU.add), reads=[pdk, ("xh", dcol)], writes=[("xh", dcol)])
                if half == 0:
                    pass
            P.fence()

        def proj(W, c0, ncols_total, rhs, rhskey, nt, kt, sink, wrows=None):
            wrows = wrows or kt * 128
            nchunks = (ncols_total + 127) // 128
            j = 0
            while j < nchunks:
                nb = min(2, nchunks - j)
                ncols = min(ncols_total - j * 128, nb * 128)
                wv, wk = wload(W, 0, wrows, c0 + j * 128, ncols)
                for ci in range(nb):
                    m = min(128, ncols - ci * 128)
                    pt, pk = ps()
                    P.group("tensor", [lambda e, k=k, ci=ci, m=m, pt=pt: e.matmul(
                        pt[0:m, 0:nt], lhsT=wv[:, k, ci * 128:ci * 128 + m], rhs=rhs[:, k, 0:nt],
                        start=(k == 0), stop=(k == kt - 1)) for k in range(kt)], reads=[wk, rhskey], writes=[pk])
                    sink(j + ci, pt, pk, m)
                j += nb

        def mem_kv():
            mx = arena[:, 0:KT * NMEM].rearrange("p (k n) -> p k n", n=NMEM)
            P.dma("sync", mx, memT.rearrange("(k p) n -> p k n", p=128), writes=["mx"])
            mxn = arena[:, 4096:4096 + KT * NMEM // 2].bitcast(BF16).rearrange("p (k n) -> p k n", n=NMEM)
            rmsnorm(mx, G_MEM, NMEM, mxn, "mx", "mxn")
            if _small == "norm":
                P.fence()
                return
            kf = arena[:, 8192:8192 + 4 * NMEM].rearrange("p (k n) -> p k n", n=NMEM)

            def sink_k(j, pt, pk, m):
                P.op("vector", lambda e: e.tensor_copy(kf[:, j, :], pt[:, 0:NMEM]), reads=[pk], writes=[("kf", j)])
                if True:
                    P.op("vector", lambda e: e.tensor_copy(KTb[:, j, :], pt[:, 0:NMEM]), reads=[pk], writes=[("KTb", j)])
                else:
                    P.op("scalar", lambda e: e.activation(KTb[:, j, :], pt[:, 0:NMEM], AF.Copy), reads=[pk],
                         writes=[("KTb", j)])
            proj(wmk, 0, 512, mxn, "mxn", NMEM, KT, sink_k)
            P.dma("sync", o_memkT, kf, reads=["kf"])
            if _small == "k":
                P.fence()
                return
            vf = arena[:, 12288:12288 + 1024].rearrange("p (k n) -> p k n", n=512)
            pts = [ps(), ps()]
            for cb in range(2):
                wv, wk = wload(wmv, 0, D, cb * 256, 256)
                for mt in range(2):
                    pt, pk = pts[mt]
                    for k in range(KT):
                        P.op("tensor", lambda e, k=k, mt=mt, pt=pt, cb=cb: e.matmul(
                            pt[:, cb * 256:(cb + 1) * 256], lhsT=mxn[:, k, mt * 128:(mt + 1) * 128], rhs=wv[:, k, :],
                            start=(k == 0), stop=(k == KT - 1)), reads=[wk, "mxn"], writes=[(pk, cb)])
            for mt in range(2):
                pt, pk = pts[mt]
                P.op("vector", lambda e, mt=mt, pt=pt: e.tensor_copy(vf[:, mt, :], pt[:, :]), reads=[pk],
                     writes=[("vf", mt)])
                P.op("vector", lambda e, mt=mt, pt=pt: e.tensor_copy(Vb[:, mt, :], pt[:, :]), reads=[pk],
                     writes=[("Vb", mt)])
            P.dma("sync", o_memv, vf, reads=["vf"])
            P.fence()


        A = lambda off, n: arena[:, off:off + n]
        AB = lambda off, n: arena[:, off:off + (n + 1) // 2].bitcast(BF16)[:, 0:n]
        v3 = lambda ap, n: ap.rearrange("p (k n) -> p k n", n=n)
        SCALE = float(128 ** -0.5)
        zc = sb("zc", [128, 27, 1])
        PCb = sb("PCb", [128, 8])
        ypb = [(psb[6], "pb6"), (psb[7], "pb7")]

        def evac(i, out, in_, reads, writes):
            if i % 2 == 0:
                P.op("vector", lambda e: e.tensor_copy(out, in_), reads=reads, writes=writes)
            else:
                P.op("scalar", lambda e: e.activation(out, in_, AF.Identity), reads=reads, writes=writes)

        def token_shift(last):
            D3 = v3(A(0, 27 * NTB), NTB)
            P.op("vector", lambda e: e.tensor_copy(zc[:], zr[:, :, NTB:NTB + 1]), reads=["zr"], writes=["zc"])
            mub = vec[:, V_MU:V_MU + 27].unsqueeze(2).to_broadcast([128, 27, NTB])
            P.op("vector", lambda e: e.tensor_tensor(D3, zr[:, :, 0:NTB], zr[:, :, 1:1 + NTB], op=ALU.subtract),
                 reads=["zr"], writes=["D3"])
            P.op("vector", lambda e: e.tensor_tensor(D3, D3, mub, op=ALU.mult), reads=["D3", "vec"], writes=["D3"])
            P.op("vector", lambda e: e.tensor_tensor(zr[:, :, 1:1 + NTB], zr[:, :, 1:1 + NTB], D3, op=ALU.add),
                 reads=["D3", "zr", "zc"], writes=["zr"])
            if last:
                sshift = v3(A(7424, 27 * NS), NS)
                P.dma("sync", sshift, sshiftT.rearrange("(k p) n -> p k n", p=128), writes=["sshift"])
                Ds = v3(A(27 * NTB, 27 * NS), NS)
                mus = vec[:, V_MU:V_MU + 27].unsqueeze(2).to_broadcast([128, 27, NS])
                zs = zr[:, :, 1 + NTB:1 + NTM]
                P.op("vector", lambda e: e.tensor_tensor(Ds, sshift, zs, op=ALU.subtract), reads=["zr", "sshift"], writes=["Ds"])
                P.op("vector", lambda e: e.tensor_tensor(Ds, Ds, mus, op=ALU.mult), reads=["Ds", "vec"], writes=["Ds"])
                P.op("vector", lambda e: e.tensor_tensor(zs, zs, Ds, op=ALU.add), reads=["Ds", "zr"], writes=["zr"])
            P.fence()

        def rwkv_elem(cs, n, sample):
            W8 = 8 * n
            xk = zr[:, 8:16, cs]
            sg = A(0, 1024)
            asig, kkn, kmod, bbv = v3(A(1024, W8), n), v3(A(2048, W8), n), v3(A(3072, W8), n), v3(A(4096, W8), n)
            et, t1 = v3(A(5120, W8), n), v3(A(6144, W8), n)
            th = A(8192, 128)
            vcol = lambda c: vec[:, c:c + 8].unsqueeze(2).to_broadcast([128, 8, n])
            P.op("scalar", lambda e: e.activation(th[0:64, 0:n], zr[0:64, 24, cs], AF.Tanh), reads=["zr"], writes=["th"])

            def fm_lora(prow, roff, rhs_ap, sink):
                for hb in range(2):
                    pt, pk = ps()
                    for c in range(4):
                        ch = hb * 4 + c
                        P.op("tensor", lambda e, pt=pt, c=c, ch=ch: e.matmul(
                            pt[:, c * n:(c + 1) * n], lhsT=lora[prow, ch * 128:(ch + 1) * 128], rhs=rhs_ap,
                            start=True, stop=False), reads=["lora", "zr", "th"], writes=[pk])
                        P.op("tensor", lambda e, pt=pt, c=c, ch=ch: e.matmul(
                            pt[:, c * n:(c + 1) * n], lhsT=rows_sb[:, roff + ch * 128:roff + (ch + 1) * 128],
                            rhs=ones_row[:, 0:n], start=False, stop=True), reads=["rows", "cst"], writes=[pk])
                    sink(hb, v3(pt[:, 0:4 * n], n), pk)

            if sample:
                sgf = v3(A(0, W8), n)

                def sink_w(hb, p3, pk):
                    P.op("scalar", lambda e: e.activation(sgf[:, hb * 4:(hb + 1) * 4, :], p3, AF.Sigmoid), reads=[pk], writes=["sg"])
                fm_lora(slice(0, 64), 0, th[0:64, 0:n], sink_w)
                P.op("scalar", lambda e: e.activation(sgf, sgf, AF.Exp, scale=-EDEC), reads=["sg"], writes=["sg"])
            else:
                for hb in range(2):
                    pt, pk = ps()
                    P.op("tensor", lambda e, pt=pt, hb=hb: e.matmul(pt[:, :], lhsT=th[0:64, 0:128], rhs=lora[0:64, hb * 512:(hb + 1) * 512],
                                                       start=True, stop=False), reads=["th", "lora"], writes=[pk])
                    P.op("tensor", lambda e, pt=pt, hb=hb: e.matmul(pt[:, :], lhsT=ones_row[:, 0:128], rhs=rows_sb[:, hb * 512:(hb + 1) * 512],
                                                       start=False, stop=True), reads=["cst", "rows"], writes=[pk])
                    P.op("scalar", lambda e, pt=pt, hb=hb: e.activation(sg[:, hb * 512:(hb + 1) * 512], pt[:, :], AF.Sigmoid),
                         reads=[pk], writes=["sg"])

            def sink_a(hb, p3, pk):
                P.op("scalar", lambda e: e.activation(asig[:, hb * 4:(hb + 1) * 4, :], p3, AF.Sigmoid), reads=[pk], writes=["asig"])
            fm_lora(slice(64, 128), 1024, zr[64:128, 24, cs], sink_a)
            P.op("vector", lambda e: e.tensor_tensor(t1, xk, vcol(V_KK), op=ALU.mult), reads=["zr", "vec"], writes=["t1"])
            P.op("scalar", lambda e: e.activation(et, t1, AF.Square), reads=["t1"], writes=["et"])
            for hb in range(2):
                pt, pk = ps()
                for c in range(4):
                    ch = hb * 4 + c
                    P.op("tensor", lambda e, pt=pt, c=c, ch=ch: e.matmul(pt[:, c * n:(c + 1) * n], lhsT=bones, rhs=et[:, ch, :],
                                                       start=True, stop=True), reads=["cst", "et"], writes=[pk])
                P.op("vector", lambda e, pt=pt, hb=hb: e.tensor_scalar_max(kkn[:, hb * 4:(hb + 1) * 4, :], v3(pt[:, 0:4 * n], n), 1e-24),
                     reads=[pk], writes=["kkn"])
            P.op("scalar", lambda e: e.activation(kkn, kkn, AF.Sqrt), reads=["kkn"], writes=["kkn"])
            P.op("vector", lambda e: e.reciprocal(kkn, kkn), reads=["kkn"], writes=["kkn"])
            P.op("vector", lambda e: e.tensor_tensor(kkn, kkn, t1, op=ALU.mult), reads=["kkn", "t1"], writes=["kkn"])
            P.op("vector", lambda e: e.scalar_tensor_tensor(out=et, in0=asig, scalar=-1.0, in1=vcol(V_KA), op0=ALU.add, op1=ALU.mult),
                 reads=["asig", "vec", "et"], writes=["et"])
            P.op("vector", lambda e: e.scalar_tensor_tensor(out=kmod, in0=et, scalar=1.0, in1=xk, op0=ALU.add, op1=ALU.mult),
                 reads=["et", "zr"], writes=["kmod"])
            P.op("vector", lambda e: e.tensor_tensor(bbv, kkn, asig, op=ALU.mult), reads=["kkn", "asig"], writes=["bbv"])

        def scan_prompt(cs, want_y):
            n = 128
            xr, xv = zr[:, 0:8, cs], zr[:, 16:24, cs]
            sg = A(0, 1024)
            kkn, kmod, bbv = v3(A(2048, 1024), n), v3(A(3072, 1024), n), v3(A(4096, 1024), n)
            et = v3(A(5120, 1024), n)
            b3 = lambda off: v3(AB(off, 1024), n)
            Afm, Bfm, Kfm, Rfm = b3(8320), b3(8832), b3(9344), b3(9856)
            AT, BhT, KhT, VT = AB(10368, 1024), AB(10880, 1024), AB(11392, 1024), AB(11904, 1024)
            tb = b3(12416)
            hs = lambda x, hb: x[:, hb * 4:(hb + 1) * 4, :]

            def cum(tri, sink):
                for hb in range(2):
                    pt, pk = ps()
                    for c in range(4):
                        ch = hb * 4 + c
                        P.op("tensor", lambda e, pt=pt, c=c, ch=ch: e.matmul(pt[:, c * 128:(c + 1) * 128], lhsT=sg[:, ch * 128:(ch + 1) * 128], rhs=tri,
                                                           start=True, stop=True), reads=["sg", "cst"], writes=[pk])
                    sink(hb, v3(pt[:, :], n), pk)

            def sink_incl(hb, p3, pk):
                if want_y:
                    P.op("scalar", lambda e: e.activation(hs(et, hb), p3, AF.Exp), reads=[pk], writes=["et"])
                    P.op("vector", lambda e: e.tensor_tensor(hs(Rfm, hb), hs(xr, hb), hs(et, hb), op=ALU.mult),
                         reads=["et", "zr"], writes=["Rfm"])
                P.op("scalar", lambda e: e.activation(PCb[:, hb * 4:(hb + 1) * 4], p3[:, :, 127], AF.Exp), reads=[pk], writes=["PCb"])
                P.op("scalar", lambda e: e.activation(hs(et, hb), p3, AF.Exp, scale=-1.0), reads=[pk, "Rfm"], writes=["et"])
                P.op("vector", lambda e: e.tensor_tensor(hs(Bfm, hb), hs(bbv, hb), hs(et, hb), op=ALU.mult),
                     reads=["et", "bbv"], writes=["Bfm"])
                P.op("vector", lambda e: e.tensor_tensor(hs(Kfm, hb), hs(kmod, hb), hs(et, hb), op=ALU.mult),
                     reads=["et", "kmod"], writes=["Kfm"])
            cum(triI, sink_incl)

            def sink_excl(hb, p3, pk):
                P.op("scalar", lambda e: e.activation(hs(et, hb), p3, AF.Exp), reads=[pk, "Bfm", "Kfm"], writes=["et"])
                P.op("vector", lambda e: e.scalar_tensor_tensor(out=hs(Afm, hb), in0=hs(kkn, hb), scalar=-1.0, in1=hs(et, hb),
                                                                 op0=ALU.mult, op1=ALU.mult), reads=["et", "kkn"], writes=["Afm"])
            cum(triX, sink_excl)

            def tr_to(src3, srckey, dst, dstkey):
                for hb in range(2):
                    pt, pk = ps()
                    for c in range(4):
                        ch = hb * 4 + c
                        P.op("tensor", lambda e, pt=pt, c=c, ch=ch: e.matmul(pt[:, c * 128:(c + 1) * 128], lhsT=src3[:, ch, :], rhs=ident_bf,
                                                           start=True, stop=True), reads=[srckey, "cbf"], writes=[pk])
                    evac(hb, dst[:, hb * 512:(hb + 1) * 512], pt[:, :], [pk], [dstkey])
            tr_to(Afm, "Afm", AT, "AT")

            def sink_suf(hb, p3, pk):
                P.op("scalar", lambda e: e.activation(hs(et, hb), p3, AF.Exp), reads=[pk, "Afm"], writes=["et"])
                P.op("vector", lambda e: e.tensor_tensor(hs(tb, hb), hs(bbv, hb), hs(et, hb), op=ALU.mult),
                     reads=["et", "bbv"], writes=["tb"])
            cum(triS, sink_suf)
            tr_to(tb, "tb", BhT, "BhT")
            P.op("vector", lambda e: e.tensor_tensor(tb, kmod, et, op=ALU.mult), reads=["et", "kmod", "BhT", "tb"], writes=["tb"])
            tr_to(tb, "tb", KhT, "KhT")
            P.op("vector", lambda e: e.tensor_copy(tb, xv), reads=["zr", "KhT", "tb"], writes=["tb"])
            tr_to(tb, "tb", VT, "VT")

            GOFF = [13056, 13568, 14080, 14592, 15104, 15616, 0, 512, 1024, 1536, 5120]
            gm = lambda i: AB(GOFF[i], 1024)
            UT = AB(16128, 512)
            pos = lambda i: (i % 2) * 4 + i // 2
            blk = lambda i: slice(pos(i) * 128, (pos(i) + 1) * 128)
            yfm = v3(A(7168, 1024), n)
            P.fence()

            def mm8(dst, dk, lt, ltk, rt, rtk, add=None, addk=None):
                bank = [ps(), ps()]
                for i in range(8):
                    pt, pk = bank[i // 4]
                    P.op("tensor", lambda e, pt=pt, i=i: e.matmul(pt[:, (i % 4) * 128:(i % 4 + 1) * 128], lhsT=lt[:, i * 128:(i + 1) * 128],
                                                      rhs=rt[:, i * 128:(i + 1) * 128], start=True, stop=True),
                         reads=[ltk, rtk], writes=[pk])
                for hb in range(2):
                    pt, pk = bank[hb]
                    dk_ = (dk, hb)
                    if add is None:
                        evac(1, dst[:, hb * 512:(hb + 1) * 512], pt[:, :], [pk], [dk_])
                    else:
                        P.op("vector", lambda e, pt=pt, hb=hb: e.tensor_tensor(dst[:, hb * 512:(hb + 1) * 512], pt[:, :],
                                                                  add[:, hb * 512:(hb + 1) * 512], op=ALU.add),
                             reads=[pk, addk], writes=[dk_])

            for G in range(2):
                heads = [8 * G + i for i in range(8)]
                hrow = lambda x3, h: x3[(h % 2) * 64:(h % 2) * 64 + 64, h // 2, :]
                NakT, Mrb, Mrk, Apf, Nakp = gm(6), gm(7), gm(8), gm(9), gm(10)
                specs = [(gm(0), "g0", Bfm, "Bfm", Afm, "Afm", mSU), (gm(1), "g1", Afm, "Afm", Bfm, "Bfm", mSL),
                         (NakT, "g6", Afm, "Afm", Kfm, "Kfm", mSL)]
                if want_y:
                    specs += [(Mrb, "g7", Bfm, "Bfm", Rfm, "Rfm", mIU), (Mrk, "g8", Kfm, "Kfm", Rfm, "Rfm", mIU)]
                for (dst, dk, la, lak, ra, rak, msk) in specs:
                    bank = [ps(), ps()]
                    for i, h in enumerate(heads):
                        pt, pk = bank[h % 2]
                        P.op("tensor", lambda e, pt=pt, i=i, h=h, la=la, ra=ra: e.matmul(
                            pt[:, (i // 2) * 128:(i // 2 + 1) * 128], lhsT=hrow(la, h), rhs=hrow(ra, h), start=True, stop=True),
                             reads=[lak, rak], writes=[pk])
                    for par in range(2):
                        pt, pk = bank[par]
                        P.op("vector", lambda e, pt=pt, dst=dst, msk=msk, par=par: e.tensor_tensor(
                            dst[:, par * 512:(par + 1) * 512], pt[:, :], msk, op=ALU.mult),
                             reads=[pk, "cst"], writes=[(dk, par)])
                Nc, Nck, Lc, Lck = gm(0), "g0", gm(1), "g1"
                Nn, Nnk, Ln, Lnk = gm(2), "g2", gm(3), "g3"
                Tc, Tck, Tn, Tnk = gm(4), "g4", gm(5), "g5"
                for hb in range(2):
                    P.op("vector", lambda e, hb=hb: e.tensor_tensor(Tc[:, hb * 512:(hb + 1) * 512], Nc[:, hb * 512:(hb + 1) * 512], I4, op=ALU.add),
                         reads=[Nck, "cst"], writes=[(Tck, hb)])
                for k in range(1, 7):
                    if k < 6:
                        mm8(Nn, Nnk, Lc, Lck, Nc, Nck)
                    mm8(Ln, Lnk, Nc, Nck, Lc, Lck)
                    mm8(Tn, Tnk, Ln, Lnk, Tc, Tck, add=Tc, addk=Tck)
                    Nc, Nck, Nn, Nnk = Nn, Nnk, Nc, Nck
                    Lc, Lck, Ln, Lnk = Ln, Lnk, Lc, Lck
                    Tc, Tck, Tn, Tnk = Tn, Tnk, Tc, Tck
                T_, Tk = Tc, Tck
                bank = [ps(), ps()]
                for i, h in enumerate(heads):
                    pt, pk = bank[h % 2]
                    P.op("tensor", lambda e, pt=pt, i=i, h=h: e.matmul(
                        pt[(h % 2) * 64:(h % 2) * 64 + 64, (i // 2) * 128:(i // 2 + 1) * 128],
                        lhsT=AT[:, h * 64:(h + 1) * 64], rhs=T_[:, blk(i)], start=True, stop=True),
                         reads=["AT", Tk], writes=[pk])
                for par in range(2):
                    pt, pk = bank[par]
                    rs = slice(par * 64, par * 64 + 64)
                    evac(par, Apf[rs, 0:512], pt[rs, 0:512], [pk], [("g9", par)])
                mm8(Nakp, "g10", NakT, "g6", T_, Tk)
                bank = [ps(), ps()]
                for i, h in enumerate(heads):
                    hp, h2 = h // 2, h % 2
                    rs = slice(h2 * 64, h2 * 64 + 64)
                    pt, pk = bank[h2]
                    P.op("tensor", lambda e, pt=pt, i=i, hp=hp, rs=rs: e.matmul(
                        pt[:, (i // 2) * 64:(i // 2 + 1) * 64], lhsT=Apf[rs, (i // 2) * 128:(i // 2 + 1) * 128],
                        rhs=Sbf[rs, hp, :], start=True, stop=False), reads=["g9", "Sbf"], writes=[pk])
                    P.op("tensor", lambda e, pt=pt, i=i, h=h: e.matmul(
                        pt[:, (i // 2) * 64:(i // 2 + 1) * 64], lhsT=Nakp[:, blk(i)],
                        rhs=VT[:, h * 64:(h + 1) * 64], start=False, stop=True), reads=["g10", "VT"], writes=[pk])
                for par in range(2):
                    pt, pk = bank[par]
                    evac(par, UT[:, par * 256:(par + 1) * 256], pt[:, 0:256], [pk], [("UT", par)])
                UTb = lambda i: UT[:, pos(i) * 64:(pos(i) + 1) * 64]
                if want_y:
                    bank = [ps(), ps()]
                    for i, h in enumerate(heads):
                        hp, h2 = h // 2, h % 2
                        rs = slice(h2 * 64, h2 * 64 + 64)
                        pt, pk = bank[h2]
                        yo = pt[rs, (i // 2) * 128:(i // 2 + 1) * 128]
                        P.op("tensor", lambda e, yo=yo, rs=rs, hp=hp: e.matmul(yo, lhsT=Sbf[rs, hp, :], rhs=Rfm[rs, hp, :], start=True, stop=False),
                             reads=["Sbf", "Rfm"], writes=[pk])
                        P.op("tensor", lambda e, yo=yo, i=i: e.matmul(yo, lhsT=UTb(i), rhs=Mrb[:, blk(i)], start=False, stop=False),
                             reads=["UT", "g7"], writes=[pk])
                        P.op("tensor", lambda e, yo=yo, i=i, h=h: e.matmul(yo, lhsT=VT[:, h * 64:(h + 1) * 64], rhs=Mrk[:, blk(i)], start=False, stop=True),
                             reads=["VT", "g8"], writes=[pk])
                    for par in range(2):
                        pt, pk = bank[par]
                        rs = slice(par * 64, par * 64 + 64)
                        evac(par, yfm[rs, 4 * G:4 * G + 4, :], v3(pt[rs, 0:512], 128), [pk], [("yfm", G * 2 + par)])
                bank = [ps(), ps()]
                for i, h in enumerate(heads):
                    h2 = h % 2
                    rs = slice(h2 * 64, h2 * 64 + 64)
                    pt, pk = bank[h2]
                    so = pt[rs, (i // 2) * 64:(i // 2 + 1) * 64]
                    P.op("tensor", lambda e, so=so, i=i, h=h: e.matmul(so, lhsT=BhT[:, h * 64:(h + 1) * 64], rhs=UTb(i), start=True, stop=False),
                         reads=["BhT", "UT"], writes=[pk])
                    P.op("tensor", lambda e, so=so, h=h: e.matmul(so, lhsT=KhT[:, h * 64:(h + 1) * 64], rhs=VT[:, h * 64:(h + 1) * 64], start=False, stop=True),
                         reads=["KhT", "VT"], writes=[pk])
                hps = slice(4 * G, 4 * G + 4)
                P.op("vector", lambda e, hps=hps: e.tensor_tensor(S32[:, hps, :], S32[:, hps, :],
                                                                   PCb[:, hps].unsqueeze(2).to_broadcast([128, 4, 64]), op=ALU.mult),
                     reads=["PCb", "S32", "Sbf"], writes=["S32"])
                for par in range(2):
                    pt, pk = bank[par]
                    rs = slice(par * 64, par * 64 + 64)
                    P.op("vector", lambda e, hps=hps, pt=pt, rs=rs: e.tensor_tensor(S32[rs, hps, :], S32[rs, hps, :], v3(pt[rs, 0:256], 64), op=ALU.add),
                         reads=[pk, "S32"], writes=["S32"])
                P.op("vector", lambda e, hps=hps: e.tensor_copy(Sbf[:, hps, :], S32[:, hps, :]), reads=["S32", "Sbf"], writes=["Sbf"])
            P.fence()

        def rwkv_post(cs, n, ycol0, from_psum):
            W8 = 8 * n
            xr, xv = zr[:, 0:8, cs], zr[:, 16:24, cs]
            rstdv, kmod, bbv = v3(A(1024, W8), n), v3(A(3072, W8), n), v3(A(4096, W8), n)
            et, t1, yfm = v3(A(5120, W8), n), v3(A(6144, W8), n), v3(A(7168, W8), n)
            sgl = v3(AB(12928, 256), 128)
            vcol = lambda c: vec[:, c:c + 8].unsqueeze(2).to_broadcast([128, 8, n])
            hs = lambda x, hb: x[:, hb * 4:(hb + 1) * 4, :]
            if from_psum:
                for hb in range(2):
                    evac(hb, hs(yfm, hb), v3(ypb[hb][0][:, :], n), [ypb[hb][1]], ["yfm"])

            def bmm(lhs, src, srck, sink):
                for hb in range(2):
                    pt, pk = ps()
                    for c in range(4):
                        ch = hb * 4 + c
                        P.op("tensor", lambda e, pt=pt, c=c, ch=ch: e.matmul(pt[:, c * n:(c + 1) * n], lhsT=lhs, rhs=src[:, ch, :],
                                                           start=True, stop=True), reads=["cst", srck], writes=[pk])
                    sink(hb, v3(pt[:, 0:4 * n], n), pk)
            bmm(bones64, yfm, "yfm", lambda hb, p3, pk: P.op(
                "vector", lambda e: e.tensor_tensor(hs(t1, hb), hs(yfm, hb), p3, op=ALU.subtract), reads=["yfm", pk], writes=["t1"]))
            P.op("scalar", lambda e: e.activation(et, t1, AF.Square), reads=["t1"], writes=["et"])
            bmm(bones64, et, "et", lambda hb, p3, pk: P.op(
                "scalar", lambda e: e.activation(hs(rstdv, hb), p3, AF.Sqrt, bias=GN_EPS, scale=1.0), reads=[pk], writes=["rstdv"]))
            P.op("vector", lambda e: e.reciprocal(rstdv, rstdv), reads=["rstdv"], writes=["rstdv"])
            P.op("vector", lambda e: e.tensor_tensor(t1, t1, rstdv, op=ALU.mult), reads=["t1", "rstdv"], writes=["t1"])
            P.op("vector", lambda e: e.tensor_tensor(t1, t1, vcol(V_LG), op=ALU.mult), reads=["t1", "vec"], writes=["t1"])
            P.op("vector", lambda e: e.tensor_tensor(t1, t1, vcol(V_LB), op=ALU.add), reads=["t1", "vec"], writes=["t1"])
            P.op("vector", lambda e: e.tensor_tensor(et, xr, kmod, op=ALU.mult), reads=["zr", "kmod", "et"], writes=["et"])
            P.op("vector", lambda e: e.tensor_tensor(et, et, vcol(V_RK), op=ALU.mult), reads=["et", "vec"], writes=["et"])
            bmm(bones, et, "et", lambda hb, p3, pk: P.op(
                "vector", lambda e: e.tensor_tensor(hs(bbv, hb), p3, hs(xv, hb), op=ALU.mult), reads=[pk, "zr", "bbv"], writes=["bbv"]))
            P.op("vector", lambda e: e.tensor_tensor(t1, t1, bbv, op=ALU.add), reads=["t1", "bbv"], writes=["t1"])
            P.op("scalar", lambda e: e.activation(sgl[:, 0, 0:n], zr[:, 25, cs], AF.Sigmoid), reads=["zr"], writes=["sgl"])
            P.op("scalar", lambda e: e.activation(sgl[0:32, 1, 0:n], zr[0:32, 26, cs], AF.Sigmoid), reads=["zr", "sgl"], writes=["sgl"])
            for hb in range(2):
                pt, pk = ps()
                for c in range(4):
                    ch = hb * 4 + c
                    P.op("tensor", lambda e, pt=pt, c=c, ch=ch: e.matmul(pt[:, c * n:(c + 1) * n], lhsT=gupb[:, 0, ch * 128:(ch + 1) * 128],
                                                       rhs=sgl[:, 0, 0:n], start=True, stop=False), reads=["gupb", "sgl"], writes=[pk])
                    P.op("tensor", lambda e, pt=pt, c=c, ch=ch: e.matmul(pt[:, c * n:(c + 1) * n], lhsT=gupb[0:32, 1, ch * 128:(ch + 1) * 128],
                                                       rhs=sgl[0:32, 1, 0:n], start=False, stop=True), reads=["gupb", "sgl"], writes=[pk])
                P.op("vector", lambda e, pt=pt, hb=hb: e.tensor_tensor(yg[:, hb * 4:(hb + 1) * 4, ycol0:ycol0 + n], hs(t1, hb),
                                                          v3(pt[:, 0:4 * n], n), op=ALU.mult), reads=[pk, "t1"], writes=["yg"])

        def sample_scan():
            n = NS
            cs = slice(1 + NTB, 1 + NTM)
            xr, xv = zr[:, 0:8, cs], zr[:, 16:24, cs]
            sgf, kkn, kmod, bbv = v3(A(0, 128), n), v3(A(2048, 128), n), v3(A(3072, 128), n), v3(A(4096, 128), n)
            Vs = A(5120, 256)[0:64, :].rearrange("p (h n) -> p h n", n=n)
            Ys = A(5376, 256)[0:64, :].rearrange("p (h n) -> p h n", n=n)
            rowb = [arena[0:16, 7168:8192], arena[0:16, 9216:10240]]
            ops = [(kkn, "kkn", -1.0), (sgf, "sg", 1.0), (bbv, "bbv", 1.0), (kmod, "kmod", 1.0), (xr, "zr", 1.0)]
            for oi, (X, xk_, sc) in enumerate(ops):
                rb = rowb[oi % 2]
                rbk = f"rowb{oi % 2}"
                for hb in range(2):
                    pt, pk = ps()
                    for c in range(4):
                        ch = hb * 4 + c
                        P.op("tensor", lambda e, pt=pt, c=c, ch=ch, X=X: e.matmul(pt[0:16, c * 128:(c + 1) * 128], lhsT=X[:, ch, :], rhs=ident,
                                                                start=True, stop=True), reads=[xk_, "cst"], writes=[pk])
                    P.op("scalar", lambda e, pt=pt, hb=hb, rb=rb, sc=sc: e.activation(rb[:, hb * 512:(hb + 1) * 512], pt[0:16, :], AF.Identity, scale=sc),
                         reads=[pk], writes=[rbk])
                P.dma("sync", scr_rows[oi], rb, reads=[rbk], writes=["scr_rows"])
            Vs4 = Vs.rearrange("p (hp h2) n -> p hp h2 n", h2=2)
            P.op("vector", lambda e: e.tensor_copy(Vs4[:, :, 0, :], xv[0:64, :, :]), reads=["zr"], writes=["Vs"])
            P.dma("sync", Vs4[:, :, 1, :], xv[64:128, :, :], reads=["zr"], writes=["Vs"])
            P.fence()
            S = v3(A(8192, 1024)[0:64, :], 64)
            bc = [v3(A(9216 + i * 1024, 1024)[0:64, :], 64) for i in range(5)]
            tt = v3(A(14336, 1024)[0:64, :], 64)
            S1 = v3(A(15360, 1024)[0:64, :], 64)
            sa = A(7168, 16)[0:64, :]
            for s_ in range(NS):
                P.dma("sync", S, swkv[:, s_], writes=["S"])
                for i in range(5):
                    P.dma("sync", A(9216 + i * 1024, 1024)[0:64, :], scr_rows[i, s_:s_ + 1, :].to_broadcast([64, 1024]),
                          writes=[f"bc{i}"])
                a_, w_, b_, k_, r_ = bc
                P.op("vector", lambda e: e.tensor_tensor(tt, S, a_, op=ALU.mult), reads=["S", "bc0", "tt"], writes=["tt"])
                P.op("vector", lambda e: e.tensor_reduce(out=sa, in_=tt, axis=AX.X, op=ALU.add), reads=["tt"], writes=["sa"])
                P.op("vector", lambda e: e.tensor_tensor(S1, S, w_, op=ALU.mult), reads=["S", "bc1", "S1"], writes=["S1"])
                P.op("vector", lambda e: e.tensor_tensor(tt, b_, sa.unsqueeze(2).to_broadcast([64, 16, 64]), op=ALU.mult),
                     reads=["bc2", "sa", "tt"], writes=["tt"])
                P.op("vector", lambda e: e.tensor_tensor(S1, S1, tt, op=ALU.add), reads=["S1", "tt"], writes=["S1"])
                P.op("vector", lambda e, s_=s_: e.tensor_tensor(tt, k_, Vs[:, :, s_].unsqueeze(2).to_broadcast([64, 16, 64]), op=ALU.mult),
                     reads=["bc3", "Vs", "tt"], writes=["tt"])
                P.op("vector", lambda e: e.tensor_tensor(S1, S1, tt, op=ALU.add), reads=["S1", "tt"], writes=["S1"])
                P.op("vector", lambda e: e.tensor_tensor(tt, S1, r_, op=ALU.mult), reads=["S1", "bc4", "tt"], writes=["tt"])
                P.op("vector", lambda e, s_=s_: e.tensor_reduce(out=Ys[:, :, s_], in_=tt, axis=AX.X, op=ALU.add), reads=["tt"], writes=["Ys"])
                P.dma("sync", o_wkvs[:, s_], S1, reads=["S1"])
            yfm = v3(A(7168, 128), n)
            Ys4 = Ys.rearrange("p (hp h2) n -> p hp h2 n", h2=2)
            P.op("vector", lambda e: e.tensor_copy(yfm[0:64, :, :], Ys4[:, :, 0, :]), reads=["Ys", "sa"], writes=["yfm"])
            P.dma("sync", yfm[64:128, :, :], Ys4[:, :, 1, :], reads=["Ys", "sa"], writes=["yfm"])
            P.fence()

        def pool_branch(t_own, nt, last):
            E = 15 + NTB
            icnt = v3(A(8192, 4 * NTM), NTM)
            P.dma("sync", icnt, invc[t_own:t_own + 1, :].to_broadcast([128, 4 * NTM]).rearrange("p (g n) -> p g n", n=NTM),
                  writes=["icnt"])
            lv = [A(i * 288, E) for i in range(4)]
            pbf = AB(1152, NTM)
            if last:
                spl = A(1408, 4 * NS * 15).rearrange("p (g n t) -> p g n t", g=4, n=NS)
                P.dma("sync", spl, spoolT.rearrange("(g p) n t -> p g n t", p=128), writes=["spl"])
                ssum = A(2368, NS)
            for g in range(4):
                w = 2 << g
                x = zp[:, g, 0:E]
                src = x
                for l in range(g + 1):
                    sh = 1 << l
                    lo = 2 * sh - 1
                    dst = lv[l]
                    P.op("vector", lambda e, dst=dst, src=src, lo=lo, sh=sh: e.tensor_tensor(dst[:, lo:E], src[:, lo:E], src[:, lo - sh:E - sh], op=ALU.add),
                         reads=["zp", "lv"], writes=["lv"])
                    src = dst
                P.op("vector", lambda e, src=src, g=g: e.tensor_tensor(lv[3][:, 15:E] if g < 3 else lv[2][:, 15:E], src[:, 15:E], icnt[:, g, 0:NTB], op=ALU.mult),
                     reads=["lv", "icnt"], writes=["lv"])
                pm = lv[3] if g < 3 else lv[2]
                P.op("vector", lambda e, pm=pm, x=x: e.tensor_tensor(pbf[:, 0:NTB], pm[:, 15:E], x[:, 15:E], op=ALU.subtract),
                     reads=["lv", "zp", "pbf"], writes=["pbf"])
                if last:
                    xs = zp[:, g, 15 + NTB:15 + NTM]
                    P.op("vector", lambda e, g=g, w=w: e.tensor_reduce(out=ssum, in_=spl[:, g, :, 15 - (w - 1):15], axis=AX.X, op=ALU.add),
                         reads=["spl", "ssum"], writes=["ssum"])
                    P.op("vector", lambda e, xs=xs: e.tensor_tensor(ssum, ssum, xs, op=ALU.add), reads=["ssum", "zp"], writes=["ssum"])
                    P.op("vector", lambda e, xs=xs, w=w: e.scalar_tensor_tensor(out=pbf[:, NTB:NTM], in0=ssum, scalar=1.0 / w, in1=xs,
                                                                          op0=ALU.mult, op1=ALU.subtract), reads=["ssum", "zp", "pbf"], writes=["pbf"])
                pt, pk = ps()
                P.op("tensor", lambda e, pt=pt, g=g: e.matmul(pt[:, 0:nt], lhsT=pgwb[:, g, :], rhs=pbf[:, 0:nt], start=True, stop=True),
                     reads=["pgwb", "pbf"], writes=[pk])
                P.op("vector", lambda e, pt=pt, g=g: e.tensor_scalar_mul(mix[:, g, 0:nt], pt[:, 0:nt], vec[:, V_PS + g:V_PS + g + 1]),
                     reads=[pk, "vec"], writes=["mix"])

        def attn_prompt(c0):
            n = 128
            sc = v3(A(4096, 1024), 256)
            pnb = v3(AB(5120, 1024), 256)
            pTb = v3(AB(5632, 1024), 128)
            mx, nb, sm, rs = A(6144, 4), A(6148, 4), A(6152, 4), A(6156, 4)
            for hp_ in range(2):
                pt, pk = ps()
                for hh in range(2):
                    h = hp_ * 2 + hh
                    P.op("tensor", lambda e, pt=pt, hh=hh, h=h: e.matmul(pt[:, hh * 256:(hh + 1) * 256], lhsT=zq[:, h, c0:c0 + n], rhs=KTb[:, h, :],
                                                          start=True, stop=True), reads=["zq", "KTb"], writes=[pk])
                P.op("vector", lambda e, pt=pt, hp_=hp_: e.tensor_reduce(out=mx[:, hp_ * 2:hp_ * 2 + 2], in_=v3(pt[:, :], 256), axis=AX.X, op=ALU.max),
                     reads=[pk, "mx"], writes=["mx"])
                P.op("vector", lambda e, hp_=hp_: e.tensor_scalar_mul(nb[:, hp_ * 2:hp_ * 2 + 2], mx[:, hp_ * 2:hp_ * 2 + 2], -SCALE),
                     reads=["mx", "nb"], writes=["nb"])
                for hh in range(2):
                    h = hp_ * 2 + hh
                    P.op("scalar", lambda e, pt=pt, hh=hh, h=h: e.activation(sc[:, h, :], pt[:, hh * 256:(hh + 1) * 256], AF.Exp, bias=nb[:, h:h + 1],
                                                              scale=SCALE, accum_out=sm[:, h:h + 1]), reads=[pk, "nb", "sc", "sm"], writes=["sc", "sm"])
            P.op("vector", lambda e: e.reciprocal(rs, sm), reads=["sm", "rs"], writes=["rs"])
            P.op("vector", lambda e: e.tensor_tensor(pnb, sc, rs.unsqueeze(2).to_broadcast([128, 4, 256]), op=ALU.mult),
                 reads=["sc", "rs", "pnb"], writes=["pnb"])
            for hp_ in range(2):
                pt, pk = ps()
                for hh in range(2):
                    h = hp_ * 2 + hh
                    for mt in range(2):
                        j = hh * 2 + mt
                        P.op("tensor", lambda e, pt=pt, j=j, h=h, mt=mt: e.matmul(pt[:, j * 128:(j + 1) * 128], lhsT=pnb[:, h, mt * 128:(mt + 1) * 128], rhs=ident_bf,
                                                                start=True, stop=True), reads=["pnb", "cbf"], writes=[pk])
                evac(hp_, pTb[:, hp_ * 4:(hp_ + 1) * 4, :], v3(pt[:, :], 128), [pk, "pTb"], ["pTb"])
            pt, pk = ps()
            for h in range(4):
                for mt in range(2):
                    P.op("tensor", lambda e, pt=pt, h=h, mt=mt: e.matmul(pt[:, h * 128:(h + 1) * 128], lhsT=Vb[:, mt, h * 128:(h + 1) * 128], rhs=pTb[:, h * 2 + mt, :],
                                                          start=(mt == 0), stop=(mt == 1)), reads=["Vb", "pTb"], writes=[pk])
            P.op("vector", lambda e, pt=pt: e.tensor_copy(oT[:, :, c0:c0 + n], v3(pt[:, :], 128)), reads=[pk], writes=["oT"])

        def attn_sample():
            qrow = arena[0:16, 0:512]
            pt, pk = ps()
            for h in range(4):
                P.op("tensor", lambda e, pt=pt, h=h: e.matmul(pt[0:16, h * 128:(h + 1) * 128], lhsT=zq[:, h, NTB:NTM], rhs=ident_bf, start=True, stop=True),
                     reads=["zq", "cbf"], writes=[pk])
            P.op("vector", lambda e, pt=pt: e.tensor_copy(qrow, pt[0:16, :]), reads=[pk], writes=["qrow"])
            P.dma("sync", scr_q, qrow, reads=["qrow"], writes=["scr_q"])
            qbc = v3(A(8192, NS * 512), 512)
            P.dma("sync", qbc, scr_q.rearrange("n c -> (n c)").unsqueeze(0).to_broadcast([128, NS * 512]).rearrange("p (n c) -> p n c", c=512),
                  reads=["scr_q"], writes=["qbc"])
            KV = [v3(A(512 + i * 1024, 1024), 512) for i in range(2)]
            prod = A(2560, 512)
            s_all = v3(A(3072, 128), 64)
            p64 = A(3200, 256)
            pTs = v3(A(3456, 128), 64)
            mx, nb, sm, rs = A(3584, 1), A(3585, 1), A(3586, 1), A(3587, 1)
            for s_ in range(NS):
                kb = KV[s_ % 2]
                kbk = f"kv{s_ % 2}"
                P.dma("sync", kb, kc[s_].rearrange("(mt p) c -> p mt c", p=128), writes=[kbk])
                for mt in range(2):
                    P.op("vector", lambda e, kb=kb, mt=mt, s_=s_: e.tensor_tensor(prod, kb[:, mt, :], qbc[:, s_, :], op=ALU.mult),
                         reads=[kbk, "qbc", "prod"], writes=["prod"])
                    P.op("vector", lambda e, mt=mt, s_=s_: e.tensor_reduce(out=s_all[:, mt, s_ * 4:(s_ + 1) * 4], in_=v3(prod, 128), axis=AX.X, op=ALU.add),
                         reads=["prod", "s_all"], writes=["s_all"])
            pt, pk = ps()
            for mt in range(2):
                P.op("tensor", lambda e, pt=pt, mt=mt: e.matmul(pt[0:64, mt * 128:(mt + 1) * 128], lhsT=s_all[:, mt, :], rhs=ident, start=True, stop=True),
                     reads=["s_all", "cst"], writes=[pk])
            P.op("vector", lambda e, pt=pt: e.tensor_reduce(out=mx[0:64, :], in_=pt[0:64, 0:256], axis=AX.X, op=ALU.max), reads=[pk], writes=["mx"])
            P.op("vector", lambda e: e.tensor_scalar_mul(nb[0:64, :], mx[0:64, :], -SCALE), reads=["mx"], writes=["nb"])
            P.op("scalar", lambda e, pt=pt: e.activation(p64[0:64, :], pt[0:64, 0:256], AF.Exp, bias=nb[0:64, :], scale=SCALE, accum_out=sm[0:64, :]),
                 reads=[pk, "nb"], writes=["p64", "sm"])
            P.op("vector", lambda e: e.reciprocal(rs[0:64, :], sm[0:64, :]), reads=["sm"], writes=["rs"])
            P.op("vector", lambda e: e.tensor_scalar_mul(p64[0:64, :], p64[0:64, :], rs[0:64, :]), reads=["p64", "rs"], writes=["p64"])
            pt2, pk2 = ps()
            for mt in range(2):
                P.op("tensor", lambda e, mt=mt: e.matmul(pt2[:, mt * 64:(mt + 1) * 64], lhsT=p64[0:64, mt * 128:(mt + 1) * 128], rhs=ident[0:64, 0:64],
                                                  start=True, stop=True), reads=["p64", "cst"], writes=[pk2])
            P.op("vector", lambda e: e.tensor_copy(pTs, v3(pt2[:, 0:128], 64)), reads=[pk2], writes=["pTs"])
            pt3, pk3 = ps()
            for s_ in range(NS):
                vb_ = KV[s_ % 2]
                vbk = f"kv{s_ % 2}"
                P.dma("sync", vb_, vc[s_].rearrange("(mt p) c -> p mt c", p=128), writes=[vbk])
                for h in range(4):
                    col = s_ * 4 + h
                    for mt in range(2):
                        P.op("tensor", lambda e, vb_=vb_, h=h, mt=mt, col=col: e.matmul(pt3[:, col:col + 1], lhsT=vb_[:, mt, h * 128:(h + 1) * 128],
                                                                  rhs=pTs[:, mt, col:col + 1], start=(mt == 0), stop=(mt == 1)),
                             reads=[vbk, "pTs"], writes=[pk3])
            P.op("vector", lambda e: e.tensor_copy(oT[:, :, NTB:NTM], pt3[:, 0:64].rearrange("p (n h) -> p h n", h=4)), reads=[pk3], writes=["oT"])

        def merge_wo(nt):
            acc = [A(0, NTM), A(288, NTM)]
            sgt = A(576, NTM)
            merged = v3(AB(1024, KT * NTM), NTM)
            branches = [(C_G, pout, mix, "mix", 4), (C_G + D, rout, yg, "yg", 8), (C_G + 2 * D, xout, oT, "oT", 4)]
            for ip in range(8):
                for bi, (gc0, Wb, src, srck, ktb) in enumerate(branches):
                    wgt, wgk = wload(win, 0, D, gc0 + ip * 256, 256)
                    wbr, wbk = wload(Wb, 0, ktb * 128, ip * 256, 256)
                    for ci in range(2):
                        pg, pgk = ps()
                        po, pok = ps()
                        P.group("tensor", [lambda e, pg=pg, k=k, ci=ci, wgt=wgt: e.matmul(
                            pg[:, 0:nt], lhsT=wgt[:, k, ci * 128:(ci + 1) * 128], rhs=xn[:, k, 0:nt],
                            start=(k == 0), stop=(k == KT - 1)) for k in range(KT)], reads=[wgk, "xn"], writes=[pgk])
                        P.group("tensor", [lambda e, po=po, k=k, ci=ci, wbr=wbr, src=src, ktb=ktb: e.matmul(
                            po[:, 0:nt], lhsT=wbr[:, k, ci * 128:(ci + 1) * 128], rhs=src[:, k, 0:nt],
                            start=(k == 0), stop=(k == ktb - 1)) for k in range(ktb)], reads=[wbk, srck], writes=[pok])
                        P.op("scalar", lambda e, pg=pg: e.activation(sgt[:, 0:nt], pg[:, 0:nt], AF.Sigmoid), reads=[pgk, "sgt"], writes=["sgt"])
                        if bi == 0:
                            P.op("vector", lambda e, po=po, ci=ci: e.tensor_tensor(acc[ci][:, 0:nt], sgt[:, 0:nt], po[:, 0:nt], op=ALU.mult),
                                 reads=["sgt", pok, f"acc{ci}"], writes=[f"acc{ci}"])
                        else:
                            P.op("vector", lambda e, po=po: e.tensor_tensor(sgt[:, 0:nt], sgt[:, 0:nt], po[:, 0:nt], op=ALU.mult),
                                 reads=["sgt", pok], writes=["sgt"])
                            P.op("vector", lambda e, ci=ci: e.tensor_tensor(acc[ci][:, 0:nt], acc[ci][:, 0:nt], sgt[:, 0:nt], op=ALU.add),
                                 reads=["sgt", f"acc{ci}"], writes=[f"acc{ci}"])
                for ci in range(2):
                    i = ip * 2 + ci
                    P.op("vector", lambda e, ci=ci, i=i: e.tensor_copy(merged[:, i, 0:nt], acc[ci][:, 0:nt]), reads=[f"acc{ci}"], writes=[("merged", i)])

            def sink_o(j, pt, pk, m):
                P.op("vector", lambda e: e.tensor_tensor(xh[:, j, 0:nt], xh[:, j, 0:nt], pt[:, 0:nt], op=ALU.add),
                     reads=[pk, ("xh", j)], writes=[("xh", j)])
            proj(wo, 0, D, merged, "merged", nt, KT, sink_o)
            P.fence()

        P.op("vector", lambda e: e.memset(zr[:], 0.0), writes=["zr"])
        mem_kv()
        P.dma("sync", o_poolso, spool[:, 1:15, :])
        xT3 = xT.rearrange("(k p) n -> p k n", p=128)
        TSEL = [int(v) for v in os.environ.get('K_TILES', '0,1,2,3,4,5,6,7').split(',') if v != ''] if _DEBUG else list(range(8))
        STAGE = int(os.environ.get("K_STAGE", "9")) if _DEBUG else 9
        for t in TSEL:
            own = t >= 4
            last = t == 7
            nt = NTB + (NS if last else 0)
            P.dma("sync", xh[:, :, 0:NTB], xT3[:, :, t * NTB:(t + 1) * NTB], writes=["xh"])
            if last:
                P.dma("sync", xh[:, :, NTB:NTM], xT3[:, :, 2048:2048 + NS], writes=["xh"])
            ffn(f1g, f1u, f1d, G_F1, nt, True)
            rmsnorm(xh, G_MIX, nt, xn, "xh", "xn")
            if t >= 3:
                def sink_zp(j, pt, pk, m, nt=nt):
                    P.op("vector", lambda e: e.tensor_copy(zp[:, j, 15:15 + nt], pt[:, 0:nt]), reads=[pk], writes=["zp"])
                proj(win, C_POOL, 512, xn, "xn", nt, KT, sink_zp)

            def sink_zr(j, pt, pk, m, nt=nt):
                evac(j, zr[0:m, j, 1:1 + nt], pt[0:m, 0:nt], [pk], ["zr"])
            proj(win, C_R, RPW, xn, "xn", nt, KT, sink_zr)
            if own:
                def sink_zq(j, pt, pk, m, nt=nt):
                    evac(j, zq[:, j, 0:nt], pt[:, 0:nt], [pk], ["zq"])
                proj(win, C_XQ, 512, xn, "xn", nt, KT, sink_zq)
            if last:
                P.dma("sync", o_shiftT, zr[:, :, NTB:NTB + 1 + NS], reads=["zr"])
                P.dma("sync", o_poolpT, zp[:, :, NTB:NTB + 15], reads=["zp"])
                P.dma("sync", o_poolsn, zp[:, :, 15 + NTB:15 + NTM], reads=["zp"])
            P.fence()
            SUB = 9
            if STAGE >= 2:
                token_shift(last)
                for sub in range(2):
                    cs = slice(1 + sub * 128, 1 + (sub + 1) * 128)
                    if SUB >= 2:
                        rwkv_elem(cs, 128, False)
                    if SUB >= 3:
                        scan_prompt(cs, own)
                    if own and SUB >= 4:
                        rwkv_post(cs, 128, sub * 128, False)
                    P.fence()
                if last:
                    cs = slice(1 + NTB, 1 + NTM)
                    rwkv_elem(cs, NS, True)
                    P.fence()
                    sample_scan()
                    rwkv_post(cs, NS, NTB, False)
                    P.fence()
                P.op("vector", lambda e: e.tensor_copy(zr[:, :, 0:1], zc[:]), reads=["zc", "zr"], writes=["zr"])
            if own and STAGE >= 3:
                pool_branch(t - 4, nt, last)
                for sub in range(2):
                    attn_prompt(sub * 128)
                P.fence()
                if last:
                    attn_sample()
                    P.fence()
            if t >= 3:
                P.op("vector", lambda e: e.tensor_copy(zp[:, :, 0:15], zp[:, :, NTB:NTB + 15]), reads=["zp"], writes=["zp"])
            if own:
                P.fence()
                if STAGE >= 4:
                    merge_wo(nt)
                ffn(f2g, f2u, f2d, G_F2, nt, True)
                yo = v3(A(0, KT * NTM), NTM)
                rmsnorm(xh, G_FIN, nt, yo, "xh", "yo")
                P.dma("sync", yT[:, :, (t - 4) * NTB:(t - 3) * NTB], yo[:, :, 0:NTB], reads=["yo"])
                if last:
                    P.dma("sync", yT[:, :, 1024:1024 + NS], yo[:, :, NTB:NTM], reads=["yo"])
            P.fence()
        P.dma("sync", o_wkvp, S32[:].rearrange("p k n -> p (k n)"), reads=["S32"])
        P.finish()
    return nc


def _consts():
    c = np.zeros((128, 3584), np.float32)
    s = np.arange(128)[:, None]
    t = np.arange(128)[None, :]
    eye = np.eye(128, dtype=np.float32)
    c[:, 0:128] = eye
    c[:, 128:256] = -EDEC * (s <= t)
    c[:, 256:384] = -EDEC * (s < t)
    c[:, 384:512] = -EDEC * (s > t)
    bo = ((s // 64) == (t // 64)).astype(np.float32)
    c[:, 512:640] = bo
    c[:, 640:768] = bo / 64.0
    c[:, 768:1280] = np.tile((s < t).astype(np.float32), (1, 4))
    c[:, 1280:1792] = np.tile((s > t).astype(np.float32), (1, 4))
    c[:, 1792:2304] = np.tile((s <= t).astype(np.float32), (1, 4))
    c[:, 2304:2816] = np.tile(eye, (1, 4))
    c[:, 2816:3072] = 1.0
    c[:, 3072:3200] = eye
    c[:, 3200:3328] = 1.0
    return c


_NC_CACHE = {}
_DEBUG = False
_PACK_ONLY = False


def kernel(x_prompt, x_sample, mem_prompt, cache_mem_k, cache_mem_v, state_wkv, state_shift, state_pool,
           ffn1_norm_g, ffn1_w_gate, ffn1_w_up, ffn1_w_down, mix_norm_g, w_in,
           pool_group_w, pool_scale, pool_out,
           rwkv_mu, rwkv_w0, rwkv_w_up, rwkv_a0, rwkv_a_up, rwkv_g_up, rwkv_k_k, rwkv_k_a, rwkv_r_k,
           rwkv_ln_g, rwkv_ln_b, rwkv_out,
           mem_norm_g, w_mem_k, w_mem_v, xattn_out, w_o,
           ffn2_norm_g, ffn2_w_gate, ffn2_w_up, ffn2_w_down, final_norm_g):
    f = lambda a: np.ascontiguousarray(np.asarray(a, dtype=np.float32))
    x_prompt, x_sample, mem_prompt = f(x_prompt), f(x_sample), f(mem_prompt)
    B = x_prompt.shape[0]

    def kcols(v, n):
        v = f(v).reshape(-1)
        pad = np.zeros(n * 128, np.float32)
        pad[:v.size] = v
        return pad.reshape(n, 128).T

    vecs = np.zeros((128, 176), np.float32)
    vecs[:, 0:16] = kcols(ffn1_norm_g[0], 16)
    vecs[:, 16:32] = kcols(mix_norm_g[0], 16)
    vecs[:, 32:48] = kcols(mem_norm_g[0], 16)
    vecs[:, 48:64] = kcols(ffn2_norm_g[0], 16)
    vecs[:, 64:80] = kcols(final_norm_g, 16)
    vecs[:, 80:107] = kcols(rwkv_mu[0], 27)
    vecs[:, 107:111] = kcols(pool_scale[0], 4)
    vecs[:, 111:119] = kcols(rwkv_k_k[0], 8)
    vecs[:, 119:127] = kcols(rwkv_k_a[0], 8)
    vecs[:, 127:135] = kcols(rwkv_r_k[0], 8)
    vecs[:, 135:143] = kcols(rwkv_ln_g[0], 8)
    vecs[:, 143:151] = kcols(rwkv_ln_b[0], 8)
    rows = np.concatenate([f(rwkv_w0[0]), f(rwkv_a0[0])])[None, :]
    cst = _consts()
    if "nc" not in _NC_CACHE:
        _NC_CACHE["nc"] = build_nc()
    raw = {"f1g": ffn1_w_gate[0], "f1u": ffn1_w_up[0], "f1d": ffn1_w_down[0],
           "f2g": ffn2_w_gate[0], "f2u": ffn2_w_up[0], "f2d": ffn2_w_down[0],
           "win": w_in[0], "pout": pool_out[0], "rout": rwkv_out[0], "xout": xattn_out[0], "wo": w_o[0]}
    shared = {
        "pgw": f(pool_group_w[0]),
        "wup": f(rwkv_w_up[0]), "aup": f(rwkv_a_up[0]), "gup": f(rwkv_g_up[0]),
        "wmk": f(w_mem_k[0]), "wmv": f(w_mem_v[0]),
        "vecs": vecs, "rows": rows, "cst": cst,
    }
    for nm, W in raw.items():
        W = f(W)
        blks = BLOCKS.get(nm, [])
        arr = np.zeros((IN_SHAPES[nm][0], 128, 4096), np.float32)
        for bi, (r0, nrows, c0, ncols) in enumerate(blks):
            k = nrows // 128
            arr[bi, :, :k * ncols] = W[r0:r0 + nrows, c0:c0 + ncols].reshape(k, 128, ncols).transpose(1, 0, 2).reshape(128, k * ncols)
        shared[nm] = arr
    in_maps = []
    for c in range(8):
        b, half = c // 2, c % 2
        own = x_prompt[b, half * 1024:(half + 1) * 1024]
        prev = x_prompt[b, 0:1024] if half == 1 else np.zeros_like(own)
        xs = x_sample[c * NS:(c + 1) * NS, 0]
        xTc = np.ascontiguousarray(np.concatenate([prev, own, xs], axis=0).T)
        sl = slice(c * NS, (c + 1) * NS)
        sshT = np.zeros((27 * 128, NS), np.float32)
        sshT[:RPW] = f(state_shift[0, sl, 0]).T
        invc = np.zeros((4, 4, NTB + NS), np.float32)
        for t in range(4):
            pos = half * 1024 + t * NTB + np.arange(NTB)
            for g, w in enumerate((2, 4, 8, 16)):
                invc[t, g, :NTB] = 1.0 / np.minimum(pos + 1, w)
                invc[t, g, NTB:] = 1.0 / w
        m = dict(shared)
        m.update({
            "xT": xTc, "memT": np.ascontiguousarray(mem_prompt[b].T),
            "kc": f(cache_mem_k[0, sl]).reshape(NS, NMEM, 512), "vc": f(cache_mem_v[0, sl]).reshape(NS, NMEM, 512),
            "swkv": np.ascontiguousarray(f(state_wkv[0, sl]).transpose(2, 0, 1, 3)),
            "sshiftT": sshT,
            "spoolT": np.ascontiguousarray(f(state_pool[0, sl]).transpose(2, 0, 1)),
            "spool": f(state_pool[0, sl]),
            "invc": invc.reshape(4, -1),
        })
        in_maps.append(m)
    if _PACK_ONLY:
        return in_maps
    if "nc" not in _NC_CACHE:
        _NC_CACHE["nc"] = build_nc()
    res = run_bass_kernel_spmd(_NC_CACHE["nc"], in_maps, core_ids=list(range(8)))
    R = res.results
    return unpack(R, B)


def unpack(R, B=4):
    ND = NS * 8
    y_prompt = np.zeros((B, 2048, D), np.float32)
    y_sample = np.zeros((ND, 1, D), np.float32)
    mem_k = np.zeros((1, B, NMEM, 4, 128), np.float32)
    mem_v = np.zeros((1, B, NMEM, 4, 128), np.float32)
    wkv_p = np.zeros((1, B, 16, 64, 64), np.float32)
    sh_p = np.zeros((1, B, 1, RPW), np.float32)
    pl_p = np.zeros((1, B, 15, 512), np.float32)
    wkv_s = np.zeros((1, ND, 16, 64, 64), np.float32)
    sh_s = np.zeros((1, ND, 1, RPW), np.float32)
    pl_s = np.zeros((1, ND, 15, 512), np.float32)
    for c in range(8):
        b, half = c // 2, c % 2
        r = R[c]
        yT = np.asarray(r["yT"]).reshape(128, KT, 1024 + NS)
        yfull = yT.transpose(2, 1, 0).reshape(1024 + NS, D)
        y_prompt[b, half * 1024:(half + 1) * 1024] = yfull[:1024]
        y_sample[c * NS:(c + 1) * NS, 0] = yfull[1024:]
        sh = np.asarray(r["shiftT"]).reshape(128, 27, 1 + NS).transpose(2, 1, 0).reshape(1 + NS, 27 * 128)[:, :RPW]
        sh_s[0, c * NS:(c + 1) * NS, 0] = sh[1:]
        pl_s[0, c * NS:(c + 1) * NS, 0:14] = np.asarray(r["poolso"]).reshape(NS, 14, 512)
        pl_s[0, c * NS:(c + 1) * NS, 14] = np.asarray(r["poolsn"]).reshape(128, 4, NS).transpose(2, 1, 0).reshape(NS, 512)
        wkv_s[0, c * NS:(c + 1) * NS] = np.asarray(r["wkvs"]).reshape(64, NS, 16, 64).transpose(1, 2, 0, 3)
        if half == 0:
            mem_k[0, b] = np.asarray(r["memkT"]).reshape(128, 4, NMEM).transpose(2, 1, 0)
            mem_v[0, b] = np.asarray(r["memv"]).reshape(128, 2, 512).transpose(1, 0, 2).reshape(NMEM, 4, 128)
        else:
            sh_p[0, b, 0] = sh[0]
            pl_p[0, b] = np.asarray(r["poolpT"]).reshape(128, 4, 15).transpose(2, 1, 0).reshape(15, 512)
            wkv_p[0, b] = np.asarray(r["wkvp"]).reshape(2, 64, 8, 64).transpose(2, 0, 3, 1).reshape(16, 64, 64)
    return (y_prompt, y_sample, mem_k, mem_v, wkv_p, sh_p, pl_p, wkv_s, sh_s, pl_s)
```

```python
import numpy as np
from contextlib import ExitStack
import concourse.bass as bass
import concourse.mybir as mybir
from concourse.bass_utils import run_bass_kernel_spmd

F32 = mybir.dt.float32
BF16 = mybir.dt.bfloat16
AF = mybir.ActivationFunctionType
ALU = mybir.AluOpType
AX = mybir.AxisListType

D = 2048
F = 5504
KT = 16
FC = 43
NMEM = 256
RW = 1024
RPW = 3360
INW = 10528
C_POOL, C_R, C_XQ, C_G = 0, 512, 3872, 4384
NTB = 256
NS = 16
NTILE = 8
EDEC = float(np.exp(-0.5))
RMS_EPS = 1e-6
GN_EPS = 64e-5
EPOCH = 20000
NDSEM = 6


class Prog:
    COMPUTE = ("tensor", "vector", "scalar", "gpsimd")

    def __init__(self, nc, stack):
        self.nc = nc
        self.stack = stack
        self.eng = {"tensor": nc.tensor, "vector": nc.vector, "scalar": nc.scalar,
                    "gpsimd": nc.gpsimd, "sync": nc.sync}
        self.seq = {e: 0 for e in self.COMPUTE}
        self.esems = {e: [] for e in self.COMPUTE}
        self.dq = {}
        for q in ("sync", "scalar", "gpsimd"):
            self.dq[q] = {"n": 0, "sems": [self._sem(f"d_{q}_{i}") for i in range(NDSEM)]}
        self.seen = {e: {} for e in self.eng}
        self.state = {}
        self.subs = {}
        self.same_engine_sync = {"vector": True, "scalar": True, "gpsimd": True, "tensor": False}
        self.last_ev = {}

    def _sem(self, name):
        return self.stack.enter_context(self.nc.semaphore(name))

    def _esem(self, e, epoch):
        while len(self.esems[e]) <= epoch:
            self.esems[e].append(self._sem(f"e_{e}_{len(self.esems[e])}"))
        return self.esems[e][epoch]

    def _related(self, k):
        if isinstance(k, tuple):
            return [k, k[0]]
        out = [k]
        out.extend(self.subs.get(k, ()))
        return out

    def _deps(self, reads, writes):
        evs = []
        for k in reads:
            for kk in self._related(k):
                st = self.state.get(kk)
                if st and st[0] is not None:
                    evs.append(st[0])
        for k in writes:
            for kk in self._related(k):
                st = self.state.get(kk)
                if st:
                    if st[0] is not None:
                        evs.append(st[0])
                    evs.extend(st[1])
        return evs

    def _record(self, ev, reads, writes):
        for k in reads:
            if isinstance(k, tuple):
                self.subs.setdefault(k[0], set()).add(k)
            self.state.setdefault(k, [None, []])[1].append(ev)
        for k in writes:
            if isinstance(k, tuple):
                self.subs.setdefault(k[0], set()).add(k)
            self.state[k] = [ev, []]
            if not isinstance(k, tuple):
                for kk in self.subs.get(k, ()):
                    self.state[kk] = [ev, []]

    def _emit_waits(self, e, evs):
        need = {}
        for (semkey, sem, val, src) in evs:
            if src == e and not self.same_engine_sync.get(e, True):
                continue
            if self.seen[e].get(semkey, 0) >= val:
                continue
            if semkey not in need or need[semkey][1] < val:
                need[semkey] = (sem, val)
        for semkey, (sem, val) in need.items():
            self.eng[e].wait_ge(sem, val)
            self.seen[e][semkey] = val
            if semkey[0] == "E":
                for ep in range(semkey[2]):
                    self.seen[e][("E", semkey[1], ep)] = EPOCH

    def op(self, e, fn, reads=(), writes=()):
        evs = self._deps(reads, writes)
        self._emit_waits(e, evs)
        ins = fn(self.eng[e])
        s = self.seq[e]
        epoch, idx = divmod(s, EPOCH)
        sem = self._esem(e, epoch)
        ins.then_inc(sem, 1)
        self.seq[e] = s + 1
        ev = (("E", e, epoch), sem, idx + 1, e)
        self.last_ev[e] = ev
        self._record(ev, reads, writes)
        return ins

    def group(self, e, fns, reads=(), writes=()):
        evs = self._deps(reads, writes)
        self._emit_waits(e, evs)
        for fn in fns[:-1]:
            fn(self.eng[e])
        ins = fns[-1](self.eng[e])
        s = self.seq[e]
        epoch, idx = divmod(s, EPOCH)
        sem = self._esem(e, epoch)
        ins.then_inc(sem, 1)
        self.seq[e] = s + 1
        ev = (("E", e, epoch), sem, idx + 1, e)
        self.last_ev[e] = ev
        self._record(ev, reads, writes)
        return ins

    def dma(self, q, out, in_, reads=(), writes=(), **kw):
        d = self.dq[q]
        i = d["n"]
        slot, rnd = i % NDSEM, i // NDSEM
        sem = d["sems"][slot]
        evs = self._deps(reads, writes)
        if rnd > 0:
            evs.append((("D", q, slot), sem, 16 * rnd, None))
        self._emit_waits(q, evs)
        ins = self.eng[q].dma_start(out=out, in_=in_, **kw)
        ins.then_inc(sem, 16)
        d["n"] = i + 1
        ev = (("D", q, slot), sem, 16 * (rnd + 1), None)
        self._record(ev, reads, writes)
        return ins

    def _dma_events(self):
        evs = []
        for q, d in self.dq.items():
            n = d["n"]
            for slot in range(NDSEM):
                cnt = (n - slot + NDSEM - 1) // NDSEM if n > slot else 0
                if cnt > 0:
                    evs.append((("D", q, slot), d["sems"][slot], 16 * cnt, None))
        return evs

    def fence(self):
        evs = list(self.last_ev.values()) + self._dma_events()
        for e in self.eng:
            self._emit_waits(e, [ev for ev in evs if not (ev[3] == e and e == "tensor")])
        self.state = {}
        self.subs = {}

    def finish(self):
        for q in self.dq:
            self._emit_waits(q, [ev for ev in self._dma_events() if ev[0][1] == q])


IN_SHAPES = {}
BLOCKS = {}


def build_nc():
    nc = bass.Bass("TRN2", target_bir_lowering=False)
    din = {}
    dout = {}

    import os
    _small = ""
    _need = set()

    def I(name, shape, dt=F32):
        if _small and name not in _need:
            shape = [1, 8]
        IN_SHAPES[name] = list(shape)
        din[name] = nc.dram_tensor(name, list(shape), dt, kind="ExternalInput").ap()
        return din[name]

    def O(name, shape):
        dout[name] = nc.dram_tensor(name, list(shape), F32, kind="ExternalOutput").ap()
        return dout[name]

    NTOK = 2048 + NS
    xT = I("xT", [D, NTOK])
    memT = I("memT", [D, NMEM])
    kc = I("kc", [NS, NMEM, 512])
    vc = I("vc", [NS, NMEM, 512])
    swkv = I("swkv", [64, NS, 16, 64])
    sshiftT = I("sshiftT", [27 * 128, NS])
    spoolT = I("spoolT", [512, NS, 15])
    spool = I("spool", [NS, 15, 512])
    NBLK = {"f1g": 22, "f1u": 22, "f1d": 32, "f2g": 22, "f2u": 22, "f2d": 32, "win": 44,
            "pout": 8, "rout": 8, "xout": 8, "wo": 8}
    WSHAPE = {"f1g": [D, F], "f1u": [D, F], "f1d": [F, D], "f2g": [D, F], "f2u": [D, F], "f2d": [F, D],
              "win": [D, INW], "pout": [512, D], "rout": [RW, D], "xout": [512, D], "wo": [D, D]}
    f1g, f1u, f1d, f2g, f2u, f2d, win = "f1g", "f1u", "f1d", "f2g", "f2u", "f2d", "win"
    WB = {nm: I(nm, [NBLK[nm], 128, 4096]) for nm in NBLK}
    pgw = I("pgw", [4, 128, 128]); pout = "pout"
    wup = I("wup", [64, RW]); aup = I("aup", [64, RW]); gup = I("gup", [160, RW])
    rout = "rout"
    wmk = I("wmk", [D, 512]); wmv = I("wmv", [D, 512]); xout = "xout"
    wo = "wo"
    vecs = I("vecs", [128, 176])
    rows = I("rows", [1, 2048])
    cst = I("cst", [128, 3584])
    invc = I("invc", [4, 4 * (NTB + NS)])

    yT = O("yT", [128, KT, 1024 + NS])
    o_memkT = O("memkT", [128, 4, NMEM])
    o_memv = O("memv", [128, 2, 512])
    o_wkvp = O("wkvp", [128, 8 * 64])
    o_shiftT = O("shiftT", [128, 27, 1 + NS])
    o_poolpT = O("poolpT", [128, 4, 15])
    o_wkvs = O("wkvs", [64, NS, 16, 64])
    o_poolsn = O("poolsn", [128, 4, NS])
    o_poolso = O("poolso", [NS, 14, 512])

    scr_rows = nc.dram_tensor("scr_rows", [6, NS, 1024], F32, kind="Internal").ap()
    scr_y = nc.dram_tensor("scr_y", [NS, 1024], F32, kind="Internal").ap()
    scr_q = nc.dram_tensor("scr_q", [NS, 512], F32, kind="Internal").ap()

    with ExitStack() as st:
        P = Prog(nc, st)
        sb = lambda name, shape, dt=F32: st.enter_context(nc.sbuf_tensor(name, list(shape), dt))
        NTM = NTB + NS

        cst_sb = sb("cst_sb", [128, 3584])
        P.dma("sync", cst_sb[:], cst, writes=["cst"])
        ident = cst_sb[:, 0:128]
        triI = cst_sb[:, 128:256]
        triX = cst_sb[:, 256:384]
        triS = cst_sb[:, 384:512]
        bones = cst_sb[:, 512:640]
        bones64 = cst_sb[:, 640:768]
        mSU = cst_sb[:, 768:1280]
        mSL = cst_sb[:, 1280:1792]
        mIU = cst_sb[:, 1792:2304]
        I4 = cst_sb[:, 2304:2816]
        ones_row = cst_sb[0:1, 2816:3072]
        cbf = sb("cbf", [128, 256], BF16)
        P.dma("gpsimd", cbf[:], cst[:, 3072:3328], writes=["cbf"])
        ident_bf = cbf[:, 0:128]
        ones_bf = cbf[:, 128:256]
        vec = sb("vec", [128, 176])
        P.dma("sync", vec[:], vecs, writes=["vec"])
        G_F1, G_MIX, G_MEM, G_F2, G_FIN = 0, 16, 32, 48, 64
        V_MU, V_PS = 80, 107
        V_KK, V_KA, V_RK, V_LG, V_LB = 111, 119, 127, 135, 143
        rows_sb = sb("rows_sb", [1, 2048])
        P.dma("sync", rows_sb[:], rows, writes=["rows"])
        lora = sb("lora", [128, RW])
        P.dma("sync", lora[0:64, :], wup, writes=[("lora", 0)])
        P.dma("sync", lora[64:128, :], aup, writes=[("lora", 1)])
        gupb = sb("gupb", [128, 2, RW], BF16)
        P.dma("gpsimd", gupb[:, 0, :], gup[0:128, :], writes=[("gupb", 0)])
        P.dma("gpsimd", gupb[0:32, 1, :], gup[128:160, :], writes=[("gupb", 1)])
        pgwb = sb("pgwb", [128, 4, 128], BF16)
        P.dma("gpsimd", pgwb[:], pgw.rearrange("g c d -> c g d"), writes=["pgwb"])

        xh = sb("xh", [128, KT, NTM])
        xn = sb("xn", [128, KT, NTM], BF16)
        zr = sb("zr", [128, 27, 1 + NTM])
        zp = sb("zp", [128, 4, 15 + NTM])
        zq = sb("zq", [128, 4, NTM], BF16)
        mix = sb("mix", [128, 4, NTM], BF16)
        yg = sb("yg", [128, 8, NTM], BF16)
        oT = sb("oT", [128, 4, NTM], BF16)
        S32 = sb("S32", [128, 8, 64])
        Sbf = sb("Sbf", [128, 8, 64], BF16)
        KTb = sb("KTb", [128, 4, NMEM], BF16)
        Vb = sb("Vb", [128, 2, 512], BF16)
        rstd = sb("rstd", [128, NTM])
        tmpA = sb("tmpA", [128, NTM])
        tmpB = sb("tmpB", [128, NTM], BF16)
        NWB = 4
        wbuf = [sb(f"wbuf{i}", [128, 4096], BF16) for i in range(NWB)]
        ARENA = 16384
        arena = sb("arena", [128, ARENA])
        psb = [st.enter_context(nc.psum_tensor(f"pb{i}", [128, 512], F32)) for i in range(8)]
        wctr = [0]
        pctr = [0]

        def ps():
            i = pctr[0] % 8
            pctr[0] += 1
            return psb[i], f"pb{i}"

        wcache = {}

        def wload(W, r0, nrows, c0, ncols, kparts=128):
            i = wctr[0] % NWB
            wctr[0] += 1
            k = nrows // kparts
            view = wbuf[i][0:kparts, 0:k * ncols].rearrange("p (k n) -> p k n", n=ncols)
            flat = wbuf[i][0:kparts, 0:k * ncols]
            if not isinstance(W, str):
                src = W[r0:r0 + nrows, c0:c0 + ncols].rearrange("(k p) n -> p k n", p=kparts)
                P.dma("gpsimd", view, src, writes=[f"wbuf{i}"])
                return view, f"wbuf{i}"
            name = W
            blk_id = (r0, nrows, c0, ncols)
            if name not in wcache:
                wcache[name] = (nc.dram_tensor("bfc_" + name, [NBLK[name], 128, 4096], BF16, kind="Internal").ap(), {})
                BLOCKS[name] = []
            if blk_id in wcache[name][1]:
                bi = wcache[name][1][blk_id]
                P.dma("sync", flat, wcache[name][0][bi, 0:kparts, 0:k * ncols], writes=[f"wbuf{i}"])
            else:
                bi = len(wcache[name][1])
                assert bi < NBLK[name], name
                BLOCKS[name].append(blk_id)
                P.dma("gpsimd", flat, WB[name][bi, 0:kparts, 0:k * ncols], writes=[f"wbuf{i}"])
                P.dma("sync", wcache[name][0][bi, 0:kparts, 0:k * ncols], flat, reads=[f"wbuf{i}"])
                wcache[name][1][blk_id] = bi
            return view, f"wbuf{i}"

        CACHED = {"f1g", "f1u", "f1d", "f2g", "f2u", "f2d", "win", "pout", "rout", "xout", "wo"}

        P.op("vector", lambda e: e.memset(S32[:], 0.0), writes=["S32"])
        P.op("vector", lambda e: e.memset(Sbf[:], 0.0), writes=["Sbf"])
        P.op("vector", lambda e: e.memset(zr[:, :, 0:1], 0.0), writes=["zr"])
        P.op("vector", lambda e: e.memset(zp[:, :, 0:15], 0.0), writes=["zp"])

        def rmsnorm(src, gcol, nt, dst, srckey, dstkey, kt=KT):
            sq = arena[:, 14208:14208 + (kt * nt + 1) // 2].bitcast(BF16)[:, 0:kt * nt].rearrange("p (k n) -> p k n", n=nt)
            pt, pk = ps()
            P.op("scalar", lambda e: e.activation(sq, src[:, 0:kt, 0:nt], AF.Square), reads=[srckey], writes=["sq"])
            P.group("tensor", [lambda e, k=k: e.matmul(pt[:, 0:nt], lhsT=ones_bf, rhs=sq[:, k, :],
                                                        start=(k == 0), stop=(k == kt - 1)) for k in range(kt)],
                    reads=["sq", "cbf"], writes=[pk])
            P.op("scalar", lambda e: e.activation(tmpA[:, 0:nt], pt[:, 0:nt], AF.Sqrt, bias=RMS_EPS, scale=1.0 / D),
                 reads=[pk], writes=["tmpA"])
            P.op("vector", lambda e: e.reciprocal(rstd[:, 0:nt], tmpA[:, 0:nt]), reads=["tmpA"], writes=["rstd"])
            for k in range(kt):
                P.op("vector", lambda e, k=k: e.scalar_tensor_tensor(
                    out=dst[:, k, 0:nt], in0=src[:, k, 0:nt], scalar=vec[:, gcol + k:gcol + k + 1],
                    in1=rstd[:, 0:nt], op0=ALU.mult, op1=ALU.mult),
                     reads=[srckey, "rstd", "vec"], writes=[(dstkey, k)])

        def ffn(Wg, Wu, Wd, gcol, nt, keep_res):
            rmsnorm(xh, gcol, nt, xn, "xh", "xn")
            act = arena[:, 0:FC * NTM // 2].bitcast(BF16).rearrange("p (c n) -> p c n", n=NTM)
            for half in range(2):
                c_lo, c_hi = (0, 22) if half == 0 else (22, FC)
                for cb in range(c_lo, c_hi, 2):
                    ncb = min(2, c_hi - cb)
                    wg, wgk = wload(Wg, 0, D, cb * 128, ncb * 128)
                    wu, wuk = wload(Wu, 0, D, cb * 128, ncb * 128)
                    for ci in range(ncb):
                        c = cb + ci
                        pg, pgk = ps()
                        pu, puk = ps()
                        P.group("tensor", [lambda e, k=k, ci=ci: e.matmul(
                            pg[:, 0:nt], lhsT=wg[:, k, ci * 128:(ci + 1) * 128], rhs=xn[:, k, 0:nt],
                            start=(k == 0), stop=(k == KT - 1)) for k in range(KT)], reads=[wgk, "xn"], writes=[pgk])
                        P.group("tensor", [lambda e, k=k, ci=ci: e.matmul(
                            pu[:, 0:nt], lhsT=wu[:, k, ci * 128:(ci + 1) * 128], rhs=xn[:, k, 0:nt],
                            start=(k == 0), stop=(k == KT - 1)) for k in range(KT)], reads=[wuk, "xn"], writes=[puk])
                        P.op("scalar", lambda e: e.activation(tmpB[:, 0:nt], pg[:, 0:nt], AF.Silu),
                             reads=[pgk], writes=["tmpB"])
                        P.op("vector", lambda e, c=c: e.tensor_tensor(act[:, c - c_lo, 0:nt], tmpB[:, 0:nt],
                                                                       pu[:, 0:nt], op=ALU.mult),
                             reads=["tmpB", puk], writes=[("act", c)])
                nch = c_hi - c_lo
                for dcol in range(KT):
                    wd, wdk = wload(Wd, c_lo * 128, nch * 128, dcol * 128, 128)
                    pd, pdk = ps()
                    P.group("tensor", [lambda e, c=c: e.matmul(
                        pd[:, 0:nt], lhsT=wd[:, c, :], rhs=act[:, c, 0:nt],
                        start=(c == 0), stop=(c == nch - 1)) for c in range(nch)],
                            reads=[wdk, "act"], writes=[pdk])
                    P.op("vector", lambda e, dcol=dcol: e.scalar_tensor_tensor(
                        out=xh[:, dcol, 0:nt], in0=pd[:, 0:nt], scalar=0.5, in1=xh[:, dcol, 0:nt],
                        op0=ALU.mult, op1=ALU.add), reads=[pdk, ("xh", dcol)], writes=[("xh", dcol)])
                if half == 0:
                    pass
            P.fence()

        def proj(W, c0, ncols_total, rhs, rhskey, nt, kt, sink, wrows=None):
            wrows = wrows or kt * 128
            nchunks = (ncols_total + 127) // 128
            j = 0
            while j < nchunks:
                nb = min(2, nchunks - j)
                ncols = min(ncols_total - j * 128, nb * 128)
                wv, wk = wload(W, 0, wrows, c0 + j * 128, ncols)
                for ci in range(nb):
                    m = min(128, ncols - ci * 128)
                    pt, pk = ps()
                    P.group("tensor", [lambda e, k=k, ci=ci, m=m, pt=pt: e.matmul(
                        pt[0:m, 0:nt], lhsT=wv[:, k, ci * 128:ci * 128 + m], rhs=rhs[:, k, 0:nt],
                        start=(k == 0), stop=(k == kt - 1)) for k in range(kt)], reads=[wk, rhskey], writes=[pk])
                    sink(j + ci, pt, pk, m)
                j += nb

        def mem_kv():
            mx = arena[:, 0:KT * NMEM].rearrange("p (k n) -> p k n", n=NMEM)
            P.dma("sync", mx, memT.rearrange("(k p) n -> p k n", p=128), writes=["mx"])
            mxn = arena[:, 4096:4096 + KT * NMEM // 2].bitcast(BF16).rearrange("p (k n) -> p k n", n=NMEM)
            rmsnorm(mx, G_MEM, NMEM, mxn, "mx", "mxn")
            if _small == "norm":
                P.fence()
                return
            kf = arena[:, 8192:8192 + 4 * NMEM].rearrange("p (k n) -> p k n", n=NMEM)

            def sink_k(j, pt, pk, m):
                P.op("vector", lambda e: e.tensor_copy(kf[:, j, :], pt[:, 0:NMEM]), reads=[pk], writes=[("kf", j)])
                if True:
                    P.op("vector", lambda e: e.tensor_copy(KTb[:, j, :], pt[:, 0:NMEM]), reads=[pk], writes=[("KTb", j)])
                else:
                    P.op("scalar", lambda e: e.activation(KTb[:, j, :], pt[:, 0:NMEM], AF.Copy), reads=[pk],
                         writes=[("KTb", j)])
            proj(wmk, 0, 512, mxn, "mxn", NMEM, KT, sink_k)
            P.dma("sync", o_memkT, kf, reads=["kf"])
            if _small == "k":
                P.fence()
                return
            vf = arena[:, 12288:12288 + 1024].rearrange("p (k n) -> p k n", n=512)
            pts = [ps(), ps()]
            for cb in range(2):
                wv, wk = wload(wmv, 0, D, cb * 256, 256)
                for mt in range(2):
                    pt, pk = pts[mt]
                    for k in range(KT):
                        P.op("tensor", lambda e, k=k, mt=mt, pt=pt, cb=cb: e.matmul(
                            pt[:, cb * 256:(cb + 1) * 256], lhsT=mxn[:, k, mt * 128:(mt + 1) * 128], rhs=wv[:, k, :],
                            start=(k == 0), stop=(k == KT - 1)), reads=[wk, "mxn"], writes=[(pk, cb)])
            for mt in range(2):
                pt, pk = pts[mt]
                P.op("vector", lambda e, mt=mt, pt=pt: e.tensor_copy(vf[:, mt, :], pt[:, :]), reads=[pk],
                     writes=[("vf", mt)])
                P.op("vector", lambda e, mt=mt, pt=pt: e.tensor_copy(Vb[:, mt, :], pt[:, :]), reads=[pk],
                     writes=[("Vb", mt)])
            P.dma("sync", o_memv, vf, reads=["vf"])
            P.fence()


        A = lambda off, n: arena[:, off:off + n]
        AB = lambda off, n: arena[:, off:off + (n + 1) // 2].bitcast(BF16)[:, 0:n]
        v3 = lambda ap, n: ap.rearrange("p (k n) -> p k n", n=n)
        SCALE = float(128 ** -0.5)
        zc = sb("zc", [128, 27, 1])
        PCb = sb("PCb", [128, 8])
        ypb = [(psb[6], "pb6"), (psb[7], "pb7")]

        def evac(i, out, in_, reads, writes):
            if i % 2 == 0:
                P.op("vector", lambda e: e.tensor_copy(out, in_), reads=reads, writes=writes)
            else:
                P.op("scalar", lambda e: e.activation(out, in_, AF.Identity), reads=reads, writes=writes)

        def token_shift(last):
            D3 = v3(A(0, 27 * NTB), NTB)
            P.op("vector", lambda e: e.tensor_copy(zc[:], zr[:, :, NTB:NTB + 1]), reads=["zr"], writes=["zc"])
            mub = vec[:, V_MU:V_MU + 27].unsqueeze(2).to_broadcast([128, 27, NTB])
            P.op("vector", lambda e: e.tensor_tensor(D3, zr[:, :, 0:NTB], zr[:, :, 1:1 + NTB], op=ALU.subtract),
                 reads=["zr"], writes=["D3"])
            P.op("vector", lambda e: e.tensor_tensor(D3, D3, mub, op=ALU.mult), reads=["D3", "vec"], writes=["D3"])
            P.op("vector", lambda e: e.tensor_tensor(zr[:, :, 1:1 + NTB], zr[:, :, 1:1 + NTB], D3, op=ALU.add),
                 reads=["D3", "zr", "zc"], writes=["zr"])
            if last:
                sshift = v3(A(7424, 27 * NS), NS)
                P.dma("sync", sshift, sshiftT.rearrange("(k p) n -> p k n", p=128), writes=["sshift"])
                Ds = v3(A(27 * NTB, 27 * NS), NS)
                mus = vec[:, V_MU:V_MU + 27].unsqueeze(2).to_broadcast([128, 27, NS])
                zs = zr[:, :, 1 + NTB:1 + NTM]
                P.op("vector", lambda e: e.tensor_tensor(Ds, sshift, zs, op=ALU.subtract), reads=["zr", "sshift"], writes=["Ds"])
                P.op("vector", lambda e: e.tensor_tensor(Ds, Ds, mus, op=ALU.mult), reads=["Ds", "vec"], writes=["Ds"])
                P.op("vector", lambda e: e.tensor_tensor(zs, zs, Ds, op=ALU.add), reads=["Ds", "zr"], writes=["zr"])
            P.fence()

        def rwkv_elem(cs, n, sample):
            W8 = 8 * n
            xk = zr[:, 8:16, cs]
            sg = A(0, 1024)
            asig, kkn, kmod, bbv = v3(A(1024, W8), n), v3(A(2048, W8), n), v3(A(3072, W8), n), v3(A(4096, W8), n)
            et, t1 = v3(A(5120, W8), n), v3(A(6144, W8), n)
            th = A(8192, 128)
            vcol = lambda c: vec[:, c:c + 8].unsqueeze(2).to_broadcast([128, 8, n])
            P.op("scalar", lambda e: e.activation(th[0:64, 0:n], zr[0:64, 24, cs], AF.Tanh), reads=["zr"], writes=["th"])

            def fm_lora(prow, roff, rhs_ap, sink):
                for hb in range(2):
                    pt, pk = ps()
                    for c in range(4):
                        ch = hb * 4 + c
                        P.op("tensor", lambda e, pt=pt, c=c, ch=ch: e.matmul(
                            pt[:, c * n:(c + 1) * n], lhsT=lora[prow, ch * 128:(ch + 1) * 128], rhs=rhs_ap,
                            start=True, stop=False), reads=["lora", "zr", "th"], writes=[pk])
                        P.op("tensor", lambda e, pt=pt, c=c, ch=ch: e.matmul(
                            pt[:, c * n:(c + 1) * n], lhsT=rows_sb[:, roff + ch * 128:roff + (ch + 1) * 128],
                            rhs=ones_row[:, 0:n], start=False, stop=True), reads=["rows", "cst"], writes=[pk])
                    sink(hb, v3(pt[:, 0:4 * n], n), pk)

            if sample:
                sgf = v3(A(0, W8), n)

                def sink_w(hb, p3, pk):
                    P.op("scalar", lambda e: e.activation(sgf[:, hb * 4:(hb + 1) * 4, :], p3, AF.Sigmoid), reads=[pk], writes=["sg"])
                fm_lora(slice(0, 64), 0, th[0:64, 0:n], sink_w)
                P.op("scalar", lambda e: e.activation(sgf, sgf, AF.Exp, scale=-EDEC), reads=["sg"], writes=["sg"])
            else:
                for hb in range(2):
                    pt, pk = ps()
                    P.op("tensor", lambda e, pt=pt, hb=hb: e.matmul(pt[:, :], lhsT=th[0:64, 0:128], rhs=lora[0:64, hb * 512:(hb + 1) * 512],
                                                       start=True, stop=False), reads=["th", "lora"], writes=[pk])
                    P.op("tensor", lambda e, pt=pt, hb=hb: e.matmul(pt[:, :], lhsT=ones_row[:, 0:128], rhs=rows_sb[:, hb * 512:(hb + 1) * 512],
                                                       start=False, stop=True), reads=["cst", "rows"], writes=[pk])
                    P.op("scalar", lambda e, pt=pt, hb=hb: e.activation(sg[:, hb * 512:(hb + 1) * 512], pt[:, :], AF.Sigmoid),
                         reads=[pk], writes=["sg"])

            def sink_a(hb, p3, pk):
                P.op("scalar", lambda e: e.activation(asig[:, hb * 4:(hb + 1) * 4, :], p3, AF.Sigmoid), reads=[pk], writes=["asig"])
            fm_lora(slice(64, 128), 1024, zr[64:128, 24, cs], sink_a)
            P.op("vector", lambda e: e.tensor_tensor(t1, xk, vcol(V_KK), op=ALU.mult), reads=["zr", "vec"], writes=["t1"])
            P.op("scalar", lambda e: e.activation(et, t1, AF.Square), reads=["t1"], writes=["et"])
            for hb in range(2):
                pt, pk = ps()
                for c in range(4):
                    ch = hb * 4 + c
                    P.op("tensor", lambda e, pt=pt, c=c, ch=ch: e.matmul(pt[:, c * n:(c + 1) * n], lhsT=bones, rhs=et[:, ch, :],
                                                       start=True, stop=True), reads=["cst", "et"], writes=[pk])
                P.op("vector", lambda e, pt=pt, hb=hb: e.tensor_scalar_max(kkn[:, hb * 4:(hb + 1) * 4, :], v3(pt[:, 0:4 * n], n), 1e-24),
                     reads=[pk], writes=["kkn"])
            P.op("scalar", lambda e: e.activation(kkn, kkn, AF.Sqrt), reads=["kkn"], writes=["kkn"])
            P.op("vector", lambda e: e.reciprocal(kkn, kkn), reads=["kkn"], writes=["kkn"])
            P.op("vector", lambda e: e.tensor_tensor(kkn, kkn, t1, op=ALU.mult), reads=["kkn", "t1"], writes=["kkn"])
            P.op("vector", lambda e: e.scalar_tensor_tensor(out=et, in0=asig, scalar=-1.0, in1=vcol(V_KA), op0=ALU.add, op1=ALU.mult),
                 reads=["asig", "vec", "et"], writes=["et"])
            P.op("vector", lambda e: e.scalar_tensor_tensor(out=kmod, in0=et, scalar=1.0, in1=xk, op0=ALU.add, op1=ALU.mult),
                 reads=["et", "zr"], writes=["kmod"])
            P.op("vector", lambda e: e.tensor_tensor(bbv, kkn, asig, op=ALU.mult), reads=["kkn", "asig"], writes=["bbv"])

        def scan_prompt(cs, want_y):
            n = 128
            xr, xv = zr[:, 0:8, cs], zr[:, 16:24, cs]
            sg = A(0, 1024)
            kkn, kmod, bbv = v3(A(2048, 1024), n), v3(A(3072, 1024), n), v3(A(4096, 1024), n)
            et = v3(A(5120, 1024), n)
            b3 = lambda off: v3(AB(off, 1024), n)
            Afm, Bfm, Kfm, Rfm = b3(8320), b3(8832), b3(9344), b3(9856)
            AT, BhT, KhT, VT = AB(10368, 1024), AB(10880, 1024), AB(11392, 1024), AB(11904, 1024)
            tb = b3(12416)
            hs = lambda x, hb: x[:, hb * 4:(hb + 1) * 4, :]

            def cum(tri, sink):
                for hb in range(2):
                    pt, pk = ps()
                    for c in range(4):
                        ch = hb * 4 + c
                        P.op("tensor", lambda e, pt=pt, c=c, ch=ch: e.matmul(pt[:, c * 128:(c + 1) * 128], lhsT=sg[:, ch * 128:(ch + 1) * 128], rhs=tri,
                                                           start=True, stop=True), reads=["sg", "cst"], writes=[pk])
                    sink(hb, v3(pt[:, :], n), pk)

            def sink_incl(hb, p3, pk):
                if want_y:
                    P.op("scalar", lambda e: e.activation(hs(et, hb), p3, AF.Exp), reads=[pk], writes=["et"])
                    P.op("vector", lambda e: e.tensor_tensor(hs(Rfm, hb), hs(xr, hb), hs(et, hb), op=ALU.mult),
                         reads=["et", "zr"], writes=["Rfm"])
                P.op("scalar", lambda e: e.activation(PCb[:, hb * 4:(hb + 1) * 4], p3[:, :, 127], AF.Exp), reads=[pk], writes=["PCb"])
                P.op("scalar", lambda e: e.activation(hs(et, hb), p3, AF.Exp, scale=-1.0), reads=[pk, "Rfm"], writes=["et"])
                P.op("vector", lambda e: e.tensor_tensor(hs(Bfm, hb), hs(bbv, hb), hs(et, hb), op=ALU.mult),
                     reads=["et", "bbv"], writes=["Bfm"])
                P.op("vector", lambda e: e.tensor_tensor(hs(Kfm, hb), hs(kmod, hb), hs(et, hb), op=ALU.mult),
                     reads=["et", "kmod"], writes=["Kfm"])
            cum(triI, sink_incl)

            def sink_excl(hb, p3, pk):
                P.op("scalar", lambda e: e.activation(hs(et, hb), p3, AF.Exp), reads=[pk, "Bfm", "Kfm"], writes=["et"])
                P.op("vector", lambda e: e.scalar_tensor_tensor(out=hs(Afm, hb), in0=hs(kkn, hb), scalar=-1.0, in1=hs(et, hb),
                                                                 op0=ALU.mult, op1=ALU.mult), reads=["et", "kkn"], writes=["Afm"])
            cum(triX, sink_excl)

            def tr_to(src3, srckey, dst, dstkey):
                for hb in range(2):
                    pt, pk = ps()
                    for c in range(4):
                        ch = hb * 4 + c
                        P.op("tensor", lambda e, pt=pt, c=c, ch=ch: e.matmul(pt[:, c * 128:(c + 1) * 128], lhsT=src3[:, ch, :], rhs=ident_bf,
                                                           start=True, stop=True), reads=[srckey, "cbf"], writes=[pk])
                    evac(hb, dst[:, hb * 512:(hb + 1) * 512], pt[:, :], [pk], [dstkey])
            tr_to(Afm, "Afm", AT, "AT")

            def sink_suf(hb, p3, pk):
                P.op("scalar", lambda e: e.activation(hs(et, hb), p3, AF.Exp), reads=[pk, "Afm"], writes=["et"])
                P.op("vector", lambda e: e.tensor_tensor(hs(tb, hb), hs(bbv, hb), hs(et, hb), op=ALU.mult),
                     reads=["et", "bbv"], writes=["tb"])
            cum(triS, sink_suf)
            tr_to(tb, "tb", BhT, "BhT")
            P.op("vector", lambda e: e.tensor_tensor(tb, kmod, et, op=ALU.mult), reads=["et", "kmod", "BhT", "tb"], writes=["tb"])
            tr_to(tb, "tb", KhT, "KhT")
            P.op("vector", lambda e: e.tensor_copy(tb, xv), reads=["zr", "KhT", "tb"], writes=["tb"])
            tr_to(tb, "tb", VT, "VT")

            GOFF = [13056, 13568, 14080, 14592, 15104, 15616, 0, 512, 1024, 1536, 5120]
            gm = lambda i: AB(GOFF[i], 1024)
            UT = AB(16128, 512)
            pos = lambda i: (i % 2) * 4 + i // 2
            blk = lambda i: slice(pos(i) * 128, (pos(i) + 1) * 128)
            yfm = v3(A(7168, 1024), n)
            P.fence()

            def mm8(dst, dk, lt, ltk, rt, rtk, add=None, addk=None):
                bank = [ps(), ps()]
                for i in range(8):
                    pt, pk = bank[i // 4]
                    P.op("tensor", lambda e, pt=pt, i=i: e.matmul(pt[:, (i % 4) * 128:(i % 4 + 1) * 128], lhsT=lt[:, i * 128:(i + 1) * 128],
                                                      rhs=rt[:, i * 128:(i + 1) * 128], start=True, stop=True),
                         reads=[ltk, rtk], writes=[pk])
                for hb in range(2):
                    pt, pk = bank[hb]
                    dk_ = (dk, hb)
                    if add is None:
                        evac(hb, dst[:, hb * 512:(hb + 1) * 512], pt[:, :], [pk], [dk_])
                    else:
                        P.op("vector", lambda e, pt=pt, hb=hb: e.tensor_tensor(dst[:, hb * 512:(hb + 1) * 512], pt[:, :],
                                                                  add[:, hb * 512:(hb + 1) * 512], op=ALU.add),
                             reads=[pk, addk], writes=[dk_])

            for G in range(2):
                heads = [8 * G + i for i in range(8)]
                hrow = lambda x3, h: x3[(h % 2) * 64:(h % 2) * 64 + 64, h // 2, :]
                NakT, Mrb, Mrk, Apf, Nakp = gm(6), gm(7), gm(8), gm(9), gm(10)
                specs = [(gm(0), "g0", Bfm, "Bfm", Afm, "Afm", mSU), (gm(1), "g1", Afm, "Afm", Bfm, "Bfm", mSL),
                         (NakT, "g6", Afm, "Afm", Kfm, "Kfm", mSL)]
                if want_y:
                    specs += [(Mrb, "g7", Bfm, "Bfm", Rfm, "Rfm", mIU), (Mrk, "g8", Kfm, "Kfm", Rfm, "Rfm", mIU)]
                for (dst, dk, la, lak, ra, rak, msk) in specs:
                    bank = [ps(), ps()]
                    for i, h in enumerate(heads):
                        pt, pk = bank[h % 2]
                        P.op("tensor", lambda e, pt=pt, i=i, h=h, la=la, ra=ra: e.matmul(
                            pt[:, (i // 2) * 128:(i // 2 + 1) * 128], lhsT=hrow(la, h), rhs=hrow(ra, h), start=True, stop=True),
                             reads=[lak, rak], writes=[pk])
                    for par in range(2):
                        pt, pk = bank[par]
                        P.op("vector", lambda e, pt=pt, dst=dst, msk=msk, par=par: e.tensor_tensor(
                            dst[:, par * 512:(par + 1) * 512], pt[:, :], msk, op=ALU.mult),
                             reads=[pk, "cst"], writes=[(dk, par)])
                Nc, Nck, Lc, Lck = gm(0), "g0", gm(1), "g1"
                Nn, Nnk, Ln, Lnk = gm(2), "g2", gm(3), "g3"
                Tc, Tck, Tn, Tnk = gm(4), "g4", gm(5), "g5"
                for hb in range(2):
                    P.op("vector", lambda e, hb=hb: e.tensor_tensor(Tc[:, hb * 512:(hb + 1) * 512], Nc[:, hb * 512:(hb + 1) * 512], I4, op=ALU.add),
                         reads=[Nck, "cst"], writes=[(Tck, hb)])
                for k in range(1, 7):
                    if k < 6:
                        mm8(Nn, Nnk, Lc, Lck, Nc, Nck)
                    mm8(Ln, Lnk, Nc, Nck, Lc, Lck)
                    mm8(Tn, Tnk, Ln, Lnk, Tc, Tck, add=Tc, addk=Tck)
                    Nc, Nck, Nn, Nnk = Nn, Nnk, Nc, Nck
                    Lc, Lck, Ln, Lnk = Ln, Lnk, Lc, Lck
                    Tc, Tck, Tn, Tnk = Tn, Tnk, Tc, Tck
                T_, Tk = Tc, Tck
                bank = [ps(), ps()]
                for i, h in enumerate(heads):
                    pt, pk = bank[h % 2]
                    P.op("tensor", lambda e, pt=pt, i=i, h=h: e.matmul(
                        pt[(h % 2) * 64:(h % 2) * 64 + 64, (i // 2) * 128:(i // 2 + 1) * 128],
                        lhsT=AT[:, h * 64:(h + 1) * 64], rhs=T_[:, blk(i)], start=True, stop=True),
                         reads=["AT", Tk], writes=[pk])
                for par in range(2):
                    pt, pk = bank[par]
                    rs = slice(par * 64, par * 64 + 64)
                    evac(par, Apf[rs, 0:512], pt[rs, 0:512], [pk], [("g9", par)])
                mm8(Nakp, "g10", NakT, "g6", T_, Tk)
                bank = [ps(), ps()]
                for i, h in enumerate(heads):
                    hp, h2 = h // 2, h % 2
                    rs = slice(h2 * 64, h2 * 64 + 64)
                    pt, pk = bank[h2]
                    P.op("tensor", lambda e, pt=pt, i=i, hp=hp, rs=rs: e.matmul(
                        pt[:, (i // 2) * 64:(i // 2 + 1) * 64], lhsT=Apf[rs, (i // 2) * 128:(i // 2 + 1) * 128],
                        rhs=Sbf[rs, hp, :], start=True, stop=False), reads=["g9", "Sbf"], writes=[pk])
                    P.op("tensor", lambda e, pt=pt, i=i, h=h: e.matmul(
                        pt[:, (i // 2) * 64:(i // 2 + 1) * 64], lhsT=Nakp[:, blk(i)],
                        rhs=VT[:, h * 64:(h + 1) * 64], start=False, stop=True), reads=["g10", "VT"], writes=[pk])
                for par in range(2):
                    pt, pk = bank[par]
                    evac(par, UT[:, par * 256:(par + 1) * 256], pt[:, 0:256], [pk], [("UT", par)])
                UTb = lambda i: UT[:, pos(i) * 64:(pos(i) + 1) * 64]
                if want_y:
                    bank = [ps(), ps()]
                    for i, h in enumerate(heads):
                        hp, h2 = h // 2, h % 2
                        rs = slice(h2 * 64, h2 * 64 + 64)
                        pt, pk = bank[h2]
                        yo = pt[rs, (i // 2) * 128:(i // 2 + 1) * 128]
                        P.op("tensor", lambda e, yo=yo, rs=rs, hp=hp: e.matmul(yo, lhsT=Sbf[rs, hp, :], rhs=Rfm[rs, hp, :], start=True, stop=False),
                             reads=["Sbf", "Rfm"], writes=[pk])
                        P.op("tensor", lambda e, yo=yo, i=i: e.matmul(yo, lhsT=UTb(i), rhs=Mrb[:, blk(i)], start=False, stop=False),
                             reads=["UT", "g7"], writes=[pk])
                        P.op("tensor", lambda e, yo=yo, i=i, h=h: e.matmul(yo, lhsT=VT[:, h * 64:(h + 1) * 64], rhs=Mrk[:, blk(i)], start=False, stop=True),
                             reads=["VT", "g8"], writes=[pk])
                    for par in range(2):
                        pt, pk = bank[par]
                        rs = slice(par * 64, par * 64 + 64)
                        evac(par, yfm[rs, 4 * G:4 * G + 4, :], v3(pt[rs, 0:512], 128), [pk], [("yfm", G * 2 + par)])
                bank = [ps(), ps()]
                for i, h in enumerate(heads):
                    h2 = h % 2
                    rs = slice(h2 * 64, h2 * 64 + 64)
                    pt, pk = bank[h2]
                    so = pt[rs, (i // 2) * 64:(i // 2 + 1) * 64]
                    P.op("tensor", lambda e, so=so, i=i, h=h: e.matmul(so, lhsT=BhT[:, h * 64:(h + 1) * 64], rhs=UTb(i), start=True, stop=False),
                         reads=["BhT", "UT"], writes=[pk])
                    P.op("tensor", lambda e, so=so, h=h: e.matmul(so, lhsT=KhT[:, h * 64:(h + 1) * 64], rhs=VT[:, h * 64:(h + 1) * 64], start=False, stop=True),
                         reads=["KhT", "VT"], writes=[pk])
                hps = slice(4 * G, 4 * G + 4)
                P.op("vector", lambda e, hps=hps: e.tensor_tensor(S32[:, hps, :], S32[:, hps, :],
                                                                   PCb[:, hps].unsqueeze(2).to_broadcast([128, 4, 64]), op=ALU.mult),
                     reads=["PCb", "S32", "Sbf"], writes=["S32"])
                for par in range(2):
                    pt, pk = bank[par]
                    rs = slice(par * 64, par * 64 + 64)
                    P.op("vector", lambda e, hps=hps, pt=pt, rs=rs: e.tensor_tensor(S32[rs, hps, :], S32[rs, hps, :], v3(pt[rs, 0:256], 64), op=ALU.add),
                         reads=[pk, "S32"], writes=["S32"])
                P.op("vector", lambda e, hps=hps: e.tensor_copy(Sbf[:, hps, :], S32[:, hps, :]), reads=["S32", "Sbf"], writes=["Sbf"])
            P.fence()

        def rwkv_post(cs, n, ycol0, from_psum):
            W8 = 8 * n
            xr, xv = zr[:, 0:8, cs], zr[:, 16:24, cs]
            rstdv, kmod, bbv = v3(A(1024, W8), n), v3(A(3072, W8), n), v3(A(4096, W8), n)
            et, t1, yfm = v3(A(5120, W8), n), v3(A(6144, W8), n), v3(A(7168, W8), n)
            sgl = v3(AB(12928, 256), 128)
            vcol = lambda c: vec[:, c:c + 8].unsqueeze(2).to_broadcast([128, 8, n])
            hs = lambda x, hb: x[:, hb * 4:(hb + 1) * 4, :]
            if from_psum:
                for hb in range(2):
                    evac(hb, hs(yfm, hb), v3(ypb[hb][0][:, :], n), [ypb[hb][1]], ["yfm"])

            def bmm(lhs, src, srck, sink):
                for hb in range(2):
                    pt, pk = ps()
                    for c in range(4):
                        ch = hb * 4 + c
                        P.op("tensor", lambda e, pt=pt, c=c, ch=ch: e.matmul(pt[:, c * n:(c + 1) * n], lhsT=lhs, rhs=src[:, ch, :],
                                                           start=True, stop=True), reads=["cst", srck], writes=[pk])
                    sink(hb, v3(pt[:, 0:4 * n], n), pk)
            bmm(bones64, yfm, "yfm", lambda hb, p3, pk: P.op(
                "vector", lambda e: e.tensor_tensor(hs(t1, hb), hs(yfm, hb), p3, op=ALU.subtract), reads=["yfm", pk], writes=["t1"]))
            P.op("scalar", lambda e: e.activation(et, t1, AF.Square), reads=["t1"], writes=["et"])
            bmm(bones64, et, "et", lambda hb, p3, pk: P.op(
                "scalar", lambda e: e.activation(hs(rstdv, hb), p3, AF.Sqrt, bias=GN_EPS, scale=1.0), reads=[pk], writes=["rstdv"]))
            P.op("vector", lambda e: e.reciprocal(rstdv, rstdv), reads=["rstdv"], writes=["rstdv"])
            P.op("vector", lambda e: e.tensor_tensor(t1, t1, rstdv, op=ALU.mult), reads=["t1", "rstdv"], writes=["t1"])
            P.op("vector", lambda e: e.tensor_tensor(t1, t1, vcol(V_LG), op=ALU.mult), reads=["t1", "vec"], writes=["t1"])
            P.op("vector", lambda e: e.tensor_tensor(t1, t1, vcol(V_LB), op=ALU.add), reads=["t1", "vec"], writes=["t1"])
            P.op("vector", lambda e: e.tensor_tensor(et, xr, kmod, op=ALU.mult), reads=["zr", "kmod", "et"], writes=["et"])
            P.op("vector", lambda e: e.tensor_tensor(et, et, vcol(V_RK), op=ALU.mult), reads=["et", "vec"], writes=["et"])
            bmm(bones, et, "et", lambda hb, p3, pk: P.op(
                "vector", lambda e: e.tensor_tensor(hs(bbv, hb), p3, hs(xv, hb), op=ALU.mult), reads=[pk, "zr", "bbv"], writes=["bbv"]))
            P.op("vector", lambda e: e.tensor_tensor(t1, t1, bbv, op=ALU.add), reads=["t1", "bbv"], writes=["t1"])
            P.op("scalar", lambda e: e.activation(sgl[:, 0, 0:n], zr[:, 25, cs], AF.Sigmoid), reads=["zr"], writes=["sgl"])
            P.op("scalar", lambda e: e.activation(sgl[0:32, 1, 0:n], zr[0:32, 26, cs], AF.Sigmoid), reads=["zr", "sgl"], writes=["sgl"])
            for hb in range(2):
                pt, pk = ps()
                for c in range(4):
                    ch = hb * 4 + c
                    P.op("tensor", lambda e, pt=pt, c=c, ch=ch: e.matmul(pt[:, c * n:(c + 1) * n], lhsT=gupb[:, 0, ch * 128:(ch + 1) * 128],
                                                       rhs=sgl[:, 0, 0:n], start=True, stop=False), reads=["gupb", "sgl"], writes=[pk])
                    P.op("tensor", lambda e, pt=pt, c=c, ch=ch: e.matmul(pt[:, c * n:(c + 1) * n], lhsT=gupb[0:32, 1, ch * 128:(ch + 1) * 128],
                                                       rhs=sgl[0:32, 1, 0:n], start=False, stop=True), reads=["gupb", "sgl"], writes=[pk])
                P.op("vector", lambda e, pt=pt, hb=hb: e.tensor_tensor(yg[:, hb * 4:(hb + 1) * 4, ycol0:ycol0 + n], hs(t1, hb),
                                                          v3(pt[:, 0:4 * n], n), op=ALU.mult), reads=[pk, "t1"], writes=["yg"])

        def sample_scan():
            n = NS
            cs = slice(1 + NTB, 1 + NTM)
            xr, xv = zr[:, 0:8, cs], zr[:, 16:24, cs]
            sgf, kkn, kmod, bbv = v3(A(0, 128), n), v3(A(2048, 128), n), v3(A(3072, 128), n), v3(A(4096, 128), n)
            Vs = A(5120, 256)[0:64, :].rearrange("p (h n) -> p h n", n=n)
            Ys = A(5376, 256)[0:64, :].rearrange("p (h n) -> p h n", n=n)
            rowb = [arena[0:16, 7168:8192], arena[0:16, 9216:10240]]
            ops = [(kkn, "kkn", -1.0), (sgf, "sg", 1.0), (bbv, "bbv", 1.0), (kmod, "kmod", 1.0), (xr, "zr", 1.0)]
            for oi, (X, xk_, sc) in enumerate(ops):
                rb = rowb[oi % 2]
                rbk = f"rowb{oi % 2}"
                for hb in range(2):
                    pt, pk = ps()
                    for c in range(4):
                        ch = hb * 4 + c
                        P.op("tensor", lambda e, pt=pt, c=c, ch=ch, X=X: e.matmul(pt[0:16, c * 128:(c + 1) * 128], lhsT=X[:, ch, :], rhs=ident,
                                                                start=True, stop=True), reads=[xk_, "cst"], writes=[pk])
                    P.op("scalar", lambda e, pt=pt, hb=hb, rb=rb, sc=sc: e.activation(rb[:, hb * 512:(hb + 1) * 512], pt[0:16, :], AF.Identity, scale=sc),
                         reads=[pk], writes=[rbk])
                P.dma("sync", scr_rows[oi], rb, reads=[rbk], writes=["scr_rows"])
            Vs4 = Vs.rearrange("p (hp h2) n -> p hp h2 n", h2=2)
            P.op("vector", lambda e: e.tensor_copy(Vs4[:, :, 0, :], xv[0:64, :, :]), reads=["zr"], writes=["Vs"])
            P.dma("sync", Vs4[:, :, 1, :], xv[64:128, :, :], reads=["zr"], writes=["Vs"])
            P.fence()
            S = v3(A(8192, 1024)[0:64, :], 64)
            bc = [v3(A(9216 + i * 1024, 1024)[0:64, :], 64) for i in range(5)]
            tt = v3(A(14336, 1024)[0:64, :], 64)
            S1 = v3(A(15360, 1024)[0:64, :], 64)
            sa = A(7168, 16)[0:64, :]
            for s_ in range(NS):
                P.dma("sync", S, swkv[:, s_], writes=["S"])
                for i in range(5):
                    P.dma("sync", A(9216 + i * 1024, 1024)[0:64, :], scr_rows[i, s_:s_ + 1, :].to_broadcast([64, 1024]),
                          writes=[f"bc{i}"])
                a_, w_, b_, k_, r_ = bc
                P.op("vector", lambda e: e.tensor_tensor(tt, S, a_, op=ALU.mult), reads=["S", "bc0", "tt"], writes=["tt"])
                P.op("vector", lambda e: e.tensor_reduce(out=sa, in_=tt, axis=AX.X, op=ALU.add), reads=["tt"], writes=["sa"])
                P.op("vector", lambda e: e.tensor_tensor(S1, S, w_, op=ALU.mult), reads=["S", "bc1", "S1"], writes=["S1"])
                P.op("vector", lambda e: e.tensor_tensor(tt, b_, sa.unsqueeze(2).to_broadcast([64, 16, 64]), op=ALU.mult),
                     reads=["bc2", "sa", "tt"], writes=["tt"])
                P.op("vector", lambda e: e.tensor_tensor(S1, S1, tt, op=ALU.add), reads=["S1", "tt"], writes=["S1"])
                P.op("vector", lambda e, s_=s_: e.tensor_tensor(tt, k_, Vs[:, :, s_].unsqueeze(2).to_broadcast([64, 16, 64]), op=ALU.mult),
                     reads=["bc3", "Vs", "tt"], writes=["tt"])
                P.op("vector", lambda e: e.tensor_tensor(S1, S1, tt, op=ALU.add), reads=["S1", "tt"], writes=["S1"])
                P.op("vector", lambda e: e.tensor_tensor(tt, S1, r_, op=ALU.mult), reads=["S1", "bc4", "tt"], writes=["tt"])
                P.op("vector", lambda e, s_=s_: e.tensor_reduce(out=Ys[:, :, s_], in_=tt, axis=AX.X, op=ALU.add), reads=["tt"], writes=["Ys"])
                P.dma("sync", o_wkvs[:, s_], S1, reads=["S1"])
            yfm = v3(A(7168, 128), n)
            Ys4 = Ys.rearrange("p (hp h2) n -> p hp h2 n", h2=2)
            P.op("vector", lambda e: e.tensor_copy(yfm[0:64, :, :], Ys4[:, :, 0, :]), reads=["Ys", "sa"], writes=["yfm"])
            P.dma("sync", yfm[64:128, :, :], Ys4[:, :, 1, :], reads=["Ys", "sa"], writes=["yfm"])
            P.fence()

        def pool_branch(t_own, nt, last):
            E = 15 + NTB
            icnt = v3(A(8192, 4 * NTM), NTM)
            P.dma("sync", icnt, invc[t_own:t_own + 1, :].to_broadcast([128, 4 * NTM]).rearrange("p (g n) -> p g n", n=NTM),
                  writes=["icnt"])
            lv = [A(i * 288, E) for i in range(4)]
            pbf = AB(1152, NTM)
            if last:
                spl = A(1408, 4 * NS * 15).rearrange("p (g n t) -> p g n t", g=4, n=NS)
                P.dma("sync", spl, spoolT.rearrange("(g p) n t -> p g n t", p=128), writes=["spl"])
                ssum = A(2368, NS)
            for g in range(4):
                w = 2 << g
                x = zp[:, g, 0:E]
                src = x
                for l in range(g + 1):
                    sh = 1 << l
                    lo = 2 * sh - 1
                    dst = lv[l]
                    P.op("vector", lambda e, dst=dst, src=src, lo=lo, sh=sh: e.tensor_tensor(dst[:, lo:E], src[:, lo:E], src[:, lo - sh:E - sh], op=ALU.add),
                         reads=["zp", "lv"], writes=["lv"])
                    src = dst
                P.op("vector", lambda e, src=src, g=g: e.tensor_tensor(lv[3][:, 15:E] if g < 3 else lv[2][:, 15:E], src[:, 15:E], icnt[:, g, 0:NTB], op=ALU.mult),
                     reads=["lv", "icnt"], writes=["lv"])
                pm = lv[3] if g < 3 else lv[2]
                P.op("vector", lambda e, pm=pm, x=x: e.tensor_tensor(pbf[:, 0:NTB], pm[:, 15:E], x[:, 15:E], op=ALU.subtract),
                     reads=["lv", "zp", "pbf"], writes=["pbf"])
                if last:
                    xs = zp[:, g, 15 + NTB:15 + NTM]
                    P.op("vector", lambda e, g=g, w=w: e.tensor_reduce(out=ssum, in_=spl[:, g, :, 15 - (w - 1):15], axis=AX.X, op=ALU.add),
                         reads=["spl", "ssum"], writes=["ssum"])
                    P.op("vector", lambda e, xs=xs: e.tensor_tensor(ssum, ssum, xs, op=ALU.add), reads=["ssum", "zp"], writes=["ssum"])
                    P.op("vector", lambda e, xs=xs, w=w: e.scalar_tensor_tensor(out=pbf[:, NTB:NTM], in0=ssum, scalar=1.0 / w, in1=xs,
                                                                          op0=ALU.mult, op1=ALU.subtract), reads=["ssum", "zp", "pbf"], writes=["pbf"])
                pt, pk = ps()
                P.op("tensor", lambda e, pt=pt, g=g: e.matmul(pt[:, 0:nt], lhsT=pgwb[:, g, :], rhs=pbf[:, 0:nt], start=True, stop=True),
                     reads=["pgwb", "pbf"], writes=[pk])
                P.op("vector", lambda e, pt=pt, g=g: e.tensor_scalar_mul(mix[:, g, 0:nt], pt[:, 0:nt], vec[:, V_PS + g:V_PS + g + 1]),
                     reads=[pk, "vec"], writes=["mix"])

        def attn_prompt(c0):
            n = 128
            sc = v3(A(4096, 1024), 256)
            pnb = v3(AB(5120, 1024), 256)
            pTb = v3(AB(5632, 1024), 128)
            mx, nb, sm, rs = A(6144, 4), A(6148, 4), A(6152, 4), A(6156, 4)
            for hp_ in range(2):
                pt, pk = ps()
                for hh in range(2):
                    h = hp_ * 2 + hh
                    P.op("tensor", lambda e, pt=pt, hh=hh, h=h: e.matmul(pt[:, hh * 256:(hh + 1) * 256], lhsT=zq[:, h, c0:c0 + n], rhs=KTb[:, h, :],
                                                          start=True, stop=True), reads=["zq", "KTb"], writes=[pk])
                P.op("vector", lambda e, pt=pt, hp_=hp_: e.tensor_reduce(out=mx[:, hp_ * 2:hp_ * 2 + 2], in_=v3(pt[:, :], 256), axis=AX.X, op=ALU.max),
                     reads=[pk, "mx"], writes=["mx"])
                P.op("vector", lambda e, hp_=hp_: e.tensor_scalar_mul(nb[:, hp_ * 2:hp_ * 2 + 2], mx[:, hp_ * 2:hp_ * 2 + 2], -SCALE),
                     reads=["mx", "nb"], writes=["nb"])
                for hh in range(2):
                    h = hp_ * 2 + hh
                    P.op("scalar", lambda e, pt=pt, hh=hh, h=h: e.activation(sc[:, h, :], pt[:, hh * 256:(hh + 1) * 256], AF.Exp, bias=nb[:, h:h + 1],
                                                              scale=SCALE, accum_out=sm[:, h:h + 1]), reads=[pk, "nb", "sc", "sm"], writes=["sc", "sm"])
            P.op("vector", lambda e: e.reciprocal(rs, sm), reads=["sm", "rs"], writes=["rs"])
            P.op("vector", lambda e: e.tensor_tensor(pnb, sc, rs.unsqueeze(2).to_broadcast([128, 4, 256]), op=ALU.mult),
                 reads=["sc", "rs", "pnb"], writes=["pnb"])
            for hp_ in range(2):
                pt, pk = ps()
                for hh in range(2):
                    h = hp_ * 2 + hh
                    for mt in range(2):
                        j = hh * 2 + mt
                        P.op("tensor", lambda e, pt=pt, j=j, h=h, mt=mt: e.matmul(pt[:, j * 128:(j + 1) * 128], lhsT=pnb[:, h, mt * 128:(mt + 1) * 128], rhs=ident_bf,
                                                                start=True, stop=True), reads=["pnb", "cbf"], writes=[pk])
                evac(hp_, pTb[:, hp_ * 4:(hp_ + 1) * 4, :], v3(pt[:, :], 128), [pk, "pTb"], ["pTb"])
            pt, pk = ps()
            for h in range(4):
                for mt in range(2):
                    P.op("tensor", lambda e, pt=pt, h=h, mt=mt: e.matmul(pt[:, h * 128:(h + 1) * 128], lhsT=Vb[:, mt, h * 128:(h + 1) * 128], rhs=pTb[:, h * 2 + mt, :],
                                                          start=(mt == 0), stop=(mt == 1)), reads=["Vb", "pTb"], writes=[pk])
            P.op("vector", lambda e, pt=pt: e.tensor_copy(oT[:, :, c0:c0 + n], v3(pt[:, :], 128)), reads=[pk], writes=["oT"])

        def attn_sample():
            qrow = arena[0:16, 0:512]
            pt, pk = ps()
            for h in range(4):
                P.op("tensor", lambda e, pt=pt, h=h: e.matmul(pt[0:16, h * 128:(h + 1) * 128], lhsT=zq[:, h, NTB:NTM], rhs=ident_bf, start=True, stop=True),
                     reads=["zq", "cbf"], writes=[pk])
            P.op("vector", lambda e, pt=pt: e.tensor_copy(qrow, pt[0:16, :]), reads=[pk], writes=["qrow"])
            P.dma("sync", scr_q, qrow, reads=["qrow"], writes=["scr_q"])
            qbc = v3(A(8192, NS * 512), 512)
            P.dma("sync", qbc, scr_q.rearrange("n c -> (n c)").unsqueeze(0).to_broadcast([128, NS * 512]).rearrange("p (n c) -> p n c", c=512),
                  reads=["scr_q"], writes=["qbc"])
            KV = [v3(A(512 + i * 1024, 1024), 512) for i in range(2)]
            prod = A(2560, 512)
            s_all = v3(A(3072, 128), 64)
            p64 = A(3200, 256)
            pTs = v3(A(3456, 128), 64)
            mx, nb, sm, rs = A(3584, 1), A(3585, 1), A(3586, 1), A(3587, 1)
            for s_ in range(NS):
                kb = KV[s_ % 2]
                kbk = f"kv{s_ % 2}"
                P.dma("sync", kb, kc[s_].rearrange("(mt p) c -> p mt c", p=128), writes=[kbk])
                for mt in range(2):
                    P.op("vector", lambda e, kb=kb, mt=mt, s_=s_: e.tensor_tensor(prod, kb[:, mt, :], qbc[:, s_, :], op=ALU.mult),
                         reads=[kbk, "qbc", "prod"], writes=["prod"])
                    P.op("vector", lambda e, mt=mt, s_=s_: e.tensor_reduce(out=s_all[:, mt, s_ * 4:(s_ + 1) * 4], in_=v3(prod, 128), axis=AX.X, op=ALU.add),
                         reads=["prod", "s_all"], writes=["s_all"])
            pt, pk = ps()
            for mt in range(2):
                P.op("tensor", lambda e, pt=pt, mt=mt: e.matmul(pt[0:64, mt * 128:(mt + 1) * 128], lhsT=s_all[:, mt, :], rhs=ident, start=True, stop=True),
                     reads=["s_all", "cst"], writes=[pk])
            P.op("vector", lambda e, pt=pt: e.tensor_reduce(out=mx[0:64, :], in_=pt[0:64, 0:256], axis=AX.X, op=ALU.max), reads=[pk], writes=["mx"])
            P.op("vector", lambda e: e.tensor_scalar_mul(nb[0:64, :], mx[0:64, :], -SCALE), reads=["mx"], writes=["nb"])
            P.op("scalar", lambda e, pt=pt: e.activation(p64[0:64, :], pt[0:64, 0:256], AF.Exp, bias=nb[0:64, :], scale=SCALE, accum_out=sm[0:64, :]),
                 reads=[pk, "nb"], writes=["p64", "sm"])
            P.op("vector", lambda e: e.reciprocal(rs[0:64, :], sm[0:64, :]), reads=["sm"], writes=["rs"])
            P.op("vector", lambda e: e.tensor_scalar_mul(p64[0:64, :], p64[0:64, :], rs[0:64, :]), reads=["p64", "rs"], writes=["p64"])
            pt2, pk2 = ps()
            for mt in range(2):
                P.op("tensor", lambda e, mt=mt: e.matmul(pt2[:, mt * 64:(mt + 1) * 64], lhsT=p64[0:64, mt * 128:(mt + 1) * 128], rhs=ident[0:64, 0:64],
                                                  start=True, stop=True), reads=["p64", "cst"], writes=[pk2])
            P.op("vector", lambda e: e.tensor_copy(pTs, v3(pt2[:, 0:128], 64)), reads=[pk2], writes=["pTs"])
            pt3, pk3 = ps()
            for s_ in range(NS):
                vb_ = KV[s_ % 2]
                vbk = f"kv{s_ % 2}"
                P.dma("sync", vb_, vc[s_].rearrange("(mt p) c -> p mt c", p=128), writes=[vbk])
                for h in range(4):
                    col = s_ * 4 + h
                    for mt in range(2):
                        P.op("tensor", lambda e, vb_=vb_, h=h, mt=mt, col=col: e.matmul(pt3[:, col:col + 1], lhsT=vb_[:, mt, h * 128:(h + 1) * 128],
                                                                  rhs=pTs[:, mt, col:col + 1], start=(mt == 0), stop=(mt == 1)),
                             reads=[vbk, "pTs"], writes=[pk3])
            P.op("vector", lambda e: e.tensor_copy(oT[:, :, NTB:NTM], pt3[:, 0:64].rearrange("p (n h) -> p h n", h=4)), reads=[pk3], writes=["oT"])

        def merge_wo(nt):
            acc = [A(0, NTM), A(288, NTM)]
            sgt = A(576, NTM)
            merged = v3(AB(1024, KT * NTM), NTM)
            branches = [(C_G, pout, mix, "mix", 4), (C_G + D, rout, yg, "yg", 8), (C_G + 2 * D, xout, oT, "oT", 4)]
            for ip in range(8):
                for bi, (gc0, Wb, src, srck, ktb) in enumerate(branches):
                    wgt, wgk = wload(win, 0, D, gc0 + ip * 256, 256)
                    wbr, wbk = wload(Wb, 0, ktb * 128, ip * 256, 256)
                    for ci in range(2):
                        pg, pgk = ps()
                        po, pok = ps()
                        P.group("tensor", [lambda e, pg=pg, k=k, ci=ci, wgt=wgt: e.matmul(
                            pg[:, 0:nt], lhsT=wgt[:, k, ci * 128:(ci + 1) * 128], rhs=xn[:, k, 0:nt],
                            start=(k == 0), stop=(k == KT - 1)) for k in range(KT)], reads=[wgk, "xn"], writes=[pgk])
                        P.group("tensor", [lambda e, po=po, k=k, ci=ci, wbr=wbr, src=src, ktb=ktb: e.matmul(
                            po[:, 0:nt], lhsT=wbr[:, k, ci * 128:(ci + 1) * 128], rhs=src[:, k, 0:nt],
                            start=(k == 0), stop=(k == ktb - 1)) for k in range(ktb)], reads=[wbk, srck], writes=[pok])
                        P.op("scalar", lambda e, pg=pg: e.activation(sgt[:, 0:nt], pg[:, 0:nt], AF.Sigmoid), reads=[pgk, "sgt"], writes=["sgt"])
                        if bi == 0:
                            P.op("vector", lambda e, po=po, ci=ci: e.tensor_tensor(acc[ci][:, 0:nt], sgt[:, 0:nt], po[:, 0:nt], op=ALU.mult),
                                 reads=["sgt", pok, f"acc{ci}"], writes=[f"acc{ci}"])
                        else:
                            P.op("vector", lambda e, po=po: e.tensor_tensor(sgt[:, 0:nt], sgt[:, 0:nt], po[:, 0:nt], op=ALU.mult),
                                 reads=["sgt", pok], writes=["sgt"])
                            P.op("vector", lambda e, ci=ci: e.tensor_tensor(acc[ci][:, 0:nt], acc[ci][:, 0:nt], sgt[:, 0:nt], op=ALU.add),
                                 reads=["sgt", f"acc{ci}"], writes=[f"acc{ci}"])
                for ci in range(2):
                    i = ip * 2 + ci
                    P.op("vector", lambda e, ci=ci, i=i: e.tensor_copy(merged[:, i, 0:nt], acc[ci][:, 0:nt]), reads=[f"acc{ci}"], writes=[("merged", i)])

            def sink_o(j, pt, pk, m):
                P.op("vector", lambda e: e.tensor_tensor(xh[:, j, 0:nt], xh[:, j, 0:nt], pt[:, 0:nt], op=ALU.add),
                     reads=[pk, ("xh", j)], writes=[("xh", j)])
            proj(wo, 0, D, merged, "merged", nt, KT, sink_o)
            P.fence()

        P.op("vector", lambda e: e.memset(zr[:], 0.0), writes=["zr"])
        mem_kv()
        P.dma("sync", o_poolso, spool[:, 1:15, :])
        xT3 = xT.rearrange("(k p) n -> p k n", p=128)
        TSEL = [int(v) for v in os.environ.get('K_TILES', '0,1,2,3,4,5,6,7').split(',') if v != ''] if _DEBUG else list(range(8))
        STAGE = int(os.environ.get("K_STAGE", "9")) if _DEBUG else 9
        for t in TSEL:
            own = t >= 4
            last = t == 7
            nt = NTB + (NS if last else 0)
            P.dma("sync", xh[:, :, 0:NTB], xT3[:, :, t * NTB:(t + 1) * NTB], writes=["xh"])
            if last:
                P.dma("sync", xh[:, :, NTB:NTM], xT3[:, :, 2048:2048 + NS], writes=["xh"])
            ffn(f1g, f1u, f1d, G_F1, nt, True)
            rmsnorm(xh, G_MIX, nt, xn, "xh", "xn")
            if t >= 3:
                def sink_zp(j, pt, pk, m, nt=nt):
                    P.op("vector", lambda e: e.tensor_copy(zp[:, j, 15:15 + nt], pt[:, 0:nt]), reads=[pk], writes=["zp"])
                proj(win, C_POOL, 512, xn, "xn", nt, KT, sink_zp)

            def sink_zr(j, pt, pk, m, nt=nt):
                evac(j, zr[0:m, j, 1:1 + nt], pt[0:m, 0:nt], [pk], ["zr"])
            proj(win, C_R, RPW, xn, "xn", nt, KT, sink_zr)
            if own:
                def sink_zq(j, pt, pk, m, nt=nt):
                    evac(j, zq[:, j, 0:nt], pt[:, 0:nt], [pk], ["zq"])
                proj(win, C_XQ, 512, xn, "xn", nt, KT, sink_zq)
            if last:
                P.dma("sync", o_shiftT, zr[:, :, NTB:NTB + 1 + NS], reads=["zr"])
                P.dma("sync", o_poolpT, zp[:, :, NTB:NTB + 15], reads=["zp"])
                P.dma("sync", o_poolsn, zp[:, :, 15 + NTB:15 + NTM], reads=["zp"])
            P.fence()
            SUB = 9
            if STAGE >= 2:
                token_shift(last)
                for sub in range(2):
                    cs = slice(1 + sub * 128, 1 + (sub + 1) * 128)
                    if SUB >= 2:
                        rwkv_elem(cs, 128, False)
                    if SUB >= 3:
                        scan_prompt(cs, own)
                    if own and SUB >= 4:
                        rwkv_post(cs, 128, sub * 128, False)
                    P.fence()
                if last:
                    cs = slice(1 + NTB, 1 + NTM)
                    rwkv_elem(cs, NS, True)
                    P.fence()
                    sample_scan()
                    rwkv_post(cs, NS, NTB, False)
                    P.fence()
                P.op("vector", lambda e: e.tensor_copy(zr[:, :, 0:1], zc[:]), reads=["zc", "zr"], writes=["zr"])
            if own and STAGE >= 3:
                pool_branch(t - 4, nt, last)
                for sub in range(2):
                    attn_prompt(sub * 128)
                P.fence()
                if last:
                    attn_sample()
                    P.fence()
            if t >= 3:
                P.op("vector", lambda e: e.tensor_copy(zp[:, :, 0:15], zp[:, :, NTB:NTB + 15]), reads=["zp"], writes=["zp"])
            if own:
                P.fence()
                if STAGE >= 4:
                    merge_wo(nt)
                ffn(f2g, f2u, f2d, G_F2, nt, True)
                yo = v3(A(0, KT * NTM), NTM)
                rmsnorm(xh, G_FIN, nt, yo, "xh", "yo")
                P.dma("sync", yT[:, :, (t - 4) * NTB:(t - 3) * NTB], yo[:, :, 0:NTB], reads=["yo"])
                if last:
                    P.dma("sync", yT[:, :, 1024:1024 + NS], yo[:, :, NTB:NTM], reads=["yo"])
            P.fence()
        P.dma("sync", o_wkvp, S32[:].rearrange("p k n -> p (k n)"), reads=["S32"])
        P.finish()
    return nc


def _consts():
    c = np.zeros((128, 3584), np.float32)
    s = np.arange(128)[:, None]
    t = np.arange(128)[None, :]
    eye = np.eye(128, dtype=np.float32)
    c[:, 0:128] = eye
    c[:, 128:256] = -EDEC * (s <= t)
    c[:, 256:384] = -EDEC * (s < t)
    c[:, 384:512] = -EDEC * (s > t)
    bo = ((s // 64) == (t // 64)).astype(np.float32)
    c[:, 512:640] = bo
    c[:, 640:768] = bo / 64.0
    c[:, 768:1280] = np.tile((s < t).astype(np.float32), (1, 4))
    c[:, 1280:1792] = np.tile((s > t).astype(np.float32), (1, 4))
    c[:, 1792:2304] = np.tile((s <= t).astype(np.float32), (1, 4))
    c[:, 2304:2816] = np.tile(eye, (1, 4))
    c[:, 2816:3072] = 1.0
    c[:, 3072:3200] = eye
    c[:, 3200:3328] = 1.0
    return c


_NC_CACHE = {}
_DEBUG = False
_PACK_ONLY = False


def kernel(x_prompt, x_sample, mem_prompt, cache_mem_k, cache_mem_v, state_wkv, state_shift, state_pool,
           ffn1_norm_g, ffn1_w_gate, ffn1_w_up, ffn1_w_down, mix_norm_g, w_in,
           pool_group_w, pool_scale, pool_out,
           rwkv_mu, rwkv_w0, rwkv_w_up, rwkv_a0, rwkv_a_up, rwkv_g_up, rwkv_k_k, rwkv_k_a, rwkv_r_k,
           rwkv_ln_g, rwkv_ln_b, rwkv_out,
           mem_norm_g, w_mem_k, w_mem_v, xattn_out, w_o,
           ffn2_norm_g, ffn2_w_gate, ffn2_w_up, ffn2_w_down, final_norm_g):
    f = lambda a: np.ascontiguousarray(np.asarray(a, dtype=np.float32))
    x_prompt, x_sample, mem_prompt = f(x_prompt), f(x_sample), f(mem_prompt)
    B = x_prompt.shape[0]

    def kcols(v, n):
        v = f(v).reshape(-1)
        pad = np.zeros(n * 128, np.float32)
        pad[:v.size] = v
        return pad.reshape(n, 128).T

    vecs = np.zeros((128, 176), np.float32)
    vecs[:, 0:16] = kcols(ffn1_norm_g[0], 16)
    vecs[:, 16:32] = kcols(mix_norm_g[0], 16)
    vecs[:, 32:48] = kcols(mem_norm_g[0], 16)
    vecs[:, 48:64] = kcols(ffn2_norm_g[0], 16)
    vecs[:, 64:80] = kcols(final_norm_g, 16)
    vecs[:, 80:107] = kcols(rwkv_mu[0], 27)
    vecs[:, 107:111] = kcols(pool_scale[0], 4)
    vecs[:, 111:119] = kcols(rwkv_k_k[0], 8)
    vecs[:, 119:127] = kcols(rwkv_k_a[0], 8)
    vecs[:, 127:135] = kcols(rwkv_r_k[0], 8)
    vecs[:, 135:143] = kcols(rwkv_ln_g[0], 8)
    vecs[:, 143:151] = kcols(rwkv_ln_b[0], 8)
    rows = np.concatenate([f(rwkv_w0[0]), f(rwkv_a0[0])])[None, :]
    cst = _consts()
    if "nc" not in _NC_CACHE:
        _NC_CACHE["nc"] = build_nc()
    raw = {"f1g": ffn1_w_gate[0], "f1u": ffn1_w_up[0], "f1d": ffn1_w_down[0],
           "f2g": ffn2_w_gate[0], "f2u": ffn2_w_up[0], "f2d": ffn2_w_down[0],
           "win": w_in[0], "pout": pool_out[0], "rout": rwkv_out[0], "xout": xattn_out[0], "wo": w_o[0]}
    shared = {
        "pgw": f(pool_group_w[0]),
        "wup": f(rwkv_w_up[0]), "aup": f(rwkv_a_up[0]), "gup": f(rwkv_g_up[0]),
        "wmk": f(w_mem_k[0]), "wmv": f(w_mem_v[0]),
        "vecs": vecs, "rows": rows, "cst": cst,
    }
    for nm, W in raw.items():
        W = f(W)
        blks = BLOCKS.get(nm, [])
        arr = np.zeros((IN_SHAPES[nm][0], 128, 4096), np.float32)
        for bi, (r0, nrows, c0, ncols) in enumerate(blks):
            k = nrows // 128
            arr[bi, :, :k * ncols] = W[r0:r0 + nrows, c0:c0 + ncols].reshape(k, 128, ncols).transpose(1, 0, 2).reshape(128, k * ncols)
        shared[nm] = arr
    in_maps = []
    for c in range(8):
        b, half = c // 2, c % 2
        own = x_prompt[b, half * 1024:(half + 1) * 1024]
        prev = x_prompt[b, 0:1024] if half == 1 else np.zeros_like(own)
        xs = x_sample[c * NS:(c + 1) * NS, 0]
        xTc = np.ascontiguousarray(np.concatenate([prev, own, xs], axis=0).T)
        sl = slice(c * NS, (c + 1) * NS)
        sshT = np.zeros((27 * 128, NS), np.float32)
        sshT[:RPW] = f(state_shift[0, sl, 0]).T
        invc = np.zeros((4, 4, NTB + NS), np.float32)
        for t in range(4):
            pos = half * 1024 + t * NTB + np.arange(NTB)
            for g, w in enumerate((2, 4, 8, 16)):
                invc[t, g, :NTB] = 1.0 / np.minimum(pos + 1, w)
                invc[t, g, NTB:] = 1.0 / w
        m = dict(shared)
        m.update({
            "xT": xTc, "memT": np.ascontiguousarray(mem_prompt[b].T),
            "kc": f(cache_mem_k[0, sl]).reshape(NS, NMEM, 512), "vc": f(cache_mem_v[0, sl]).reshape(NS, NMEM, 512),
            "swkv": np.ascontiguousarray(f(state_wkv[0, sl]).transpose(2, 0, 1, 3)),
            "sshiftT": sshT,
            "spoolT": np.ascontiguousarray(f(state_pool[0, sl]).transpose(2, 0, 1)),
            "spool": f(state_pool[0, sl]),
            "invc": invc.reshape(4, -1),
        })
        in_maps.append(m)
    if _PACK_ONLY:
        return in_maps
    if "nc" not in _NC_CACHE:
        _NC_CACHE["nc"] = build_nc()
    res = run_bass_kernel_spmd(_NC_CACHE["nc"], in_maps, core_ids=list(range(8)))
    R = res.results
    return unpack(R, B)


def unpack(R, B=4):
    ND = NS * 8
    y_prompt = np.zeros((B, 2048, D), np.float32)
    y_sample = np.zeros((ND, 1, D), np.float32)
    mem_k = np.zeros((1, B, NMEM, 4, 128), np.float32)
    mem_v = np.zeros((1, B, NMEM, 4, 128), np.float32)
    wkv_p = np.zeros((1, B, 16, 64, 64), np.float32)
    sh_p = np.zeros((1, B, 1, RPW), np.float32)
    pl_p = np.zeros((1, B, 15, 512), np.float32)
    wkv_s = np.zeros((1, ND, 16, 64, 64), np.float32)
    sh_s = np.zeros((1, ND, 1, RPW), np.float32)
    pl_s = np.zeros((1, ND, 15, 512), np.float32)
    for c in range(8):
        b, half = c // 2, c % 2
        r = R[c]
        yT = np.asarray(r["yT"]).reshape(128, KT, 1024 + NS)
        yfull = yT.transpose(2, 1, 0).reshape(1024 + NS, D)
        y_prompt[b, half * 1024:(half + 1) * 1024] = yfull[:1024]
        y_sample[c * NS:(c + 1) * NS, 0] = yfull[1024:]
        sh = np.asarray(r["shiftT"]).reshape(128, 27, 1 + NS).transpose(2, 1, 0).reshape(1 + NS, 27 * 128)[:, :RPW]
        sh_s[0, c * NS:(c + 1) * NS, 0] = sh[1:]
        pl_s[0, c * NS:(c + 1) * NS, 0:14] = np.asarray(r["poolso"]).reshape(NS, 14, 512)
        pl_s[0, c * NS:(c + 1) * NS, 14] = np.asarray(r["poolsn"]).reshape(128, 4, NS).transpose(2, 1, 0).reshape(NS, 512)
        wkv_s[0, c * NS:(c + 1) * NS] = np.asarray(r["wkvs"]).reshape(64, NS, 16, 64).transpose(1, 2, 0, 3)
        if half == 0:
            mem_k[0, b] = np.asarray(r["memkT"]).reshape(128, 4, NMEM).transpose(2, 1, 0)
            mem_v[0, b] = np.asarray(r["memv"]).reshape(128, 2, 512).transpose(1, 0, 2).reshape(NMEM, 4, 128)
        else:
            sh_p[0, b, 0] = sh[0]
            pl_p[0, b] = np.asarray(r["poolpT"]).reshape(128, 4, 15).transpose(2, 1, 0).reshape(15, 512)
            wkv_p[0, b] = np.asarray(r["wkvp"]).reshape(2, 64, 8, 64).transpose(2, 0, 3, 1).reshape(16, 64, 64)
    return (y_prompt, y_sample, mem_k, mem_v, wkv_p, sh_p, pl_p, wkv_s, sh_s, pl_s)
```
